# Optimizing a Trainium2 kernel written in Bass

```python
import math
import jax, jax.numpy as jnp
from jax import lax
import numpy as np

D_MODEL = 1024
BATCH = 32
SEQ = 256
DEPTH = 4
DEC_BATCH = 2
DEC_SEQ = 2048
PAST_LEN = 512

GRID_W = 64
N_BRANCH = 4
BRANCH_W = D_MODEL // 2
SSM_GROUP = 16
SSM_GROUPS = BRANCH_W // SSM_GROUP
SSM_STATE = 64
FFT_GROUPS = 4
FFT_GW = BRANCH_W // FFT_GROUPS
MLA_HEADS = 8
NOPE_DIM = 64
ROPE_DIM = 32
V_DIM = BRANCH_W // MLA_HEADS
QK_DIM = NOPE_DIM + ROPE_DIM
Q_RANK = 3 * D_MODEL // 8
KV_RANK = D_MODEL // 4
ROPE_THETA = 10000.0
CONV_W = 3
D_FF = -(-8 * D_MODEL // (3 * 256)) * 256
Q_BLOCK = 128
EPS = 1e-6

OFF_SSM = 0
OFF_FFT = OFF_SSM + BRANCH_W
OFF_CQ = OFF_FFT + BRANCH_W
OFF_CKV = OFF_CQ + Q_RANK
OFF_KPE = OFF_CKV + KV_RANK
OFF_CONV = OFF_KPE + ROPE_DIM
IN_COLS = OFF_CONV + 3 * BRANCH_W

kernel_name = 'hybrid_diffusion_trunk_step'


def rms_norm(x, g):
    xf = x.astype(jnp.float32)
    y = xf * lax.rsqrt(jnp.mean(xf * xf, axis=-1, keepdims=True) + EPS)
    return (y * g.astype(jnp.float32)).astype(x.dtype)


def modulation(cond, w_ada, b_ada):
    m = jax.nn.silu(cond) @ w_ada + b_ada
    return jnp.split(m[:, None, :], 6, axis=-1)


def axial_rope(n_tok):
    rows = n_tok // GRID_W
    row = jnp.repeat(jnp.arange(rows, dtype=jnp.float32), GRID_W)
    col = jnp.tile(jnp.arange(GRID_W, dtype=jnp.float32), rows)
    n_freq = ROPE_DIM // 4
    inv = ROPE_THETA ** (-jnp.arange(n_freq, dtype=jnp.float32) / n_freq)
    ang = jnp.concatenate([row[:, None] * inv, col[:, None] * inv], axis=-1)
    return jnp.cos(ang), jnp.sin(ang)


def apply_rope(x, cos, sin):
    xf = x.astype(jnp.float32).reshape(x.shape[:-1] + (ROPE_DIM // 2, 2))
    xe, xo = xf[..., 0], xf[..., 1]
    c, s = cos[:, None, :], sin[:, None, :]
    out = jnp.stack([xe * c - xo * s, xe * s + xo * c], axis=-1)
    return out.reshape(x.shape).astype(x.dtype)


def attention(q, k, v):
    b, lq, h, dk = q.shape
    nb = lq // Q_BLOCK
    qb = q.reshape(b, nb, Q_BLOCK, h, dk).transpose(1, 0, 2, 3, 4)
    scale = dk ** -0.5

    def one_block(qblk):
        s = jnp.einsum('bqhd,bkhd->bhqk', qblk, k).astype(jnp.float32) * scale
        p = jax.nn.softmax(s, axis=-1).astype(v.dtype)
        return jnp.einsum('bhqk,bkhd->bqhd', p, v)

    o = lax.map(one_block, qb)
    return o.transpose(1, 0, 2, 3, 4).reshape(b, lq, h, v.shape[-1])


def mla_queries(c_q, q_a_norm_g, w_uq, q_norm_g, rope_cs):
    b, l, _ = c_q.shape
    q = (rms_norm(c_q, q_a_norm_g) @ w_uq).reshape(b, l, MLA_HEADS, QK_DIM)
    q = rms_norm(q, q_norm_g)
    if rope_cs is not None:
        q = jnp.concatenate([q[..., :NOPE_DIM], apply_rope(q[..., NOPE_DIM:], *rope_cs)], axis=-1)
    return q


def mla_keys_values(ckv_n, k_pe, w_ukv, k_norm_g, rope_cs):
    b, l, _ = ckv_n.shape
    kv = (ckv_n @ w_ukv).reshape(b, l, MLA_HEADS, NOPE_DIM + V_DIM)
    k_nope, v = kv[..., :NOPE_DIM], kv[..., NOPE_DIM:]
    k_rot = jnp.broadcast_to(k_pe[:, :, None, :], (b, l, MLA_HEADS, ROPE_DIM)).astype(k_nope.dtype)
    k = rms_norm(jnp.concatenate([k_nope, k_rot], axis=-1), k_norm_g)
    if rope_cs is not None:
        k = jnp.concatenate([k[..., :NOPE_DIM], apply_rope(k[..., NOPE_DIM:], *rope_cs)], axis=-1)
    return k, v


def _cplx_affine_combine(e1, e2):
    a1r, a1i, b1r, b1i = e1
    a2r, a2i, b2r, b2i = e2
    return (a2r * a1r - a2i * a1i,
            a2r * a1i + a2i * a1r,
            a2r * b1r - a2i * b1i + b2r,
            a2r * b1i + a2i * b1r + b2i)


def ssm_scan(uf, h0, lam_re, lam_im, log_dt, b_re, b_im, c_re, c_im, reverse):
    f32 = jnp.float32
    lam_re, lam_im = lam_re.astype(f32), lam_im.astype(f32)
    dt = jnp.exp(log_dt.astype(f32))[:, None]
    mag = jnp.exp(lam_re * dt)
    a_re, a_im = mag * jnp.cos(lam_im * dt), mag * jnp.sin(lam_im * dt)
    den = lam_re * lam_re + lam_im * lam_im
    f_re = ((a_re - 1.0) * lam_re + a_im * lam_im) / den
    f_im = (a_im * lam_re - (a_re - 1.0) * lam_im) / den
    b_re, b_im = b_re.astype(f32), b_im.astype(f32)
    bb_re = f_re[..., None] * b_re - f_im[..., None] * b_im
    bb_im = f_re[..., None] * b_im + f_im[..., None] * b_re
    x_re = jnp.einsum('blgc,gpc->blgp', uf, bb_re)
    x_im = jnp.einsum('blgc,gpc->blgp', uf, bb_im)
    a_re_t = jnp.broadcast_to(a_re, x_re.shape)
    a_im_t = jnp.broadcast_to(a_im, x_re.shape)
    acc_re, acc_im, h_re, h_im = lax.associative_scan(
        _cplx_affine_combine, (a_re_t, a_im_t, x_re, x_im), axis=1, reverse=reverse)
    if h0 is not None:
        s_re, s_im = h0[..., 0][:, None], h0[..., 1][:, None]
        h_re, h_im = (h_re + acc_re * s_re - acc_im * s_im,
                      h_im + acc_re * s_im + acc_im * s_re)
    y = (jnp.einsum('blgp,gcp->blgc', h_re, c_re.astype(f32))
         - jnp.einsum('blgp,gcp->blgc', h_im, c_im.astype(f32)))
    end = 0 if reverse else -1
    final = jnp.stack([h_re[:, end], h_im[:, end]], axis=-1)
    return y, final


def s5_branch(u, h0, lam_re, lam_im, log_dt, b_re, b_im, c_re, c_im, d_skip, w_glu):
    b, l, _ = u.shape
    uf = u.astype(jnp.float32).reshape(b, l, SSM_GROUPS, SSM_GROUP)
    ys, finals = [], []
    for d, rev in ((0, False), (1, True)):
        init = None if h0 is None else h0[:, d].astype(jnp.float32)
        y_d, h_d = ssm_scan(uf, init, lam_re[d], lam_im[d], log_dt[d], b_re[d], b_im[d],
                            c_re[d], c_im[d], rev)
        ys.append(y_d)
        finals.append(h_d)
    y = ys[0] + ys[1] + uf * d_skip.astype(jnp.float32).reshape(SSM_GROUPS, SSM_GROUP)
    y = jax.nn.gelu(y.reshape(b, l, BRANCH_W)).astype(u.dtype)
    y = y * jax.nn.sigmoid(y @ w_glu)
    return y, jnp.stack(finals, axis=1).astype(u.dtype)


def fourier_branch(u):
    b, l, _ = u.shape
    ug = u.astype(jnp.float32).reshape(b, l, FFT_GROUPS, FFT_GW)
    f = jnp.fft.fft2(ug, axes=(1, 3), norm='ortho')
    return jnp.real(f).reshape(b, l, BRANCH_W).astype(u.dtype)


def short_conv(z, conv_w):
    rhs = conv_w[:, None, :].astype(z.dtype)
    return lax.conv_general_dilated(z, rhs, window_strides=(1,), padding=((1, 1),),
                                    dimension_numbers=('NWC', 'WIO', 'NWC'),
                                    feature_group_count=z.shape[-1])


def mixer(xn, lp, ctx, rope_cs):
    b, l, _ = xn.shape
    z = xn @ lp['w_in']
    u_ssm = z[..., OFF_SSM:OFF_SSM + BRANCH_W]
    u_fft = z[..., OFF_FFT:OFF_FFT + BRANCH_W]
    c_q = z[..., OFF_CQ:OFF_CQ + Q_RANK]
    c_kv = z[..., OFF_CKV:OFF_CKV + KV_RANK]
    k_pe = z[..., OFF_KPE:OFF_KPE + ROPE_DIM]
    h_in = z[..., OFF_CONV:OFF_CONV + BRANCH_W]
    g_b = z[..., OFF_CONV + BRANCH_W:OFF_CONV + 2 * BRANCH_W]
    g_c = z[..., OFF_CONV + 2 * BRANCH_W:OFF_CONV + 3 * BRANCH_W]

    ckv_n = rms_norm(c_kv, lp['kv_a_norm_g'])
    q = mla_queries(c_q, lp['q_a_norm_g'], lp['w_uq'], lp['q_norm_g'], rope_cs)
    k, v = mla_keys_values(ckv_n, k_pe, lp['w_ukv'], lp['k_norm_g'], rope_cs)
    if ctx is None:
        h0 = None
    else:
        ckv_ctx, kpe_ctx, h0 = ctx
        k_c, v_c = mla_keys_values(ckv_ctx, kpe_ctx, lp['w_ukv'], lp['k_norm_g'], None)
        k = jnp.concatenate([k, k_c.astype(k.dtype)], axis=1)
        v = jnp.concatenate([v, v_c.astype(v.dtype)], axis=1)
    y_attn = attention(q, k, v).reshape(b, l, MLA_HEADS * V_DIM)

    y_ssm, ssm_final = s5_branch(u_ssm, h0, lp['ssm_lam_re'], lp['ssm_lam_im'], lp['ssm_log_dt'],
                                 lp['ssm_b_re'], lp['ssm_b_im'], lp['ssm_c_re'], lp['ssm_c_im'],
                                 lp['ssm_d'], lp['w_glu'])
    y_fft = fourier_branch(u_fft)
    y_conv = g_b * short_conv(g_c * h_in, lp['conv_w'])

    branches = jnp.stack([y_ssm, y_fft, y_attn.astype(y_ssm.dtype), y_conv], axis=2)
    proj = jnp.einsum('blkw,kwd->blkd', branches, lp['w_branch'])
    gates = jax.nn.sigmoid(xn @ lp['w_gate'] + lp['b_gate']).reshape(b, l, N_BRANCH, D_MODEL)
    out = jnp.sum(gates * proj, axis=2) @ lp['w_out']
    return out, ckv_n, k_pe, ssm_final


def swiglu(xn, w_ffn_in, w_ffn_out):
    gu = xn @ w_ffn_in
    g, u = gu[..., :D_FF], gu[..., D_FF:]
    return (jax.nn.silu(g) * u) @ w_ffn_out


def layer(x, cond, lp, ctx, rope_cs):
    sh_m, sc_m, g_m, sh_f, sc_f, g_f = modulation(cond, lp['w_ada'], lp['b_ada'])
    xn = rms_norm(x, lp['norm_mix_g']) * (1 + sc_m) + sh_m
    mix, ckv_n, k_pe, ssm_final = mixer(xn, lp, ctx, rope_cs)
    x = x + g_m * mix
    xn = rms_norm(x, lp['norm_ffn_g']) * (1 + sc_f) + sh_f
    x = x + g_f * swiglu(xn, lp['w_ffn_in'], lp['w_ffn_out'])
    return x, ckv_n, k_pe, ssm_final


def setup_inputs(seed: int = 0) -> dict:
    key = jax.random.key(seed)
    ks = iter(jax.random.split(key, 48))
    f32 = jnp.float32

    def nrm(shape, scale):
        return jax.random.normal(next(ks), shape, f32) * scale

    n_idx = jnp.arange(SSM_STATE, dtype=f32)
    sdir = (DEPTH, 2, SSM_GROUPS)
    return {
        'x_prompt': nrm((BATCH, SEQ, D_MODEL), 1.0),
        'x_sample': nrm((DEC_BATCH, DEC_SEQ, D_MODEL), 1.0),
        'cache_ckv': nrm((DEC_BATCH, DEPTH, PAST_LEN, KV_RANK), 1.0),
        'cache_kpe': nrm((DEC_BATCH, DEPTH, PAST_LEN, ROPE_DIM), 1.0),
        'state_ssm': nrm((DEC_BATCH, DEPTH, 2, SSM_GROUPS, SSM_STATE, 2), 0.1),
        'c': nrm((DEC_BATCH, D_MODEL), 1.0),
        'c_ctx': nrm((D_MODEL,), 1.0),
        'norm_mix_g': 1.0 + nrm((DEPTH, D_MODEL), 0.02),
        'norm_ffn_g': 1.0 + nrm((DEPTH, D_MODEL), 0.02),
        'w_ada': nrm((DEPTH, D_MODEL, 6 * D_MODEL), 0.5 * D_MODEL ** -0.5),
        'b_ada': nrm((DEPTH, 6 * D_MODEL), 0.01),
        'w_in': nrm((DEPTH, D_MODEL, IN_COLS), D_MODEL ** -0.5),
        'q_a_norm_g': 1.0 + nrm((DEPTH, Q_RANK), 0.02),
        'kv_a_norm_g': 1.0 + nrm((DEPTH, KV_RANK), 0.02),
        'w_uq': nrm((DEPTH, Q_RANK, MLA_HEADS * QK_DIM), Q_RANK ** -0.5),
        'w_ukv': nrm((DEPTH, KV_RANK, MLA_HEADS * (NOPE_DIM + V_DIM)), KV_RANK ** -0.5),
        'q_norm_g': 1.0 + nrm((DEPTH, QK_DIM), 0.02),
        'k_norm_g': 1.0 + nrm((DEPTH, QK_DIM), 0.02),
        'ssm_lam_re': -0.5 + nrm(sdir + (SSM_STATE,), 0.01),
        'ssm_lam_im': jnp.pi * n_idx + nrm(sdir + (SSM_STATE,), 0.01),
        'ssm_log_dt': jax.random.uniform(next(ks), sdir, f32, math.log(1e-3), math.log(1e-1)),
        'ssm_b_re': nrm(sdir + (SSM_STATE, SSM_GROUP), (2 * SSM_GROUP) ** -0.5),
        'ssm_b_im': nrm(sdir + (SSM_STATE, SSM_GROUP), (2 * SSM_GROUP) ** -0.5),
        'ssm_c_re': nrm(sdir + (SSM_GROUP, SSM_STATE), (2 * SSM_STATE) ** -0.5 * 4.0),
        'ssm_c_im': nrm(sdir + (SSM_GROUP, SSM_STATE), (2 * SSM_STATE) ** -0.5 * 4.0),
        'ssm_d': nrm((DEPTH, BRANCH_W), 1.0),
        'w_glu': nrm((DEPTH, BRANCH_W, BRANCH_W), BRANCH_W ** -0.5),
        'conv_w': nrm((DEPTH, CONV_W, BRANCH_W), CONV_W ** -0.5),
        'w_branch': nrm((DEPTH, N_BRANCH, BRANCH_W, D_MODEL), BRANCH_W ** -0.5),
        'w_gate': nrm((DEPTH, D_MODEL, N_BRANCH * D_MODEL), D_MODEL ** -0.5),
        'b_gate': nrm((DEPTH, N_BRANCH * D_MODEL), 0.01),
        'w_out': nrm((DEPTH, D_MODEL, D_MODEL), D_MODEL ** -0.5),
        'w_ffn_in': nrm((DEPTH, D_MODEL, 2 * D_FF), D_MODEL ** -0.5),
        'w_ffn_out': nrm((DEPTH, D_FF, D_MODEL), D_FF ** -0.5),
    }


def reference(x_prompt, x_sample, cache_ckv, cache_kpe, state_ssm, c, c_ctx,
              norm_mix_g, norm_ffn_g, w_ada, b_ada, w_in, q_a_norm_g, kv_a_norm_g,
              w_uq, w_ukv, q_norm_g, k_norm_g, ssm_lam_re, ssm_lam_im, ssm_log_dt,
              ssm_b_re, ssm_b_im, ssm_c_re, ssm_c_im, ssm_d, w_glu, conv_w,
              w_branch, w_gate, b_gate, w_out, w_ffn_in, w_ffn_out):
    rope_cs = axial_rope(x_sample.shape[1])
    cond_ctx = c_ctx[None, :]
    y_p, y_s = x_prompt, x_sample
    ckv_list, kpe_list, ssm_list = [], [], []
    for l in range(DEPTH):
        lp = {
            'norm_mix_g': norm_mix_g[l], 'norm_ffn_g': norm_ffn_g[l],
            'w_ada': w_ada[l], 'b_ada': b_ada[l], 'w_in': w_in[l],
            'q_a_norm_g': q_a_norm_g[l], 'kv_a_norm_g': kv_a_norm_g[l],
            'w_uq': w_uq[l], 'w_ukv': w_ukv[l], 'q_norm_g': q_norm_g[l], 'k_norm_g': k_norm_g[l],
            'ssm_lam_re': ssm_lam_re[l], 'ssm_lam_im': ssm_lam_im[l], 'ssm_log_dt': ssm_log_dt[l],
            'ssm_b_re': ssm_b_re[l], 'ssm_b_im': ssm_b_im[l],
            'ssm_c_re': ssm_c_re[l], 'ssm_c_im': ssm_c_im[l],
            'ssm_d': ssm_d[l], 'w_glu': w_glu[l], 'conv_w': conv_w[l],
            'w_branch': w_branch[l], 'w_gate': w_gate[l], 'b_gate': b_gate[l], 'w_out': w_out[l],
            'w_ffn_in': w_ffn_in[l], 'w_ffn_out': w_ffn_out[l],
        }
        y_p, ckv_n, k_pe, ssm_final = layer(y_p, cond_ctx, lp, None, None)
        ckv_list.append(ckv_n)
        kpe_list.append(k_pe)
        ssm_list.append(ssm_final)
        y_s, _, _, _ = layer(y_s, c, lp, (cache_ckv[:, l], cache_kpe[:, l], state_ssm[:, l]), rope_cs)
    new_ckv = jnp.stack(ckv_list, axis=1)
    new_kpe = jnp.stack(kpe_list, axis=1)
    new_ssm = jnp.stack(ssm_list, axis=1)
    return (y_p, y_s, new_ckv, new_kpe, new_ssm)
```

```python
import math
import numpy as np
import ml_dtypes
import concourse.bass as bass
import concourse.mybir as mybir
from concourse.bass_utils import run_bass_kernel_spmd

F32 = mybir.dt.float32
BF16 = mybir.dt.bfloat16
I32 = mybir.dt.int32
ALU = mybir.AluOpType
AF = mybir.ActivationFunctionType
NPBF = ml_dtypes.bfloat16

D = 1024
NT = 2048
DEPTH = 4
NL = DEPTH
PAST = 512
NK = NT + PAST
BW = 512
QR = 384
KVR = 256
RD = 32
QK = 96
NH = 8
DFF = 2816
OFF_SSM, OFF_FFT, OFF_CQ, OFF_CKV, OFF_KPE, OFF_CONV = 0, 512, 1024, 1408, 1664, 1696
IN_COLS = 3232
EPS = 1e-6
BIG = 30000.0
TT = 512
NTT = NT // TT
BRANCHES = 'ABCD'
import os
ATT_SKIP = bool(int(os.environ.get('ATT_SKIP', '0')))
CUT = int(os.environ.get('CUT', '9'))
HSTOP = int(os.environ.get('HSTOP', '9'))
ASTOP = int(os.environ.get('ASTOP', '9'))
A3 = int(os.environ.get('A3', '9'))


class Sched:
    def __init__(self, nc, nds=24):
        self.nc = nc
        self.eng = {'pe': nc.tensor, 'act': nc.scalar, 'dve': nc.vector, 'pool': nc.gpsimd, 'sp': nc.sync}
        self.sem = {e: nc.alloc_semaphore(name='sem_' + e) for e in self.eng}
        self.cnt = {e: 0 for e in self.eng}
        self.dsem = [nc.alloc_semaphore(name='dsem%d' % i) for i in range(nds)]
        self.dcnt = [0] * nds
        self.dpool = {'sp': list(range(0, nds // 2)), 'pool': list(range(nds // 2, nds))}
        self.dnext = {'sp': 0, 'pool': 0}
        self.waited = {e: {} for e in self.eng}
        self.lastw = {}
        self.readers = {}
        self.nwaits = 0

    def _wait(self, e, toks):
        w = self.waited[e]
        best = {}
        for t in toks:
            if t is None:
                continue
            key = (t[0], t[1])
            if t[0] == 'c' and t[1] == e and e == 'pe':
                continue
            if w.get(key, 0) >= t[2]:
                continue
            if best.get(key, 0) < t[2]:
                best[key] = t[2]
        for key, v in best.items():
            s = self.sem[key[1]] if key[0] == 'c' else self.dsem[key[1]]
            self.eng[e].wait_ge(s, v)
            w[key] = v
            self.nwaits += 1

    def _deps(self, reads, writes):
        deps = set()
        for k in reads:
            t = self.lastw.get(k)
            if t is not None:
                deps.add(t)
        for k in writes:
            t = self.lastw.get(k)
            if t is not None:
                deps.add(t)
            for t in self.readers.get(k, {}).values():
                deps.add(t)
        return deps

    def _commit(self, tok, reads, writes):
        for k in writes:
            self.lastw[k] = tok
            self.readers[k] = {}
        for k in reads:
            r = self.readers.setdefault(k, {})
            r[(tok[0], tok[1])] = tok

    mute = False

    def op(self, e, fn, reads=(), writes=()):
        if self.mute:
            return
        self._wait(e, self._deps(reads, writes))
        inst = fn(self.eng[e])
        self.cnt[e] += 1
        inst.then_inc(self.sem[e], 1)
        self._commit(('c', e, self.cnt[e]), reads, writes)

    def dma(self, e, out, in_, reads=(), writes=(), **kw):
        if self.mute:
            return
        deps = self._deps(reads, writes)
        pl = self.dpool[e]
        i = pl[self.dnext[e]]
        self.dnext[e] = (self.dnext[e] + 1) % len(pl)
        if self.dcnt[i] > 0:
            deps.add(('d', i, self.dcnt[i] * 16))
        self._wait(e, deps)
        inst = self.eng[e].dma_start(out=out, in_=in_, **kw)
        self.dcnt[i] += 1
        inst.then_inc(self.dsem[i], 16)
        self._commit(('d', i, self.dcnt[i] * 16), reads, writes)

    def barrier(self):
        toks = [('c', e, self.cnt[e]) for e in self.eng if self.cnt[e] > 0]
        toks += [('d', i, c * 16) for i, c in enumerate(self.dcnt) if c > 0]
        for e in self.eng:
            self._wait(e, toks)
        self.lastw = {}
        self.readers = {}


def build(dbg=None, nlayers=NL):
    dbg = dbg or []
    nc = bass.Bass("TRN2", target_bir_lowering=False)
    S = Sched(nc)

    def din(name, shape, dt=F32):
        return nc.dram_tensor(name, list(shape), dt, kind="ExternalInput").ap()

    def dout(name, shape, dt=F32):
        return nc.dram_tensor(name, list(shape), dt, kind="ExternalOutput").ap()

    xin = din("xin", [NT, D])
    condT = din("condT", [128, 8])
    cache_ckv = din("cache_ckv", [NL, PAST, KVR])
    cache_kpe = din("cache_kpe", [NL, PAST, RD])
    h0 = din("h0", [NL, 128, 128])
    w_ada = din("w_ada", [NL, D, 6 * D])
    b_adaT = din("b_adaT", [NL, 128, 48])
    nmgT = din("nmgT", [NL, 128, 8])
    nfgT = din("nfgT", [NL, 128, 8])
    w_in = din("w_in", [NL, D, IN_COLS])
    qagT = din("qagT", [NL, 128, 3])
    kvag = din("kvag", [NL, 128, 2])
    w_uq = din("w_uq", [NL, QR, NH * QK])
    w_uq_sw = din("w_uq_sw", [NL, QR, NH * QK])
    w_uk = din("w_uk", [NL, KVR, NH * QK])
    w_uv = din("w_uv", [NL, KVR, NH * 64])
    qng = din("qng", [NL, 128, 4])
    dskT = din("dskT", [NL, 128, 32])
    lam2 = din("lam2", [NL, 128, 2, 32])
    logdt2 = din("logdt2", [NL, 128, 32])
    b2 = din("b2", [NL, 128, 2, 32, 16])
    c2 = din("c2", [NL, 128, 2, 32, 16])
    ramp32 = din("ramp32", [128, 32, 24])
    halfmask = din("halfmask", [128, 2])
    w_glu = din("w_glu", [NL, BW, BW])
    convwT = din("convwT", [NL, 128, 4, 3])
    w_branch = din("w_branch", [NL, 4, BW, D])
    w_gate = din("w_gate", [NL, D, 4 * D])
    b_gateT = din("b_gateT", [NL, 128, 32])
    w_out = din("w_out", [NL, D, D])
    w_ffn_in = din("w_ffn_in", [NL, D, 2 * DFF])
    w_ffn_out = din("w_ffn_out", [NL, DFF, D])
    ropeC = din("ropeC", [128, NT], BF16)
    ropeS = din("ropeS", [128, NT], BF16)
    qind = din("qind", [8, NT], BF16)
    kind = din("kind", [8, NK], BF16)
    dftC = din("dftC", [NT, NT], BF16)
    dftS = din("dftS", [NT, NT], BF16)
    ccsc = din("ccsc", [128, 256], BF16)
    seqflag = din("seqflag", [128, 1])
    smask = din("smask", [128, 2, 257])
    tzmask = din("tzmask", [128, 2, 512])
    selpad = din("selpad", [128, 8, 240], BF16)
    seltpad = din("seltpad", [128, 8, 240], BF16)
    shiftm = din("shiftm", [32, 2, 96], BF16)
    identf = din("identf", [128, 128])

    y_out = dout("y_out", [NT, D])
    ckv_out = dout("ckv_out", [NL, NT, KVR])
    kpe_out = dout("kpe_out", [NL, NT, RD])
    ssm_out = dout("ssm_out", [NL, 128, 512])
    dbg_out = {}

    from contextlib import ExitStack
    es = ExitStack()

    _uid = [0]

    def T(name, shape, dt=F32):
        _uid[0] += 1
        return nc.sbuf_tensor("%s_%d" % (name, _uid[0]), list(shape), dt)

    def sb(name, shape, dt=F32):
        return es.enter_context(T(name, list(shape), dt))

    with es:
        x = sb("x", [128, 8, NT])
        xn = sb("xn", [128, 8, NT], BF16)
        ones_bf = sb("ones_bf", [128, 128], BF16)
        ident = sb("ident", [128, 128])
        ident_bf = sb("ident_bf", [128, 128], BF16)
        mod = sb("mod", [128, 48])
        vecs = sb("vecs", [128, 64])
        epsb = sb("epsb", [128, 1])
        psum = [es.enter_context(nc.psum_tensor("ps%d" % i, [128, 512], F32)) for i in range(8)]
        pstate = {'i': 0}

        def PS():
            i = pstate['i']
            pstate['i'] = (i + 1) % 8
            return psum[i], 'ps%d' % i

        def mm(out, lhsT, rhs, start, stop, reads, writes):
            S.op('pe', lambda e: e.matmul(out, lhsT, rhs, start=start, stop=stop), reads, writes)

        def tr(out, in_, idn, reads, writes):
            S.op('pe', lambda e: e.transpose(out, in_, idn), reads, writes)

        def act(out, in_, func, reads, writes, bias=None, scale=None):
            kw = {}
            if bias is not None:
                kw['bias'] = bias
            if scale is not None:
                kw['scale'] = scale
            S.op('act', lambda e: e.activation(out=out, in_=in_, func=func, **kw), reads, writes)

        def tt(out, in0, in1, op, reads, writes, eng='dve'):
            S.op(eng, lambda e: e.tensor_tensor(out=out, in0=in0, in1=in1, op=op), reads, writes)

        def ts(out, in0, s1, s2, op0, op1, reads, writes, eng='dve'):
            if op1 is None:
                S.op(eng, lambda e: e.tensor_scalar(out=out, in0=in0, scalar1=s1, scalar2=None, op0=op0), reads, writes)
            else:
                S.op(eng, lambda e: e.tensor_scalar(out=out, in0=in0, scalar1=s1, scalar2=s2, op0=op0, op1=op1), reads, writes)

        def stt(out, in0, scalar, in1, op0, op1, reads, writes):
            S.op('dve', lambda e: e.scalar_tensor_tensor(out=out, in0=in0, scalar=scalar, in1=in1, op0=op0, op1=op1), reads, writes)

        def cp(out, in_, reads, writes, eng='dve'):
            S.op(eng, lambda e: e.tensor_copy(out=out, in_=in_), reads, writes)

        def recip(out, in_, reads, writes):
            S.op('dve', lambda e: e.reciprocal(out=out, in_=in_), reads, writes)

        def memset(t_ap, val, writes, eng='dve'):
            S.op(eng, lambda e: e.memset(t_ap, val), (), writes)

        def dump(name, ap, shape, dt, key):
            if name in dbg:
                o = dout("dbg_" + name, shape, dt)
                dbg_out[name] = o
                S.dma('sp', o, ap, reads=[key], writes=['dbg_' + name])

        def rsqrt_inplace(ap, key, scale):
            act(ap, ap, AF.Sqrt, [key, 'epsb'], [key], bias=epsb[0:ap.shape[0], 0:1], scale=scale)
            recip(ap, ap, [key], [key])

        memset(ones_bf[:], 1.0, ['ones_bf'])
        memset(epsb[:], EPS, ['epsb'])
        S.dma('sp', ident[:], identf[:, :], (), ['ident'])
        cp(ident_bf[:], ident[:], ['ident'], ['ident_bf'])

        with T("xtm", [128, 4, D], F32) as xtm:
            for t in range(NTT):
                S.dma('sp', xtm[:], xin[t * TT:(t + 1) * TT, :].rearrange("(n p) c -> p n c", p=128), (), ['xtm'])
                for oc in range(8):
                    ps, pk = PS()
                    for n in range(4):
                        tr(ps[:, n * 128:(n + 1) * 128], xtm[:, n, oc * 128:(oc + 1) * 128], ident[:], ['xtm', 'ident'], [pk])
                    if oc % 2 == 0:
                        cp(x[:, oc, t * TT:(t + 1) * TT], ps[:], [pk], [('x', oc, t)])
                    else:
                        act(x[:, oc, t * TT:(t + 1) * TT], ps[:], AF.Copy, [pk], [('x', oc, t)])
            S.barrier()

        def load_w(dst, src, kchunks, keyw, eng='pool'):
            S.dma(eng, dst[:, 0:kchunks, :], src.rearrange("(kc p) n -> p kc n", p=128), (), [keyw])

        def rmsnorm_mod(l, gname, sc_c, sh_c):
            with T("nsq", [128, 2, TT], BF16) as nsq, \
                    T("nrs", [128, TT], F32) as nrs, \
                    T("ntmp", [128, 2, TT], F32) as ntmp:
                for t in range(NTT):
                    sl = slice(t * TT, (t + 1) * TT)
                    ps, pk = PS()
                    for oc in range(8):
                        b = oc % 2
                        act(nsq[:, b, :], x[:, oc, sl], AF.Square, [('x', oc, t)], [('nsq', b)])
                        mm(ps[:], ones_bf[:], nsq[:, b, :], oc == 0, oc == 7, ['ones_bf', ('nsq', b)], [pk])
                    cp(nrs[:], ps[:], [pk], ['nrs'])
                    rsqrt_inplace(nrs[:], 'nrs', 1.0 / D)
                    for oc in range(8):
                        b = oc % 2
                        tt(ntmp[:, b, :], x[:, oc, sl], nrs[:], ALU.mult, [('x', oc, t), 'nrs'], [('ntmp', b)])
                        act(xn[:, oc, sl], ntmp[:, b, :], AF.Identity, [('ntmp', b), 'vecs'], [('xn', oc, t)],
                            bias=vecs[:, sh_c + oc:sh_c + oc + 1], scale=vecs[:, sc_c + oc:sc_c + oc + 1])
            S.barrier()

        for l in range(nlayers):
            with T("wada", [128, 8, 1024], BF16) as wada, \
                    T("scond", [128, 8], BF16) as scond, \
                    T("ctmp", [128, 8], F32) as ctmp, \
                    T("mtmp", [128, 64], F32) as mtmp:
                S.dma('sp', ctmp[:], condT[:, :], (), ['ctmp'])
                act(scond[:], ctmp[:], AF.Silu, ['ctmp'], ['scond'])
                S.dma('sp', mtmp[:, 0:48], b_adaT[l], (), ['mtmp'])
                S.dma('sp', mtmp[:, 48:56], nmgT[l], (), ['mtmp'])
                S.dma('sp', mtmp[:, 56:64], nfgT[l], (), ['mtmp'])
                psm, pkm = PS()
                for piece in range(6):
                    load_w(wada, w_ada[l][:, piece * 1024:(piece + 1) * 1024], 8, 'wada')
                    for c in range(8):
                        col = piece * 8 + c
                        for kc in range(8):
                            mm(psm[:, col:col + 1], wada[:, kc, c * 128:(c + 1) * 128], scond[:, kc:kc + 1],
                               kc == 0, kc == 7, ['wada', 'scond'], [pkm])
                tt(mod[:], psm[:, 0:48], mtmp[:, 0:48], ALU.add, [pkm, 'mtmp'], ['mod'])
                stt(vecs[:, 0:8], mod[:, 8:16], 1.0, mtmp[:, 48:56], ALU.add, ALU.mult, ['mod', 'mtmp'], ['vecs'])
                cp(vecs[:, 8:16], mod[:, 0:8], ['mod'], ['vecs'])
                stt(vecs[:, 16:24], mod[:, 32:40], 1.0, mtmp[:, 56:64], ALU.add, ALU.mult, ['mod', 'mtmp'], ['vecs'])
                cp(vecs[:, 24:32], mod[:, 24:32], ['mod'], ['vecs'])
                S.barrier()
            dump("mod%d" % l, mod[:], [128, 48], F32, 'mod')

            rmsnorm_mod(l, None, 0, 8)
            dump("xn%d" % l, xn[:], [128, 8, NT], BF16, ('xn', 0, 0))

            def epilogue(k, ysrc, ykeyfn, wb, wg, wo, gp, sg, t):
                sl = slice(t * TT, (t + 1) * TT)
                for oc in range(8):
                    pa, pka = PS()
                    for kc in range(4):
                        mm(pa[:], wb[:, kc, oc * 128:(oc + 1) * 128], ysrc(kc, t), kc == 0, kc == 3,
                           ['wb', ykeyfn(kc, t)], [pka])
                    pb, pkb = PS()
                    for kc in range(8):
                        mm(pb[:], wg[:, kc, oc * 128:(oc + 1) * 128], xn[:, kc, sl], kc == 0, kc == 7,
                           ['wg', ('xn', kc, t)], [pkb])
                    b = oc % 2
                    act(sg[:, b, :], pb[:], AF.Sigmoid, [pkb, 'vecs'], [('sg', b)],
                        bias=vecs[:, 32 + k * 8 + oc:32 + k * 8 + oc + 1])
                    tt(gp[:, oc, :], pa[:], sg[:, b, :], ALU.mult, [pka, ('sg', b)], [('gp', oc)])
                for oc2 in range(8):
                    pc, pkc = PS()
                    for kc in range(8):
                        mm(pc[:], wo[:, kc, oc2 * 128:(oc2 + 1) * 128], gp[:, kc, :], kc == 0, kc == 7,
                           ['wo', ('gp', kc)], [pkc])
                    stt(x[:, oc2, sl], pc[:], mod[:, 16 + oc2:17 + oc2], x[:, oc2, sl], ALU.mult, ALU.add,
                        [pkc, 'mod', ('x', oc2, t)], [('x', oc2, t)])

            S.dma('sp', vecs[:, 32:64], b_gateT[l], (), ['vecs'])

            def run_epi(k, ybuf):
                with T("wo", [128, 8, D], BF16) as wo, \
                        T("wb", [128, 4, D], BF16) as wb, \
                        T("wg", [128, 8, D], BF16) as wg, \
                        T("gp", [128, 8, TT], BF16) as gp, \
                        T("sg", [128, 2, TT], F32) as sg:
                    load_w(wb, w_branch[l, k], 4, 'wb')
                    load_w(wg, w_gate[l][:, k * D:(k + 1) * D], 8, 'wg')
                    load_w(wo, w_out[l], 8, 'wo')
                    for t in range(NTT if not int(os.environ.get('SKE', '0')) else 0):
                        epilogue(k, lambda kc, t_: ybuf[:, kc, t_ * TT:(t_ + 1) * TT],
                                 lambda kc, t_: ('y', kc, t_), wb, wg, wo, gp, sg, t)
                    S.barrier()

            if 'D' in BRANCHES:
              with T("ybuf", [128, 4, NT], BF16) as ybuf:
                with T("wc", [128, 8, 512], BF16) as wc, \
                        T("Tb", [128, 4, 8, 258], BF16) as Tb, \
                        T("ctm", [128, 3, TT], F32) as ctm, \
                        T("cvw", [128, 4, 3], F32) as cvw, \
                        T("sflag", [128, 1], F32) as sflag:
                    S.dma('sp', cvw[:], convwT[l], (), ['cvw'])
                    S.dma('sp', sflag[:], seqflag[:, :], (), ['sflag'])
                    for oc in range(4):
                        memset(Tb[:, oc, :, 0:1], 0.0, [('Tb', oc)])
                        memset(Tb[:, oc, :, 257:258], 0.0, [('Tb', oc)])
                    load_w(wc, w_in[l][:, OFF_CONV:OFF_CONV + 512], 8, 'wc')
                    for t in range(NTT):
                        sl = slice(t * TT, (t + 1) * TT)
                        for oc in range(4):
                            ph, pkh = PS()
                            for kc in range(8):
                                mm(ph[:], wc[:, kc, oc * 128:(oc + 1) * 128], xn[:, kc, sl], kc == 0, kc == 7,
                                   ['wc', ('xn', kc, t)], [pkh])
                            act(Tb[:, oc, 2 * t:2 * t + 2, 1:257], ph[:].rearrange("p (a b) -> p a b", a=2), AF.Copy, [pkh], [('Tb', oc)])
                    load_w(wc, w_in[l][:, OFF_CONV + 1024:OFF_CONV + 1536], 8, 'wc')
                    for t in range(NTT):
                        sl = slice(t * TT, (t + 1) * TT)
                        for oc in range(4):
                            pg, pkg = PS()
                            for kc in range(8):
                                mm(pg[:], wc[:, kc, oc * 128:(oc + 1) * 128], xn[:, kc, sl], kc == 0, kc == 7,
                                   ['wc', ('xn', kc, t)], [pkg])
                            tt(Tb[:, oc, 2 * t:2 * t + 2, 1:257], pg[:].rearrange("p (a b) -> p a b", a=2),
                               Tb[:, oc, 2 * t:2 * t + 2, 1:257], ALU.mult, [pkg, ('Tb', oc)], [('Tb', oc)])
                    for oc in range(4):
                        ts(Tb[:, oc, 1:8, 0:1], Tb[:, oc, 0:7, 256:257], sflag[:, 0:1], None, ALU.mult, None,
                           [('Tb', oc), 'sflag'], [('Tb', oc)])
                        ts(Tb[:, oc, 0:7, 257:258], Tb[:, oc, 1:8, 1:2], sflag[:, 0:1], None, ALU.mult, None,
                           [('Tb', oc), 'sflag'], [('Tb', oc)])
                    load_w(wc, w_in[l][:, OFF_CONV + 512:OFF_CONV + 1024], 8, 'wc')
                    for t in range(NTT):
                        sl = slice(t * TT, (t + 1) * TT)
                        for oc in range(4):
                            pgb, pkgb = PS()
                            for kc in range(8):
                                mm(pgb[:], wc[:, kc, oc * 128:(oc + 1) * 128], xn[:, kc, sl], kc == 0, kc == 7,
                                   ['wc', ('xn', kc, t)], [pkgb])
                            b = oc % 3
                            cv = ctm[:, b, :].rearrange("p (a b) -> p a b", a=2)
                            ts(cv, Tb[:, oc, 2 * t:2 * t + 2, 1:257], cvw[:, oc, 1:2], None, ALU.mult, None,
                               [('Tb', oc), 'cvw'], [('ctm', b)])
                            stt(cv, Tb[:, oc, 2 * t:2 * t + 2, 0:256], cvw[:, oc, 0:1], cv, ALU.mult, ALU.add,
                                [('Tb', oc), ('ctm', b), 'cvw'], [('ctm', b)])
                            stt(cv, Tb[:, oc, 2 * t:2 * t + 2, 2:258], cvw[:, oc, 2:3], cv, ALU.mult, ALU.add,
                                [('Tb', oc), ('ctm', b), 'cvw'], [('ctm', b)])
                            tt(ybuf[:, oc, sl], pgb[:], ctm[:, b, :], ALU.mult, [pkgb, ('ctm', b)], [('y', oc, t)])
                    S.barrier()
                dump("yconv%d" % l, ybuf[:], [128, 4, NT], BF16, ('y', 0, 0))
                run_epi(3, ybuf)

            if 'B' in BRANCHES:
              with T("ybuf", [128, 4, NT], BF16) as ybuf:
                with T("AB", [128, 16, 4, 256], BF16) as AB:
                    with T("wf", [128, 8, 512], BF16) as wf, \
                            T("uf", [128, 4, NT], BF16) as uf, \
                            T("ccsc_sb", [128, 256], BF16) as ccsc_sb:
                        load_w(wf, w_in[l][:, OFF_FFT:OFF_FFT + 512], 8, 'wf')
                        S.dma('sp', ccsc_sb[:], ccsc[:, :], (), ['ccsc'])
                        for t in range(NTT):
                            sl = slice(t * TT, (t + 1) * TT)
                            for grp in range(4):
                                pu, pku = PS()
                                for kc in range(8):
                                    mm(pu[:], wf[:, kc, grp * 128:(grp + 1) * 128], xn[:, kc, sl], kc == 0, kc == 7,
                                       ['wf', ('xn', kc, t)], [pku])
                                if grp % 2 == 0:
                                    act(uf[:, grp, sl], pu[:], AF.Copy, [pku], [('uf', grp, t)])
                                else:
                                    cp(uf[:, grp, sl], pu[:], [pku], [('uf', grp, t)])
                        for grp in range(4):
                            for n2 in range(8):
                                pa, pka = PS()
                                for j in range(2):
                                    n = n2 * 2 + j
                                    mm(pa[:, j * 256:(j + 1) * 256], uf[:, grp, n * 128:(n + 1) * 128], ccsc_sb[:], True, True,
                                       [('uf', grp, n // 4), 'ccsc'], [pka])
                                if n2 % 2 == 0:
                                    act(AB[:, n2 * 2:n2 * 2 + 2, grp, :], pa[:].rearrange("p (a b) -> p a b", a=2), AF.Copy, [pka], [('AB', grp)])
                                else:
                                    cp(AB[:, n2 * 2:n2 * 2 + 2, grp, :], pa[:].rearrange("p (a b) -> p a b", a=2), [pka], [('AB', grp)])
                        S.barrier()
                    with T("tabC", [128, 2, 16, 256], BF16) as tabC, T("tabS", [128, 2, 16, 256], BF16) as tabS:
                        for kt in range(8):
                            b = kt % 2
                            S.dma('sp', tabC[:, b], dftC[:, kt * 256:(kt + 1) * 256].rearrange("(nt p) k -> p nt k", p=128), (), [('tabC', b)])
                            S.dma('sp', tabS[:, b], dftS[:, kt * 256:(kt + 1) * 256].rearrange("(nt p) k -> p nt k", p=128), (), [('tabS', b)])
                            for g2 in range(2):
                                py, pky = PS()
                                for gg in range(2):
                                    grp = g2 * 2 + gg
                                    for n in range(16):
                                        mm(py[:, gg * 256:(gg + 1) * 256], AB[:, n, grp, 0:128], tabC[:, b, n, :], n == 0, False,
                                           [('AB', grp), ('tabC', b)], [pky])
                                        mm(py[:, gg * 256:(gg + 1) * 256], AB[:, n, grp, 128:256], tabS[:, b, n, :], False, n == 15,
                                           [('AB', grp), ('tabS', b)], [pky])
                                cp(ybuf[:, g2 * 2:g2 * 2 + 2, kt * 256:(kt + 1) * 256], py[:].rearrange("p (a b) -> p a b", a=2),
                                   [pky], [('y', g2 * 2, kt // 2), ('y', g2 * 2 + 1, kt // 2)])
                        S.barrier()
                dump("yfft%d" % l, ybuf[:], [128, 4, NT], BF16, ('y', 0, 0))
                run_epi(1, ybuf)
            if 'C' in BRANCHES:
              with T("ybuf", [128, 4, NT], BF16) as ybuf:
                with T("cqn", [128, 3, NT], BF16) as cqn, T("ckvT", [128, 2, NK], BF16) as ckvT, \
                        T("kpeT", [128, NK], BF16) as kpeT:
                    with T("wa", [128, 8, 384], BF16) as wa, T("wkv", [128, 8, 288], BF16) as wkv, T("ctk", [128, 4, 288], F32) as ctk, T("cqf", [128, 3, TT], F32) as cqf, \
                            T("asq", [128, 2, TT], BF16) as asq, T("ars", [128, TT], F32) as ars, \
                            T("atmp", [128, 2, TT], F32) as atmp, T("tkm", [128, 2, 288], F32) as tkm, \
                            T("junk", [128, 256], F32) as junk, T("ss", [128, 2, 8], F32) as ss, \
                            T("kvag_sb", [128, 2], F32) as kvag_sb, T("qag", [128, 3], F32) as qag:
                        load_w(wa, w_in[l][:, OFF_CQ:OFF_CQ + 384], 8, 'wa')
                        load_w(wkv, w_in[l][:, OFF_CKV:OFF_CKV + 288], 8, 'wkv')
                        memset(ss[:, 0, :], 1.0, [('ss', 0)])
                        memset(ss[:, 1, :], 1.0, [('ss', 1)])
                        S.dma('sp', kvag_sb[:], kvag[l], (), ['kvag'])
                        S.dma('sp', qag[:], qagT[l], (), ['qag'])
                        for t in range(NTT if not int(os.environ.get('SKQ','0')) else 0):
                            sl = slice(t * TT, (t + 1) * TT)
                            for c in range(3):
                                pq, pkq = PS()
                                for kc in range(8):
                                    mm(pq[:], wa[:, kc, c * 128:(c + 1) * 128], xn[:, kc, sl], kc == 0, kc == 7, ['wa', ('xn', kc, t)], [pkq])
                                cp(cqf[:, c, :], pq[:], [pkq], [('cqf', c)])
                            pss, pkss = PS()
                            for c in range(3):
                                act(asq[:, c % 2, :], cqf[:, c, :], AF.Square, [('cqf', c)], [('asq', c % 2)])
                                mm(pss[:], ones_bf[:], asq[:, c % 2, :], c == 0, c == 2, ['ones_bf', ('asq', c % 2)], [pkss])
                            cp(ars[:], pss[:], [pkss], ['ars'])
                            rsqrt_inplace(ars[:], 'ars', 1.0 / QR)
                            for c in range(3):
                                tt(atmp[:, c % 2, :], cqf[:, c, :], ars[:], ALU.mult, [('cqf', c), 'ars'], [('atmp', c % 2)])
                                ts(cqn[:, c, sl], atmp[:, c % 2, :], qag[:, c:c + 1], None, ALU.mult, None,
                                   [('atmp', c % 2), 'qag'], [('cqn', c, t)])

                        def tok2feat(b, n):
                            pst, pkst = PS()
                            tr(pst[:, 0:128], tkm[:, b, 0:128], ident[:], [('tkm', b), 'ident'], [pkst])
                            tr(pst[:, 128:256], tkm[:, b, 128:256], ident[:], [('tkm', b), 'ident'], [pkst])
                            tr(pst[0:32, 256:384], tkm[:, b, 256:288], ident[:], [('tkm', b), 'ident'], [pkst])
                            cp(ckvT[:, :, n * 128:(n + 1) * 128], pst[:, 0:256].rearrange("p (a b) -> p a b", a=2), [pkst], [('ckvT', n // 4)])
                            act(kpeT[0:32, n * 128:(n + 1) * 128], pst[0:32, 256:384], AF.Copy, [pkst], [('kpeT', n // 4), pkst])

                        for t in range(NTT):
                            sl = slice(t * TT, (t + 1) * TT)
                            for c in range(2):
                                pq, pkq = PS()
                                for kc in range(8):
                                    mm(pq[:], wkv[:, kc, c * 128:(c + 1) * 128], xn[:, kc, sl], kc == 0, kc == 7, ['wkv', ('xn', kc, t)], [pkq])
                                cp(cqf[:, c, :], pq[:], [pkq], [('cqf', c)])
                            if CUT >= 2:
                                pq, pkq = PS()
                                for kc in range(8):
                                    mm(pq[0:32, :], wkv[:, kc, 256:288], xn[:, kc, sl], kc == 0, kc == 7, ['wkv', ('xn', kc, t)], [pkq])
                                cp(cqf[0:32, 2, :], pq[0:32, :], [pkq], [('cqf', 2)])
                                act(kpeT[0:32, sl], pq[0:32, :], AF.Copy, [pkq], [('kpeT', t), pkq])
                            if CUT >= 3:
                                pss, pkss = PS()
                                for c in range(2):
                                    act(asq[:, c, :], cqf[:, c, :], AF.Square, [('cqf', c)], [('asq', c)])
                                    mm(pss[:], ones_bf[:], asq[:, c, :], c == 0, c == 1, ['ones_bf', ('asq', c)], [pkss])
                                cp(ars[:], pss[:], [pkss], ['ars'])
                                rsqrt_inplace(ars[:], 'ars', 1.0 / KVR)
                                for c in range(2):
                                    tt(atmp[:, c, :], cqf[:, c, :], ars[:], ALU.mult, [('cqf', c), 'ars'], [('atmp', c)])
                                    ts(cqf[:, c, :], atmp[:, c, :], kvag_sb[:, c:c + 1], None, ALU.mult, None,
                                       [('atmp', c), 'kvag'], [('cqf', c)])
                                    act(ckvT[:, c, sl], cqf[:, c, :], AF.Copy, [('cqf', c)], [('ckvT', t)])
                            if CUT >= 4:
                                for blk in range(4):
                                    b = blk % 2
                                    n = t * 4 + blk
                                    bs = slice(blk * 128, (blk + 1) * 128)
                                    pso, pkso = PS()
                                    tr(pso[:, 0:128], cqf[:, 0, bs], ident[:], [('cqf', 0), 'ident'], [pkso])
                                    tr(pso[:, 128:256], cqf[:, 1, bs], ident[:], [('cqf', 1), 'ident'], [pkso])
                                    tr(pso[:, 256:288], cqf[0:32, 2, bs], ident[0:32, 0:32], [('cqf', 2), 'ident'], [pkso])
                                    cp(tkm[:, b, :], pso[:, 0:288], [pkso], [('tkm', b)])
                                    if not int(os.environ.get('SKA', '0')):
                                        S.dma('sp', ckv_out[l, n * 128:(n + 1) * 128, :], tkm[:, b, 0:256], [('tkm', b)], [('ckv_out', n)])
                                        S.dma('sp', kpe_out[l, n * 128:(n + 1) * 128, :], tkm[:, b, 256:288], [('tkm', b)], [('kpe_out', n)])
                        S.barrier()
                        if not int(os.environ.get('SK2', '0')):
                            S.dma('sp', ctk[:, :, 0:256], cache_ckv[l].rearrange("(n p) c -> p n c", p=128), (), ['ctk'])
                            S.dma('sp', ctk[:, :, 256:288], cache_kpe[l].rearrange("(n p) c -> p n c", p=128), (), ['ctk'])
                            for n in range(PAST // 128):
                                pst, pkst = PS()
                                tr(pst[:, 0:128], ctk[:, n, 0:128], ident[:], ['ctk', 'ident'], [pkst])
                                tr(pst[:, 128:256], ctk[:, n, 128:256], ident[:], ['ctk', 'ident'], [pkst])
                                tr(pst[0:32, 256:384], ctk[:, n, 256:288], ident[:], ['ctk', 'ident'], [pkst])
                                nn_ = NT // 128 + n
                                cp(ckvT[:, :, nn_ * 128:(nn_ + 1) * 128], pst[:, 0:256].rearrange("p (a b) -> p a b", a=2), [pkst], [('ckvT', nn_ // 4)])
                                act(kpeT[0:32, nn_ * 128:(nn_ + 1) * 128], pst[0:32, 256:384], AF.Copy, [pkst], [('kpeT', nn_ // 4), pkst])
                        S.barrier()
                    es3 = ExitStack()
                    with es3:
                        def A(name, shape, dt=F32):
                            return es3.enter_context(T(name, shape, dt))
                        wuq_sb = A("wuq_sb", [128, 3, NH * QK], BF16)
                        wuqs_sb = A("wuqs_sb", [128, 3, NH * QK], BF16)
                        wuk_sb = A("wuk_sb", [128, 2, NH * QK], BF16)
                        wuv_sb = A("wuv_sb", [128, 2, NH * 64], BF16)
                        shift_sb = A("shift_sb", [128, 2, 96], BF16)
                        qng_sb = A("qng_sb", [128, 4])
                        rC = A("rC", [128, NT], BF16)
                        rS = A("rS", [128, NT], BF16)
                        Qh = A("Qh", [128, NT], BF16)
                        Kh = A("Kh", [128, NK], BF16)
                        Ve = A("Ve", [128, NK // 128, 128], BF16)
                        Vo = A("Vo", [128, NK // 128, 128], BF16)
                        Pb = A("Pb", [128, 3, TT], BF16)
                        qf = A("qf", [128, 2, TT])
                        hsq = A("hsq", [128, TT], BF16)
                        hrs = A("hrs", [128, TT])
                        ht = A("ht", [128, 2, TT])
                        rec = A("rec", [128, TT])
                        load_w(wuq_sb, w_uq[l], 3, 'wuq')
                        load_w(wuqs_sb, w_uq_sw[l], 3, 'wuqs')
                        load_w(wuk_sb, w_uk[l], 2, 'wuk')
                        load_w(wuv_sb, w_uv[l], 2, 'wuv')
                        S.dma('sp', shift_sb[0:32], shiftm[:, :, :], (), ['shift'])
                        S.dma('sp', qng_sb[:], qng[l], (), ['qng'])
                        S.dma('sp', rC[:], ropeC[:, :], (), ['rC'])
                        S.dma('sp', rS[:], ropeS[:, :], (), ['rS'])
                        memset(Qh[96:128, :], 0.0, ['Qind'])
                        memset(Kh[96:128, :], 0.0, ['Kind'])
                        S.dma('sp', Qh[96:104, :], qind[:, :], (), ['Qind'])
                        S.dma('sp', Kh[96:104, :], kind[:, :], (), ['Kind'])
                        memset(Ve[:, :, 64:128], 1.0, ['Ve1'])
                        memset(Vo[:, :, 0:64], 1.0, ['Vo1'])

                        def normrope(pq, pkq, pqs, pkqs, gcol, dest, dkey, sl):
                            act(qf[0:96, 0, :], pq[0:96, :], AF.Copy, [pkq], [('qf', 0)])
                            act(hsq[0:96, :], qf[0:96, 0, :], AF.Square, [('qf', 0)], ['hsq'])
                            pz, pkz = PS()
                            mm(pz[0:96, :], ones_bf[0:96, 0:96], hsq[0:96, :], True, True, ['ones_bf', 'hsq'], [pkz])
                            cp(hrs[0:96, :], pz[0:96, :], [pkz], ['hrs'])
                            rsqrt_inplace(hrs[0:96, :], 'hrs', 1.0 / QK)
                            if pqs is not None:
                                act(qf[0:96, 1, :], pqs[0:96, :], AF.Copy, [pkqs], [('qf', 1)])
                                stt(ht[0:96, 0, :], qf[0:96, 0, :], qng_sb[0:96, gcol:gcol + 1], rC[0:96, sl], ALU.mult, ALU.mult,
                                    [('qf', 0), 'qng', 'rC'], [('ht', 0)])
                                stt(ht[0:96, 1, :], qf[0:96, 1, :], qng_sb[0:96, gcol + 1:gcol + 2], rS[0:96, sl], ALU.mult, ALU.mult,
                                    [('qf', 1), 'qng', 'rS'], [('ht', 1)])
                                tt(ht[0:96, 0, :], ht[0:96, 0, :], ht[0:96, 1, :], ALU.add, [('ht', 0), ('ht', 1)], [('ht', 0)])
                                tt(dest, ht[0:96, 0, :], hrs[0:96, :], ALU.mult, [('ht', 0), 'hrs'], [dkey])
                            else:
                                stt(dest, qf[0:96, 0, :], qng_sb[0:96, gcol:gcol + 1], hrs[0:96, :], ALU.mult, ALU.mult,
                                    [('qf', 0), 'qng', 'hrs'], [dkey])

                        for h in range(NH if not ATT_SKIP else 0):
                            V = Ve if h % 2 == 0 else Vo
                            vkey = 'Ve' if h % 2 == 0 else 'Vo'
                            voff = 0 if h % 2 == 0 else 64
                            hs = slice(h * QK, (h + 1) * QK)
                            for t in range(NTT):
                                sl = slice(t * TT, (t + 1) * TT)
                                pq, pkq = PS()
                                for c in range(3):
                                    mm(pq[0:96, :], wuq_sb[:, c, hs], cqn[:, c, sl], c == 0, c == 2, ['wuq', ('cqn', c, t)], [pkq])
                                pqs, pkqs = PS()
                                for c in range(3):
                                    mm(pqs[0:96, :], wuqs_sb[:, c, hs], cqn[:, c, sl], c == 0, c == 2, ['wuqs', ('cqn', c, t)], [pkqs])
                                normrope(pq, pkq, pqs, pkqs, 0, Qh[0:96, sl], ('Qh', t), sl)
                            for kt in range(NK // TT if HSTOP >= 2 else 0):
                                sl = slice(kt * TT, (kt + 1) * TT)
                                pk_, pkk = PS()
                                for c in range(2):
                                    mm(pk_[0:96, :], wuk_sb[:, c, hs], ckvT[:, c, sl], c == 0, False, ['wuk', ('ckvT', kt)], [pkk])
                                mm(pk_[0:96, :], shift_sb[0:32, 0, :], kpeT[0:32, sl], False, True, ['shift', ('kpeT', kt)], [pkk])
                                if kt < NTT:
                                    pks, pkks = PS()
                                    mm(pks[0:96, :], shift_sb[0:32, 1, :], kpeT[0:32, sl], True, True, ['shift', ('kpeT', kt)], [pkks])
                                    normrope(pk_, pkk, pks, pkks, 2, Kh[0:96, sl], ('Kh', kt), sl)
                                else:
                                    normrope(pk_, pkk, None, None, 2, Kh[0:96, sl], ('Kh', kt), sl)
                            for n0 in (range(0, NK // 128, 8) if HSTOP >= 3 else []):
                                nn = min(8, NK // 128 - n0)
                                pv, pkv = PS()
                                for j in range(nn):
                                    n = n0 + j
                                    for c in range(2):
                                        mm(pv[:, j * 64:(j + 1) * 64], ckvT[:, c, n * 128:(n + 1) * 128], wuv_sb[:, c, h * 64:(h + 1) * 64],
                                           c == 0, c == 1, [('ckvT', n // 4), 'wuv'], [pkv])
                                cp(V[:, n0:n0 + nn, voff:voff + 64], pv[:, 0:nn * 64].rearrange("p (a b) -> p a b", a=nn), [pkv], [vkey])
                            for qt in range(NTT if HSTOP >= 4 else 0):
                                sl = slice(qt * TT, (qt + 1) * TT)
                                po, pko = PS()
                                for n in range(NK // 128):
                                    ps_, pks_ = PS()
                                    if ps_ is po:
                                        ps_, pks_ = PS()
                                    mm(ps_[:], Kh[:, n * 128:(n + 1) * 128], Qh[:, sl], True, True,
                                       [('Kh', n // 4), 'Kind', ('Qh', qt), 'Qind'], [pks_])
                                    pb = n % 3
                                    act(Pb[:, pb, :], ps_[:], AF.Exp, [pks_], [('Pb', pb)], scale=float(QK) ** -0.5)
                                    mm(po[:], V[:, n, :], Pb[:, pb, :], n == 0, n == NK // 128 - 1, [vkey, vkey + '1', ('Pb', pb)], [pko])
                                if HSTOP < 5:
                                    continue
                                if h % 2 == 0:
                                    recip(rec[0:64, :], po[64:128, :], [pko], ['rec'])
                                    tt(ybuf[0:64, h // 2, sl], po[0:64, :], rec[0:64, :], ALU.mult, [pko, 'rec'], [('y', h // 2, qt)])
                                else:
                                    recip(rec[64:128, :], po[0:64, :], [pko], ['rec'])
                                    tt(ybuf[64:128, h // 2, sl], po[64:128, :], rec[64:128, :], ALU.mult, [pko, 'rec'], [('y', h // 2, qt)])
                        S.barrier()
                dump("yattn%d" % l, ybuf[:], [128, 4, NT], BF16, ('y', 0, 0))
                run_epi(2, ybuf)
            if 'A' in BRANCHES:
              with T("ybuf", [128, 4, NT], BF16) as ybuf:
                esA = ExitStack()
                with esA:
                    def AA(name, shape, dt=F32):
                        return esA.enter_context(T(name, shape, dt))
                    U = AA("U", [128, 32, 256], BF16)
                    Tz = AA("Tz", [128, 32, 128], BF16)
                    Bb = AA("Bb", [128, 2, 32, 16])
                    Cc = AA("Cc", [128, 2, 32, 16])
                    Qp = AA("Qp", [128, 2, 32, 24])
                    A1x = AA("A1x", [128, 2, 32])
                    A2x = AA("A2x", [128, 2, 32])
                    dsk = AA("dsk", [128, 32])
                    smk = AA("smk", [128, 257])
                    seltp = AA("seltp", [128, 8, 240], BF16)
                    tabA = AA("tabA", [128, 2, 2, 128], BF16)
                    tabB = AA("tabB", [128, 2, 2, 128], BF16)
                    otmp = AA("otmp", [128, 4, 128])
                    tabM = AA("tabM", [128, 2, 2, 2, 128], BF16)
                    hmk = AA("hmk", [128, 2])
                    S.dma('sp', hmk[:], halfmask[:, :], (), ['hmk'])
                    S.dma('sp', Cc[:], c2[l], (), ['Cc'])
                    S.dma('sp', dsk[:], dskT[l], (), ['dsk'])
                    S.dma('sp', smk[:], smask[:, 0, :], (), ['smk'])
                    S.dma('sp', seltp[:], seltpad[:, :, :], (), ['seltp'])
                    with T("wssm", [128, 8, 512], BF16) as wssm, T("uf", [128, 4, NT], BF16) as uf, T("selp", [128, 8, 240], BF16) as selp:
                        S.dma('sp', selp[:], selpad[:, :, :], (), ['selp'])
                        load_w(wssm, w_in[l][:, OFF_SSM:OFF_SSM + 512], 8, 'wssm')
                        for t in range(NTT):
                            sl = slice(t * TT, (t + 1) * TT)
                            for oc in range(4):
                                pu, pku = PS()
                                for kc in range(8):
                                    mm(pu[:], wssm[:, kc, oc * 128:(oc + 1) * 128], xn[:, kc, sl], kc == 0, kc == 7,
                                       ['wssm', ('xn', kc, t)], [pku])
                                if oc % 2 == 0:
                                    act(uf[:, oc, sl], pu[:], AF.Copy, [pku], [('uf', oc)])
                                else:
                                    cp(uf[:, oc, sl], pu[:], [pku], [('uf', oc)])
                        for g2 in range(16):
                            pu, pku = PS()
                            for gi in range(2):
                                g = g2 * 2 + gi
                                for r in range(8):
                                    mm(pu[:, gi * 256:(gi + 1) * 256], selp[:, g % 8, 112 - 16 * r:240 - 16 * r], uf[:, g // 8, r:NT:8],
                                       r == 0, r == 7, ['selp', ('uf', g // 8)], [pku])
                            if g2 % 2 == 0:
                                act(U[:, g2 * 2:g2 * 2 + 2, :], pu[:].rearrange("p (a b) -> p a b", a=2), AF.Copy, [pku], [('U', g2)])
                            else:
                                cp(U[:, g2 * 2:g2 * 2 + 2, :], pu[:].rearrange("p (a b) -> p a b", a=2), [pku], [('U', g2)])
                        S.barrier()
                    S.mute = ASTOP < 2
                    with T("lam", [128, 2, 32]) as lam, T("ldt", [128, 32]) as ldt, T("braw", [128, 2, 32, 16]) as braw, \
                            T("ramp", [128, 32, 24]) as ramp, T("w1", [128, 32, 24]) as w1, T("w2", [128, 32, 24]) as w2, \
                            T("w3", [128, 32, 24]) as w3, T("w4", [128, 32, 24]) as w4, T("wi", [128, 32, 24], I32) as wi, \
                            T("v1", [128, 8, 32]) as v1, T("bt", [128, 2, 32, 16]) as bt:
                        S.dma('sp', lam[:], lam2[l], (), ['lam'])
                        S.dma('sp', ldt[:], logdt2[l], (), ['ldt'])
                        S.dma('sp', braw[:], b2[l], (), ['braw'])
                        S.dma('sp', ramp[:], ramp32[:, :, :], (), ['ramp'])
                        K_ = ['tbl']
                        def bc(ap2):
                            return ap2.unsqueeze(2).broadcast_to([128, 32, 24])
                        act(ldt[:], ldt[:], AF.Exp, ['ldt'], K_)
                        tt(v1[:, 0, :], lam[:, 0, :], ldt[:], ALU.mult, ['lam'] + K_, K_)
                        tt(v1[:, 1, :], lam[:, 1, :], ldt[:], ALU.mult, ['lam'] + K_, K_)
                        tt(w1[:], bc(v1[:, 1, :]), ramp[:], ALU.mult, ['ramp'] + K_, K_)
                        tt(w2[:], bc(v1[:, 0, :]), ramp[:], ALU.mult, ['ramp'] + K_, K_)
                        act(w2[:], w2[:], AF.Exp, K_, K_)
                        ts(w1[:], w1[:], 1.0 / (2.0 * math.pi), None, ALU.mult, None, K_, K_)
                        cp(wi[:], w1[:], K_, K_)
                        cp(w3[:], wi[:], K_, K_)
                        tt(w1[:], w1[:], w3[:], ALU.subtract, K_, K_)
                        act(w3[:], w1[:], AF.Sin, K_, K_, scale=math.pi)
                        act(w4[:], w1[:], AF.Sin, K_, K_, scale=math.pi / 2.0)
                        tt(w4[:], w4[:], w4[:], ALU.mult, K_, K_)
                        ts(w4[:], w4[:], -2.0, 1.0, ALU.mult, ALU.add, K_, K_)
                        tt(w4[:], w4[:], w3[:], ALU.mult, K_, K_)
                        ts(w4[:], w4[:], 2.0, None, ALU.mult, None, K_, K_)
                        tt(w3[:], w3[:], w3[:], ALU.mult, K_, K_)
                        ts(w3[:], w3[:], -2.0, 1.0, ALU.mult, ALU.add, K_, K_)
                        tt(Qp[:, 0], w2[:], w3[:], ALU.mult, K_, K_)
                        tt(Qp[:, 1], w2[:], w4[:], ALU.mult, K_, K_)
                        for ri in range(2):
                            cp(v1[:, 2 + ri, 0:16], Qp[:, ri, 0:16, 8], K_, K_)
                            cp(v1[:, 2 + ri, 16:32], Qp[:, ri, 16:32, 1], K_, K_)
                        cp(A1x[:, 0, 0:16], Qp[:, 0, 0:16, 15], K_, K_)
                        cp(A1x[:, 0, 16:32], Qp[:, 0, 16:32, 8], K_, K_)
                        cp(A1x[:, 1, :], A1x[:, 0, :], K_, K_)
                        cp(A2x[:, 1, 0:16], Qp[:, 1, 0:16, 15], K_, K_)
                        cp(A2x[:, 1, 16:32], Qp[:, 1, 16:32, 8], K_, K_)
                        ts(A2x[:, 0, :], A2x[:, 1, :], -1.0, None, ALU.mult, None, K_, K_)
                        tt(v1[:, 4, :], lam[:, 0, :], lam[:, 0, :], ALU.mult, K_, K_)
                        tt(v1[:, 5, :], lam[:, 1, :], lam[:, 1, :], ALU.mult, K_, K_)
                        tt(v1[:, 4, :], v1[:, 4, :], v1[:, 5, :], ALU.add, K_, K_)
                        recip(v1[:, 4, :], v1[:, 4, :], K_, K_)
                        ts(v1[:, 2, :], v1[:, 2, :], -1.0, None, ALU.add, None, K_, K_)
                        tt(v1[:, 5, :], v1[:, 2, :], lam[:, 0, :], ALU.mult, K_, K_)
                        tt(v1[:, 6, :], v1[:, 3, :], lam[:, 1, :], ALU.mult, K_, K_)
                        tt(v1[:, 5, :], v1[:, 5, :], v1[:, 6, :], ALU.add, K_, K_)
                        tt(v1[:, 5, :], v1[:, 5, :], v1[:, 4, :], ALU.mult, K_, K_)
                        tt(v1[:, 6, :], v1[:, 3, :], lam[:, 0, :], ALU.mult, K_, K_)
                        tt(v1[:, 7, :], v1[:, 2, :], lam[:, 1, :], ALU.mult, K_, K_)
                        tt(v1[:, 6, :], v1[:, 6, :], v1[:, 7, :], ALU.subtract, K_, K_)
                        tt(v1[:, 6, :], v1[:, 6, :], v1[:, 4, :], ALU.mult, K_, K_)
                        def bc16(ap2):
                            return ap2.unsqueeze(2).broadcast_to([128, 32, 16])
                        tt(bt[:, 0], bc16(v1[:, 5, :]), braw[:, 0], ALU.mult, ['braw'] + K_, K_)
                        tt(bt[:, 1], bc16(v1[:, 6, :]), braw[:, 1], ALU.mult, ['braw'] + K_, K_)
                        tt(Bb[:, 0], bt[:, 0], bt[:, 1], ALU.subtract, K_, K_)
                        tt(bt[:, 0], bc16(v1[:, 5, :]), braw[:, 1], ALU.mult, ['braw'] + K_, K_)
                        tt(bt[:, 1], bc16(v1[:, 6, :]), braw[:, 0], ALU.mult, ['braw'] + K_, K_)
                        tt(Bb[:, 1], bt[:, 0], bt[:, 1], ALU.add, K_, K_)
                        S.barrier()

                    def outer(dst, dkey, dp, kind, X, xkey, neg_im):
                        q_re = Qp[:, 0, dp, kind * 8:(kind + 1) * 8].unsqueeze(2).broadcast_to([128, 8, 16])
                        q_im = Qp[:, 1, dp, kind * 8:(kind + 1) * 8].unsqueeze(2).broadcast_to([128, 8, 16])
                        x_re = X[:, 0, dp, :].unsqueeze(1).broadcast_to([128, 8, 16])
                        x_im = X[:, 1, dp, :].unsqueeze(1).broadcast_to([128, 8, 16])
                        o = [otmp[:, i, :].rearrange("p (a b) -> p a b", a=8) for i in range(4)]
                        d_re = dst[:, 0, :].rearrange("p (a b) -> p a b", a=8)
                        d_im = dst[:, 1, :].rearrange("p (a b) -> p a b", a=8)
                        tt(o[0], q_re, x_re, ALU.mult, ['tbl', xkey], [('otmp', 0)])
                        tt(o[1], q_im, x_im, ALU.mult, ['tbl', xkey], [('otmp', 1)])
                        tt(d_re, o[0], o[1], ALU.subtract, [('otmp', 0), ('otmp', 1)], [dkey])
                        tt(o[2], q_re, x_im, ALU.mult, ['tbl', xkey], [('otmp', 2)])
                        tt(o[3], q_im, x_re, ALU.mult, ['tbl', xkey], [('otmp', 3)])
                        if neg_im:
                            stt(d_im, o[2], -1.0, o[3], ALU.mult, ALU.subtract, [('otmp', 2), ('otmp', 3)], [dkey])
                        else:
                            tt(d_im, o[2], o[3], ALU.add, [('otmp', 2), ('otmp', 3)], [dkey])

                    def maskE(d):
                        for ge in range(2):
                            ts(tabM[:, ge, d].rearrange("p a b -> p (a b)"), tabB[:, d].rearrange("p a b -> p (a b)"), hmk[:, ge:ge + 1], None,
                               ALU.mult, None, [('tabB', d), 'hmk'], [('tabM', d)])

                    S.mute = ASTOP < 3
                    with T("tzt", [128, 2, 256]) as tzt, T("tzm", [128, 2, 256]) as tzm:
                        S.dma('sp', tzm[:], tzmask[:, :, 0:256], (), ['tzm'])
                        for pair in range(16):
                            pT = []
                            for d in range(2):
                                dp = d * 16 + pair
                                outer(tabA[:, d], ('tabA', d), dp, 2, Bb, 'tbl', False)
                                outer(tabB[:, d], ('tabB', d), dp, 1, Cc, 'Cc', True)
                                pt_, pkt_ = PS()
                                maskE(d)
                                for ge in range(2 if A3 >= 2 else 0):
                                    mm(pt_[:, ge * 128:(ge + 1) * 128], tabA[:, d, 0, :], tabM[:, ge, d, 0, :], True, False,
                                       [('tabA', d), ('tabM', d)], [pkt_])
                                    mm(pt_[:, ge * 128:(ge + 1) * 128], tabA[:, d, 1, :], tabM[:, ge, d, 1, :], False, True,
                                       [('tabA', d), ('tabM', d)], [pkt_])
                                pT.append((pt_, pkt_))
                            if A3 < 3:
                                continue
                            tt(tzt[:, 0, :], pT[0][0][:, 0:256], tzm[:, 0, :], ALU.mult, [pT[0][1], 'tzm'], [('tzt', 0)])
                            tt(tzt[:, 1, :], pT[1][0][:, 0:256], tzm[:, 1, :], ALU.mult, [pT[1][1], 'tzm'], [('tzt', 1)])
                            tt(tzt[:, 0, :], tzt[:, 0, :], tzt[:, 1, :], ALU.add, [('tzt', 0), ('tzt', 1)], [('tzt', 0)])
                            for ge in range(2):
                                g = 2 * pair + ge
                                stt(Tz[:, g, :], ident[:], dsk[:, g:g + 1], tzt[:, 0, ge * 128:(ge + 1) * 128], ALU.mult, ALU.add,
                                    ['ident', 'dsk', ('tzt', 0)], [('Tz', g)])
                        S.barrier()

                    S.mute = ASTOP < 4
                    with T("SH", [128, 2, 32, 257], BF16) as SH, T("Zs", [128, 2, 32]) as Zs, T("Vs", [128, 2, 32]) as Vs, \
                            T("P1", [128, 2, 32]) as P1, T("P2", [128, 2, 32]) as P2, T("fin", [128, 8, 2, 2, 16]) as fin, \
                            T("Wt", [128, 2, 2, 128], BF16) as Wt, T("Ysb", [128, 8, 256], BF16) as Ysb, \
                            T("gl", [128, 2, TT]) as gl:
                        S.dma('sp', Zs[:], h0[l][:, 0:64], (), ['Zs'])
                        cp(SH[:, :, :, 0], Zs[:], ['Zs'], [('SH', 0)])
                        for pair in range(16):
                            for d in range(2):
                                dp = d * 16 + pair
                                outer(tabA[:, d], ('tabA', d), dp, 0, Bb, 'tbl', False)
                                ptr_, pktr = PS()
                                ptb = ptr_[:].bitcast(BF16)
                                tr(ptb[:, 0:128], tabA[:, d, 0, :], ident_bf[:], [('tabA', d), 'ident_bf'], [pktr])
                                tr(ptb[:, 128:256], tabA[:, d, 1, :], ident_bf[:], [('tabA', d), 'ident_bf'], [pktr])
                                cp(Wt[:, d, :, :], ptb[:, 0:256].rearrange("p (a b) -> p a b", a=2), [pktr], [('Wt', d)])
                                ps_, pks_ = PS()
                                for ri in range(2):
                                    for ge in range(2):
                                        mm(ps_[ge * 64:(ge + 1) * 64, ri * 256:(ri + 1) * 256], Wt[:, d, ri, ge * 64:(ge + 1) * 64],
                                           U[:, 2 * pair + ge, :], True, True, [('Wt', d), ('U', pair)], [pks_])
                                src_ = ps_[:].rearrange("p (a b) -> p a b", a=2)
                                if d == 0:
                                    cp(SH[:, :, dp, 1:257], src_, [pks_], [('SHs', dp)])
                                else:
                                    cp(SH[:, :, dp, 256:0:-1], src_, [pks_], [('SHs', dp)])
                        S.barrier()
                        S.mute = ASTOP < 5
                        SK = ['scan']
                        for i in range(256):
                            tt(P1[:], Zs[:], A1x[:], ALU.mult, SK, SK)
                            tt(P2[:], Zs[:, ::-1, :], A2x[:], ALU.mult, SK, SK)
                            tt(Vs[:], P1[:], SH[:, :, :, i + 1], ALU.add, SK, SK)
                            tt(Vs[:], Vs[:], P2[:], ALU.add, SK, SK)
                            if (i + 1) % 32 == 0:
                                q = i // 32
                                cp(fin[:, q, :, 0, :], Vs[:, :, 0:16], SK, SK)
                                cp(fin[:, 7 - q, :, 1, :], Vs[:, :, 16:32], SK, SK)
                            ts(Zs[:], Vs[:], smk[:, i + 1:i + 2], None, ALU.mult, None, SK + ['smk'], SK)
                            act(SH[:, :, :, i + 1], Zs[:], AF.Copy, SK, SK)
                        S.dma('sp', ssm_out[l], fin[:].rearrange("p a b c d -> p (a b c d)"), SK, ['ssm_out'])
                        S.barrier()
                        S.mute = ASTOP < 6
                        for oc in range(4):
                            for pl in range(4):
                                pair = oc * 4 + pl
                                py, pky = PS()
                                for d in range(2):
                                    outer(tabB[:, d], ('tabB', d), d * 16 + pair, 1, Cc, 'Cc', True)
                                    maskE(d)
                                for ge in range(2):
                                    g = 2 * pair + ge
                                    o_ = py[:, ge * 256:(ge + 1) * 256]
                                    mm(o_, Tz[:, g, :], U[:, g, :], True, False, [('Tz', g), ('U', pair)], [pky])
                                    mm(o_, tabM[:, ge, 0, 0, :], SH[:, 0, pair, 0:256], False, False, [('tabM', 0), 'scan'], [pky])
                                    mm(o_, tabM[:, ge, 0, 1, :], SH[:, 1, pair, 0:256], False, False, [('tabM', 0), 'scan'], [pky])
                                    mm(o_, tabM[:, ge, 1, 0, :], SH[:, 0, 16 + pair, 255::-1], False, False, [('tabM', 1), 'scan'], [pky])
                                    mm(o_, tabM[:, ge, 1, 1, :], SH[:, 1, 16 + pair, 255::-1], False, True, [('tabM', 1), 'scan'], [pky])
                                cp(Ysb[:, 2 * pl:2 * pl + 2, :], py[:].rearrange("p (a b) -> p a b", a=2), [pky], [('Ysb', pl)])
                            yv = ybuf[:, oc, :].rearrange("p (j r) -> p r j", r=8)
                            for r2 in range(4):
                                pz, pkz = PS()
                                for ri in range(2):
                                    r = r2 * 2 + ri
                                    for gl_ in range(8):
                                        mm(pz[:, ri * 256:(ri + 1) * 256], seltp[:, r, 112 - 16 * gl_:240 - 16 * gl_], Ysb[:, gl_, :],
                                           gl_ == 0, gl_ == 7, ['seltp', ('Ysb', gl_ // 2)], [pkz])
                                b = 0
                                act(gl[:, 2 * b, :], pz[:], AF.Copy, [pkz], [('gl', 2 * b)])
                                tt(gl[:, 2 * b + 1, :], gl[:, 2 * b, :], gl[:, 2 * b, :], ALU.mult, [('gl', 2 * b)], [('gl', 2 * b + 1)])
                                ts(gl[:, 2 * b + 1, :], gl[:, 2 * b + 1, :], 0.044715, 1.0, ALU.mult, ALU.add, [('gl', 2 * b + 1)], [('gl', 2 * b + 1)])
                                tt(gl[:, 2 * b + 1, :], gl[:, 2 * b + 1, :], gl[:, 2 * b, :], ALU.mult, [('gl', 2 * b), ('gl', 2 * b + 1)], [('gl', 2 * b + 1)])
                                act(gl[:, 2 * b + 1, :], gl[:, 2 * b + 1, :], AF.Sigmoid, [('gl', 2 * b + 1)], [('gl', 2 * b + 1)], scale=1.5957691216057308)
                                tt(yv[:, r2 * 2:r2 * 2 + 2, :], gl[:, 2 * b, :].rearrange("p (a b) -> p a b", a=2),
                                   gl[:, 2 * b + 1, :].rearrange("p (a b) -> p a b", a=2), ALU.mult,
                                   [('gl', 2 * b), ('gl', 2 * b + 1)], [('y', oc, 0), ('y', oc, 1), ('y', oc, 2), ('y', oc, 3)])
                        S.barrier()
                S.mute = False
                with T("wglu", [128, 4, BW], BF16) as wglu, T("gsg", [128, 4, TT]) as gsg:
                    load_w(wglu, w_glu[l], 4, 'wglu')
                    for t in range(NTT):
                        sl = slice(t * TT, (t + 1) * TT)
                        for oc in range(4):
                            pg, pkg = PS()
                            for kc in range(4):
                                mm(pg[:], wglu[:, kc, oc * 128:(oc + 1) * 128], ybuf[:, kc, sl], kc == 0, kc == 3,
                                   ['wglu', ('y', kc, t)], [pkg])
                            act(gsg[:, oc, :], pg[:], AF.Sigmoid, [pkg], [('gsg', oc)])
                        for oc in range(4):
                            tt(ybuf[:, oc, sl], ybuf[:, oc, sl], gsg[:, oc, :], ALU.mult, [('y', oc, t), ('gsg', oc)], [('y', oc, t)])
                    S.barrier()
                dump("yssm%d" % l, ybuf[:], [128, 4, NT], BF16, ('y', 0, 0))
                run_epi(0, ybuf)
            S.barrier()
            dump("xmid%d" % l, x[:], [128, 8, NT], F32, ('x', 0, 0))

            rmsnorm_mod(l, None, 16, 24)
            for half in range(2 if not int(os.environ.get('SKF', '0')) else 0):
                with T("hT", [128, 22, 1024], BF16) as hT, \
                        T("wfi", [128, 2, 8, 256], BF16) as wfi, \
                        T("wfo", [128, 2, 2, D], BF16) as wfo, \
                        T("fsg", [128, 2, TT], F32) as fsg:
                    for c in range(22):
                        b = c % 2
                        S.dma('pool', wfi[:, b, :, 0:128], w_ffn_in[l][:, c * 128:(c + 1) * 128].rearrange("(kc p) n -> p kc n", p=128),
                              (), [('wfi', b)])
                        S.dma('pool', wfi[:, b, :, 128:256],
                              w_ffn_in[l][:, DFF + c * 128:DFF + (c + 1) * 128].rearrange("(kc p) n -> p kc n", p=128),
                              (), [('wfi', b)])
                        for tt_ in range(2):
                            t = half * 2 + tt_
                            sl = slice(t * TT, (t + 1) * TT)
                            pg, pkg = PS()
                            for kc in range(8):
                                mm(pg[:], wfi[:, b, kc, 0:128], xn[:, kc, sl], kc == 0, kc == 7, [('wfi', b), ('xn', kc, t)], [pkg])
                            pu, pku = PS()
                            for kc in range(8):
                                mm(pu[:], wfi[:, b, kc, 128:256], xn[:, kc, sl], kc == 0, kc == 7, [('wfi', b), ('xn', kc, t)], [pku])
                            act(fsg[:, tt_, :], pg[:], AF.Silu, [pkg], [('fsg', tt_)])
                            tt(hT[:, c, tt_ * TT:(tt_ + 1) * TT], pu[:], fsg[:, tt_, :], ALU.mult, [pku, ('fsg', tt_)], [('hT', c, tt_)])
                    for tt_ in range(2):
                        t = half * 2 + tt_
                        sl = slice(t * TT, (t + 1) * TT)
                        pcs = [PS() for _ in range(8)] if False else None
                    for oc in range(8):
                        pcs = [PS(), PS()]
                        for kp in range(11):
                            b = kp % 2
                            S.dma('pool', wfo[:, b, :, 0:128],
                                  w_ffn_out[l][kp * 256:(kp + 1) * 256, oc * 128:(oc + 1) * 128].rearrange("(j p) n -> p j n", p=128),
                                  (), [('wfo', b)])
                            for tt_ in range(2):
                                for j in range(2):
                                    kc = kp * 2 + j
                                    mm(pcs[tt_][0][:], wfo[:, b, j, 0:128], hT[:, kc, tt_ * TT:(tt_ + 1) * TT],
                                       kc == 0, kc == 21, [('wfo', b), ('hT', kc, tt_)], [pcs[tt_][1]])
                        for tt_ in range(2):
                            t = half * 2 + tt_
                            sl = slice(t * TT, (t + 1) * TT)
                            stt(x[:, oc, sl], pcs[tt_][0][:], mod[:, 40 + oc:41 + oc], x[:, oc, sl], ALU.mult, ALU.add,
                                [pcs[tt_][1], 'mod', ('x', oc, t)], [('x', oc, t)])
                    S.barrier()
            dump("xout%d" % l, x[:], [128, 8, NT], F32, ('x', 0, 0))

        with T("ytm", [128, 2, D], F32) as ytm:
            for n in range(NT // 128):
                b = n % 2
                for hh in range(2):
                    ps, pk = PS()
                    for q in range(4):
                        oc = hh * 4 + q
                        tr(ps[:, q * 128:(q + 1) * 128], x[:, oc, n * 128:(n + 1) * 128], ident[:],
                           [('x', oc, n // 4), 'ident'], [pk])
                    if hh == 0:
                        cp(ytm[:, b, 0:512], ps[:], [pk], [('ytm', b)])
                    else:
                        act(ytm[:, b, 512:1024], ps[:], AF.Copy, [pk], [('ytm', b)])
                S.dma('sp', y_out[n * 128:(n + 1) * 128, :], ytm[:, b, :], [('ytm', b)], ['y_out'])
        S.barrier()
    return nc, dbg_out, S


def _bf(a):
    return np.ascontiguousarray(np.asarray(a, dtype=np.float32)).astype(NPBF)


def _consts(nseq):
    Ls = NT // nseq
    t = np.arange(NT)
    seq = t // Ls
    c = {}
    C = np.ones((128, NT), np.float64)
    Sg = np.zeros((128, NT), np.float64)
    if nseq == 1:
        GRID_W = 64
        row = (t // GRID_W).astype(np.float32)
        col = (t % GRID_W).astype(np.float32)
        inv = (np.float32(10000.0) ** (-np.arange(8, dtype=np.float32) / np.float32(8))).astype(np.float32)
        ang = np.concatenate([row[:, None] * inv, col[:, None] * inv], axis=-1).astype(np.float32)
        cs, sn = np.cos(ang.astype(np.float64)), np.sin(ang.astype(np.float64))
        for i in range(16):
            C[64 + 2 * i] = cs[:, i]
            C[64 + 2 * i + 1] = cs[:, i]
            Sg[64 + 2 * i] = -sn[:, i]
            Sg[64 + 2 * i + 1] = sn[:, i]
    c['ropeC'] = _bf(C)
    c['ropeS'] = _bf(Sg)
    qi = np.zeros((8, NT), np.float32)
    qi[seq, t] = 1.0
    ki = np.zeros((8, NK), np.float32)
    if nseq > 1:
        ki[:, :NT] = -BIG
        ki[seq, t] = 0.0
        ki[:, NT:] = -BIG
    c['qind'] = _bf(qi)
    c['kind'] = _bf(ki)
    nl = (t % Ls).astype(np.float64)
    same = (seq[:, None] == seq[None, :])
    ph = 2.0 * np.pi * ((nl[:, None] * nl[None, :]) % Ls) / Ls
    c['dftC'] = _bf(np.where(same, np.cos(ph) / np.sqrt(Ls), 0.0))
    c['dftS'] = _bf(np.where(same, -np.sin(ph) / np.sqrt(Ls), 0.0))
    cc = np.arange(128, dtype=np.float64)
    ph2 = 2.0 * np.pi * ((cc[:, None] * cc[None, :]) % 128) / 128.0
    c['ccsc'] = _bf(np.concatenate([np.cos(ph2), np.sin(ph2)], axis=1) / np.sqrt(128.0))
    mL = (t % Ls != 0).astype(np.float32)
    mR = (t % Ls != Ls - 1).astype(np.float32)
    c['seqflag'] = np.full((128, 1), 1.0 if nseq == 1 else 0.0, np.float32)
    nchunk = NT // 8
    cps = nchunk // nseq
    mf = np.ones(257, np.float32)
    mb = np.ones(257, np.float32)
    for j in range(nchunk):
        if (j + 1) % cps == 0 and (j + 1) < nchunk:
            mf[j + 1] = 0.0
            mb[j + 1] = 0.0
    c['smask'] = np.ascontiguousarray(np.broadcast_to(np.stack([mf, mb])[None], (128, 2, 257))).astype(np.float32)
    r = np.arange(8, dtype=np.float32)
    kr = np.zeros((2, 3, 8), np.float32)
    kr[0, 0] = 7 - r; kr[0, 1] = r + 1; kr[0, 2] = -(1 + r)
    kr[1, 0] = r;     kr[1, 1] = 8 - r; kr[1, 2] = r - 8
    c['ramp32'] = np.ascontiguousarray(np.broadcast_to(np.repeat(kr.reshape(2, 1, 24), 16, axis=1).reshape(1, 32, 24), (128, 32, 24))).astype(np.float32)
    rr = np.arange(128) // 16
    mF = (rr[None, :] >= rr[:, None]).astype(np.float32)
    mB = (rr[None, :] <= rr[:, None]).astype(np.float32)
    c['tzmask'] = np.ascontiguousarray(np.stack([np.tile(mF, (1, 4)), np.tile(mB, (1, 4))], axis=1)).astype(np.float32)
    sp = np.zeros((128, 8, 240), np.float32)
    stp = np.zeros((128, 8, 240), np.float32)
    for gl in range(8):
        for cc_ in range(16):
            sp[gl * 16 + cc_, gl, 112 + cc_] = 1.0
    for r_ in range(8):
        for cc_ in range(16):
            stp[r_ * 16 + cc_, r_, 112 + cc_] = 1.0
    c['selpad'] = _bf(sp)
    c['seltpad'] = _bf(stp)
    sh = np.zeros((32, 2, 96), np.float32)
    for d in range(32):
        sh[d, 0, 64 + d] = 1.0
        sh[d ^ 1, 1, 64 + d] = 1.0
    c['shiftm'] = _bf(sh)
    c['identf'] = np.eye(128, dtype=np.float32)
    hm = np.zeros((128, 2), np.float32); hm[:64, 0] = 1.0; hm[64:, 1] = 1.0
    c['halfmask'] = hm
    return c


def _colT(v, n):
    return np.ascontiguousarray(np.asarray(v, np.float32).reshape(n, 128).T)


def _shared(inp):
    f = lambda a: np.ascontiguousarray(np.asarray(a, np.float32))
    s = {}
    s['w_ada'] = f(inp['w_ada'])
    s['b_adaT'] = np.stack([_colT(inp['b_ada'][l], 48) for l in range(NL)])
    s['nmgT'] = np.stack([_colT(inp['norm_mix_g'][l], 8) for l in range(NL)])
    s['nfgT'] = np.stack([_colT(inp['norm_ffn_g'][l], 8) for l in range(NL)])
    s['w_in'] = f(inp['w_in'])
    s['qagT'] = np.stack([_colT(inp['q_a_norm_g'][l], 3) for l in range(NL)])
    s['kvag'] = np.stack([_colT(inp['kv_a_norm_g'][l], 2) for l in range(NL)])
    wuq = np.asarray(inp['w_uq'], np.float32)
    s['w_uq'] = f(wuq)
    perm = np.arange(NH * QK)
    dd = perm % QK
    perm = np.where(dd >= 64, (perm // QK) * QK + 64 + ((dd - 64) ^ 1), perm)
    s['w_uq_sw'] = f(wuq[:, :, perm])
    wukv = np.asarray(inp['w_ukv'], np.float32).reshape(NL, KVR, NH, 128)
    wuk = np.zeros((NL, KVR, NH, QK), np.float32)
    wuk[..., :64] = wukv[..., :64]
    s['w_uk'] = f(wuk.reshape(NL, KVR, NH * QK))
    s['w_uv'] = f(wukv[..., 64:].reshape(NL, KVR, NH * 64))
    qg = np.asarray(inp['q_norm_g'], np.float32)
    kg = np.asarray(inp['k_norm_g'], np.float32)
    swp = np.arange(QK)
    swp = np.where(swp >= 64, 64 + ((swp - 64) ^ 1), swp)
    qng = np.zeros((NL, 128, 4), np.float32)
    qng[:, :QK, 0] = qg
    qng[:, :QK, 1] = qg[:, swp]
    qng[:, :QK, 2] = kg
    qng[:, :QK, 3] = kg[:, swp]
    s['qng'] = qng
    def gp(a):
        a = np.asarray(a, np.float32)
        rest = a.shape[4:]
        a = a.reshape((NL, 2, 16, 2, 64) + rest)
        perm = (0, 3, 4, 1, 2) + tuple(range(5, 5 + len(rest)))
        return np.ascontiguousarray(a.transpose(perm).reshape((NL, 128, 32) + rest))
    s['lam2'] = np.ascontiguousarray(np.stack([gp(inp['ssm_lam_re']), gp(inp['ssm_lam_im'])], axis=2))
    ld = np.broadcast_to(np.asarray(inp['ssm_log_dt'], np.float32)[:, :, :, None], (NL, 2, 32, 64))
    s['logdt2'] = gp(ld)
    s['b2'] = np.ascontiguousarray(np.stack([gp(inp['ssm_b_re']), gp(inp['ssm_b_im'])], axis=2))
    cr_ = np.asarray(inp['ssm_c_re'], np.float32).transpose(0, 1, 2, 4, 3)
    ci_ = np.asarray(inp['ssm_c_im'], np.float32).transpose(0, 1, 2, 4, 3)
    s['c2'] = np.ascontiguousarray(np.stack([gp(cr_), gp(ci_)], axis=2))
    dsk = np.asarray(inp['ssm_d'], np.float32).reshape(NL, 32, 16)
    s['dskT'] = f(np.broadcast_to(dsk.transpose(0, 2, 1)[:, None, :, :], (NL, 8, 16, 32)).reshape(NL, 128, 32))
    s['w_glu'] = f(inp['w_glu'])
    cw = np.asarray(inp['conv_w'], np.float32)
    s['convwT'] = f(cw.reshape(NL, 3, 4, 128).transpose(0, 3, 2, 1))
    s['w_branch'] = f(inp['w_branch'])
    s['w_gate'] = f(inp['w_gate'])
    s['b_gateT'] = np.stack([_colT(inp['b_gate'][l], 32) for l in range(NL)])
    s['w_out'] = f(inp['w_out'])
    s['w_ffn_in'] = f(inp['w_ffn_in'])
    s['w_ffn_out'] = f(inp['w_ffn_out'])
    return s


def _core_map(inp, shared, consts, core):
    m = dict(shared)
    if core < 2:
        b = core
        m.update(consts[1])
        m['xin'] = np.ascontiguousarray(np.asarray(inp['x_sample'][b], np.float32))
        m['condT'] = _colT(inp['c'][b], 8)
        m['cache_ckv'] = np.ascontiguousarray(np.asarray(inp['cache_ckv'][b], np.float32))
        m['cache_kpe'] = np.ascontiguousarray(np.asarray(inp['cache_kpe'][b], np.float32))
        st = np.asarray(inp['state_ssm'][b], np.float32)
        st = st.reshape(NL, 2, 16, 2, 64, 2)
        m['h0'] = np.ascontiguousarray(st.transpose(0, 3, 4, 5, 1, 2).reshape(NL, 128, 64))
        h0 = np.zeros((NL, 128, 128), np.float32)
        h0[:, :, :64] = m['h0']
        m['h0'] = h0
    else:
        pc = (core - 2) % 4
        m.update(consts[8])
        m['xin'] = np.ascontiguousarray(np.asarray(inp['x_prompt'][pc * 8:(pc + 1) * 8], np.float32).reshape(NT, D))
        m['condT'] = _colT(inp['c_ctx'], 8)
        m['cache_ckv'] = np.zeros((NL, PAST, KVR), np.float32)
        m['cache_kpe'] = np.zeros((NL, PAST, RD), np.float32)
        m['h0'] = np.zeros((NL, 128, 128), np.float32)
    return m


_CACHE = {}


def kernel(**inputs):
    if 'nc' not in _CACHE:
        _CACHE['nc'] = build()[0]
        _CACHE['consts'] = {1: _consts(1), 8: _consts(8)}
    nc = _CACHE['nc']
    shared = _shared(inputs)
    maps = [_core_map(inputs, shared, _CACHE['consts'], c) for c in range(8)]
    res = run_bass_kernel_spmd(nc, maps, core_ids=list(range(8)))
    R = res.results
    y_s = np.stack([R[b]['y_out'] for b in range(2)]).astype(np.float32)
    y_p = np.concatenate([R[2 + i]['y_out'].reshape(8, 256, D) for i in range(4)]).astype(np.float32)
    ckv = np.concatenate([R[2 + i]['ckv_out'].reshape(NL, 8, 256, KVR).transpose(1, 0, 2, 3) for i in range(4)])
    kpe = np.concatenate([R[2 + i]['kpe_out'].reshape(NL, 8, 256, RD).transpose(1, 0, 2, 3) for i in range(4)])
    ss = []
    for i in range(4):
        a = R[2 + i]['ssm_out'].reshape(NL, 2, 64, 8, 2, 2, 16)
        a = a.transpose(3, 0, 5, 6, 1, 2, 4).reshape(8, NL, 2, 32, 64, 2)
        ss.append(a)
    ssm = np.concatenate(ss)
    return (y_p, y_s, ckv.astype(np.float32), kpe.astype(np.float32), ssm.astype(np.float32))
```

```python
import math
import numpy as np
import ml_dtypes
import concourse.bass as bass
import concourse.mybir as mybir
from concourse.bass_utils import run_bass_kernel_spmd

F32 = mybir.dt.float32
BF16 = mybir.dt.bfloat16
I32 = mybir.dt.int32
ALU = mybir.AluOpType
AF = mybir.ActivationFunctionType
NPBF = ml_dtypes.bfloat16

D = 1024
NT = 2048
DEPTH = 4
NL = DEPTH
PAST = 512
NK = NT + PAST
BW = 512
QR = 384
KVR = 256
RD = 32
QK = 96
NH = 8
DFF = 2816
OFF_SSM, OFF_FFT, OFF_CQ, OFF_CKV, OFF_KPE, OFF_CONV = 0, 512, 1024, 1408, 1664, 1696
IN_COLS = 3232
EPS = 1e-6
BIG = 30000.0
TT = 512
NTT = NT // TT
BRANCHES = 'ABCD'
import os
ATT_SKIP = bool(int(os.environ.get('ATT_SKIP', '0')))
CUT = int(os.environ.get('CUT', '9'))
HSTOP = int(os.environ.get('HSTOP', '9'))
ASTOP = int(os.environ.get('ASTOP', '9'))
A3 = int(os.environ.get('A3', '9'))


class Sched:
    def __init__(self, nc, nds=24):
        self.nc = nc
        self.eng = {'pe': nc.tensor, 'act': nc.scalar, 'dve': nc.vector, 'pool': nc.gpsimd, 'sp': nc.sync}
        self.sem = {e: nc.alloc_semaphore(name='sem_' + e) for e in self.eng}
        self.cnt = {e: 0 for e in self.eng}
        self.dsem = [nc.alloc_semaphore(name='dsem%d' % i) for i in range(nds)]
        self.dcnt = [0] * nds
        self.dpool = {'sp': list(range(0, nds // 2)), 'pool': list(range(nds // 2, nds))}
        self.dnext = {'sp': 0, 'pool': 0}
        self.waited = {e: {} for e in self.eng}
        self.lastw = {}
        self.readers = {}
        self.nwaits = 0

    def _wait(self, e, toks):
        w = self.waited[e]
        best = {}
        for t in toks:
            if t is None:
                continue
            key = (t[0], t[1])
            if t[0] == 'c' and t[1] == e and e == 'pe':
                continue
            if w.get(key, 0) >= t[2]:
                continue
            if best.get(key, 0) < t[2]:
                best[key] = t[2]
        for key, v in best.items():
            s = self.sem[key[1]] if key[0] == 'c' else self.dsem[key[1]]
            self.eng[e].wait_ge(s, v)
            w[key] = v
            self.nwaits += 1

    def _deps(self, reads, writes):
        deps = set()
        for k in reads:
            t = self.lastw.get(k)
            if t is not None:
                deps.add(t)
        for k in writes:
            t = self.lastw.get(k)
            if t is not None:
                deps.add(t)
            for t in self.readers.get(k, {}).values():
                deps.add(t)
        return deps

    def _commit(self, tok, reads, writes):
        for k in writes:
            self.lastw[k] = tok
            self.readers[k] = {}
        for k in reads:
            r = self.readers.setdefault(k, {})
            r[(tok[0], tok[1])] = tok

    mute = False

    def op(self, e, fn, reads=(), writes=()):
        if self.mute:
            return
        self._wait(e, self._deps(reads, writes))
        inst = fn(self.eng[e])
        self.cnt[e] += 1
        inst.then_inc(self.sem[e], 1)
        self._commit(('c', e, self.cnt[e]), reads, writes)

    def dma(self, e, out, in_, reads=(), writes=(), **kw):
        if self.mute:
            return
        deps = self._deps(reads, writes)
        pl = self.dpool[e]
        i = pl[self.dnext[e]]
        self.dnext[e] = (self.dnext[e] + 1) % len(pl)
        if self.dcnt[i] > 0:
            deps.add(('d', i, self.dcnt[i] * 16))
        self._wait(e, deps)
        inst = self.eng[e].dma_start(out=out, in_=in_, **kw)
        self.dcnt[i] += 1
        inst.then_inc(self.dsem[i], 16)
        self._commit(('d', i, self.dcnt[i] * 16), reads, writes)

    def barrier(self):
        toks = [('c', e, self.cnt[e]) for e in self.eng if self.cnt[e] > 0]
        toks += [('d', i, c * 16) for i, c in enumerate(self.dcnt) if c > 0]
        for e in self.eng:
            self._wait(e, toks)
        self.lastw = {}
        self.readers = {}


def build(dbg=None, nlayers=NL):
    dbg = dbg or []
    nc = bass.Bass("TRN2", target_bir_lowering=False)
    S = Sched(nc)

    def din(name, shape, dt=F32):
        return nc.dram_tensor(name, list(shape), dt, kind="ExternalInput").ap()

    def dout(name, shape, dt=F32):
        return nc.dram_tensor(name, list(shape), dt, kind="ExternalOutput").ap()

    xin = din("xin", [NT, D])
    condT = din("condT", [128, 8])
    cache_ckv = din("cache_ckv", [NL, PAST, KVR])
    cache_kpe = din("cache_kpe", [NL, PAST, RD])
    h0 = din("h0", [NL, 128, 128])
    w_ada = din("w_ada", [NL, D, 6 * D])
    b_adaT = din("b_adaT", [NL, 128, 48])
    nmgT = din("nmgT", [NL, 128, 8])
    nfgT = din("nfgT", [NL, 128, 8])
    w_in = din("w_in", [NL, D, IN_COLS])
    qagT = din("qagT", [NL, 128, 3])
    kvag = din("kvag", [NL, 128, 2])
    w_uq = din("w_uq", [NL, QR, NH * QK])
    w_uq_sw = din("w_uq_sw", [NL, QR, NH * QK])
    w_uk = din("w_uk", [NL, KVR, NH * QK])
    w_uv = din("w_uv", [NL, KVR, NH * 64])
    qng = din("qng", [NL, 128, 4])
    dskT = din("dskT", [NL, 128, 32])
    lam2 = din("lam2", [NL, 128, 2, 32])
    logdt2 = din("logdt2", [NL, 128, 32])
    b2 = din("b2", [NL, 128, 2, 32, 16])
    c2 = din("c2", [NL, 128, 2, 32, 16])
    ramp32 = din("ramp32", [128, 32, 24])
    halfmask = din("halfmask", [128, 2])
    w_glu = din("w_glu", [NL, BW, BW])
    convwT = din("convwT", [NL, 128, 4, 3])
    w_branch = din("w_branch", [NL, 4, BW, D])
    w_gate = din("w_gate", [NL, D, 4 * D])
    b_gateT = din("b_gateT", [NL, 128, 32])
    w_out = din("w_out", [NL, D, D])
    w_ffn_in = din("w_ffn_in", [NL, D, 2 * DFF])
    w_ffn_out = din("w_ffn_out", [NL, DFF, D])
    ropeC = din("ropeC", [128, NT], BF16)
    ropeS = din("ropeS", [128, NT], BF16)
    qind = din("qind", [8, NT], BF16)
    kind = din("kind", [8, NK], BF16)
    dftC = din("dftC", [NT, NT], BF16)
    dftS = din("dftS", [NT, NT], BF16)
    ccsc = din("ccsc", [128, 256], BF16)
    seqflag = din("seqflag", [128, 1])
    smask = din("smask", [128, 2, 257])
    tzmask = din("tzmask", [128, 2, 512])
    selpad = din("selpad", [128, 8, 240], BF16)
    seltpad = din("seltpad", [128, 8, 240], BF16)
    shiftm = din("shiftm", [32, 2, 96], BF16)
    identf = din("identf", [128, 128])

    y_out = dout("y_out", [NT, D])
    ckv_out = dout("ckv_out", [NL, NT, KVR])
    kpe_out = dout("kpe_out", [NL, NT, RD])
    ssm_out = dout("ssm_out", [NL, 128, 512])
    dbg_out = {}

    from contextlib import ExitStack
    es = ExitStack()

    _uid = [0]

    def T(name, shape, dt=F32):
        _uid[0] += 1
        return nc.sbuf_tensor("%s_%d" % (name, _uid[0]), list(shape), dt)

    def sb(name, shape, dt=F32):
        return es.enter_context(T(name, list(shape), dt))

    with es:
        x = sb("x", [128, 8, NT])
        xn = sb("xn", [128, 8, NT], BF16)
        ones_bf = sb("ones_bf", [128, 128], BF16)
        ident = sb("ident", [128, 128])
        ident_bf = sb("ident_bf", [128, 128], BF16)
        mod = sb("mod", [128, 48])
        vecs = sb("vecs", [128, 64])
        epsb = sb("epsb", [128, 1])
        psum = [es.enter_context(nc.psum_tensor("ps%d" % i, [128, 512], F32)) for i in range(8)]
        pstate = {'i': 0}

        def PS():
            i = pstate['i']
            pstate['i'] = (i + 1) % 8
            return psum[i], 'ps%d' % i

        def mm(out, lhsT, rhs, start, stop, reads, writes):
            S.op('pe', lambda e: e.matmul(out, lhsT, rhs, start=start, stop=stop), reads, writes)

        def tr(out, in_, idn, reads, writes):
            S.op('pe', lambda e: e.transpose(out, in_, idn), reads, writes)

        def act(out, in_, func, reads, writes, bias=None, scale=None):
            kw = {}
            if bias is not None:
                kw['bias'] = bias
            if scale is not None:
                kw['scale'] = scale
            S.op('act', lambda e: e.activation(out=out, in_=in_, func=func, **kw), reads, writes)

        def tt(out, in0, in1, op, reads, writes, eng='dve'):
            S.op(eng, lambda e: e.tensor_tensor(out=out, in0=in0, in1=in1, op=op), reads, writes)

        def ts(out, in0, s1, s2, op0, op1, reads, writes, eng='dve'):
            if op1 is None:
                S.op(eng, lambda e: e.tensor_scalar(out=out, in0=in0, scalar1=s1, scalar2=None, op0=op0), reads, writes)
            else:
                S.op(eng, lambda e: e.tensor_scalar(out=out, in0=in0, scalar1=s1, scalar2=s2, op0=op0, op1=op1), reads, writes)

        def stt(out, in0, scalar, in1, op0, op1, reads, writes):
            S.op('dve', lambda e: e.scalar_tensor_tensor(out=out, in0=in0, scalar=scalar, in1=in1, op0=op0, op1=op1), reads, writes)

        def cp(out, in_, reads, writes, eng='dve'):
            S.op(eng, lambda e: e.tensor_copy(out=out, in_=in_), reads, writes)

        def recip(out, in_, reads, writes):
            S.op('dve', lambda e: e.reciprocal(out=out, in_=in_), reads, writes)

        def memset(t_ap, val, writes, eng='dve'):
            S.op(eng, lambda e: e.memset(t_ap, val), (), writes)

        def dump(name, ap, shape, dt, key):
            if name in dbg:
                o = dout("dbg_" + name, shape, dt)
                dbg_out[name] = o
                S.dma('sp', o, ap, reads=[key], writes=['dbg_' + name])

        def rsqrt_inplace(ap, key, scale):
            act(ap, ap, AF.Sqrt, [key, 'epsb'], [key], bias=epsb[0:ap.shape[0], 0:1], scale=scale)
            recip(ap, ap, [key], [key])

        memset(ones_bf[:], 1.0, ['ones_bf'])
        memset(epsb[:], EPS, ['epsb'])
        S.dma('sp', ident[:], identf[:, :], (), ['ident'])
        cp(ident_bf[:], ident[:], ['ident'], ['ident_bf'])

        with T("xtm", [128, 4, D], F32) as xtm:
            for t in range(NTT):
                S.dma('sp', xtm[:], xin[t * TT:(t + 1) * TT, :].rearrange("(n p) c -> p n c", p=128), (), ['xtm'])
                for oc in range(8):
                    ps, pk = PS()
                    for n in range(4):
                        tr(ps[:, n * 128:(n + 1) * 128], xtm[:, n, oc * 128:(oc + 1) * 128], ident[:], ['xtm', 'ident'], [pk])
                    if oc % 2 == 0:
                        cp(x[:, oc, t * TT:(t + 1) * TT], ps[:], [pk], [('x', oc, t)])
                    else:
                        act(x[:, oc, t * TT:(t + 1) * TT], ps[:], AF.Copy, [pk], [('x', oc, t)])
            S.barrier()

        def load_w(dst, src, kchunks, keyw, eng='pool'):
            S.dma(eng, dst[:, 0:kchunks, :], src.rearrange("(kc p) n -> p kc n", p=128), (), [keyw])

        def rmsnorm_mod(l, gname, sc_c, sh_c):
            with T("nsq", [128, 2, TT], BF16) as nsq, \
                    T("nrs", [128, TT], F32) as nrs, \
                    T("ntmp", [128, 2, TT], F32) as ntmp:
                for t in range(NTT):
                    sl = slice(t * TT, (t + 1) * TT)
                    ps, pk = PS()
                    for oc in range(8):
                        b = oc % 2
                        act(nsq[:, b, :], x[:, oc, sl], AF.Square, [('x', oc, t)], [('nsq', b)])
                        mm(ps[:], ones_bf[:], nsq[:, b, :], oc == 0, oc == 7, ['ones_bf', ('nsq', b)], [pk])
                    cp(nrs[:], ps[:], [pk], ['nrs'])
                    rsqrt_inplace(nrs[:], 'nrs', 1.0 / D)
                    for oc in range(8):
                        b = oc % 2
                        tt(ntmp[:, b, :], x[:, oc, sl], nrs[:], ALU.mult, [('x', oc, t), 'nrs'], [('ntmp', b)])
                        act(xn[:, oc, sl], ntmp[:, b, :], AF.Identity, [('ntmp', b), 'vecs'], [('xn', oc, t)],
                            bias=vecs[:, sh_c + oc:sh_c + oc + 1], scale=vecs[:, sc_c + oc:sc_c + oc + 1])
            S.barrier()

        for l in range(nlayers):
            with T("wada", [128, 8, 1024], BF16) as wada, \
                    T("scond", [128, 8], BF16) as scond, \
                    T("ctmp", [128, 8], F32) as ctmp, \
                    T("mtmp", [128, 64], F32) as mtmp:
                S.dma('sp', ctmp[:], condT[:, :], (), ['ctmp'])
                act(scond[:], ctmp[:], AF.Silu, ['ctmp'], ['scond'])
                S.dma('sp', mtmp[:, 0:48], b_adaT[l], (), ['mtmp'])
                S.dma('sp', mtmp[:, 48:56], nmgT[l], (), ['mtmp'])
                S.dma('sp', mtmp[:, 56:64], nfgT[l], (), ['mtmp'])
                psm, pkm = PS()
                for piece in range(6):
                    load_w(wada, w_ada[l][:, piece * 1024:(piece + 1) * 1024], 8, 'wada')
                    for c in range(8):
                        col = piece * 8 + c
                        for kc in range(8):
                            mm(psm[:, col:col + 1], wada[:, kc, c * 128:(c + 1) * 128], scond[:, kc:kc + 1],
                               kc == 0, kc == 7, ['wada', 'scond'], [pkm])
                tt(mod[:], psm[:, 0:48], mtmp[:, 0:48], ALU.add, [pkm, 'mtmp'], ['mod'])
                stt(vecs[:, 0:8], mod[:, 8:16], 1.0, mtmp[:, 48:56], ALU.add, ALU.mult, ['mod', 'mtmp'], ['vecs'])
                cp(vecs[:, 8:16], mod[:, 0:8], ['mod'], ['vecs'])
                stt(vecs[:, 16:24], mod[:, 32:40], 1.0, mtmp[:, 56:64], ALU.add, ALU.mult, ['mod', 'mtmp'], ['vecs'])
                cp(vecs[:, 24:32], mod[:, 24:32], ['mod'], ['vecs'])
                S.barrier()
            dump("mod%d" % l, mod[:], [128, 48], F32, 'mod')

            rmsnorm_mod(l, None, 0, 8)
            dump("xn%d" % l, xn[:], [128, 8, NT], BF16, ('xn', 0, 0))

            def epilogue(k, ysrc, ykeyfn, wb, wg, wo, gp, sg, t):
                sl = slice(t * TT, (t + 1) * TT)
                for oc in range(8):
                    pa, pka = PS()
                    for kc in range(4):
                        mm(pa[:], wb[:, kc, oc * 128:(oc + 1) * 128], ysrc(kc, t), kc == 0, kc == 3,
                           ['wb', ykeyfn(kc, t)], [pka])
                    pb, pkb = PS()
                    for kc in range(8):
                        mm(pb[:], wg[:, kc, oc * 128:(oc + 1) * 128], xn[:, kc, sl], kc == 0, kc == 7,
                           ['wg', ('xn', kc, t)], [pkb])
                    b = oc % 2
                    act(sg[:, b, :], pb[:], AF.Sigmoid, [pkb, 'vecs'], [('sg', b)],
                        bias=vecs[:, 32 + k * 8 + oc:32 + k * 8 + oc + 1])
                    tt(gp[:, oc, :], pa[:], sg[:, b, :], ALU.mult, [pka, ('sg', b)], [('gp', oc)])
                for oc2 in range(8):
                    pc, pkc = PS()
                    for kc in range(8):
                        mm(pc[:], wo[:, kc, oc2 * 128:(oc2 + 1) * 128], gp[:, kc, :], kc == 0, kc == 7,
                           ['wo', ('gp', kc)], [pkc])
                    stt(x[:, oc2, sl], pc[:], mod[:, 16 + oc2:17 + oc2], x[:, oc2, sl], ALU.mult, ALU.add,
                        [pkc, 'mod', ('x', oc2, t)], [('x', oc2, t)])

            S.dma('sp', vecs[:, 32:64], b_gateT[l], (), ['vecs'])

            def run_epi(k, ybuf):
                with T("wo", [128, 8, D], BF16) as wo, \
                        T("wb", [128, 4, D], BF16) as wb, \
                        T("wg", [128, 8, D], BF16) as wg, \
                        T("gp", [128, 8, TT], BF16) as gp, \
                        T("sg", [128, 2, TT], F32) as sg:
                    load_w(wb, w_branch[l, k], 4, 'wb')
                    load_w(wg, w_gate[l][:, k * D:(k + 1) * D], 8, 'wg')
                    load_w(wo, w_out[l], 8, 'wo')
                    for t in range(NTT if not int(os.environ.get('SKE', '0')) else 0):
                        epilogue(k, lambda kc, t_: ybuf[:, kc, t_ * TT:(t_ + 1) * TT],
                                 lambda kc, t_: ('y', kc, t_), wb, wg, wo, gp, sg, t)
                    S.barrier()

            if 'D' in BRANCHES:
              with T("ybuf", [128, 4, NT], BF16) as ybuf:
                with T("wc", [128, 8, 512], BF16) as wc, \
                        T("Tb", [128, 4, 8, 258], BF16) as Tb, \
                        T("ctm", [128, 3, TT], F32) as ctm, \
                        T("cvw", [128, 4, 3], F32) as cvw, \
                        T("sflag", [128, 1], F32) as sflag:
                    S.dma('sp', cvw[:], convwT[l], (), ['cvw'])
                    S.dma('sp', sflag[:], seqflag[:, :], (), ['sflag'])
                    for oc in range(4):
                        memset(Tb[:, oc, :, 0:1], 0.0, [('Tb', oc)])
                        memset(Tb[:, oc, :, 257:258], 0.0, [('Tb', oc)])
                    load_w(wc, w_in[l][:, OFF_CONV:OFF_CONV + 512], 8, 'wc')
                    for t in range(NTT):
                        sl = slice(t * TT, (t + 1) * TT)
                        for oc in range(4):
                            ph, pkh = PS()
                            for kc in range(8):
                                mm(ph[:], wc[:, kc, oc * 128:(oc + 1) * 128], xn[:, kc, sl], kc == 0, kc == 7,
                                   ['wc', ('xn', kc, t)], [pkh])
                            act(Tb[:, oc, 2 * t:2 * t + 2, 1:257], ph[:].rearrange("p (a b) -> p a b", a=2), AF.Copy, [pkh], [('Tb', oc)])
                    load_w(wc, w_in[l][:, OFF_CONV + 1024:OFF_CONV + 1536], 8, 'wc')
                    for t in range(NTT):
                        sl = slice(t * TT, (t + 1) * TT)
                        for oc in range(4):
                            pg, pkg = PS()
                            for kc in range(8):
                                mm(pg[:], wc[:, kc, oc * 128:(oc + 1) * 128], xn[:, kc, sl], kc == 0, kc == 7,
                                   ['wc', ('xn', kc, t)], [pkg])
                            tt(Tb[:, oc, 2 * t:2 * t + 2, 1:257], pg[:].rearrange("p (a b) -> p a b", a=2),
                               Tb[:, oc, 2 * t:2 * t + 2, 1:257], ALU.mult, [pkg, ('Tb', oc)], [('Tb', oc)])
                    for oc in range(4):
                        ts(Tb[:, oc, 1:8, 0:1], Tb[:, oc, 0:7, 256:257], sflag[:, 0:1], None, ALU.mult, None,
                           [('Tb', oc), 'sflag'], [('Tb', oc)])
                        ts(Tb[:, oc, 0:7, 257:258], Tb[:, oc, 1:8, 1:2], sflag[:, 0:1], None, ALU.mult, None,
                           [('Tb', oc), 'sflag'], [('Tb', oc)])
                    load_w(wc, w_in[l][:, OFF_CONV + 512:OFF_CONV + 1024], 8, 'wc')
                    for t in range(NTT):
                        sl = slice(t * TT, (t + 1) * TT)
                        for oc in range(4):
                            pgb, pkgb = PS()
                            for kc in range(8):
                                mm(pgb[:], wc[:, kc, oc * 128:(oc + 1) * 128], xn[:, kc, sl], kc == 0, kc == 7,
                                   ['wc', ('xn', kc, t)], [pkgb])
                            b = oc % 3
                            cv = ctm[:, b, :].rearrange("p (a b) -> p a b", a=2)
                            ts(cv, Tb[:, oc, 2 * t:2 * t + 2, 1:257], cvw[:, oc, 1:2], None, ALU.mult, None,
                               [('Tb', oc), 'cvw'], [('ctm', b)])
                            stt(cv, Tb[:, oc, 2 * t:2 * t + 2, 0:256], cvw[:, oc, 0:1], cv, ALU.mult, ALU.add,
                                [('Tb', oc), ('ctm', b), 'cvw'], [('ctm', b)])
                            stt(cv, Tb[:, oc, 2 * t:2 * t + 2, 2:258], cvw[:, oc, 2:3], cv, ALU.mult, ALU.add,
                                [('Tb', oc), ('ctm', b), 'cvw'], [('ctm', b)])
                            tt(ybuf[:, oc, sl], pgb[:], ctm[:, b, :], ALU.mult, [pkgb, ('ctm', b)], [('y', oc, t)])
                    S.barrier()
                dump("yconv%d" % l, ybuf[:], [128, 4, NT], BF16, ('y', 0, 0))
                run_epi(3, ybuf)

            if 'B' in BRANCHES:
              with T("ybuf", [128, 4, NT], BF16) as ybuf:
                with T("AB", [128, 16, 4, 256], BF16) as AB:
                    with T("wf", [128, 8, 512], BF16) as wf, \
                            T("uf", [128, 4, NT], BF16) as uf, \
                            T("ccsc_sb", [128, 256], BF16) as ccsc_sb:
                        load_w(wf, w_in[l][:, OFF_FFT:OFF_FFT + 512], 8, 'wf')
                        S.dma('sp', ccsc_sb[:], ccsc[:, :], (), ['ccsc'])
                        for t in range(NTT):
                            sl = slice(t * TT, (t + 1) * TT)
                            for grp in range(4):
                                pu, pku = PS()
                                for kc in range(8):
                                    mm(pu[:], wf[:, kc, grp * 128:(grp + 1) * 128], xn[:, kc, sl], kc == 0, kc == 7,
                                       ['wf', ('xn', kc, t)], [pku])
                                if grp % 2 == 0:
                                    act(uf[:, grp, sl], pu[:], AF.Copy, [pku], [('uf', grp, t)])
                                else:
                                    cp(uf[:, grp, sl], pu[:], [pku], [('uf', grp, t)])
                        for grp in range(4):
                            for n2 in range(8):
                                pa, pka = PS()
                                for j in range(2):
                                    n = n2 * 2 + j
                                    mm(pa[:, j * 256:(j + 1) * 256], uf[:, grp, n * 128:(n + 1) * 128], ccsc_sb[:], True, True,
                                       [('uf', grp, n // 4), 'ccsc'], [pka])
                                if n2 % 2 == 0:
                                    act(AB[:, n2 * 2:n2 * 2 + 2, grp, :], pa[:].rearrange("p (a b) -> p a b", a=2), AF.Copy, [pka], [('AB', grp)])
                                else:
                                    cp(AB[:, n2 * 2:n2 * 2 + 2, grp, :], pa[:].rearrange("p (a b) -> p a b", a=2), [pka], [('AB', grp)])
                        S.barrier()
                    with T("tabC", [128, 2, 16, 256], BF16) as tabC, T("tabS", [128, 2, 16, 256], BF16) as tabS:
                        for kt in range(8):
                            b = kt % 2
                            S.dma('sp', tabC[:, b], dftC[:, kt * 256:(kt + 1) * 256].rearrange("(nt p) k -> p nt k", p=128), (), [('tabC', b)])
                            S.dma('sp', tabS[:, b], dftS[:, kt * 256:(kt + 1) * 256].rearrange("(nt p) k -> p nt k", p=128), (), [('tabS', b)])
                            for g2 in range(2):
                                py, pky = PS()
                                for gg in range(2):
                                    grp = g2 * 2 + gg
                                    for n in range(16):
                                        mm(py[:, gg * 256:(gg + 1) * 256], AB[:, n, grp, 0:128], tabC[:, b, n, :], n == 0, False,
                                           [('AB', grp), ('tabC', b)], [pky])
                                        mm(py[:, gg * 256:(gg + 1) * 256], AB[:, n, grp, 128:256], tabS[:, b, n, :], False, n == 15,
                                           [('AB', grp), ('tabS', b)], [pky])
                                cp(ybuf[:, g2 * 2:g2 * 2 + 2, kt * 256:(kt + 1) * 256], py[:].rearrange("p (a b) -> p a b", a=2),
                                   [pky], [('y', g2 * 2, kt // 2), ('y', g2 * 2 + 1, kt // 2)])
                        S.barrier()
                dump("yfft%d" % l, ybuf[:], [128, 4, NT], BF16, ('y', 0, 0))
                run_epi(1, ybuf)
            if 'C' in BRANCHES:
              with T("ybuf", [128, 4, NT], BF16) as ybuf:
                with T("cqn", [128, 3, NT], BF16) as cqn, T("ckvT", [128, 2, NK], BF16) as ckvT, \
                        T("kpeT", [128, NK], BF16) as kpeT:
                    with T("wa", [128, 8, 384], BF16) as wa, T("wkv", [128, 8, 288], BF16) as wkv, T("ctk", [128, 4, 288], F32) as ctk, T("cqf", [128, 3, TT], F32) as cqf, \
                            T("asq", [128, 2, TT], BF16) as asq, T("ars", [128, TT], F32) as ars, \
                            T("atmp", [128, 2, TT], F32) as atmp, T("tkm", [128, 2, 288], F32) as tkm, \
                            T("junk", [128, 256], F32) as junk, T("ss", [128, 2, 8], F32) as ss, \
                            T("kvag_sb", [128, 2], F32) as kvag_sb, T("qag", [128, 3], F32) as qag:
                        load_w(wa, w_in[l][:, OFF_CQ:OFF_CQ + 384], 8, 'wa')
                        load_w(wkv, w_in[l][:, OFF_CKV:OFF_CKV + 288], 8, 'wkv')
                        memset(ss[:, 0, :], 1.0, [('ss', 0)])
                        memset(ss[:, 1, :], 1.0, [('ss', 1)])
                        S.dma('sp', kvag_sb[:], kvag[l], (), ['kvag'])
                        S.dma('sp', qag[:], qagT[l], (), ['qag'])
                        for t in range(NTT if not int(os.environ.get('SKQ','0')) else 0):
                            sl = slice(t * TT, (t + 1) * TT)
                            for c in range(3):
                                pq, pkq = PS()
                                for kc in range(8):
                                    mm(pq[:], wa[:, kc, c * 128:(c + 1) * 128], xn[:, kc, sl], kc == 0, kc == 7, ['wa', ('xn', kc, t)], [pkq])
                                cp(cqf[:, c, :], pq[:], [pkq], [('cqf', c)])
                            pss, pkss = PS()
                            for c in range(3):
                                act(asq[:, c % 2, :], cqf[:, c, :], AF.Square, [('cqf', c)], [('asq', c % 2)])
                                mm(pss[:], ones_bf[:], asq[:, c % 2, :], c == 0, c == 2, ['ones_bf', ('asq', c % 2)], [pkss])
                            cp(ars[:], pss[:], [pkss], ['ars'])
                            rsqrt_inplace(ars[:], 'ars', 1.0 / QR)
                            for c in range(3):
                                tt(atmp[:, c % 2, :], cqf[:, c, :], ars[:], ALU.mult, [('cqf', c), 'ars'], [('atmp', c % 2)])
                                ts(cqn[:, c, sl], atmp[:, c % 2, :], qag[:, c:c + 1], None, ALU.mult, None,
                                   [('atmp', c % 2), 'qag'], [('cqn', c, t)])

                        def tok2feat(b, n):
                            pst, pkst = PS()
                            tr(pst[:, 0:128], tkm[:, b, 0:128], ident[:], [('tkm', b), 'ident'], [pkst])
                            tr(pst[:, 128:256], tkm[:, b, 128:256], ident[:], [('tkm', b), 'ident'], [pkst])
                            tr(pst[0:32, 256:384], tkm[:, b, 256:288], ident[:], [('tkm', b), 'ident'], [pkst])
                            cp(ckvT[:, :, n * 128:(n + 1) * 128], pst[:, 0:256].rearrange("p (a b) -> p a b", a=2), [pkst], [('ckvT', n // 4)])
                            act(kpeT[0:32, n * 128:(n + 1) * 128], pst[0:32, 256:384], AF.Copy, [pkst], [('kpeT', n // 4), pkst])

                        for t in range(NTT):
                            sl = slice(t * TT, (t + 1) * TT)
                            for c in range(2):
                                pq, pkq = PS()
                                for kc in range(8):
                                    mm(pq[:], wkv[:, kc, c * 128:(c + 1) * 128], xn[:, kc, sl], kc == 0, kc == 7, ['wkv', ('xn', kc, t)], [pkq])
                                cp(cqf[:, c, :], pq[:], [pkq], [('cqf', c)])
                            if CUT >= 2:
                                pq, pkq = PS()
                                for kc in range(8):
                                    mm(pq[0:32, :], wkv[:, kc, 256:288], xn[:, kc, sl], kc == 0, kc == 7, ['wkv', ('xn', kc, t)], [pkq])
                                cp(cqf[0:32, 2, :], pq[0:32, :], [pkq], [('cqf', 2)])
                                act(kpeT[0:32, sl], pq[0:32, :], AF.Copy, [pkq], [('kpeT', t), pkq])
                            if CUT >= 3:
                                pss, pkss = PS()
                                for c in range(2):
                                    act(asq[:, c, :], cqf[:, c, :], AF.Square, [('cqf', c)], [('asq', c)])
                                    mm(pss[:], ones_bf[:], asq[:, c, :], c == 0, c == 1, ['ones_bf', ('asq', c)], [pkss])
                                cp(ars[:], pss[:], [pkss], ['ars'])
                                rsqrt_inplace(ars[:], 'ars', 1.0 / KVR)
                                for c in range(2):
                                    tt(atmp[:, c, :], cqf[:, c, :], ars[:], ALU.mult, [('cqf', c), 'ars'], [('atmp', c)])
                                    ts(cqf[:, c, :], atmp[:, c, :], kvag_sb[:, c:c + 1], None, ALU.mult, None,
                                       [('atmp', c), 'kvag'], [('cqf', c)])
                                    act(ckvT[:, c, sl], cqf[:, c, :], AF.Copy, [('cqf', c)], [('ckvT', t)])
                            if CUT >= 4:
                                for blk in range(4):
                                    b = blk % 2
                                    n = t * 4 + blk
                                    bs = slice(blk * 128, (blk + 1) * 128)
                                    pso, pkso = PS()
                                    tr(pso[:, 0:128], cqf[:, 0, bs], ident[:], [('cqf', 0), 'ident'], [pkso])
                                    tr(pso[:, 128:256], cqf[:, 1, bs], ident[:], [('cqf', 1), 'ident'], [pkso])
                                    tr(pso[:, 256:288], cqf[0:32, 2, bs], ident[0:32, 0:32], [('cqf', 2), 'ident'], [pkso])
                                    cp(tkm[:, b, :], pso[:, 0:288], [pkso], [('tkm', b)])
                                    if not int(os.environ.get('SKA', '0')):
                                        S.dma('sp', ckv_out[l, n * 128:(n + 1) * 128, :], tkm[:, b, 0:256], [('tkm', b)], [('ckv_out', n)])
                                        S.dma('sp', kpe_out[l, n * 128:(n + 1) * 128, :], tkm[:, b, 256:288], [('tkm', b)], [('kpe_out', n)])
                        S.barrier()
                        if not int(os.environ.get('SK2', '0')):
                            S.dma('sp', ctk[:, :, 0:256], cache_ckv[l].rearrange("(n p) c -> p n c", p=128), (), ['ctk'])
                            S.dma('sp', ctk[:, :, 256:288], cache_kpe[l].rearrange("(n p) c -> p n c", p=128), (), ['ctk'])
                            for n in range(PAST // 128):
                                pst, pkst = PS()
                                tr(pst[:, 0:128], ctk[:, n, 0:128], ident[:], ['ctk', 'ident'], [pkst])
                                tr(pst[:, 128:256], ctk[:, n, 128:256], ident[:], ['ctk', 'ident'], [pkst])
                                tr(pst[0:32, 256:384], ctk[:, n, 256:288], ident[:], ['ctk', 'ident'], [pkst])
                                nn_ = NT // 128 + n
                                cp(ckvT[:, :, nn_ * 128:(nn_ + 1) * 128], pst[:, 0:256].rearrange("p (a b) -> p a b", a=2), [pkst], [('ckvT', nn_ // 4)])
                                act(kpeT[0:32, nn_ * 128:(nn_ + 1) * 128], pst[0:32, 256:384], AF.Copy, [pkst], [('kpeT', nn_ // 4), pkst])
                        S.barrier()
                    es3 = ExitStack()
                    with es3:
                        def A(name, shape, dt=F32):
                            return es3.enter_context(T(name, shape, dt))
                        wuq_sb = A("wuq_sb", [128, 3, NH * QK], BF16)
                        wuqs_sb = A("wuqs_sb", [128, 3, NH * QK], BF16)
                        wuk_sb = A("wuk_sb", [128, 2, NH * QK], BF16)
                        wuv_sb = A("wuv_sb", [128, 2, NH * 64], BF16)
                        shift_sb = A("shift_sb", [128, 2, 96], BF16)
                        qng_sb = A("qng_sb", [128, 4])
                        rC = A("rC", [128, NT], BF16)
                        rS = A("rS", [128, NT], BF16)
                        Qh = A("Qh", [128, NT], BF16)
                        Kh = A("Kh", [128, NK], BF16)
                        Ve = A("Ve", [128, NK // 128, 128], BF16)
                        Vo = A("Vo", [128, NK // 128, 128], BF16)
                        Pb = A("Pb", [128, 3, TT], BF16)
                        qf = A("qf", [128, 2, TT])
                        hsq = A("hsq", [128, TT], BF16)
                        hrs = A("hrs", [128, TT])
                        ht = A("ht", [128, 2, TT])
                        rec = A("rec", [128, TT])
                        load_w(wuq_sb, w_uq[l], 3, 'wuq')
                        load_w(wuqs_sb, w_uq_sw[l], 3, 'wuqs')
                        load_w(wuk_sb, w_uk[l], 2, 'wuk')
                        load_w(wuv_sb, w_uv[l], 2, 'wuv')
                        S.dma('sp', shift_sb[0:32], shiftm[:, :, :], (), ['shift'])
                        S.dma('sp', qng_sb[:], qng[l], (), ['qng'])
                        S.dma('sp', rC[:], ropeC[:, :], (), ['rC'])
                        S.dma('sp', rS[:], ropeS[:, :], (), ['rS'])
                        memset(Qh[96:128, :], 0.0, ['Qind'])
                        memset(Kh[96:128, :], 0.0, ['Kind'])
                        S.dma('sp', Qh[96:104, :], qind[:, :], (), ['Qind'])
                        S.dma('sp', Kh[96:104, :], kind[:, :], (), ['Kind'])
                        memset(Ve[:, :, 64:128], 1.0, ['Ve1'])
                        memset(Vo[:, :, 0:64], 1.0, ['Vo1'])

                        def normrope(pq, pkq, pqs, pkqs, gcol, dest, dkey, sl):
                            act(qf[0:96, 0, :], pq[0:96, :], AF.Copy, [pkq], [('qf', 0)])
                            act(hsq[0:96, :], qf[0:96, 0, :], AF.Square, [('qf', 0)], ['hsq'])
                            pz, pkz = PS()
                            mm(pz[0:96, :], ones_bf[0:96, 0:96], hsq[0:96, :], True, True, ['ones_bf', 'hsq'], [pkz])
                            cp(hrs[0:96, :], pz[0:96, :], [pkz], ['hrs'])
                            rsqrt_inplace(hrs[0:96, :], 'hrs', 1.0 / QK)
                            if pqs is not None:
                                act(qf[0:96, 1, :], pqs[0:96, :], AF.Copy, [pkqs], [('qf', 1)])
                                stt(ht[0:96, 0, :], qf[0:96, 0, :], qng_sb[0:96, gcol:gcol + 1], rC[0:96, sl], ALU.mult, ALU.mult,
                                    [('qf', 0), 'qng', 'rC'], [('ht', 0)])
                                stt(ht[0:96, 1, :], qf[0:96, 1, :], qng_sb[0:96, gcol + 1:gcol + 2], rS[0:96, sl], ALU.mult, ALU.mult,
                                    [('qf', 1), 'qng', 'rS'], [('ht', 1)])
                                tt(ht[0:96, 0, :], ht[0:96, 0, :], ht[0:96, 1, :], ALU.add, [('ht', 0), ('ht', 1)], [('ht', 0)])
                                tt(dest, ht[0:96, 0, :], hrs[0:96, :], ALU.mult, [('ht', 0), 'hrs'], [dkey])
                            else:
                                stt(dest, qf[0:96, 0, :], qng_sb[0:96, gcol:gcol + 1], hrs[0:96, :], ALU.mult, ALU.mult,
                                    [('qf', 0), 'qng', 'hrs'], [dkey])

                        for h in range(NH if not ATT_SKIP else 0):
                            V = Ve if h % 2 == 0 else Vo
                            vkey = 'Ve' if h % 2 == 0 else 'Vo'
                            voff = 0 if h % 2 == 0 else 64
                            hs = slice(h * QK, (h + 1) * QK)
                            for t in range(NTT):
                                sl = slice(t * TT, (t + 1) * TT)
                                pq, pkq = PS()
                                for c in range(3):
                                    mm(pq[0:96, :], wuq_sb[:, c, hs], cqn[:, c, sl], c == 0, c == 2, ['wuq', ('cqn', c, t)], [pkq])
                                pqs, pkqs = PS()
                                for c in range(3):
                                    mm(pqs[0:96, :], wuqs_sb[:, c, hs], cqn[:, c, sl], c == 0, c == 2, ['wuqs', ('cqn', c, t)], [pkqs])
                                normrope(pq, pkq, pqs, pkqs, 0, Qh[0:96, sl], ('Qh', t), sl)
                            for kt in range(NK // TT if HSTOP >= 2 else 0):
                                sl = slice(kt * TT, (kt + 1) * TT)
                                pk_, pkk = PS()
                                for c in range(2):
                                    mm(pk_[0:96, :], wuk_sb[:, c, hs], ckvT[:, c, sl], c == 0, False, ['wuk', ('ckvT', kt)], [pkk])
                                mm(pk_[0:96, :], shift_sb[0:32, 0, :], kpeT[0:32, sl], False, True, ['shift', ('kpeT', kt)], [pkk])
                                if kt < NTT:
                                    pks, pkks = PS()
                                    mm(pks[0:96, :], shift_sb[0:32, 1, :], kpeT[0:32, sl], True, True, ['shift', ('kpeT', kt)], [pkks])
                                    normrope(pk_, pkk, pks, pkks, 2, Kh[0:96, sl], ('Kh', kt), sl)
                                else:
                                    normrope(pk_, pkk, None, None, 2, Kh[0:96, sl], ('Kh', kt), sl)
                            for n0 in (range(0, NK // 128, 8) if HSTOP >= 3 else []):
                                nn = min(8, NK // 128 - n0)
                                pv, pkv = PS()
                                for j in range(nn):
                                    n = n0 + j
                                    for c in range(2):
                                        mm(pv[:, j * 64:(j + 1) * 64], ckvT[:, c, n * 128:(n + 1) * 128], wuv_sb[:, c, h * 64:(h + 1) * 64],
                                           c == 0, c == 1, [('ckvT', n // 4), 'wuv'], [pkv])
                                cp(V[:, n0:n0 + nn, voff:voff + 64], pv[:, 0:nn * 64].rearrange("p (a b) -> p a b", a=nn), [pkv], [vkey])
                            for qt in range(NTT if HSTOP >= 4 else 0):
                                sl = slice(qt * TT, (qt + 1) * TT)
                                po, pko = PS()
                                NKT = NK // 128
                                LOOK = 3
                                stiles = {}

                                def emitS(n):
                                    ps_, pks_ = PS()
                                    if ps_ is po:
                                        ps_, pks_ = PS()
                                    mm(ps_[:], Kh[:, n * 128:(n + 1) * 128], Qh[:, sl], True, True,
                                       [('Kh', n // 4), 'Kind', ('Qh', qt), 'Qind'], [pks_])
                                    stiles[n] = (ps_, pks_)

                                for n in range(min(LOOK, NKT)):
                                    emitS(n)
                                for n in range(NKT):
                                    if n + LOOK < NKT:
                                        emitS(n + LOOK)
                                    ps_, pks_ = stiles.pop(n)
                                    pb = n % 3
                                    act(Pb[:, pb, :], ps_[:], AF.Exp, [pks_], [('Pb', pb)], scale=float(QK) ** -0.5)
                                    mm(po[:], V[:, n, :], Pb[:, pb, :], n == 0, n == NKT - 1, [vkey, vkey + '1', ('Pb', pb)], [pko])
                                if HSTOP < 5:
                                    continue
                                if h % 2 == 0:
                                    recip(rec[0:64, :], po[64:128, :], [pko], ['rec'])
                                    tt(ybuf[0:64, h // 2, sl], po[0:64, :], rec[0:64, :], ALU.mult, [pko, 'rec'], [('y', h // 2, qt)])
                                else:
                                    recip(rec[64:128, :], po[0:64, :], [pko], ['rec'])
                                    tt(ybuf[64:128, h // 2, sl], po[64:128, :], rec[64:128, :], ALU.mult, [pko, 'rec'], [('y', h // 2, qt)])
                        S.barrier()
                dump("yattn%d" % l, ybuf[:], [128, 4, NT], BF16, ('y', 0, 0))
                run_epi(2, ybuf)
            if 'A' in BRANCHES:
              with T("ybuf", [128, 4, NT], BF16) as ybuf:
                esA = ExitStack()
                with esA:
                    def AA(name, shape, dt=F32):
                        return esA.enter_context(T(name, shape, dt))
                    U = AA("U", [128, 32, 256], BF16)
                    Tz = AA("Tz", [128, 32, 128], BF16)
                    Bb = AA("Bb", [128, 2, 32, 16])
                    Cc = AA("Cc", [128, 2, 32, 16])
                    Qp = AA("Qp", [128, 2, 32, 24])
                    A1x = AA("A1x", [128, 2, 32])
                    A2x = AA("A2x", [128, 2, 32])
                    dsk = AA("dsk", [128, 32])
                    smk = AA("smk", [128, 257])
                    seltp = AA("seltp", [128, 8, 240], BF16)
                    tabA = AA("tabA", [128, 2, 2, 128], BF16)
                    tabB = AA("tabB", [128, 2, 2, 128], BF16)
                    otmp = AA("otmp", [128, 4, 128])
                    tabM = AA("tabM", [128, 2, 2, 2, 128], BF16)
                    hmk = AA("hmk", [128, 2])
                    S.dma('sp', hmk[:], halfmask[:, :], (), ['hmk'])
                    S.dma('sp', Cc[:], c2[l], (), ['Cc'])
                    S.dma('sp', dsk[:], dskT[l], (), ['dsk'])
                    S.dma('sp', smk[:], smask[:, 0, :], (), ['smk'])
                    S.dma('sp', seltp[:], seltpad[:, :, :], (), ['seltp'])
                    with T("wssm", [128, 8, 512], BF16) as wssm, T("uf", [128, 4, NT], BF16) as uf, T("selp", [128, 8, 240], BF16) as selp:
                        S.dma('sp', selp[:], selpad[:, :, :], (), ['selp'])
                        load_w(wssm, w_in[l][:, OFF_SSM:OFF_SSM + 512], 8, 'wssm')
                        for t in range(NTT):
                            sl = slice(t * TT, (t + 1) * TT)
                            for oc in range(4):
                                pu, pku = PS()
                                for kc in range(8):
                                    mm(pu[:], wssm[:, kc, oc * 128:(oc + 1) * 128], xn[:, kc, sl], kc == 0, kc == 7,
                                       ['wssm', ('xn', kc, t)], [pku])
                                if oc % 2 == 0:
                                    act(uf[:, oc, sl], pu[:], AF.Copy, [pku], [('uf', oc)])
                                else:
                                    cp(uf[:, oc, sl], pu[:], [pku], [('uf', oc)])
                        for g2 in range(16):
                            pu, pku = PS()
                            for gi in range(2):
                                g = g2 * 2 + gi
                                for r in range(8):
                                    mm(pu[:, gi * 256:(gi + 1) * 256], selp[:, g % 8, 112 - 16 * r:240 - 16 * r], uf[:, g // 8, r:NT:8],
                                       r == 0, r == 7, ['selp', ('uf', g // 8)], [pku])
                            if g2 % 2 == 0:
                                act(U[:, g2 * 2:g2 * 2 + 2, :], pu[:].rearrange("p (a b) -> p a b", a=2), AF.Copy, [pku], [('U', g2)])
                            else:
                                cp(U[:, g2 * 2:g2 * 2 + 2, :], pu[:].rearrange("p (a b) -> p a b", a=2), [pku], [('U', g2)])
                        S.barrier()
                    S.mute = ASTOP < 2
                    with T("lam", [128, 2, 32]) as lam, T("ldt", [128, 32]) as ldt, T("braw", [128, 2, 32, 16]) as braw, \
                            T("ramp", [128, 32, 24]) as ramp, T("w1", [128, 32, 24]) as w1, T("w2", [128, 32, 24]) as w2, \
                            T("w3", [128, 32, 24]) as w3, T("w4", [128, 32, 24]) as w4, T("wi", [128, 32, 24], I32) as wi, \
                            T("v1", [128, 8, 32]) as v1, T("bt", [128, 2, 32, 16]) as bt:
                        S.dma('sp', lam[:], lam2[l], (), ['lam'])
                        S.dma('sp', ldt[:], logdt2[l], (), ['ldt'])
                        S.dma('sp', braw[:], b2[l], (), ['braw'])
                        S.dma('sp', ramp[:], ramp32[:, :, :], (), ['ramp'])
                        K_ = ['tbl']
                        def bc(ap2):
                            return ap2.unsqueeze(2).broadcast_to([128, 32, 24])
                        act(ldt[:], ldt[:], AF.Exp, ['ldt'], K_)
                        tt(v1[:, 0, :], lam[:, 0, :], ldt[:], ALU.mult, ['lam'] + K_, K_)
                        tt(v1[:, 1, :], lam[:, 1, :], ldt[:], ALU.mult, ['lam'] + K_, K_)
                        tt(w1[:], bc(v1[:, 1, :]), ramp[:], ALU.mult, ['ramp'] + K_, K_)
                        tt(w2[:], bc(v1[:, 0, :]), ramp[:], ALU.mult, ['ramp'] + K_, K_)
                        act(w2[:], w2[:], AF.Exp, K_, K_)
                        ts(w1[:], w1[:], 1.0 / (2.0 * math.pi), None, ALU.mult, None, K_, K_)
                        cp(wi[:], w1[:], K_, K_)
                        cp(w3[:], wi[:], K_, K_)
                        tt(w1[:], w1[:], w3[:], ALU.subtract, K_, K_)
                        act(w3[:], w1[:], AF.Sin, K_, K_, scale=math.pi)
                        act(w4[:], w1[:], AF.Sin, K_, K_, scale=math.pi / 2.0)
                        tt(w4[:], w4[:], w4[:], ALU.mult, K_, K_)
                        ts(w4[:], w4[:], -2.0, 1.0, ALU.mult, ALU.add, K_, K_)
                        tt(w4[:], w4[:], w3[:], ALU.mult, K_, K_)
                        ts(w4[:], w4[:], 2.0, None, ALU.mult, None, K_, K_)
                        tt(w3[:], w3[:], w3[:], ALU.mult, K_, K_)
                        ts(w3[:], w3[:], -2.0, 1.0, ALU.mult, ALU.add, K_, K_)
                        tt(Qp[:, 0], w2[:], w3[:], ALU.mult, K_, K_)
                        tt(Qp[:, 1], w2[:], w4[:], ALU.mult, K_, K_)
                        for ri in range(2):
                            cp(v1[:, 2 + ri, 0:16], Qp[:, ri, 0:16, 8], K_, K_)
                            cp(v1[:, 2 + ri, 16:32], Qp[:, ri, 16:32, 1], K_, K_)
                        cp(A1x[:, 0, 0:16], Qp[:, 0, 0:16, 15], K_, K_)
                        cp(A1x[:, 0, 16:32], Qp[:, 0, 16:32, 8], K_, K_)
                        cp(A1x[:, 1, :], A1x[:, 0, :], K_, K_)
                        cp(A2x[:, 1, 0:16], Qp[:, 1, 0:16, 15], K_, K_)
                        cp(A2x[:, 1, 16:32], Qp[:, 1, 16:32, 8], K_, K_)
                        ts(A2x[:, 0, :], A2x[:, 1, :], -1.0, None, ALU.mult, None, K_, K_)
                        tt(v1[:, 4, :], lam[:, 0, :], lam[:, 0, :], ALU.mult, K_, K_)
                        tt(v1[:, 5, :], lam[:, 1, :], lam[:, 1, :], ALU.mult, K_, K_)
                        tt(v1[:, 4, :], v1[:, 4, :], v1[:, 5, :], ALU.add, K_, K_)
                        recip(v1[:, 4, :], v1[:, 4, :], K_, K_)
                        ts(v1[:, 2, :], v1[:, 2, :], -1.0, None, ALU.add, None, K_, K_)
                        tt(v1[:, 5, :], v1[:, 2, :], lam[:, 0, :], ALU.mult, K_, K_)
                        tt(v1[:, 6, :], v1[:, 3, :], lam[:, 1, :], ALU.mult, K_, K_)
                        tt(v1[:, 5, :], v1[:, 5, :], v1[:, 6, :], ALU.add, K_, K_)
                        tt(v1[:, 5, :], v1[:, 5, :], v1[:, 4, :], ALU.mult, K_, K_)
                        tt(v1[:, 6, :], v1[:, 3, :], lam[:, 0, :], ALU.mult, K_, K_)
                        tt(v1[:, 7, :], v1[:, 2, :], lam[:, 1, :], ALU.mult, K_, K_)
                        tt(v1[:, 6, :], v1[:, 6, :], v1[:, 7, :], ALU.subtract, K_, K_)
                        tt(v1[:, 6, :], v1[:, 6, :], v1[:, 4, :], ALU.mult, K_, K_)
                        def bc16(ap2):
                            return ap2.unsqueeze(2).broadcast_to([128, 32, 16])
                        tt(bt[:, 0], bc16(v1[:, 5, :]), braw[:, 0], ALU.mult, ['braw'] + K_, K_)
                        tt(bt[:, 1], bc16(v1[:, 6, :]), braw[:, 1], ALU.mult, ['braw'] + K_, K_)
                        tt(Bb[:, 0], bt[:, 0], bt[:, 1], ALU.subtract, K_, K_)
                        tt(bt[:, 0], bc16(v1[:, 5, :]), braw[:, 1], ALU.mult, ['braw'] + K_, K_)
                        tt(bt[:, 1], bc16(v1[:, 6, :]), braw[:, 0], ALU.mult, ['braw'] + K_, K_)
                        tt(Bb[:, 1], bt[:, 0], bt[:, 1], ALU.add, K_, K_)
                        S.barrier()

                    def outer(dst, dkey, dp, kind, X, xkey, neg_im):
                        q_re = Qp[:, 0, dp, kind * 8:(kind + 1) * 8].unsqueeze(2).broadcast_to([128, 8, 16])
                        q_im = Qp[:, 1, dp, kind * 8:(kind + 1) * 8].unsqueeze(2).broadcast_to([128, 8, 16])
                        x_re = X[:, 0, dp, :].unsqueeze(1).broadcast_to([128, 8, 16])
                        x_im = X[:, 1, dp, :].unsqueeze(1).broadcast_to([128, 8, 16])
                        o = [otmp[:, i, :].rearrange("p (a b) -> p a b", a=8) for i in range(4)]
                        d_re = dst[:, 0, :].rearrange("p (a b) -> p a b", a=8)
                        d_im = dst[:, 1, :].rearrange("p (a b) -> p a b", a=8)
                        tt(o[0], q_re, x_re, ALU.mult, ['tbl', xkey], [('otmp', 0)])
                        tt(o[1], q_im, x_im, ALU.mult, ['tbl', xkey], [('otmp', 1)])
                        tt(d_re, o[0], o[1], ALU.subtract, [('otmp', 0), ('otmp', 1)], [dkey])
                        tt(o[2], q_re, x_im, ALU.mult, ['tbl', xkey], [('otmp', 2)])
                        tt(o[3], q_im, x_re, ALU.mult, ['tbl', xkey], [('otmp', 3)])
                        if neg_im:
                            stt(d_im, o[2], -1.0, o[3], ALU.mult, ALU.subtract, [('otmp', 2), ('otmp', 3)], [dkey])
                        else:
                            tt(d_im, o[2], o[3], ALU.add, [('otmp', 2), ('otmp', 3)], [dkey])

                    def maskE(d):
                        for ge in range(2):
                            ts(tabM[:, ge, d].rearrange("p a b -> p (a b)"), tabB[:, d].rearrange("p a b -> p (a b)"), hmk[:, ge:ge + 1], None,
                               ALU.mult, None, [('tabB', d), 'hmk'], [('tabM', d)])

                    S.mute = ASTOP < 3
                    with T("tzt", [128, 2, 256]) as tzt, T("tzm", [128, 2, 256]) as tzm:
                        S.dma('sp', tzm[:], tzmask[:, :, 0:256], (), ['tzm'])
                        for pair in range(16):
                            pT = []
                            for d in range(2):
                                dp = d * 16 + pair
                                outer(tabA[:, d], ('tabA', d), dp, 2, Bb, 'tbl', False)
                                outer(tabB[:, d], ('tabB', d), dp, 1, Cc, 'Cc', True)
                                pt_, pkt_ = PS()
                                maskE(d)
                                for ge in range(2 if A3 >= 2 else 0):
                                    mm(pt_[:, ge * 128:(ge + 1) * 128], tabA[:, d, 0, :], tabM[:, ge, d, 0, :], True, False,
                                       [('tabA', d), ('tabM', d)], [pkt_])
                                    mm(pt_[:, ge * 128:(ge + 1) * 128], tabA[:, d, 1, :], tabM[:, ge, d, 1, :], False, True,
                                       [('tabA', d), ('tabM', d)], [pkt_])
                                pT.append((pt_, pkt_))
                            if A3 < 3:
                                continue
                            tt(tzt[:, 0, :], pT[0][0][:, 0:256], tzm[:, 0, :], ALU.mult, [pT[0][1], 'tzm'], [('tzt', 0)])
                            tt(tzt[:, 1, :], pT[1][0][:, 0:256], tzm[:, 1, :], ALU.mult, [pT[1][1], 'tzm'], [('tzt', 1)])
                            tt(tzt[:, 0, :], tzt[:, 0, :], tzt[:, 1, :], ALU.add, [('tzt', 0), ('tzt', 1)], [('tzt', 0)])
                            for ge in range(2):
                                g = 2 * pair + ge
                                stt(Tz[:, g, :], ident[:], dsk[:, g:g + 1], tzt[:, 0, ge * 128:(ge + 1) * 128], ALU.mult, ALU.add,
                                    ['ident', 'dsk', ('tzt', 0)], [('Tz', g)])
                        S.barrier()

                    S.mute = ASTOP < 4
                    with T("SH", [128, 2, 32, 257], BF16) as SH, T("Zs", [128, 2, 32]) as Zs, T("Vs", [128, 2, 32]) as Vs, \
                            T("P1", [128, 2, 32]) as P1, T("P2", [128, 2, 32]) as P2, T("fin", [128, 8, 2, 2, 16]) as fin, \
                            T("Wt", [128, 2, 2, 128], BF16) as Wt, T("Ysb", [128, 8, 256], BF16) as Ysb, \
                            T("gl", [128, 2, TT]) as gl:
                        S.dma('sp', Zs[:], h0[l][:, 0:64], (), ['Zs'])
                        cp(SH[:, :, :, 0], Zs[:], ['Zs'], [('SH', 0)])
                        for pair in range(16):
                            for d in range(2):
                                dp = d * 16 + pair
                                outer(tabA[:, d], ('tabA', d), dp, 0, Bb, 'tbl', False)
                                ptr_, pktr = PS()
                                ptb = ptr_[:].bitcast(BF16)
                                tr(ptb[:, 0:128], tabA[:, d, 0, :], ident_bf[:], [('tabA', d), 'ident_bf'], [pktr])
                                tr(ptb[:, 128:256], tabA[:, d, 1, :], ident_bf[:], [('tabA', d), 'ident_bf'], [pktr])
                                cp(Wt[:, d, :, :], ptb[:, 0:256].rearrange("p (a b) -> p a b", a=2), [pktr], [('Wt', d)])
                                ps_, pks_ = PS()
                                for ri in range(2):
                                    for ge in range(2):
                                        mm(ps_[ge * 64:(ge + 1) * 64, ri * 256:(ri + 1) * 256], Wt[:, d, ri, ge * 64:(ge + 1) * 64],
                                           U[:, 2 * pair + ge, :], True, True, [('Wt', d), ('U', pair)], [pks_])
                                src_ = ps_[:].rearrange("p (a b) -> p a b", a=2)
                                if d == 0:
                                    cp(SH[:, :, dp, 1:257], src_, [pks_], [('SHs', dp)])
                                else:
                                    cp(SH[:, :, dp, 256:0:-1], src_, [pks_], [('SHs', dp)])
                        S.barrier()
                        S.mute = ASTOP < 5
                        SK = ['scan']
                        for i in range(256):
                            tt(P1[:], Zs[:], A1x[:], ALU.mult, SK, SK)
                            tt(P2[:], Zs[:, ::-1, :], A2x[:], ALU.mult, SK, SK)
                            tt(Vs[:], P1[:], SH[:, :, :, i + 1], ALU.add, SK, SK)
                            tt(Vs[:], Vs[:], P2[:], ALU.add, SK, SK)
                            if (i + 1) % 32 == 0:
                                q = i // 32
                                cp(fin[:, q, :, 0, :], Vs[:, :, 0:16], SK, SK)
                                cp(fin[:, 7 - q, :, 1, :], Vs[:, :, 16:32], SK, SK)
                            ts(Zs[:], Vs[:], smk[:, i + 1:i + 2], None, ALU.mult, None, SK + ['smk'], SK)
                            act(SH[:, :, :, i + 1], Zs[:], AF.Copy, SK, SK)
                        S.dma('sp', ssm_out[l], fin[:].rearrange("p a b c d -> p (a b c d)"), SK, ['ssm_out'])
                        S.barrier()
                        S.mute = ASTOP < 6
                        for oc in range(4):
                            for pl in range(4):
                                pair = oc * 4 + pl
                                py, pky = PS()
                                for d in range(2):
                                    outer(tabB[:, d], ('tabB', d), d * 16 + pair, 1, Cc, 'Cc', True)
                                    maskE(d)
                                for ge in range(2):
                                    g = 2 * pair + ge
                                    o_ = py[:, ge * 256:(ge + 1) * 256]
                                    mm(o_, Tz[:, g, :], U[:, g, :], True, False, [('Tz', g), ('U', pair)], [pky])
                                    mm(o_, tabM[:, ge, 0, 0, :], SH[:, 0, pair, 0:256], False, False, [('tabM', 0), 'scan'], [pky])
                                    mm(o_, tabM[:, ge, 0, 1, :], SH[:, 1, pair, 0:256], False, False, [('tabM', 0), 'scan'], [pky])
                                    mm(o_, tabM[:, ge, 1, 0, :], SH[:, 0, 16 + pair, 255::-1], False, False, [('tabM', 1), 'scan'], [pky])
                                    mm(o_, tabM[:, ge, 1, 1, :], SH[:, 1, 16 + pair, 255::-1], False, True, [('tabM', 1), 'scan'], [pky])
                                cp(Ysb[:, 2 * pl:2 * pl + 2, :], py[:].rearrange("p (a b) -> p a b", a=2), [pky], [('Ysb', pl)])
                            yv = ybuf[:, oc, :].rearrange("p (j r) -> p r j", r=8)
                            for r2 in range(4):
                                pz, pkz = PS()
                                for ri in range(2):
                                    r = r2 * 2 + ri
                                    for gl_ in range(8):
                                        mm(pz[:, ri * 256:(ri + 1) * 256], seltp[:, r, 112 - 16 * gl_:240 - 16 * gl_], Ysb[:, gl_, :],
                                           gl_ == 0, gl_ == 7, ['seltp', ('Ysb', gl_ // 2)], [pkz])
                                b = 0
                                act(gl[:, 2 * b, :], pz[:], AF.Copy, [pkz], [('gl', 2 * b)])
                                tt(gl[:, 2 * b + 1, :], gl[:, 2 * b, :], gl[:, 2 * b, :], ALU.mult, [('gl', 2 * b)], [('gl', 2 * b + 1)])
                                ts(gl[:, 2 * b + 1, :], gl[:, 2 * b + 1, :], 0.044715, 1.0, ALU.mult, ALU.add, [('gl', 2 * b + 1)], [('gl', 2 * b + 1)])
                                tt(gl[:, 2 * b + 1, :], gl[:, 2 * b + 1, :], gl[:, 2 * b, :], ALU.mult, [('gl', 2 * b), ('gl', 2 * b + 1)], [('gl', 2 * b + 1)])
                                act(gl[:, 2 * b + 1, :], gl[:, 2 * b + 1, :], AF.Sigmoid, [('gl', 2 * b + 1)], [('gl', 2 * b + 1)], scale=1.5957691216057308)
                                tt(yv[:, r2 * 2:r2 * 2 + 2, :], gl[:, 2 * b, :].rearrange("p (a b) -> p a b", a=2),
                                   gl[:, 2 * b + 1, :].rearrange("p (a b) -> p a b", a=2), ALU.mult,
                                   [('gl', 2 * b), ('gl', 2 * b + 1)], [('y', oc, 0), ('y', oc, 1), ('y', oc, 2), ('y', oc, 3)])
                        S.barrier()
                S.mute = False
                with T("wglu", [128, 4, BW], BF16) as wglu, T("gsg", [128, 4, TT]) as gsg:
                    load_w(wglu, w_glu[l], 4, 'wglu')
                    for t in range(NTT):
                        sl = slice(t * TT, (t + 1) * TT)
                        for oc in range(4):
                            pg, pkg = PS()
                            for kc in range(4):
                                mm(pg[:], wglu[:, kc, oc * 128:(oc + 1) * 128], ybuf[:, kc, sl], kc == 0, kc == 3,
                                   ['wglu', ('y', kc, t)], [pkg])
                            act(gsg[:, oc, :], pg[:], AF.Sigmoid, [pkg], [('gsg', oc)])
                        for oc in range(4):
                            tt(ybuf[:, oc, sl], ybuf[:, oc, sl], gsg[:, oc, :], ALU.mult, [('y', oc, t), ('gsg', oc)], [('y', oc, t)])
                    S.barrier()
                dump("yssm%d" % l, ybuf[:], [128, 4, NT], BF16, ('y', 0, 0))
                run_epi(0, ybuf)
            S.barrier()
            dump("xmid%d" % l, x[:], [128, 8, NT], F32, ('x', 0, 0))

            rmsnorm_mod(l, None, 16, 24)
            for half in range(2 if not int(os.environ.get('SKF', '0')) else 0):
                with T("hT", [128, 22, 1024], BF16) as hT, \
                        T("wfi", [128, 2, 8, 256], BF16) as wfi, \
                        T("wfo", [128, 2, 2, D], BF16) as wfo, \
                        T("fsg", [128, 2, TT], F32) as fsg:
                    for c in range(22):
                        b = c % 2
                        S.dma('pool', wfi[:, b, :, 0:128], w_ffn_in[l][:, c * 128:(c + 1) * 128].rearrange("(kc p) n -> p kc n", p=128),
                              (), [('wfi', b)])
                        S.dma('pool', wfi[:, b, :, 128:256],
                              w_ffn_in[l][:, DFF + c * 128:DFF + (c + 1) * 128].rearrange("(kc p) n -> p kc n", p=128),
                              (), [('wfi', b)])
                        for tt_ in range(2):
                            t = half * 2 + tt_
                            sl = slice(t * TT, (t + 1) * TT)
                            pg, pkg = PS()
                            for kc in range(8):
                                mm(pg[:], wfi[:, b, kc, 0:128], xn[:, kc, sl], kc == 0, kc == 7, [('wfi', b), ('xn', kc, t)], [pkg])
                            pu, pku = PS()
                            for kc in range(8):
                                mm(pu[:], wfi[:, b, kc, 128:256], xn[:, kc, sl], kc == 0, kc == 7, [('wfi', b), ('xn', kc, t)], [pku])
                            act(fsg[:, tt_, :], pg[:], AF.Silu, [pkg], [('fsg', tt_)])
                            tt(hT[:, c, tt_ * TT:(tt_ + 1) * TT], pu[:], fsg[:, tt_, :], ALU.mult, [pku, ('fsg', tt_)], [('hT', c, tt_)])
                    for tt_ in range(2):
                        t = half * 2 + tt_
                        sl = slice(t * TT, (t + 1) * TT)
                        pcs = [PS() for _ in range(8)] if False else None
                    for oc in range(8):
                        pcs = [PS(), PS()]
                        for kp in range(11):
                            b = kp % 2
                            S.dma('pool', wfo[:, b, :, 0:128],
                                  w_ffn_out[l][kp * 256:(kp + 1) * 256, oc * 128:(oc + 1) * 128].rearrange("(j p) n -> p j n", p=128),
                                  (), [('wfo', b)])
                            for tt_ in range(2):
                                for j in range(2):
                                    kc = kp * 2 + j
                                    mm(pcs[tt_][0][:], wfo[:, b, j, 0:128], hT[:, kc, tt_ * TT:(tt_ + 1) * TT],
                                       kc == 0, kc == 21, [('wfo', b), ('hT', kc, tt_)], [pcs[tt_][1]])
                        for tt_ in range(2):
                            t = half * 2 + tt_
                            sl = slice(t * TT, (t + 1) * TT)
                            stt(x[:, oc, sl], pcs[tt_][0][:], mod[:, 40 + oc:41 + oc], x[:, oc, sl], ALU.mult, ALU.add,
                                [pcs[tt_][1], 'mod', ('x', oc, t)], [('x', oc, t)])
                    S.barrier()
            dump("xout%d" % l, x[:], [128, 8, NT], F32, ('x', 0, 0))

        with T("ytm", [128, 2, D], F32) as ytm:
            for n in range(NT // 128):
                b = n % 2
                for hh in range(2):
                    ps, pk = PS()
                    for q in range(4):
                        oc = hh * 4 + q
                        tr(ps[:, q * 128:(q + 1) * 128], x[:, oc, n * 128:(n + 1) * 128], ident[:],
                           [('x', oc, n // 4), 'ident'], [pk])
                    if hh == 0:
                        cp(ytm[:, b, 0:512], ps[:], [pk], [('ytm', b)])
                    else:
                        act(ytm[:, b, 512:1024], ps[:], AF.Copy, [pk], [('ytm', b)])
                S.dma('sp', y_out[n * 128:(n + 1) * 128, :], ytm[:, b, :], [('ytm', b)], ['y_out'])
        S.barrier()
    return nc, dbg_out, S


def _bf(a):
    return np.ascontiguousarray(np.asarray(a, dtype=np.float32)).astype(NPBF)


def _consts(nseq):
    Ls = NT // nseq
    t = np.arange(NT)
    seq = t // Ls
    c = {}
    C = np.ones((128, NT), np.float64)
    Sg = np.zeros((128, NT), np.float64)
    if nseq == 1:
        GRID_W = 64
        row = (t // GRID_W).astype(np.float32)
        col = (t % GRID_W).astype(np.float32)
        inv = (np.float32(10000.0) ** (-np.arange(8, dtype=np.float32) / np.float32(8))).astype(np.float32)
        ang = np.concatenate([row[:, None] * inv, col[:, None] * inv], axis=-1).astype(np.float32)
        cs, sn = np.cos(ang.astype(np.float64)), np.sin(ang.astype(np.float64))
        for i in range(16):
            C[64 + 2 * i] = cs[:, i]
            C[64 + 2 * i + 1] = cs[:, i]
            Sg[64 + 2 * i] = -sn[:, i]
            Sg[64 + 2 * i + 1] = sn[:, i]
    c['ropeC'] = _bf(C)
    c['ropeS'] = _bf(Sg)
    qi = np.zeros((8, NT), np.float32)
    qi[seq, t] = 1.0
    ki = np.zeros((8, NK), np.float32)
    if nseq > 1:
        ki[:, :NT] = -BIG
        ki[seq, t] = 0.0
        ki[:, NT:] = -BIG
    c['qind'] = _bf(qi)
    c['kind'] = _bf(ki)
    nl = (t % Ls).astype(np.float64)
    same = (seq[:, None] == seq[None, :])
    ph = 2.0 * np.pi * ((nl[:, None] * nl[None, :]) % Ls) / Ls
    c['dftC'] = _bf(np.where(same, np.cos(ph) / np.sqrt(Ls), 0.0))
    c['dftS'] = _bf(np.where(same, -np.sin(ph) / np.sqrt(Ls), 0.0))
    cc = np.arange(128, dtype=np.float64)
    ph2 = 2.0 * np.pi * ((cc[:, None] * cc[None, :]) % 128) / 128.0
    c['ccsc'] = _bf(np.concatenate([np.cos(ph2), np.sin(ph2)], axis=1) / np.sqrt(128.0))
    mL = (t % Ls != 0).astype(np.float32)
    mR = (t % Ls != Ls - 1).astype(np.float32)
    c['seqflag'] = np.full((128, 1), 1.0 if nseq == 1 else 0.0, np.float32)
    nchunk = NT // 8
    cps = nchunk // nseq
    mf = np.ones(257, np.float32)
    mb = np.ones(257, np.float32)
    for j in range(nchunk):
        if (j + 1) % cps == 0 and (j + 1) < nchunk:
            mf[j + 1] = 0.0
            mb[j + 1] = 0.0
    c['smask'] = np.ascontiguousarray(np.broadcast_to(np.stack([mf, mb])[None], (128, 2, 257))).astype(np.float32)
    r = np.arange(8, dtype=np.float32)
    kr = np.zeros((2, 3, 8), np.float32)
    kr[0, 0] = 7 - r; kr[0, 1] = r + 1; kr[0, 2] = -(1 + r)
    kr[1, 0] = r;     kr[1, 1] = 8 - r; kr[1, 2] = r - 8
    c['ramp32'] = np.ascontiguousarray(np.broadcast_to(np.repeat(kr.reshape(2, 1, 24), 16, axis=1).reshape(1, 32, 24), (128, 32, 24))).astype(np.float32)
    rr = np.arange(128) // 16
    mF = (rr[None, :] >= rr[:, None]).astype(np.float32)
    mB = (rr[None, :] <= rr[:, None]).astype(np.float32)
    c['tzmask'] = np.ascontiguousarray(np.stack([np.tile(mF, (1, 4)), np.tile(mB, (1, 4))], axis=1)).astype(np.float32)
    sp = np.zeros((128, 8, 240), np.float32)
    stp = np.zeros((128, 8, 240), np.float32)
    for gl in range(8):
        for cc_ in range(16):
            sp[gl * 16 + cc_, gl, 112 + cc_] = 1.0
    for r_ in range(8):
        for cc_ in range(16):
            stp[r_ * 16 + cc_, r_, 112 + cc_] = 1.0
    c['selpad'] = _bf(sp)
    c['seltpad'] = _bf(stp)
    sh = np.zeros((32, 2, 96), np.float32)
    for d in range(32):
        sh[d, 0, 64 + d] = 1.0
        sh[d ^ 1, 1, 64 + d] = 1.0
    c['shiftm'] = _bf(sh)
    c['identf'] = np.eye(128, dtype=np.float32)
    hm = np.zeros((128, 2), np.float32); hm[:64, 0] = 1.0; hm[64:, 1] = 1.0
    c['halfmask'] = hm
    return c


def _colT(v, n):
    return np.ascontiguousarray(np.asarray(v, np.float32).reshape(n, 128).T)


def _shared(inp):
    f = lambda a: np.ascontiguousarray(np.asarray(a, np.float32))
    s = {}
    s['w_ada'] = f(inp['w_ada'])
    s['b_adaT'] = np.stack([_colT(inp['b_ada'][l], 48) for l in range(NL)])
    s['nmgT'] = np.stack([_colT(inp['norm_mix_g'][l], 8) for l in range(NL)])
    s['nfgT'] = np.stack([_colT(inp['norm_ffn_g'][l], 8) for l in range(NL)])
    s['w_in'] = f(inp['w_in'])
    s['qagT'] = np.stack([_colT(inp['q_a_norm_g'][l], 3) for l in range(NL)])
    s['kvag'] = np.stack([_colT(inp['kv_a_norm_g'][l], 2) for l in range(NL)])
    wuq = np.asarray(inp['w_uq'], np.float32)
    s['w_uq'] = f(wuq)
    perm = np.arange(NH * QK)
    dd = perm % QK
    perm = np.where(dd >= 64, (perm // QK) * QK + 64 + ((dd - 64) ^ 1), perm)
    s['w_uq_sw'] = f(wuq[:, :, perm])
    wukv = np.asarray(inp['w_ukv'], np.float32).reshape(NL, KVR, NH, 128)
    wuk = np.zeros((NL, KVR, NH, QK), np.float32)
    wuk[..., :64] = wukv[..., :64]
    s['w_uk'] = f(wuk.reshape(NL, KVR, NH * QK))
    s['w_uv'] = f(wukv[..., 64:].reshape(NL, KVR, NH * 64))
    qg = np.asarray(inp['q_norm_g'], np.float32)
    kg = np.asarray(inp['k_norm_g'], np.float32)
    swp = np.arange(QK)
    swp = np.where(swp >= 64, 64 + ((swp - 64) ^ 1), swp)
    qng = np.zeros((NL, 128, 4), np.float32)
    qng[:, :QK, 0] = qg
    qng[:, :QK, 1] = qg[:, swp]
    qng[:, :QK, 2] = kg
    qng[:, :QK, 3] = kg[:, swp]
    s['qng'] = qng
    def gp(a):
        a = np.asarray(a, np.float32)
        rest = a.shape[4:]
        a = a.reshape((NL, 2, 16, 2, 64) + rest)
        perm = (0, 3, 4, 1, 2) + tuple(range(5, 5 + len(rest)))
        return np.ascontiguousarray(a.transpose(perm).reshape((NL, 128, 32) + rest))
    s['lam2'] = np.ascontiguousarray(np.stack([gp(inp['ssm_lam_re']), gp(inp['ssm_lam_im'])], axis=2))
    ld = np.broadcast_to(np.asarray(inp['ssm_log_dt'], np.float32)[:, :, :, None], (NL, 2, 32, 64))
    s['logdt2'] = gp(ld)
    s['b2'] = np.ascontiguousarray(np.stack([gp(inp['ssm_b_re']), gp(inp['ssm_b_im'])], axis=2))
    cr_ = np.asarray(inp['ssm_c_re'], np.float32).transpose(0, 1, 2, 4, 3)
    ci_ = np.asarray(inp['ssm_c_im'], np.float32).transpose(0, 1, 2, 4, 3)
    s['c2'] = np.ascontiguousarray(np.stack([gp(cr_), gp(ci_)], axis=2))
    dsk = np.asarray(inp['ssm_d'], np.float32).reshape(NL, 32, 16)
    s['dskT'] = f(np.broadcast_to(dsk.transpose(0, 2, 1)[:, None, :, :], (NL, 8, 16, 32)).reshape(NL, 128, 32))
    s['w_glu'] = f(inp['w_glu'])
    cw = np.asarray(inp['conv_w'], np.float32)
    s['convwT'] = f(cw.reshape(NL, 3, 4, 128).transpose(0, 3, 2, 1))
    s['w_branch'] = f(inp['w_branch'])
    s['w_gate'] = f(inp['w_gate'])
    s['b_gateT'] = np.stack([_colT(inp['b_gate'][l], 32) for l in range(NL)])
    s['w_out'] = f(inp['w_out'])
    s['w_ffn_in'] = f(inp['w_ffn_in'])
    s['w_ffn_out'] = f(inp['w_ffn_out'])
    return s


def _core_map(inp, shared, consts, core):
    m = dict(shared)
    if core < 2:
        b = core
        m.update(consts[1])
        m['xin'] = np.ascontiguousarray(np.asarray(inp['x_sample'][b], np.float32))
        m['condT'] = _colT(inp['c'][b], 8)
        m['cache_ckv'] = np.ascontiguousarray(np.asarray(inp['cache_ckv'][b], np.float32))
        m['cache_kpe'] = np.ascontiguousarray(np.asarray(inp['cache_kpe'][b], np.float32))
        st = np.asarray(inp['state_ssm'][b], np.float32)
        st = st.reshape(NL, 2, 16, 2, 64, 2)
        m['h0'] = np.ascontiguousarray(st.transpose(0, 3, 4, 5, 1, 2).reshape(NL, 128, 64))
        h0 = np.zeros((NL, 128, 128), np.float32)
        h0[:, :, :64] = m['h0']
        m['h0'] = h0
    else:
        pc = (core - 2) % 4
        m.update(consts[8])
        m['xin'] = np.ascontiguousarray(np.asarray(inp['x_prompt'][pc * 8:(pc + 1) * 8], np.float32).reshape(NT, D))
        m['condT'] = _colT(inp['c_ctx'], 8)
        m['cache_ckv'] = np.zeros((NL, PAST, KVR), np.float32)
        m['cache_kpe'] = np.zeros((NL, PAST, RD), np.float32)
        m['h0'] = np.zeros((NL, 128, 128), np.float32)
    return m


_CACHE = {}


def kernel(**inputs):
    if 'nc' not in _CACHE:
        _CACHE['nc'] = build()[0]
        _CACHE['consts'] = {1: _consts(1), 8: _consts(8)}
    nc = _CACHE['nc']
    shared = _shared(inputs)
    maps = [_core_map(inputs, shared, _CACHE['consts'], c) for c in range(8)]
    res = run_bass_kernel_spmd(nc, maps, core_ids=list(range(8)))
    R = res.results
    y_s = np.stack([R[b]['y_out'] for b in range(2)]).astype(np.float32)
    y_p = np.concatenate([R[2 + i]['y_out'].reshape(8, 256, D) for i in range(4)]).astype(np.float32)
    ckv = np.concatenate([R[2 + i]['ckv_out'].reshape(NL, 8, 256, KVR).transpose(1, 0, 2, 3) for i in range(4)])
    kpe = np.concatenate([R[2 + i]['kpe_out'].reshape(NL, 8, 256, RD).transpose(1, 0, 2, 3) for i in range(4)])
    ss = []
    for i in range(4):
        a = R[2 + i]['ssm_out'].reshape(NL, 2, 64, 8, 2, 2, 16)
        a = a.transpose(3, 0, 5, 6, 1, 2, 4).reshape(8, NL, 2, 32, 64, 2)
        ss.append(a)
    ssm = np.concatenate(ss)
    return (y_p, y_s, ckv.astype(np.float32), kpe.astype(np.float32), ssm.astype(np.float32))
```

```python
import math
import numpy as np
import ml_dtypes
import concourse.bass as bass
import concourse.mybir as mybir
from concourse.bass_utils import run_bass_kernel_spmd

F32 = mybir.dt.float32
BF16 = mybir.dt.bfloat16
I32 = mybir.dt.int32
ALU = mybir.AluOpType
AF = mybir.ActivationFunctionType
NPBF = ml_dtypes.bfloat16

D = 1024
NT = 2048
DEPTH = 4
NL = DEPTH
PAST = 512
NK = NT + PAST
BW = 512
QR = 384
KVR = 256
RD = 32
QK = 96
NH = 8
DFF = 2816
OFF_SSM, OFF_FFT, OFF_CQ, OFF_CKV, OFF_KPE, OFF_CONV = 0, 512, 1024, 1408, 1664, 1696
IN_COLS = 3232
EPS = 1e-6
BIG = 30000.0
TT = 512
NTT = NT // TT
BRANCHES = 'ABCD'
import os
ATT_SKIP = bool(int(os.environ.get('ATT_SKIP', '0')))
CUT = int(os.environ.get('CUT', '9'))
HSTOP = int(os.environ.get('HSTOP', '9'))
ASTOP = int(os.environ.get('ASTOP', '9'))
A3 = int(os.environ.get('A3', '9'))


class Sched:
    def __init__(self, nc, nds=24):
        self.nc = nc
        self.eng = {'pe': nc.tensor, 'act': nc.scalar, 'dve': nc.vector, 'pool': nc.gpsimd, 'sp': nc.sync}
        self.sem = {e: nc.alloc_semaphore(name='sem_' + e) for e in self.eng}
        self.cnt = {e: 0 for e in self.eng}
        self.dsem = [nc.alloc_semaphore(name='dsem%d' % i) for i in range(nds)]
        self.dcnt = [0] * nds
        self.dpool = {'sp': list(range(0, nds // 2)), 'pool': list(range(nds // 2, nds))}
        self.dnext = {'sp': 0, 'pool': 0}
        self.waited = {e: {} for e in self.eng}
        self.lastw = {}
        self.readers = {}
        self.nwaits = 0

    def _wait(self, e, toks):
        w = self.waited[e]
        best = {}
        for t in toks:
            if t is None:
                continue
            key = (t[0], t[1])
            if t[0] == 'c' and t[1] == e and e == 'pe':
                continue
            if w.get(key, 0) >= t[2]:
                continue
            if best.get(key, 0) < t[2]:
                best[key] = t[2]
        for key, v in best.items():
            s = self.sem[key[1]] if key[0] == 'c' else self.dsem[key[1]]
            self.eng[e].wait_ge(s, v)
            w[key] = v
            self.nwaits += 1

    def _deps(self, reads, writes):
        deps = set()
        for k in reads:
            t = self.lastw.get(k)
            if t is not None:
                deps.add(t)
        for k in writes:
            t = self.lastw.get(k)
            if t is not None:
                deps.add(t)
            for t in self.readers.get(k, {}).values():
                deps.add(t)
        return deps

    def _commit(self, tok, reads, writes):
        for k in writes:
            self.lastw[k] = tok
            self.readers[k] = {}
        for k in reads:
            r = self.readers.setdefault(k, {})
            r[(tok[0], tok[1])] = tok

    mute = False

    def op(self, e, fn, reads=(), writes=()):
        if self.mute:
            return
        self._wait(e, self._deps(reads, writes))
        inst = fn(self.eng[e])
        self.cnt[e] += 1
        inst.then_inc(self.sem[e], 1)
        self._commit(('c', e, self.cnt[e]), reads, writes)

    def dma(self, e, out, in_, reads=(), writes=(), **kw):
        if self.mute:
            return
        deps = self._deps(reads, writes)
        pl = self.dpool[e]
        i = pl[self.dnext[e]]
        self.dnext[e] = (self.dnext[e] + 1) % len(pl)
        if self.dcnt[i] > 0:
            deps.add(('d', i, self.dcnt[i] * 16))
        self._wait(e, deps)
        inst = self.eng[e].dma_start(out=out, in_=in_, **kw)
        self.dcnt[i] += 1
        inst.then_inc(self.dsem[i], 16)
        self._commit(('d', i, self.dcnt[i] * 16), reads, writes)

    def barrier(self):
        toks = [('c', e, self.cnt[e]) for e in self.eng if self.cnt[e] > 0]
        toks += [('d', i, c * 16) for i, c in enumerate(self.dcnt) if c > 0]
        for e in self.eng:
            self._wait(e, toks)
        self.lastw = {}
        self.readers = {}


def build(dbg=None, nlayers=NL):
    dbg = dbg or []
    nc = bass.Bass("TRN2", target_bir_lowering=False)
    S = Sched(nc)

    def din(name, shape, dt=F32):
        return nc.dram_tensor(name, list(shape), dt, kind="ExternalInput").ap()

    def dout(name, shape, dt=F32):
        return nc.dram_tensor(name, list(shape), dt, kind="ExternalOutput").ap()

    xin = din("xin", [NT, D])
    condT = din("condT", [128, 8])
    cache_ckv = din("cache_ckv", [NL, PAST, KVR])
    cache_kpe = din("cache_kpe", [NL, PAST, RD])
    h0 = din("h0", [NL, 128, 128])
    w_ada = din("w_ada", [NL, D, 6 * D])
    b_adaT = din("b_adaT", [NL, 128, 48])
    nmgT = din("nmgT", [NL, 128, 8])
    nfgT = din("nfgT", [NL, 128, 8])
    w_in = din("w_in", [NL, D, IN_COLS])
    qagT = din("qagT", [NL, 128, 3])
    kvag = din("kvag", [NL, 128, 2])
    w_uq = din("w_uq", [NL, QR, NH * QK])
    w_uq_sw = din("w_uq_sw", [NL, QR, NH * QK])
    w_uk = din("w_uk", [NL, KVR, NH * QK])
    w_uv = din("w_uv", [NL, KVR, NH * 64])
    qng = din("qng", [NL, 128, 4])
    dskT = din("dskT", [NL, 128, 32])
    lam2 = din("lam2", [NL, 128, 2, 32])
    logdt2 = din("logdt2", [NL, 128, 32])
    b2 = din("b2", [NL, 128, 2, 32, 16])
    c2 = din("c2", [NL, 128, 2, 32, 16])
    ramp32 = din("ramp32", [128, 32, 24])
    halfmask = din("halfmask", [128, 2])
    w_glu = din("w_glu", [NL, BW, BW])
    convwT = din("convwT", [NL, 128, 4, 3])
    w_branch = din("w_branch", [NL, 4, BW, D])
    w_gate = din("w_gate", [NL, D, 4 * D])
    b_gateT = din("b_gateT", [NL, 128, 32])
    w_out = din("w_out", [NL, D, D])
    w_ffn_in = din("w_ffn_in", [NL, D, 2 * DFF])
    w_ffn_out = din("w_ffn_out", [NL, DFF, D])
    ropeC = din("ropeC", [128, NT], BF16)
    ropeS = din("ropeS", [128, NT], BF16)
    qind = din("qind", [8, NT], BF16)
    kind = din("kind", [8, NK], BF16)
    dftC = din("dftC", [NT, NT], BF16)
    dftS = din("dftS", [NT, NT], BF16)
    ccsc = din("ccsc", [128, 256], BF16)
    seqflag = din("seqflag", [128, 1])
    smask = din("smask", [128, 2, 257])
    tzmask = din("tzmask", [128, 2, 512])
    selpad = din("selpad", [128, 8, 240], BF16)
    seltpad = din("seltpad", [128, 8, 240], BF16)
    shiftm = din("shiftm", [32, 2, 96], BF16)
    identf = din("identf", [128, 128])

    y_out = dout("y_out", [NT, D])
    ckv_out = dout("ckv_out", [NL, NT, KVR])
    kpe_out = dout("kpe_out", [NL, NT, RD])
    ssm_out = dout("ssm_out", [NL, 128, 512])
    dbg_out = {}

    from contextlib import ExitStack
    es = ExitStack()

    _uid = [0]

    def T(name, shape, dt=F32):
        _uid[0] += 1
        return nc.sbuf_tensor("%s_%d" % (name, _uid[0]), list(shape), dt)

    def sb(name, shape, dt=F32):
        return es.enter_context(T(name, list(shape), dt))

    with es:
        x = sb("x", [128, 8, NT])
        xn = sb("xn", [128, 8, NT], BF16)
        ones_bf = sb("ones_bf", [128, 128], BF16)
        ident = sb("ident", [128, 128])
        ident_bf = sb("ident_bf", [128, 128], BF16)
        mod = sb("mod", [128, 48])
        vecs = sb("vecs", [128, 64])
        epsb = sb("epsb", [128, 1])
        psum = [es.enter_context(nc.psum_tensor("ps%d" % i, [128, 512], F32)) for i in range(8)]
        pstate = {'i': 0}

        def PS():
            i = pstate['i']
            pstate['i'] = (i + 1) % 8
            return psum[i], 'ps%d' % i

        def mm(out, lhsT, rhs, start, stop, reads, writes):
            S.op('pe', lambda e: e.matmul(out, lhsT, rhs, start=start, stop=stop), reads, writes)

        def tr(out, in_, idn, reads, writes):
            S.op('pe', lambda e: e.transpose(out, in_, idn), reads, writes)

        def act(out, in_, func, reads, writes, bias=None, scale=None):
            kw = {}
            if bias is not None:
                kw['bias'] = bias
            if scale is not None:
                kw['scale'] = scale
            S.op('act', lambda e: e.activation(out=out, in_=in_, func=func, **kw), reads, writes)

        def tt(out, in0, in1, op, reads, writes, eng='dve'):
            S.op(eng, lambda e: e.tensor_tensor(out=out, in0=in0, in1=in1, op=op), reads, writes)

        def ts(out, in0, s1, s2, op0, op1, reads, writes, eng='dve'):
            if op1 is None:
                S.op(eng, lambda e: e.tensor_scalar(out=out, in0=in0, scalar1=s1, scalar2=None, op0=op0), reads, writes)
            else:
                S.op(eng, lambda e: e.tensor_scalar(out=out, in0=in0, scalar1=s1, scalar2=s2, op0=op0, op1=op1), reads, writes)

        def stt(out, in0, scalar, in1, op0, op1, reads, writes):
            S.op('dve', lambda e: e.scalar_tensor_tensor(out=out, in0=in0, scalar=scalar, in1=in1, op0=op0, op1=op1), reads, writes)

        def cp(out, in_, reads, writes, eng='dve'):
            S.op(eng, lambda e: e.tensor_copy(out=out, in_=in_), reads, writes)

        def recip(out, in_, reads, writes):
            S.op('dve', lambda e: e.reciprocal(out=out, in_=in_), reads, writes)

        def memset(t_ap, val, writes, eng='dve'):
            S.op(eng, lambda e: e.memset(t_ap, val), (), writes)

        def dump(name, ap, shape, dt, key):
            if name in dbg:
                o = dout("dbg_" + name, shape, dt)
                dbg_out[name] = o
                S.dma('sp', o, ap, reads=[key], writes=['dbg_' + name])

        def rsqrt_inplace(ap, key, scale):
            act(ap, ap, AF.Sqrt, [key, 'epsb'], [key], bias=epsb[0:ap.shape[0], 0:1], scale=scale)
            recip(ap, ap, [key], [key])

        memset(ones_bf[:], 1.0, ['ones_bf'])
        memset(epsb[:], EPS, ['epsb'])
        S.dma('sp', ident[:], identf[:, :], (), ['ident'])
        cp(ident_bf[:], ident[:], ['ident'], ['ident_bf'])

        with T("xtm", [128, 4, D], F32) as xtm:
            for t in range(NTT):
                S.dma('sp', xtm[:], xin[t * TT:(t + 1) * TT, :].rearrange("(n p) c -> p n c", p=128), (), ['xtm'])
                for oc in range(8):
                    ps, pk = PS()
                    for n in range(4):
                        tr(ps[:, n * 128:(n + 1) * 128], xtm[:, n, oc * 128:(oc + 1) * 128], ident[:], ['xtm', 'ident'], [pk])
                    if oc % 2 == 0:
                        cp(x[:, oc, t * TT:(t + 1) * TT], ps[:], [pk], [('x', oc, t)])
                    else:
                        act(x[:, oc, t * TT:(t + 1) * TT], ps[:], AF.Copy, [pk], [('x', oc, t)])
            S.barrier()

        def load_w(dst, src, kchunks, keyw, eng='pool'):
            S.dma(eng, dst[:, 0:kchunks, :], src.rearrange("(kc p) n -> p kc n", p=128), (), [keyw])

        def rmsnorm_mod(l, gname, sc_c, sh_c):
            with T("nsq", [128, 2, TT], BF16) as nsq, \
                    T("nrs", [128, TT], F32) as nrs, \
                    T("ntmp", [128, 2, TT], F32) as ntmp:
                for t in range(NTT):
                    sl = slice(t * TT, (t + 1) * TT)
                    ps, pk = PS()
                    for oc in range(8):
                        b = oc % 2
                        act(nsq[:, b, :], x[:, oc, sl], AF.Square, [('x', oc, t)], [('nsq', b)])
                        mm(ps[:], ones_bf[:], nsq[:, b, :], oc == 0, oc == 7, ['ones_bf', ('nsq', b)], [pk])
                    cp(nrs[:], ps[:], [pk], ['nrs'])
                    rsqrt_inplace(nrs[:], 'nrs', 1.0 / D)
                    for oc in range(8):
                        b = oc % 2
                        tt(ntmp[:, b, :], x[:, oc, sl], nrs[:], ALU.mult, [('x', oc, t), 'nrs'], [('ntmp', b)])
                        act(xn[:, oc, sl], ntmp[:, b, :], AF.Identity, [('ntmp', b), 'vecs'], [('xn', oc, t)],
                            bias=vecs[:, sh_c + oc:sh_c + oc + 1], scale=vecs[:, sc_c + oc:sc_c + oc + 1])
            S.barrier()

        for l in range(nlayers):
            with T("wada", [128, 8, 1024], BF16) as wada, \
                    T("scond", [128, 8], BF16) as scond, \
                    T("ctmp", [128, 8], F32) as ctmp, \
                    T("mtmp", [128, 64], F32) as mtmp:
                S.dma('sp', ctmp[:], condT[:, :], (), ['ctmp'])
                act(scond[:], ctmp[:], AF.Silu, ['ctmp'], ['scond'])
                S.dma('sp', mtmp[:, 0:48], b_adaT[l], (), ['mtmp'])
                S.dma('sp', mtmp[:, 48:56], nmgT[l], (), ['mtmp'])
                S.dma('sp', mtmp[:, 56:64], nfgT[l], (), ['mtmp'])
                psm, pkm = PS()
                for piece in range(6):
                    load_w(wada, w_ada[l][:, piece * 1024:(piece + 1) * 1024], 8, 'wada')
                    for c in range(8):
                        col = piece * 8 + c
                        for kc in range(8):
                            mm(psm[:, col:col + 1], wada[:, kc, c * 128:(c + 1) * 128], scond[:, kc:kc + 1],
                               kc == 0, kc == 7, ['wada', 'scond'], [pkm])
                tt(mod[:], psm[:, 0:48], mtmp[:, 0:48], ALU.add, [pkm, 'mtmp'], ['mod'])
                stt(vecs[:, 0:8], mod[:, 8:16], 1.0, mtmp[:, 48:56], ALU.add, ALU.mult, ['mod', 'mtmp'], ['vecs'])
                cp(vecs[:, 8:16], mod[:, 0:8], ['mod'], ['vecs'])
                stt(vecs[:, 16:24], mod[:, 32:40], 1.0, mtmp[:, 56:64], ALU.add, ALU.mult, ['mod', 'mtmp'], ['vecs'])
                cp(vecs[:, 24:32], mod[:, 24:32], ['mod'], ['vecs'])
                S.barrier()
            dump("mod%d" % l, mod[:], [128, 48], F32, 'mod')

            rmsnorm_mod(l, None, 0, 8)
            dump("xn%d" % l, xn[:], [128, 8, NT], BF16, ('xn', 0, 0))

            def epilogue(k, ysrc, ykeyfn, wb, wg, wo, gp, sg, t):
                sl = slice(t * TT, (t + 1) * TT)
                for oc in range(8):
                    pa, pka = PS()
                    for kc in range(4):
                        mm(pa[:], wb[:, kc, oc * 128:(oc + 1) * 128], ysrc(kc, t), kc == 0, kc == 3,
                           ['wb', ykeyfn(kc, t)], [pka])
                    pb, pkb = PS()
                    for kc in range(8):
                        mm(pb[:], wg[:, kc, oc * 128:(oc + 1) * 128], xn[:, kc, sl], kc == 0, kc == 7,
                           ['wg', ('xn', kc, t)], [pkb])
                    b = oc % 2
                    act(sg[:, b, :], pb[:], AF.Sigmoid, [pkb, 'vecs'], [('sg', b)],
                        bias=vecs[:, 32 + k * 8 + oc:32 + k * 8 + oc + 1])
                    tt(gp[:, oc, :], pa[:], sg[:, b, :], ALU.mult, [pka, ('sg', b)], [('gp', oc)])
                for oc2 in range(8):
                    pc, pkc = PS()
                    for kc in range(8):
                        mm(pc[:], wo[:, kc, oc2 * 128:(oc2 + 1) * 128], gp[:, kc, :], kc == 0, kc == 7,
                           ['wo', ('gp', kc)], [pkc])
                    stt(x[:, oc2, sl], pc[:], mod[:, 16 + oc2:17 + oc2], x[:, oc2, sl], ALU.mult, ALU.add,
                        [pkc, 'mod', ('x', oc2, t)], [('x', oc2, t)])

            S.dma('sp', vecs[:, 32:64], b_gateT[l], (), ['vecs'])

            def run_epi(k, ybuf):
                with T("wo", [128, 8, D], BF16) as wo, \
                        T("wb", [128, 4, D], BF16) as wb, \
                        T("wg", [128, 8, D], BF16) as wg, \
                        T("gp", [128, 8, TT], BF16) as gp, \
                        T("sg", [128, 2, TT], F32) as sg:
                    load_w(wb, w_branch[l, k], 4, 'wb')
                    load_w(wg, w_gate[l][:, k * D:(k + 1) * D], 8, 'wg')
                    load_w(wo, w_out[l], 8, 'wo')
                    for t in range(NTT if not int(os.environ.get('SKE', '0')) else 0):
                        epilogue(k, lambda kc, t_: ybuf[:, kc, t_ * TT:(t_ + 1) * TT],
                                 lambda kc, t_: ('y', kc, t_), wb, wg, wo, gp, sg, t)
                    S.barrier()

            if 'D' in BRANCHES:
              with T("ybuf", [128, 4, NT], BF16) as ybuf:
                with T("wc", [128, 8, 512], BF16) as wc, \
                        T("Tb", [128, 4, 8, 258], BF16) as Tb, \
                        T("ctm", [128, 3, TT], F32) as ctm, \
                        T("cvw", [128, 4, 3], F32) as cvw, \
                        T("sflag", [128, 1], F32) as sflag:
                    S.dma('sp', cvw[:], convwT[l], (), ['cvw'])
                    S.dma('sp', sflag[:], seqflag[:, :], (), ['sflag'])
                    for oc in range(4):
                        memset(Tb[:, oc, :, 0:1], 0.0, [('Tb', oc)])
                        memset(Tb[:, oc, :, 257:258], 0.0, [('Tb', oc)])
                    load_w(wc, w_in[l][:, OFF_CONV:OFF_CONV + 512], 8, 'wc')
                    for t in range(NTT):
                        sl = slice(t * TT, (t + 1) * TT)
                        for oc in range(4):
                            ph, pkh = PS()
                            for kc in range(8):
                                mm(ph[:], wc[:, kc, oc * 128:(oc + 1) * 128], xn[:, kc, sl], kc == 0, kc == 7,
                                   ['wc', ('xn', kc, t)], [pkh])
                            act(Tb[:, oc, 2 * t:2 * t + 2, 1:257], ph[:].rearrange("p (a b) -> p a b", a=2), AF.Copy, [pkh], [('Tb', oc)])
                    load_w(wc, w_in[l][:, OFF_CONV + 1024:OFF_CONV + 1536], 8, 'wc')
                    for t in range(NTT):
                        sl = slice(t * TT, (t + 1) * TT)
                        for oc in range(4):
                            pg, pkg = PS()
                            for kc in range(8):
                                mm(pg[:], wc[:, kc, oc * 128:(oc + 1) * 128], xn[:, kc, sl], kc == 0, kc == 7,
                                   ['wc', ('xn', kc, t)], [pkg])
                            tt(Tb[:, oc, 2 * t:2 * t + 2, 1:257], pg[:].rearrange("p (a b) -> p a b", a=2),
                               Tb[:, oc, 2 * t:2 * t + 2, 1:257], ALU.mult, [pkg, ('Tb', oc)], [('Tb', oc)])
                    for oc in range(4):
                        ts(Tb[:, oc, 1:8, 0:1], Tb[:, oc, 0:7, 256:257], sflag[:, 0:1], None, ALU.mult, None,
                           [('Tb', oc), 'sflag'], [('Tb', oc)])
                        ts(Tb[:, oc, 0:7, 257:258], Tb[:, oc, 1:8, 1:2], sflag[:, 0:1], None, ALU.mult, None,
                           [('Tb', oc), 'sflag'], [('Tb', oc)])
                    load_w(wc, w_in[l][:, OFF_CONV + 512:OFF_CONV + 1024], 8, 'wc')
                    for t in range(NTT):
                        sl = slice(t * TT, (t + 1) * TT)
                        for oc in range(4):
                            pgb, pkgb = PS()
                            for kc in range(8):
                                mm(pgb[:], wc[:, kc, oc * 128:(oc + 1) * 128], xn[:, kc, sl], kc == 0, kc == 7,
                                   ['wc', ('xn', kc, t)], [pkgb])
                            b = oc % 3
                            cv = ctm[:, b, :].rearrange("p (a b) -> p a b", a=2)
                            ts(cv, Tb[:, oc, 2 * t:2 * t + 2, 1:257], cvw[:, oc, 1:2], None, ALU.mult, None,
                               [('Tb', oc), 'cvw'], [('ctm', b)])
                            stt(cv, Tb[:, oc, 2 * t:2 * t + 2, 0:256], cvw[:, oc, 0:1], cv, ALU.mult, ALU.add,
                                [('Tb', oc), ('ctm', b), 'cvw'], [('ctm', b)])
                            stt(cv, Tb[:, oc, 2 * t:2 * t + 2, 2:258], cvw[:, oc, 2:3], cv, ALU.mult, ALU.add,
                                [('Tb', oc), ('ctm', b), 'cvw'], [('ctm', b)])
                            tt(ybuf[:, oc, sl], pgb[:], ctm[:, b, :], ALU.mult, [pkgb, ('ctm', b)], [('y', oc, t)])
                    S.barrier()
                dump("yconv%d" % l, ybuf[:], [128, 4, NT], BF16, ('y', 0, 0))
                run_epi(3, ybuf)

            if 'B' in BRANCHES:
              with T("ybuf", [128, 4, NT], BF16) as ybuf:
                with T("AB", [128, 16, 4, 256], BF16) as AB:
                    with T("wf", [128, 8, 512], BF16) as wf, \
                            T("uf", [128, 4, NT], BF16) as uf, \
                            T("ccsc_sb", [128, 256], BF16) as ccsc_sb:
                        load_w(wf, w_in[l][:, OFF_FFT:OFF_FFT + 512], 8, 'wf')
                        S.dma('sp', ccsc_sb[:], ccsc[:, :], (), ['ccsc'])
                        for t in range(NTT):
                            sl = slice(t * TT, (t + 1) * TT)
                            for grp in range(4):
                                pu, pku = PS()
                                for kc in range(8):
                                    mm(pu[:], wf[:, kc, grp * 128:(grp + 1) * 128], xn[:, kc, sl], kc == 0, kc == 7,
                                       ['wf', ('xn', kc, t)], [pku])
                                if grp % 2 == 0:
                                    act(uf[:, grp, sl], pu[:], AF.Copy, [pku], [('uf', grp, t)])
                                else:
                                    cp(uf[:, grp, sl], pu[:], [pku], [('uf', grp, t)])
                        for grp in range(4):
                            for n2 in range(8):
                                pa, pka = PS()
                                for j in range(2):
                                    n = n2 * 2 + j
                                    mm(pa[:, j * 256:(j + 1) * 256], uf[:, grp, n * 128:(n + 1) * 128], ccsc_sb[:], True, True,
                                       [('uf', grp, n // 4), 'ccsc'], [pka])
                                if n2 % 2 == 0:
                                    act(AB[:, n2 * 2:n2 * 2 + 2, grp, :], pa[:].rearrange("p (a b) -> p a b", a=2), AF.Copy, [pka], [('AB', grp)])
                                else:
                                    cp(AB[:, n2 * 2:n2 * 2 + 2, grp, :], pa[:].rearrange("p (a b) -> p a b", a=2), [pka], [('AB', grp)])
                        S.barrier()
                    with T("tabC", [128, 2, 16, 256], BF16) as tabC, T("tabS", [128, 2, 16, 256], BF16) as tabS:
                        for kt in range(8):
                            b = kt % 2
                            S.dma('sp', tabC[:, b], dftC[:, kt * 256:(kt + 1) * 256].rearrange("(nt p) k -> p nt k", p=128), (), [('tabC', b)])
                            S.dma('sp', tabS[:, b], dftS[:, kt * 256:(kt + 1) * 256].rearrange("(nt p) k -> p nt k", p=128), (), [('tabS', b)])
                            for g2 in range(2):
                                py, pky = PS()
                                for gg in range(2):
                                    grp = g2 * 2 + gg
                                    for n in range(16):
                                        mm(py[:, gg * 256:(gg + 1) * 256], AB[:, n, grp, 0:128], tabC[:, b, n, :], n == 0, False,
                                           [('AB', grp), ('tabC', b)], [pky])
                                        mm(py[:, gg * 256:(gg + 1) * 256], AB[:, n, grp, 128:256], tabS[:, b, n, :], False, n == 15,
                                           [('AB', grp), ('tabS', b)], [pky])
                                cp(ybuf[:, g2 * 2:g2 * 2 + 2, kt * 256:(kt + 1) * 256], py[:].rearrange("p (a b) -> p a b", a=2),
                                   [pky], [('y', g2 * 2, kt // 2), ('y', g2 * 2 + 1, kt // 2)])
                        S.barrier()
                dump("yfft%d" % l, ybuf[:], [128, 4, NT], BF16, ('y', 0, 0))
                run_epi(1, ybuf)
            if 'C' in BRANCHES:
              with T("ybuf", [128, 4, NT], BF16) as ybuf:
                with T("cqn", [128, 3, NT], BF16) as cqn, T("ckvT", [128, 2, NK], BF16) as ckvT, \
                        T("kpeT", [128, NK], BF16) as kpeT:
                    with T("wa", [128, 8, 384], BF16) as wa, T("wkv", [128, 8, 288], BF16) as wkv, T("ctk", [128, 4, 288], F32) as ctk, T("cqf", [128, 3, TT], F32) as cqf, \
                            T("asq", [128, 2, TT], BF16) as asq, T("ars", [128, TT], F32) as ars, \
                            T("atmp", [128, 2, TT], F32) as atmp, T("tkm", [128, 2, 288], F32) as tkm, \
                            T("junk", [128, 256], F32) as junk, T("ss", [128, 2, 8], F32) as ss, \
                            T("kvag_sb", [128, 2], F32) as kvag_sb, T("qag", [128, 3], F32) as qag:
                        load_w(wa, w_in[l][:, OFF_CQ:OFF_CQ + 384], 8, 'wa')
                        load_w(wkv, w_in[l][:, OFF_CKV:OFF_CKV + 288], 8, 'wkv')
                        memset(ss[:, 0, :], 1.0, [('ss', 0)])
                        memset(ss[:, 1, :], 1.0, [('ss', 1)])
                        S.dma('sp', kvag_sb[:], kvag[l], (), ['kvag'])
                        S.dma('sp', qag[:], qagT[l], (), ['qag'])
                        for t in range(NTT if not int(os.environ.get('SKQ','0')) else 0):
                            sl = slice(t * TT, (t + 1) * TT)
                            for c in range(3):
                                pq, pkq = PS()
                                for kc in range(8):
                                    mm(pq[:], wa[:, kc, c * 128:(c + 1) * 128], xn[:, kc, sl], kc == 0, kc == 7, ['wa', ('xn', kc, t)], [pkq])
                                cp(cqf[:, c, :], pq[:], [pkq], [('cqf', c)])
                            pss, pkss = PS()
                            for c in range(3):
                                act(asq[:, c % 2, :], cqf[:, c, :], AF.Square, [('cqf', c)], [('asq', c % 2)])
                                mm(pss[:], ones_bf[:], asq[:, c % 2, :], c == 0, c == 2, ['ones_bf', ('asq', c % 2)], [pkss])
                            cp(ars[:], pss[:], [pkss], ['ars'])
                            rsqrt_inplace(ars[:], 'ars', 1.0 / QR)
                            for c in range(3):
                                tt(atmp[:, c % 2, :], cqf[:, c, :], ars[:], ALU.mult, [('cqf', c), 'ars'], [('atmp', c % 2)])
                                ts(cqn[:, c, sl], atmp[:, c % 2, :], qag[:, c:c + 1], None, ALU.mult, None,
                                   [('atmp', c % 2), 'qag'], [('cqn', c, t)])

                        def tok2feat(b, n):
                            pst, pkst = PS()
                            tr(pst[:, 0:128], tkm[:, b, 0:128], ident[:], [('tkm', b), 'ident'], [pkst])
                            tr(pst[:, 128:256], tkm[:, b, 128:256], ident[:], [('tkm', b), 'ident'], [pkst])
                            tr(pst[0:32, 256:384], tkm[:, b, 256:288], ident[:], [('tkm', b), 'ident'], [pkst])
                            cp(ckvT[:, :, n * 128:(n + 1) * 128], pst[:, 0:256].rearrange("p (a b) -> p a b", a=2), [pkst], [('ckvT', n // 4)])
                            act(kpeT[0:32, n * 128:(n + 1) * 128], pst[0:32, 256:384], AF.Copy, [pkst], [('kpeT', n // 4), pkst])

                        for t in range(NTT):
                            sl = slice(t * TT, (t + 1) * TT)
                            for c in range(2):
                                pq, pkq = PS()
                                for kc in range(8):
                                    mm(pq[:], wkv[:, kc, c * 128:(c + 1) * 128], xn[:, kc, sl], kc == 0, kc == 7, ['wkv', ('xn', kc, t)], [pkq])
                                cp(cqf[:, c, :], pq[:], [pkq], [('cqf', c)])
                            if CUT >= 2:
                                pq, pkq = PS()
                                for kc in range(8):
                                    mm(pq[0:32, :], wkv[:, kc, 256:288], xn[:, kc, sl], kc == 0, kc == 7, ['wkv', ('xn', kc, t)], [pkq])
                                cp(cqf[0:32, 2, :], pq[0:32, :], [pkq], [('cqf', 2)])
                                act(kpeT[0:32, sl], pq[0:32, :], AF.Copy, [pkq], [('kpeT', t), pkq])
                            if CUT >= 3:
                                pss, pkss = PS()
                                for c in range(2):
                                    act(asq[:, c, :], cqf[:, c, :], AF.Square, [('cqf', c)], [('asq', c)])
                                    mm(pss[:], ones_bf[:], asq[:, c, :], c == 0, c == 1, ['ones_bf', ('asq', c)], [pkss])
                                cp(ars[:], pss[:], [pkss], ['ars'])
                                rsqrt_inplace(ars[:], 'ars', 1.0 / KVR)
                                for c in range(2):
                                    tt(atmp[:, c, :], cqf[:, c, :], ars[:], ALU.mult, [('cqf', c), 'ars'], [('atmp', c)])
                                    ts(cqf[:, c, :], atmp[:, c, :], kvag_sb[:, c:c + 1], None, ALU.mult, None,
                                       [('atmp', c), 'kvag'], [('cqf', c)])
                                    act(ckvT[:, c, sl], cqf[:, c, :], AF.Copy, [('cqf', c)], [('ckvT', t)])
                            if CUT >= 4:
                                for blk in range(4):
                                    b = blk % 2
                                    n = t * 4 + blk
                                    bs = slice(blk * 128, (blk + 1) * 128)
                                    pso, pkso = PS()
                                    tr(pso[:, 0:128], cqf[:, 0, bs], ident[:], [('cqf', 0), 'ident'], [pkso])
                                    tr(pso[:, 128:256], cqf[:, 1, bs], ident[:], [('cqf', 1), 'ident'], [pkso])
                                    tr(pso[:, 256:288], cqf[0:32, 2, bs], ident[0:32, 0:32], [('cqf', 2), 'ident'], [pkso])
                                    cp(tkm[:, b, :], pso[:, 0:288], [pkso], [('tkm', b)])
                                    if not int(os.environ.get('SKA', '0')):
                                        S.dma('sp', ckv_out[l, n * 128:(n + 1) * 128, :], tkm[:, b, 0:256], [('tkm', b)], [('ckv_out', n)])
                                        S.dma('sp', kpe_out[l, n * 128:(n + 1) * 128, :], tkm[:, b, 256:288], [('tkm', b)], [('kpe_out', n)])
                        S.barrier()
                        if not int(os.environ.get('SK2', '0')):
                            S.dma('sp', ctk[:, :, 0:256], cache_ckv[l].rearrange("(n p) c -> p n c", p=128), (), ['ctk'])
                            S.dma('sp', ctk[:, :, 256:288], cache_kpe[l].rearrange("(n p) c -> p n c", p=128), (), ['ctk'])
                            for n in range(PAST // 128):
                                pst, pkst = PS()
                                tr(pst[:, 0:128], ctk[:, n, 0:128], ident[:], ['ctk', 'ident'], [pkst])
                                tr(pst[:, 128:256], ctk[:, n, 128:256], ident[:], ['ctk', 'ident'], [pkst])
                                tr(pst[0:32, 256:384], ctk[:, n, 256:288], ident[:], ['ctk', 'ident'], [pkst])
                                nn_ = NT // 128 + n
                                cp(ckvT[:, :, nn_ * 128:(nn_ + 1) * 128], pst[:, 0:256].rearrange("p (a b) -> p a b", a=2), [pkst], [('ckvT', nn_ // 4)])
                                act(kpeT[0:32, nn_ * 128:(nn_ + 1) * 128], pst[0:32, 256:384], AF.Copy, [pkst], [('kpeT', nn_ // 4), pkst])
                        S.barrier()
                    es3 = ExitStack()
                    with es3:
                        def A(name, shape, dt=F32):
                            return es3.enter_context(T(name, shape, dt))
                        wuq_sb = A("wuq_sb", [128, 3, NH * QK], BF16)
                        wuqs_sb = A("wuqs_sb", [128, 3, NH * QK], BF16)
                        wuk_sb = A("wuk_sb", [128, 2, NH * QK], BF16)
                        wuv_sb = A("wuv_sb", [128, 2, NH * 64], BF16)
                        shift_sb = A("shift_sb", [128, 2, 96], BF16)
                        qng_sb = A("qng_sb", [128, 4])
                        rC = A("rC", [128, NT], BF16)
                        rS = A("rS", [128, NT], BF16)
                        Qh = A("Qh", [128, NT], BF16)
                        Kh = A("Kh", [128, NK], BF16)
                        Ve = A("Ve", [128, NK // 128, 128], BF16)
                        Vo = A("Vo", [128, NK // 128, 128], BF16)
                        Pb = A("Pb", [128, 3, TT], BF16)
                        qf = A("qf", [128, 2, TT])
                        hsq = A("hsq", [128, TT], BF16)
                        hrs = A("hrs", [128, TT])
                        ht = A("ht", [128, 2, TT])
                        rec = A("rec", [128, TT])
                        load_w(wuq_sb, w_uq[l], 3, 'wuq')
                        load_w(wuqs_sb, w_uq_sw[l], 3, 'wuqs')
                        load_w(wuk_sb, w_uk[l], 2, 'wuk')
                        load_w(wuv_sb, w_uv[l], 2, 'wuv')
                        S.dma('sp', shift_sb[0:32], shiftm[:, :, :], (), ['shift'])
                        S.dma('sp', qng_sb[:], qng[l], (), ['qng'])
                        S.dma('sp', rC[:], ropeC[:, :], (), ['rC'])
                        S.dma('sp', rS[:], ropeS[:, :], (), ['rS'])
                        memset(Qh[96:128, :], 0.0, ['Qind'])
                        memset(Kh[96:128, :], 0.0, ['Kind'])
                        S.dma('sp', Qh[96:104, :], qind[:, :], (), ['Qind'])
                        S.dma('sp', Kh[96:104, :], kind[:, :], (), ['Kind'])
                        memset(Ve[:, :, 64:128], 1.0, ['Ve1'])
                        memset(Vo[:, :, 0:64], 1.0, ['Vo1'])

                        def normrope(pq, pkq, pqs, pkqs, gcol, dest, dkey, sl):
                            act(qf[0:96, 0, :], pq[0:96, :], AF.Copy, [pkq], [('qf', 0)])
                            act(hsq[0:96, :], qf[0:96, 0, :], AF.Square, [('qf', 0)], ['hsq'])
                            pz, pkz = PS()
                            mm(pz[0:96, :], ones_bf[0:96, 0:96], hsq[0:96, :], True, True, ['ones_bf', 'hsq'], [pkz])
                            cp(hrs[0:96, :], pz[0:96, :], [pkz], ['hrs'])
                            rsqrt_inplace(hrs[0:96, :], 'hrs', 1.0 / QK)
                            if pqs is not None:
                                act(qf[0:96, 1, :], pqs[0:96, :], AF.Copy, [pkqs], [('qf', 1)])
                                stt(ht[0:96, 0, :], qf[0:96, 0, :], qng_sb[0:96, gcol:gcol + 1], rC[0:96, sl], ALU.mult, ALU.mult,
                                    [('qf', 0), 'qng', 'rC'], [('ht', 0)])
                                stt(ht[0:96, 1, :], qf[0:96, 1, :], qng_sb[0:96, gcol + 1:gcol + 2], rS[0:96, sl], ALU.mult, ALU.mult,
                                    [('qf', 1), 'qng', 'rS'], [('ht', 1)])
                                tt(ht[0:96, 0, :], ht[0:96, 0, :], ht[0:96, 1, :], ALU.add, [('ht', 0), ('ht', 1)], [('ht', 0)])
                                tt(dest, ht[0:96, 0, :], hrs[0:96, :], ALU.mult, [('ht', 0), 'hrs'], [dkey])
                            else:
                                stt(dest, qf[0:96, 0, :], qng_sb[0:96, gcol:gcol + 1], hrs[0:96, :], ALU.mult, ALU.mult,
                                    [('qf', 0), 'qng', 'hrs'], [dkey])

                        for h in range(NH if not ATT_SKIP else 0):
                            V = Ve if h % 2 == 0 else Vo
                            vkey = 'Ve' if h % 2 == 0 else 'Vo'
                            voff = 0 if h % 2 == 0 else 64
                            hs = slice(h * QK, (h + 1) * QK)
                            for t in range(NTT):
                                sl = slice(t * TT, (t + 1) * TT)
                                pq, pkq = PS()
                                for c in range(3):
                                    mm(pq[0:96, :], wuq_sb[:, c, hs], cqn[:, c, sl], c == 0, c == 2, ['wuq', ('cqn', c, t)], [pkq])
                                pqs, pkqs = PS()
                                for c in range(3):
                                    mm(pqs[0:96, :], wuqs_sb[:, c, hs], cqn[:, c, sl], c == 0, c == 2, ['wuqs', ('cqn', c, t)], [pkqs])
                                normrope(pq, pkq, pqs, pkqs, 0, Qh[0:96, sl], ('Qh', t), sl)
                            for kt in range(NK // TT if HSTOP >= 2 else 0):
                                sl = slice(kt * TT, (kt + 1) * TT)
                                pk_, pkk = PS()
                                for c in range(2):
                                    mm(pk_[0:96, :], wuk_sb[:, c, hs], ckvT[:, c, sl], c == 0, False, ['wuk', ('ckvT', kt)], [pkk])
                                mm(pk_[0:96, :], shift_sb[0:32, 0, :], kpeT[0:32, sl], False, True, ['shift', ('kpeT', kt)], [pkk])
                                if kt < NTT:
                                    pks, pkks = PS()
                                    mm(pks[0:96, :], shift_sb[0:32, 1, :], kpeT[0:32, sl], True, True, ['shift', ('kpeT', kt)], [pkks])
                                    normrope(pk_, pkk, pks, pkks, 2, Kh[0:96, sl], ('Kh', kt), sl)
                                else:
                                    normrope(pk_, pkk, None, None, 2, Kh[0:96, sl], ('Kh', kt), sl)
                            for n0 in (range(0, NK // 128, 8) if HSTOP >= 3 else []):
                                nn = min(8, NK // 128 - n0)
                                pv, pkv = PS()
                                for j in range(nn):
                                    n = n0 + j
                                    for c in range(2):
                                        mm(pv[:, j * 64:(j + 1) * 64], ckvT[:, c, n * 128:(n + 1) * 128], wuv_sb[:, c, h * 64:(h + 1) * 64],
                                           c == 0, c == 1, [('ckvT', n // 4), 'wuv'], [pkv])
                                cp(V[:, n0:n0 + nn, voff:voff + 64], pv[:, 0:nn * 64].rearrange("p (a b) -> p a b", a=nn), [pkv], [vkey])
                            for qt in range(NTT if HSTOP >= 4 else 0):
                                sl = slice(qt * TT, (qt + 1) * TT)
                                po, pko = PS()
                                NKT = NK // 128
                                LOOK = 3
                                stiles = {}

                                def emitS(n):
                                    ps_, pks_ = PS()
                                    if ps_ is po:
                                        ps_, pks_ = PS()
                                    mm(ps_[:], Kh[:, n * 128:(n + 1) * 128], Qh[:, sl], True, True,
                                       [('Kh', n // 4), 'Kind', ('Qh', qt), 'Qind'], [pks_])
                                    stiles[n] = (ps_, pks_)

                                for n in range(min(LOOK, NKT)):
                                    emitS(n)
                                for n in range(NKT):
                                    if n + LOOK < NKT:
                                        emitS(n + LOOK)
                                    ps_, pks_ = stiles.pop(n)
                                    pb = n % 3
                                    act(Pb[:, pb, :], ps_[:], AF.Exp, [pks_], [('Pb', pb)], scale=float(QK) ** -0.5)
                                    mm(po[:], V[:, n, :], Pb[:, pb, :], n == 0, n == NKT - 1, [vkey, vkey + '1', ('Pb', pb)], [pko])
                                if HSTOP < 5:
                                    continue
                                if h % 2 == 0:
                                    recip(rec[0:64, :], po[64:128, :], [pko], ['rec'])
                                    tt(ybuf[0:64, h // 2, sl], po[0:64, :], rec[0:64, :], ALU.mult, [pko, 'rec'], [('y', h // 2, qt)])
                                else:
                                    recip(rec[64:128, :], po[0:64, :], [pko], ['rec'])
                                    tt(ybuf[64:128, h // 2, sl], po[64:128, :], rec[64:128, :], ALU.mult, [pko, 'rec'], [('y', h // 2, qt)])
                        S.barrier()
                dump("yattn%d" % l, ybuf[:], [128, 4, NT], BF16, ('y', 0, 0))
                run_epi(2, ybuf)
            if 'A' in BRANCHES:
              with T("ybuf", [128, 4, NT], BF16) as ybuf:
                esA = ExitStack()
                with esA:
                    def AA(name, shape, dt=F32):
                        return esA.enter_context(T(name, shape, dt))
                    U = AA("U", [128, 32, 256], BF16)
                    Tz = AA("Tz", [128, 32, 128], BF16)
                    Bb = AA("Bb", [128, 2, 32, 16])
                    Cc = AA("Cc", [128, 2, 32, 16])
                    Qp = AA("Qp", [128, 2, 32, 24])
                    A1x = AA("A1x", [128, 2, 32])
                    A2x = AA("A2x", [128, 2, 32])
                    dsk = AA("dsk", [128, 32])
                    smk = AA("smk", [128, 257])
                    seltp = AA("seltp", [128, 8, 240], BF16)
                    tabA = AA("tabA", [128, 2, 2, 128], BF16)
                    tabB = AA("tabB", [128, 2, 2, 128], BF16)
                    otmp = AA("otmp", [128, 4, 128])
                    tabM = AA("tabM", [128, 2, 2, 2, 128], BF16)
                    hmk = AA("hmk", [128, 2])
                    S.dma('sp', hmk[:], halfmask[:, :], (), ['hmk'])
                    S.dma('sp', Cc[:], c2[l], (), ['Cc'])
                    S.dma('sp', dsk[:], dskT[l], (), ['dsk'])
                    S.dma('sp', smk[:], smask[:, 0, :], (), ['smk'])
                    S.dma('sp', seltp[:], seltpad[:, :, :], (), ['seltp'])
                    with T("wssm", [128, 8, 512], BF16) as wssm, T("uf", [128, 4, NT], BF16) as uf, T("selp", [128, 8, 240], BF16) as selp:
                        S.dma('sp', selp[:], selpad[:, :, :], (), ['selp'])
                        load_w(wssm, w_in[l][:, OFF_SSM:OFF_SSM + 512], 8, 'wssm')
                        for t in range(NTT):
                            sl = slice(t * TT, (t + 1) * TT)
                            for oc in range(4):
                                pu, pku = PS()
                                for kc in range(8):
                                    mm(pu[:], wssm[:, kc, oc * 128:(oc + 1) * 128], xn[:, kc, sl], kc == 0, kc == 7,
                                       ['wssm', ('xn', kc, t)], [pku])
                                if oc % 2 == 0:
                                    act(uf[:, oc, sl], pu[:], AF.Copy, [pku], [('uf', oc)])
                                else:
                                    cp(uf[:, oc, sl], pu[:], [pku], [('uf', oc)])
                        for g2 in range(16):
                            pu, pku = PS()
                            for gi in range(2):
                                g = g2 * 2 + gi
                                for r in range(8):
                                    mm(pu[:, gi * 256:(gi + 1) * 256], selp[:, g % 8, 112 - 16 * r:240 - 16 * r], uf[:, g // 8, r:NT:8],
                                       r == 0, r == 7, ['selp', ('uf', g // 8)], [pku])
                            if g2 % 2 == 0:
                                act(U[:, g2 * 2:g2 * 2 + 2, :], pu[:].rearrange("p (a b) -> p a b", a=2), AF.Copy, [pku], [('U', g2)])
                            else:
                                cp(U[:, g2 * 2:g2 * 2 + 2, :], pu[:].rearrange("p (a b) -> p a b", a=2), [pku], [('U', g2)])
                        S.barrier()
                    S.mute = ASTOP < 2
                    with T("lam", [128, 2, 32]) as lam, T("ldt", [128, 32]) as ldt, T("braw", [128, 2, 32, 16]) as braw, \
                            T("ramp", [128, 32, 24]) as ramp, T("w1", [128, 32, 24]) as w1, T("w2", [128, 32, 24]) as w2, \
                            T("w3", [128, 32, 24]) as w3, T("w4", [128, 32, 24]) as w4, T("wi", [128, 32, 24], I32) as wi, \
                            T("v1", [128, 8, 32]) as v1, T("bt", [128, 2, 32, 16]) as bt:
                        S.dma('sp', lam[:], lam2[l], (), ['lam'])
                        S.dma('sp', ldt[:], logdt2[l], (), ['ldt'])
                        S.dma('sp', braw[:], b2[l], (), ['braw'])
                        S.dma('sp', ramp[:], ramp32[:, :, :], (), ['ramp'])
                        K_ = ['tbl']
                        def bc(ap2):
                            return ap2.unsqueeze(2).broadcast_to([128, 32, 24])
                        act(ldt[:], ldt[:], AF.Exp, ['ldt'], K_)
                        tt(v1[:, 0, :], lam[:, 0, :], ldt[:], ALU.mult, ['lam'] + K_, K_)
                        tt(v1[:, 1, :], lam[:, 1, :], ldt[:], ALU.mult, ['lam'] + K_, K_)
                        tt(w1[:], bc(v1[:, 1, :]), ramp[:], ALU.mult, ['ramp'] + K_, K_)
                        tt(w2[:], bc(v1[:, 0, :]), ramp[:], ALU.mult, ['ramp'] + K_, K_)
                        act(w2[:], w2[:], AF.Exp, K_, K_)
                        ts(w1[:], w1[:], 1.0 / (2.0 * math.pi), None, ALU.mult, None, K_, K_)
                        cp(wi[:], w1[:], K_, K_)
                        cp(w3[:], wi[:], K_, K_)
                        tt(w1[:], w1[:], w3[:], ALU.subtract, K_, K_)
                        act(w3[:], w1[:], AF.Sin, K_, K_, scale=math.pi)
                        act(w4[:], w1[:], AF.Sin, K_, K_, scale=math.pi / 2.0)
                        tt(w4[:], w4[:], w4[:], ALU.mult, K_, K_)
                        ts(w4[:], w4[:], -2.0, 1.0, ALU.mult, ALU.add, K_, K_)
                        tt(w4[:], w4[:], w3[:], ALU.mult, K_, K_)
                        ts(w4[:], w4[:], 2.0, None, ALU.mult, None, K_, K_)
                        tt(w3[:], w3[:], w3[:], ALU.mult, K_, K_)
                        ts(w3[:], w3[:], -2.0, 1.0, ALU.mult, ALU.add, K_, K_)
                        tt(Qp[:, 0], w2[:], w3[:], ALU.mult, K_, K_)
                        tt(Qp[:, 1], w2[:], w4[:], ALU.mult, K_, K_)
                        for ri in range(2):
                            cp(v1[:, 2 + ri, 0:16], Qp[:, ri, 0:16, 8], K_, K_)
                            cp(v1[:, 2 + ri, 16:32], Qp[:, ri, 16:32, 1], K_, K_)
                        cp(A1x[:, 0, 0:16], Qp[:, 0, 0:16, 15], K_, K_)
                        cp(A1x[:, 0, 16:32], Qp[:, 0, 16:32, 8], K_, K_)
                        cp(A1x[:, 1, :], A1x[:, 0, :], K_, K_)
                        cp(A2x[:, 1, 0:16], Qp[:, 1, 0:16, 15], K_, K_)
                        cp(A2x[:, 1, 16:32], Qp[:, 1, 16:32, 8], K_, K_)
                        ts(A2x[:, 0, :], A2x[:, 1, :], -1.0, None, ALU.mult, None, K_, K_)
                        tt(v1[:, 4, :], lam[:, 0, :], lam[:, 0, :], ALU.mult, K_, K_)
                        tt(v1[:, 5, :], lam[:, 1, :], lam[:, 1, :], ALU.mult, K_, K_)
                        tt(v1[:, 4, :], v1[:, 4, :], v1[:, 5, :], ALU.add, K_, K_)
                        recip(v1[:, 4, :], v1[:, 4, :], K_, K_)
                        ts(v1[:, 2, :], v1[:, 2, :], -1.0, None, ALU.add, None, K_, K_)
                        tt(v1[:, 5, :], v1[:, 2, :], lam[:, 0, :], ALU.mult, K_, K_)
                        tt(v1[:, 6, :], v1[:, 3, :], lam[:, 1, :], ALU.mult, K_, K_)
                        tt(v1[:, 5, :], v1[:, 5, :], v1[:, 6, :], ALU.add, K_, K_)
                        tt(v1[:, 5, :], v1[:, 5, :], v1[:, 4, :], ALU.mult, K_, K_)
                        tt(v1[:, 6, :], v1[:, 3, :], lam[:, 0, :], ALU.mult, K_, K_)
                        tt(v1[:, 7, :], v1[:, 2, :], lam[:, 1, :], ALU.mult, K_, K_)
                        tt(v1[:, 6, :], v1[:, 6, :], v1[:, 7, :], ALU.subtract, K_, K_)
                        tt(v1[:, 6, :], v1[:, 6, :], v1[:, 4, :], ALU.mult, K_, K_)
                        def bc16(ap2):
                            return ap2.unsqueeze(2).broadcast_to([128, 32, 16])
                        tt(bt[:, 0], bc16(v1[:, 5, :]), braw[:, 0], ALU.mult, ['braw'] + K_, K_)
                        tt(bt[:, 1], bc16(v1[:, 6, :]), braw[:, 1], ALU.mult, ['braw'] + K_, K_)
                        tt(Bb[:, 0], bt[:, 0], bt[:, 1], ALU.subtract, K_, K_)
                        tt(bt[:, 0], bc16(v1[:, 5, :]), braw[:, 1], ALU.mult, ['braw'] + K_, K_)
                        tt(bt[:, 1], bc16(v1[:, 6, :]), braw[:, 0], ALU.mult, ['braw'] + K_, K_)
                        tt(Bb[:, 1], bt[:, 0], bt[:, 1], ALU.add, K_, K_)
                        S.barrier()

                    def outer(dst, dkey, dp, kind, X, xkey, neg_im):
                        q_re = Qp[:, 0, dp, kind * 8:(kind + 1) * 8].unsqueeze(2).broadcast_to([128, 8, 16])
                        q_im = Qp[:, 1, dp, kind * 8:(kind + 1) * 8].unsqueeze(2).broadcast_to([128, 8, 16])
                        x_re = X[:, 0, dp, :].unsqueeze(1).broadcast_to([128, 8, 16])
                        x_im = X[:, 1, dp, :].unsqueeze(1).broadcast_to([128, 8, 16])
                        o = [otmp[:, i, :].rearrange("p (a b) -> p a b", a=8) for i in range(4)]
                        d_re = dst[:, 0, :].rearrange("p (a b) -> p a b", a=8)
                        d_im = dst[:, 1, :].rearrange("p (a b) -> p a b", a=8)
                        tt(o[0], q_re, x_re, ALU.mult, ['tbl', xkey], [('otmp', 0)])
                        tt(o[1], q_im, x_im, ALU.mult, ['tbl', xkey], [('otmp', 1)])
                        tt(d_re, o[0], o[1], ALU.subtract, [('otmp', 0), ('otmp', 1)], [dkey])
                        tt(o[2], q_re, x_im, ALU.mult, ['tbl', xkey], [('otmp', 2)])
                        tt(o[3], q_im, x_re, ALU.mult, ['tbl', xkey], [('otmp', 3)])
                        if neg_im:
                            stt(d_im, o[2], -1.0, o[3], ALU.mult, ALU.subtract, [('otmp', 2), ('otmp', 3)], [dkey])
                        else:
                            tt(d_im, o[2], o[3], ALU.add, [('otmp', 2), ('otmp', 3)], [dkey])

                    def maskE(d):
                        for ge in range(2):
                            ts(tabM[:, ge, d].rearrange("p a b -> p (a b)"), tabB[:, d].rearrange("p a b -> p (a b)"), hmk[:, ge:ge + 1], None,
                               ALU.mult, None, [('tabB', d), 'hmk'], [('tabM', d)])

                    S.mute = ASTOP < 3
                    with T("tzt", [128, 2, 256]) as tzt, T("tzm", [128, 2, 256]) as tzm:
                        S.dma('sp', tzm[:], tzmask[:, :, 0:256], (), ['tzm'])
                        for pair in range(16):
                            pT = []
                            for d in range(2):
                                dp = d * 16 + pair
                                outer(tabA[:, d], ('tabA', d), dp, 2, Bb, 'tbl', False)
                                outer(tabB[:, d], ('tabB', d), dp, 1, Cc, 'Cc', True)
                                pt_, pkt_ = PS()
                                maskE(d)
                                for ge in range(2 if A3 >= 2 else 0):
                                    mm(pt_[:, ge * 128:(ge + 1) * 128], tabA[:, d, 0, :], tabM[:, ge, d, 0, :], True, False,
                                       [('tabA', d), ('tabM', d)], [pkt_])
                                    mm(pt_[:, ge * 128:(ge + 1) * 128], tabA[:, d, 1, :], tabM[:, ge, d, 1, :], False, True,
                                       [('tabA', d), ('tabM', d)], [pkt_])
                                pT.append((pt_, pkt_))
                            if A3 < 3:
                                continue
                            tt(tzt[:, 0, :], pT[0][0][:, 0:256], tzm[:, 0, :], ALU.mult, [pT[0][1], 'tzm'], [('tzt', 0)])
                            tt(tzt[:, 1, :], pT[1][0][:, 0:256], tzm[:, 1, :], ALU.mult, [pT[1][1], 'tzm'], [('tzt', 1)])
                            tt(tzt[:, 0, :], tzt[:, 0, :], tzt[:, 1, :], ALU.add, [('tzt', 0), ('tzt', 1)], [('tzt', 0)])
                            for ge in range(2):
                                g = 2 * pair + ge
                                stt(Tz[:, g, :], ident[:], dsk[:, g:g + 1], tzt[:, 0, ge * 128:(ge + 1) * 128], ALU.mult, ALU.add,
                                    ['ident', 'dsk', ('tzt', 0)], [('Tz', g)])
                        S.barrier()

                    S.mute = ASTOP < 4
                    with T("SH", [128, 2, 32, 257], BF16) as SH, T("Zs", [128, 2, 32]) as Zs, T("Vs", [128, 2, 32]) as Vs, \
                            T("P1", [128, 2, 32]) as P1, T("P2", [128, 2, 32]) as P2, T("fin", [128, 8, 2, 2, 16]) as fin, \
                            T("Wt", [128, 2, 2, 128], BF16) as Wt, T("Ysb", [128, 8, 256], BF16) as Ysb, \
                            T("gl", [128, 2, TT]) as gl:
                        S.dma('sp', Zs[:], h0[l][:, 0:64], (), ['Zs'])
                        cp(SH[:, :, :, 0], Zs[:], ['Zs'], [('SH', 0)])
                        for pair in range(16):
                            for d in range(2):
                                dp = d * 16 + pair
                                outer(tabA[:, d], ('tabA', d), dp, 0, Bb, 'tbl', False)
                                ptr_, pktr = PS()
                                ptb = ptr_[:].bitcast(BF16)
                                tr(ptb[:, 0:128], tabA[:, d, 0, :], ident_bf[:], [('tabA', d), 'ident_bf'], [pktr])
                                tr(ptb[:, 128:256], tabA[:, d, 1, :], ident_bf[:], [('tabA', d), 'ident_bf'], [pktr])
                                cp(Wt[:, d, :, :], ptb[:, 0:256].rearrange("p (a b) -> p a b", a=2), [pktr], [('Wt', d)])
                                ps_, pks_ = PS()
                                for ri in range(2):
                                    for ge in range(2):
                                        mm(ps_[ge * 64:(ge + 1) * 64, ri * 256:(ri + 1) * 256], Wt[:, d, ri, ge * 64:(ge + 1) * 64],
                                           U[:, 2 * pair + ge, :], True, True, [('Wt', d), ('U', pair)], [pks_])
                                src_ = ps_[:].rearrange("p (a b) -> p a b", a=2)
                                if d == 0:
                                    cp(SH[:, :, dp, 1:257], src_, [pks_], [('SHs', dp)])
                                else:
                                    cp(SH[:, :, dp, 256:0:-1], src_, [pks_], [('SHs', dp)])
                        S.barrier()
                        S.mute = ASTOP < 5
                        SK = ['scan']
                        for i in range(256):
                            if i == 0:
                                tt(P1[:], Zs[:], A1x[:], ALU.mult, SK, SK)
                                tt(P2[:], Zs[:, ::-1, :], A2x[:], ALU.mult, SK, SK)
                            else:
                                stt(P1[:], Vs[:], smk[:, i:i + 1], A1x[:], ALU.mult, ALU.mult, SK + ['smk'], SK)
                                stt(P2[:], Vs[:, ::-1, :], smk[:, i:i + 1], A2x[:], ALU.mult, ALU.mult, SK + ['smk'], SK)
                            tt(P1[:], P1[:], SH[:, :, :, i + 1], ALU.add, SK, SK)
                            tt(Vs[:], P1[:], P2[:], ALU.add, SK, SK)
                            if (i + 1) % 32 == 0:
                                q = i // 32
                                cp(fin[:, q, :, 0, :], Vs[:, :, 0:16], SK, ['fin'])
                                cp(fin[:, 7 - q, :, 1, :], Vs[:, :, 16:32], SK, ['fin'])
                            act(SH[:, :, :, i + 1], Vs[:], AF.Copy, SK + ['smk'], [('SHo', i)], scale=smk[:, i + 1:i + 2])
                        S.dma('sp', ssm_out[l], fin[:].rearrange("p a b c d -> p (a b c d)"), SK + ['fin'], ['ssm_out'])
                        S.barrier()
                        S.mute = ASTOP < 6
                        for oc in range(4):
                            for pl in range(4):
                                pair = oc * 4 + pl
                                py, pky = PS()
                                for d in range(2):
                                    outer(tabB[:, d], ('tabB', d), d * 16 + pair, 1, Cc, 'Cc', True)
                                    maskE(d)
                                for ge in range(2):
                                    g = 2 * pair + ge
                                    o_ = py[:, ge * 256:(ge + 1) * 256]
                                    mm(o_, Tz[:, g, :], U[:, g, :], True, False, [('Tz', g), ('U', pair)], [pky])
                                    mm(o_, tabM[:, ge, 0, 0, :], SH[:, 0, pair, 0:256], False, False, [('tabM', 0), 'scan'], [pky])
                                    mm(o_, tabM[:, ge, 0, 1, :], SH[:, 1, pair, 0:256], False, False, [('tabM', 0), 'scan'], [pky])
                                    mm(o_, tabM[:, ge, 1, 0, :], SH[:, 0, 16 + pair, 255::-1], False, False, [('tabM', 1), 'scan'], [pky])
                                    mm(o_, tabM[:, ge, 1, 1, :], SH[:, 1, 16 + pair, 255::-1], False, True, [('tabM', 1), 'scan'], [pky])
                                cp(Ysb[:, 2 * pl:2 * pl + 2, :], py[:].rearrange("p (a b) -> p a b", a=2), [pky], [('Ysb', pl)])
                            yv = ybuf[:, oc, :].rearrange("p (j r) -> p r j", r=8)
                            for r2 in range(4):
                                pz, pkz = PS()
                                for ri in range(2):
                                    r = r2 * 2 + ri
                                    for gl_ in range(8):
                                        mm(pz[:, ri * 256:(ri + 1) * 256], seltp[:, r, 112 - 16 * gl_:240 - 16 * gl_], Ysb[:, gl_, :],
                                           gl_ == 0, gl_ == 7, ['seltp', ('Ysb', gl_ // 2)], [pkz])
                                b = 0
                                act(gl[:, 2 * b, :], pz[:], AF.Copy, [pkz], [('gl', 2 * b)])
                                tt(gl[:, 2 * b + 1, :], gl[:, 2 * b, :], gl[:, 2 * b, :], ALU.mult, [('gl', 2 * b)], [('gl', 2 * b + 1)])
                                ts(gl[:, 2 * b + 1, :], gl[:, 2 * b + 1, :], 0.044715, 1.0, ALU.mult, ALU.add, [('gl', 2 * b + 1)], [('gl', 2 * b + 1)])
                                tt(gl[:, 2 * b + 1, :], gl[:, 2 * b + 1, :], gl[:, 2 * b, :], ALU.mult, [('gl', 2 * b), ('gl', 2 * b + 1)], [('gl', 2 * b + 1)])
                                act(gl[:, 2 * b + 1, :], gl[:, 2 * b + 1, :], AF.Sigmoid, [('gl', 2 * b + 1)], [('gl', 2 * b + 1)], scale=1.5957691216057308)
                                tt(yv[:, r2 * 2:r2 * 2 + 2, :], gl[:, 2 * b, :].rearrange("p (a b) -> p a b", a=2),
                                   gl[:, 2 * b + 1, :].rearrange("p (a b) -> p a b", a=2), ALU.mult,
                                   [('gl', 2 * b), ('gl', 2 * b + 1)], [('y', oc, 0), ('y', oc, 1), ('y', oc, 2), ('y', oc, 3)])
                        S.barrier()
                S.mute = False
                with T("wglu", [128, 4, BW], BF16) as wglu, T("gsg", [128, 4, TT]) as gsg:
                    load_w(wglu, w_glu[l], 4, 'wglu')
                    for t in range(NTT):
                        sl = slice(t * TT, (t + 1) * TT)
                        for oc in range(4):
                            pg, pkg = PS()
                            for kc in range(4):
                                mm(pg[:], wglu[:, kc, oc * 128:(oc + 1) * 128], ybuf[:, kc, sl], kc == 0, kc == 3,
                                   ['wglu', ('y', kc, t)], [pkg])
                            act(gsg[:, oc, :], pg[:], AF.Sigmoid, [pkg], [('gsg', oc)])
                        for oc in range(4):
                            tt(ybuf[:, oc, sl], ybuf[:, oc, sl], gsg[:, oc, :], ALU.mult, [('y', oc, t), ('gsg', oc)], [('y', oc, t)])
                    S.barrier()
                dump("yssm%d" % l, ybuf[:], [128, 4, NT], BF16, ('y', 0, 0))
                run_epi(0, ybuf)
            S.barrier()
            dump("xmid%d" % l, x[:], [128, 8, NT], F32, ('x', 0, 0))

            rmsnorm_mod(l, None, 16, 24)
            for half in range(2 if not int(os.environ.get('SKF', '0')) else 0):
                with T("hT", [128, 22, 1024], BF16) as hT, \
                        T("wfi", [128, 2, 8, 256], BF16) as wfi, \
                        T("wfo", [128, 2, 22, 128], BF16) as wfo, \
                        T("fsg", [128, 2, TT], F32) as fsg:
                    for c in range(22):
                        b = c % 2
                        S.dma('pool', wfi[:, b, :, 0:128], w_ffn_in[l][:, c * 128:(c + 1) * 128].rearrange("(kc p) n -> p kc n", p=128),
                              (), [('wfi', b)])
                        S.dma('pool', wfi[:, b, :, 128:256],
                              w_ffn_in[l][:, DFF + c * 128:DFF + (c + 1) * 128].rearrange("(kc p) n -> p kc n", p=128),
                              (), [('wfi', b)])
                        for tt_ in range(2):
                            t = half * 2 + tt_
                            sl = slice(t * TT, (t + 1) * TT)
                            pg, pkg = PS()
                            for kc in range(8):
                                mm(pg[:], wfi[:, b, kc, 0:128], xn[:, kc, sl], kc == 0, kc == 7, [('wfi', b), ('xn', kc, t)], [pkg])
                            pu, pku = PS()
                            for kc in range(8):
                                mm(pu[:], wfi[:, b, kc, 128:256], xn[:, kc, sl], kc == 0, kc == 7, [('wfi', b), ('xn', kc, t)], [pku])
                            act(fsg[:, tt_, :], pg[:], AF.Silu, [pkg], [('fsg', tt_)])
                            tt(hT[:, c, tt_ * TT:(tt_ + 1) * TT], pu[:], fsg[:, tt_, :], ALU.mult, [pku, ('fsg', tt_)], [('hT', c, tt_)])
                    for tt_ in range(2):
                        t = half * 2 + tt_
                        sl = slice(t * TT, (t + 1) * TT)
                        pcs = [PS() for _ in range(8)] if False else None
                    for oc in range(8):
                        pcs = [PS(), PS()]
                        b = oc % 2
                        S.dma('pool', wfo[:, b, :, :],
                              w_ffn_out[l][:, oc * 128:(oc + 1) * 128].rearrange("(kc p) n -> p kc n", p=128),
                              (), [('wfo', b)])
                        for kc in range(22):
                            for tt_ in range(2):
                                mm(pcs[tt_][0][:], wfo[:, b, kc, :], hT[:, kc, tt_ * TT:(tt_ + 1) * TT],
                                   kc == 0, kc == 21, [('wfo', b), ('hT', kc, tt_)], [pcs[tt_][1]])
                        for tt_ in range(2):
                            t = half * 2 + tt_
                            sl = slice(t * TT, (t + 1) * TT)
                            stt(x[:, oc, sl], pcs[tt_][0][:], mod[:, 40 + oc:41 + oc], x[:, oc, sl], ALU.mult, ALU.add,
                                [pcs[tt_][1], 'mod', ('x', oc, t)], [('x', oc, t)])
                    S.barrier()
            dump("xout%d" % l, x[:], [128, 8, NT], F32, ('x', 0, 0))

        with T("ytm", [128, 2, D], F32) as ytm:
            for n in range(NT // 128):
                b = n % 2
                for hh in range(2):
                    ps, pk = PS()
                    for q in range(4):
                        oc = hh * 4 + q
                        tr(ps[:, q * 128:(q + 1) * 128], x[:, oc, n * 128:(n + 1) * 128], ident[:],
                           [('x', oc, n // 4), 'ident'], [pk])
                    if hh == 0:
                        cp(ytm[:, b, 0:512], ps[:], [pk], [('ytm', b)])
                    else:
                        act(ytm[:, b, 512:1024], ps[:], AF.Copy, [pk], [('ytm', b)])
                S.dma('sp', y_out[n * 128:(n + 1) * 128, :], ytm[:, b, :], [('ytm', b)], ['y_out'])
        S.barrier()
    return nc, dbg_out, S


def _bf(a):
    return np.ascontiguousarray(np.asarray(a, dtype=np.float32)).astype(NPBF)


def _consts(nseq):
    Ls = NT // nseq
    t = np.arange(NT)
    seq = t // Ls
    c = {}
    C = np.ones((128, NT), np.float64)
    Sg = np.zeros((128, NT), np.float64)
    if nseq == 1:
        GRID_W = 64
        row = (t // GRID_W).astype(np.float32)
        col = (t % GRID_W).astype(np.float32)
        inv = (np.float32(10000.0) ** (-np.arange(8, dtype=np.float32) / np.float32(8))).astype(np.float32)
        ang = np.concatenate([row[:, None] * inv, col[:, None] * inv], axis=-1).astype(np.float32)
        cs, sn = np.cos(ang.astype(np.float64)), np.sin(ang.astype(np.float64))
        for i in range(16):
            C[64 + 2 * i] = cs[:, i]
            C[64 + 2 * i + 1] = cs[:, i]
            Sg[64 + 2 * i] = -sn[:, i]
            Sg[64 + 2 * i + 1] = sn[:, i]
    c['ropeC'] = _bf(C)
    c['ropeS'] = _bf(Sg)
    qi = np.zeros((8, NT), np.float32)
    qi[seq, t] = 1.0
    ki = np.zeros((8, NK), np.float32)
    if nseq > 1:
        ki[:, :NT] = -BIG
        ki[seq, t] = 0.0
        ki[:, NT:] = -BIG
    c['qind'] = _bf(qi)
    c['kind'] = _bf(ki)
    nl = (t % Ls).astype(np.float64)
    same = (seq[:, None] == seq[None, :])
    ph = 2.0 * np.pi * ((nl[:, None] * nl[None, :]) % Ls) / Ls
    c['dftC'] = _bf(np.where(same, np.cos(ph) / np.sqrt(Ls), 0.0))
    c['dftS'] = _bf(np.where(same, -np.sin(ph) / np.sqrt(Ls), 0.0))
    cc = np.arange(128, dtype=np.float64)
    ph2 = 2.0 * np.pi * ((cc[:, None] * cc[None, :]) % 128) / 128.0
    c['ccsc'] = _bf(np.concatenate([np.cos(ph2), np.sin(ph2)], axis=1) / np.sqrt(128.0))
    mL = (t % Ls != 0).astype(np.float32)
    mR = (t % Ls != Ls - 1).astype(np.float32)
    c['seqflag'] = np.full((128, 1), 1.0 if nseq == 1 else 0.0, np.float32)
    nchunk = NT // 8
    cps = nchunk // nseq
    mf = np.ones(257, np.float32)
    mb = np.ones(257, np.float32)
    for j in range(nchunk):
        if (j + 1) % cps == 0 and (j + 1) < nchunk:
            mf[j + 1] = 0.0
            mb[j + 1] = 0.0
    c['smask'] = np.ascontiguousarray(np.broadcast_to(np.stack([mf, mb])[None], (128, 2, 257))).astype(np.float32)
    r = np.arange(8, dtype=np.float32)
    kr = np.zeros((2, 3, 8), np.float32)
    kr[0, 0] = 7 - r; kr[0, 1] = r + 1; kr[0, 2] = -(1 + r)
    kr[1, 0] = r;     kr[1, 1] = 8 - r; kr[1, 2] = r - 8
    c['ramp32'] = np.ascontiguousarray(np.broadcast_to(np.repeat(kr.reshape(2, 1, 24), 16, axis=1).reshape(1, 32, 24), (128, 32, 24))).astype(np.float32)
    rr = np.arange(128) // 16
    mF = (rr[None, :] >= rr[:, None]).astype(np.float32)
    mB = (rr[None, :] <= rr[:, None]).astype(np.float32)
    c['tzmask'] = np.ascontiguousarray(np.stack([np.tile(mF, (1, 4)), np.tile(mB, (1, 4))], axis=1)).astype(np.float32)
    sp = np.zeros((128, 8, 240), np.float32)
    stp = np.zeros((128, 8, 240), np.float32)
    for gl in range(8):
        for cc_ in range(16):
            sp[gl * 16 + cc_, gl, 112 + cc_] = 1.0
    for r_ in range(8):
        for cc_ in range(16):
            stp[r_ * 16 + cc_, r_, 112 + cc_] = 1.0
    c['selpad'] = _bf(sp)
    c['seltpad'] = _bf(stp)
    sh = np.zeros((32, 2, 96), np.float32)
    for d in range(32):
        sh[d, 0, 64 + d] = 1.0
        sh[d ^ 1, 1, 64 + d] = 1.0
    c['shiftm'] = _bf(sh)
    c['identf'] = np.eye(128, dtype=np.float32)
    hm = np.zeros((128, 2), np.float32); hm[:64, 0] = 1.0; hm[64:, 1] = 1.0
    c['halfmask'] = hm
    return c


def _colT(v, n):
    return np.ascontiguousarray(np.asarray(v, np.float32).reshape(n, 128).T)


def _shared(inp):
    f = lambda a: np.ascontiguousarray(np.asarray(a, np.float32))
    s = {}
    s['w_ada'] = f(inp['w_ada'])
    s['b_adaT'] = np.stack([_colT(inp['b_ada'][l], 48) for l in range(NL)])
    s['nmgT'] = np.stack([_colT(inp['norm_mix_g'][l], 8) for l in range(NL)])
    s['nfgT'] = np.stack([_colT(inp['norm_ffn_g'][l], 8) for l in range(NL)])
    s['w_in'] = f(inp['w_in'])
    s['qagT'] = np.stack([_colT(inp['q_a_norm_g'][l], 3) for l in range(NL)])
    s['kvag'] = np.stack([_colT(inp['kv_a_norm_g'][l], 2) for l in range(NL)])
    wuq = np.asarray(inp['w_uq'], np.float32)
    s['w_uq'] = f(wuq)
    perm = np.arange(NH * QK)
    dd = perm % QK
    perm = np.where(dd >= 64, (perm // QK) * QK + 64 + ((dd - 64) ^ 1), perm)
    s['w_uq_sw'] = f(wuq[:, :, perm])
    wukv = np.asarray(inp['w_ukv'], np.float32).reshape(NL, KVR, NH, 128)
    wuk = np.zeros((NL, KVR, NH, QK), np.float32)
    wuk[..., :64] = wukv[..., :64]
    s['w_uk'] = f(wuk.reshape(NL, KVR, NH * QK))
    s['w_uv'] = f(wukv[..., 64:].reshape(NL, KVR, NH * 64))
    qg = np.asarray(inp['q_norm_g'], np.float32)
    kg = np.asarray(inp['k_norm_g'], np.float32)
    swp = np.arange(QK)
    swp = np.where(swp >= 64, 64 + ((swp - 64) ^ 1), swp)
    qng = np.zeros((NL, 128, 4), np.float32)
    qng[:, :QK, 0] = qg
    qng[:, :QK, 1] = qg[:, swp]
    qng[:, :QK, 2] = kg
    qng[:, :QK, 3] = kg[:, swp]
    s['qng'] = qng
    def gp(a):
        a = np.asarray(a, np.float32)
        rest = a.shape[4:]
        a = a.reshape((NL, 2, 16, 2, 64) + rest)
        perm = (0, 3, 4, 1, 2) + tuple(range(5, 5 + len(rest)))
        return np.ascontiguousarray(a.transpose(perm).reshape((NL, 128, 32) + rest))
    s['lam2'] = np.ascontiguousarray(np.stack([gp(inp['ssm_lam_re']), gp(inp['ssm_lam_im'])], axis=2))
    ld = np.broadcast_to(np.asarray(inp['ssm_log_dt'], np.float32)[:, :, :, None], (NL, 2, 32, 64))
    s['logdt2'] = gp(ld)
    s['b2'] = np.ascontiguousarray(np.stack([gp(inp['ssm_b_re']), gp(inp['ssm_b_im'])], axis=2))
    cr_ = np.asarray(inp['ssm_c_re'], np.float32).transpose(0, 1, 2, 4, 3)
    ci_ = np.asarray(inp['ssm_c_im'], np.float32).transpose(0, 1, 2, 4, 3)
    s['c2'] = np.ascontiguousarray(np.stack([gp(cr_), gp(ci_)], axis=2))
    dsk = np.asarray(inp['ssm_d'], np.float32).reshape(NL, 32, 16)
    s['dskT'] = f(np.broadcast_to(dsk.transpose(0, 2, 1)[:, None, :, :], (NL, 8, 16, 32)).reshape(NL, 128, 32))
    s['w_glu'] = f(inp['w_glu'])
    cw = np.asarray(inp['conv_w'], np.float32)
    s['convwT'] = f(cw.reshape(NL, 3, 4, 128).transpose(0, 3, 2, 1))
    s['w_branch'] = f(inp['w_branch'])
    s['w_gate'] = f(inp['w_gate'])
    s['b_gateT'] = np.stack([_colT(inp['b_gate'][l], 32) for l in range(NL)])
    s['w_out'] = f(inp['w_out'])
    s['w_ffn_in'] = f(inp['w_ffn_in'])
    s['w_ffn_out'] = f(inp['w_ffn_out'])
    return s


def _core_map(inp, shared, consts, core):
    m = dict(shared)
    if core < 2:
        b = core
        m.update(consts[1])
        m['xin'] = np.ascontiguousarray(np.asarray(inp['x_sample'][b], np.float32))
        m['condT'] = _colT(inp['c'][b], 8)
        m['cache_ckv'] = np.ascontiguousarray(np.asarray(inp['cache_ckv'][b], np.float32))
        m['cache_kpe'] = np.ascontiguousarray(np.asarray(inp['cache_kpe'][b], np.float32))
        st = np.asarray(inp['state_ssm'][b], np.float32)
        st = st.reshape(NL, 2, 16, 2, 64, 2)
        m['h0'] = np.ascontiguousarray(st.transpose(0, 3, 4, 5, 1, 2).reshape(NL, 128, 64))
        h0 = np.zeros((NL, 128, 128), np.float32)
        h0[:, :, :64] = m['h0']
        m['h0'] = h0
    else:
        pc = (core - 2) % 4
        m.update(consts[8])
        m['xin'] = np.ascontiguousarray(np.asarray(inp['x_prompt'][pc * 8:(pc + 1) * 8], np.float32).reshape(NT, D))
        m['condT'] = _colT(inp['c_ctx'], 8)
        m['cache_ckv'] = np.zeros((NL, PAST, KVR), np.float32)
        m['cache_kpe'] = np.zeros((NL, PAST, RD), np.float32)
        m['h0'] = np.zeros((NL, 128, 128), np.float32)
    return m


_CACHE = {}


def kernel(**inputs):
    if 'nc' not in _CACHE:
        _CACHE['nc'] = build()[0]
        _CACHE['consts'] = {1: _consts(1), 8: _consts(8)}
    nc = _CACHE['nc']
    shared = _shared(inputs)
    maps = [_core_map(inputs, shared, _CACHE['consts'], c) for c in range(8)]
    res = run_bass_kernel_spmd(nc, maps, core_ids=list(range(8)))
    R = res.results
    y_s = np.stack([R[b]['y_out'] for b in range(2)]).astype(np.float32)
    y_p = np.concatenate([R[2 + i]['y_out'].reshape(8, 256, D) for i in range(4)]).astype(np.float32)
    ckv = np.concatenate([R[2 + i]['ckv_out'].reshape(NL, 8, 256, KVR).transpose(1, 0, 2, 3) for i in range(4)])
    kpe = np.concatenate([R[2 + i]['kpe_out'].reshape(NL, 8, 256, RD).transpose(1, 0, 2, 3) for i in range(4)])
    ss = []
    for i in range(4):
        a = R[2 + i]['ssm_out'].reshape(NL, 2, 64, 8, 2, 2, 16)
        a = a.transpose(3, 0, 5, 6, 1, 2, 4).reshape(8, NL, 2, 32, 64, 2)
        ss.append(a)
    ssm = np.concatenate(ss)
    return (y_p, y_s, ckv.astype(np.float32), kpe.astype(np.float32), ssm.astype(np.float32))
```

```python
import math
import numpy as np
import ml_dtypes
import concourse.bass as bass
import concourse.mybir as mybir
from concourse.bass_utils import run_bass_kernel_spmd

F32 = mybir.dt.float32
BF16 = mybir.dt.bfloat16
I32 = mybir.dt.int32
ALU = mybir.AluOpType
AF = mybir.ActivationFunctionType
NPBF = ml_dtypes.bfloat16

D = 1024
NT = 2048
DEPTH = 4
NL = DEPTH
PAST = 512
NK = NT + PAST
BW = 512
QR = 384
KVR = 256
RD = 32
QK = 96
NH = 8
DFF = 2816
OFF_SSM, OFF_FFT, OFF_CQ, OFF_CKV, OFF_KPE, OFF_CONV = 0, 512, 1024, 1408, 1664, 1696
IN_COLS = 3232
EPS = 1e-6
BIG = 30000.0
TT = 512
NTT = NT // TT
BRANCHES = 'ABCD'
import os
ATT_SKIP = bool(int(os.environ.get('ATT_SKIP', '0')))
CUT = int(os.environ.get('CUT', '9'))
HSTOP = int(os.environ.get('HSTOP', '9'))
ASTOP = int(os.environ.get('ASTOP', '9'))
A3 = int(os.environ.get('A3', '9'))


class Sched:
    def __init__(self, nc, nds=24):
        self.nc = nc
        self.eng = {'pe': nc.tensor, 'act': nc.scalar, 'dve': nc.vector, 'pool': nc.gpsimd, 'sp': nc.sync}
        self.sem = {e: nc.alloc_semaphore(name='sem_' + e) for e in self.eng}
        self.cnt = {e: 0 for e in self.eng}
        self.dsem = [nc.alloc_semaphore(name='dsem%d' % i) for i in range(nds)]
        self.dcnt = [0] * nds
        self.dpool = {'sp': list(range(0, nds // 2)), 'pool': list(range(nds // 2, nds))}
        self.dnext = {'sp': 0, 'pool': 0}
        self.waited = {e: {} for e in self.eng}
        self.lastw = {}
        self.readers = {}
        self.nwaits = 0

    def _wait(self, e, toks):
        w = self.waited[e]
        best = {}
        for t in toks:
            if t is None:
                continue
            key = (t[0], t[1])
            if t[0] == 'c' and t[1] == e and e == 'pe':
                continue
            if w.get(key, 0) >= t[2]:
                continue
            if best.get(key, 0) < t[2]:
                best[key] = t[2]
        for key, v in best.items():
            s = self.sem[key[1]] if key[0] == 'c' else self.dsem[key[1]]
            self.eng[e].wait_ge(s, v)
            w[key] = v
            self.nwaits += 1

    def _deps(self, reads, writes):
        deps = set()
        for k in reads:
            t = self.lastw.get(k)
            if t is not None:
                deps.add(t)
        for k in writes:
            t = self.lastw.get(k)
            if t is not None:
                deps.add(t)
            for t in self.readers.get(k, {}).values():
                deps.add(t)
        return deps

    def _commit(self, tok, reads, writes):
        for k in writes:
            self.lastw[k] = tok
            self.readers[k] = {}
        for k in reads:
            r = self.readers.setdefault(k, {})
            r[(tok[0], tok[1])] = tok

    mute = False

    def op(self, e, fn, reads=(), writes=()):
        if self.mute:
            return
        self._wait(e, self._deps(reads, writes))
        inst = fn(self.eng[e])
        self.cnt[e] += 1
        inst.then_inc(self.sem[e], 1)
        self._commit(('c', e, self.cnt[e]), reads, writes)

    def dma(self, e, out, in_, reads=(), writes=(), **kw):
        if self.mute:
            return
        deps = self._deps(reads, writes)
        pl = self.dpool[e]
        i = pl[self.dnext[e]]
        self.dnext[e] = (self.dnext[e] + 1) % len(pl)
        if self.dcnt[i] > 0:
            deps.add(('d', i, self.dcnt[i] * 16))
        self._wait(e, deps)
        inst = self.eng[e].dma_start(out=out, in_=in_, **kw)
        self.dcnt[i] += 1
        inst.then_inc(self.dsem[i], 16)
        self._commit(('d', i, self.dcnt[i] * 16), reads, writes)

    def barrier(self):
        toks = [('c', e, self.cnt[e]) for e in self.eng if self.cnt[e] > 0]
        toks += [('d', i, c * 16) for i, c in enumerate(self.dcnt) if c > 0]
        for e in self.eng:
            self._wait(e, toks)
        self.lastw = {}
        self.readers = {}


def build(dbg=None, nlayers=NL):
    dbg = dbg or []
    nc = bass.Bass("TRN2", target_bir_lowering=False)
    S = Sched(nc)

    def din(name, shape, dt=F32):
        return nc.dram_tensor(name, list(shape), dt, kind="ExternalInput").ap()

    def dout(name, shape, dt=F32):
        return nc.dram_tensor(name, list(shape), dt, kind="ExternalOutput").ap()

    xin = din("xin", [NT, D])
    condT = din("condT", [128, 8])
    cache_ckv = din("cache_ckv", [NL, PAST, KVR])
    cache_kpe = din("cache_kpe", [NL, PAST, RD])
    h0 = din("h0", [NL, 128, 128])
    w_ada = din("w_ada", [NL, D, 6 * D])
    b_adaT = din("b_adaT", [NL, 128, 48])
    nmgT = din("nmgT", [NL, 128, 8])
    nfgT = din("nfgT", [NL, 128, 8])
    w_in = din("w_in", [NL, D, IN_COLS])
    qagT = din("qagT", [NL, 128, 3])
    kvag = din("kvag", [NL, 128, 2])
    w_uq = din("w_uq", [NL, QR, NH * QK])
    w_uq_sw = din("w_uq_sw", [NL, QR, NH * QK])
    w_uk = din("w_uk", [NL, KVR, NH * QK])
    w_uv = din("w_uv", [NL, KVR, NH * 64])
    qng = din("qng", [NL, 128, 4])
    dskT = din("dskT", [NL, 128, 32])
    lam2 = din("lam2", [NL, 128, 2, 32])
    logdt2 = din("logdt2", [NL, 128, 32])
    b2 = din("b2", [NL, 128, 2, 32, 16])
    c2 = din("c2", [NL, 128, 2, 32, 16])
    ramp32 = din("ramp32", [128, 32, 24])
    halfmask = din("halfmask", [128, 2])
    w_glu = din("w_glu", [NL, BW, BW])
    convwT = din("convwT", [NL, 128, 4, 3])
    w_branch = din("w_branch", [NL, 4, BW, D])
    w_gate = din("w_gate", [NL, D, 4 * D])
    b_gateT = din("b_gateT", [NL, 128, 32])
    w_out = din("w_out", [NL, D, D])
    w_ffn_in = din("w_ffn_in", [NL, D, 2 * DFF])
    w_ffn_out = din("w_ffn_out", [NL, DFF, D])
    ropeC = din("ropeC", [128, NT], BF16)
    ropeS = din("ropeS", [128, NT], BF16)
    qind = din("qind", [8, NT], BF16)
    kind = din("kind", [8, NK], BF16)
    dftC = din("dftC", [NT, NT], BF16)
    dftS = din("dftS", [NT, NT], BF16)
    ccsc = din("ccsc", [128, 256], BF16)
    seqflag = din("seqflag", [128, 1])
    smask = din("smask", [128, 2, 257])
    tzmask = din("tzmask", [128, 2, 512])
    selpad = din("selpad", [128, 8, 240], BF16)
    seltpad = din("seltpad", [128, 8, 240], BF16)
    shiftm = din("shiftm", [32, 2, 96], BF16)
    identf = din("identf", [128, 128])

    y_out = dout("y_out", [NT, D])
    ckv_out = dout("ckv_out", [NL, NT, KVR])
    kpe_out = dout("kpe_out", [NL, NT, RD])
    ssm_out = dout("ssm_out", [NL, 128, 512])
    dbg_out = {}

    from contextlib import ExitStack
    es = ExitStack()

    _uid = [0]

    def T(name, shape, dt=F32):
        _uid[0] += 1
        return nc.sbuf_tensor("%s_%d" % (name, _uid[0]), list(shape), dt)

    def sb(name, shape, dt=F32):
        return es.enter_context(T(name, list(shape), dt))

    with es:
        x = sb("x", [128, 8, NT])
        xn = sb("xn", [128, 8, NT], BF16)
        ones_bf = sb("ones_bf", [128, 128], BF16)
        ident = sb("ident", [128, 128])
        ident_bf = sb("ident_bf", [128, 128], BF16)
        mod = sb("mod", [128, 48])
        vecs = sb("vecs", [128, 64])
        epsb = sb("epsb", [128, 1])
        psum = [es.enter_context(nc.psum_tensor("ps%d" % i, [128, 512], F32)) for i in range(8)]
        pstate = {'i': 0}

        def PS():
            i = pstate['i']
            pstate['i'] = (i + 1) % 8
            return psum[i], 'ps%d' % i

        def mm(out, lhsT, rhs, start, stop, reads, writes):
            S.op('pe', lambda e: e.matmul(out, lhsT, rhs, start=start, stop=stop), reads, writes)

        def tr(out, in_, idn, reads, writes):
            S.op('pe', lambda e: e.transpose(out, in_, idn), reads, writes)

        def act(out, in_, func, reads, writes, bias=None, scale=None):
            kw = {}
            if bias is not None:
                kw['bias'] = bias
            if scale is not None:
                kw['scale'] = scale
            S.op('act', lambda e: e.activation(out=out, in_=in_, func=func, **kw), reads, writes)

        def tt(out, in0, in1, op, reads, writes, eng='dve'):
            S.op(eng, lambda e: e.tensor_tensor(out=out, in0=in0, in1=in1, op=op), reads, writes)

        def ts(out, in0, s1, s2, op0, op1, reads, writes, eng='dve'):
            if op1 is None:
                S.op(eng, lambda e: e.tensor_scalar(out=out, in0=in0, scalar1=s1, scalar2=None, op0=op0), reads, writes)
            else:
                S.op(eng, lambda e: e.tensor_scalar(out=out, in0=in0, scalar1=s1, scalar2=s2, op0=op0, op1=op1), reads, writes)

        def stt(out, in0, scalar, in1, op0, op1, reads, writes):
            S.op('dve', lambda e: e.scalar_tensor_tensor(out=out, in0=in0, scalar=scalar, in1=in1, op0=op0, op1=op1), reads, writes)

        def cp(out, in_, reads, writes, eng='dve'):
            S.op(eng, lambda e: e.tensor_copy(out=out, in_=in_), reads, writes)

        def recip(out, in_, reads, writes):
            S.op('dve', lambda e: e.reciprocal(out=out, in_=in_), reads, writes)

        def memset(t_ap, val, writes, eng='dve'):
            S.op(eng, lambda e: e.memset(t_ap, val), (), writes)

        def dump(name, ap, shape, dt, key):
            if name in dbg:
                o = dout("dbg_" + name, shape, dt)
                dbg_out[name] = o
                S.dma('sp', o, ap, reads=[key], writes=['dbg_' + name])

        def rsqrt_inplace(ap, key, scale):
            act(ap, ap, AF.Sqrt, [key, 'epsb'], [key], bias=epsb[0:ap.shape[0], 0:1], scale=scale)
            recip(ap, ap, [key], [key])

        memset(ones_bf[:], 1.0, ['ones_bf'])
        memset(epsb[:], EPS, ['epsb'])
        S.dma('sp', ident[:], identf[:, :], (), ['ident'])
        cp(ident_bf[:], ident[:], ['ident'], ['ident_bf'])

        with T("xtm", [128, 4, D], F32) as xtm:
            for t in range(NTT):
                S.dma('sp', xtm[:], xin[t * TT:(t + 1) * TT, :].rearrange("(n p) c -> p n c", p=128), (), ['xtm'])
                for oc in range(8):
                    ps, pk = PS()
                    for n in range(4):
                        tr(ps[:, n * 128:(n + 1) * 128], xtm[:, n, oc * 128:(oc + 1) * 128], ident[:], ['xtm', 'ident'], [pk])
                    if oc % 2 == 0:
                        cp(x[:, oc, t * TT:(t + 1) * TT], ps[:], [pk], [('x', oc, t)])
                    else:
                        act(x[:, oc, t * TT:(t + 1) * TT], ps[:], AF.Copy, [pk], [('x', oc, t)])
            S.barrier()

        def load_w(dst, src, kchunks, keyw, eng='pool'):
            S.dma(eng, dst[:, 0:kchunks, :], src.rearrange("(kc p) n -> p kc n", p=128), (), [keyw])

        def rmsnorm_mod(l, gname, sc_c, sh_c):
            with T("nsq", [128, 2, TT], BF16) as nsq, \
                    T("nrs", [128, TT], F32) as nrs, \
                    T("ntmp", [128, 2, TT], F32) as ntmp:
                for t in range(NTT):
                    sl = slice(t * TT, (t + 1) * TT)
                    ps, pk = PS()
                    for oc in range(8):
                        b = oc % 2
                        act(nsq[:, b, :], x[:, oc, sl], AF.Square, [('x', oc, t)], [('nsq', b)])
                        mm(ps[:], ones_bf[:], nsq[:, b, :], oc == 0, oc == 7, ['ones_bf', ('nsq', b)], [pk])
                    cp(nrs[:], ps[:], [pk], ['nrs'])
                    rsqrt_inplace(nrs[:], 'nrs', 1.0 / D)
                    for oc in range(8):
                        b = oc % 2
                        tt(ntmp[:, b, :], x[:, oc, sl], nrs[:], ALU.mult, [('x', oc, t), 'nrs'], [('ntmp', b)])
                        act(xn[:, oc, sl], ntmp[:, b, :], AF.Identity, [('ntmp', b), 'vecs'], [('xn', oc, t)],
                            bias=vecs[:, sh_c + oc:sh_c + oc + 1], scale=vecs[:, sc_c + oc:sc_c + oc + 1])
            S.barrier()

        for l in range(nlayers):
            with T("wada", [128, 8, 1024], BF16) as wada, \
                    T("scond", [128, 8], BF16) as scond, \
                    T("ctmp", [128, 8], F32) as ctmp, \
                    T("mtmp", [128, 64], F32) as mtmp:
                S.dma('sp', ctmp[:], condT[:, :], (), ['ctmp'])
                act(scond[:], ctmp[:], AF.Silu, ['ctmp'], ['scond'])
                S.dma('sp', mtmp[:, 0:48], b_adaT[l], (), ['mtmp'])
                S.dma('sp', mtmp[:, 48:56], nmgT[l], (), ['mtmp'])
                S.dma('sp', mtmp[:, 56:64], nfgT[l], (), ['mtmp'])
                psm, pkm = PS()
                for piece in range(6):
                    load_w(wada, w_ada[l][:, piece * 1024:(piece + 1) * 1024], 8, 'wada')
                    for c in range(8):
                        col = piece * 8 + c
                        for kc in range(8):
                            mm(psm[:, col:col + 1], wada[:, kc, c * 128:(c + 1) * 128], scond[:, kc:kc + 1],
                               kc == 0, kc == 7, ['wada', 'scond'], [pkm])
                tt(mod[:], psm[:, 0:48], mtmp[:, 0:48], ALU.add, [pkm, 'mtmp'], ['mod'])
                stt(vecs[:, 0:8], mod[:, 8:16], 1.0, mtmp[:, 48:56], ALU.add, ALU.mult, ['mod', 'mtmp'], ['vecs'])
                cp(vecs[:, 8:16], mod[:, 0:8], ['mod'], ['vecs'])
                stt(vecs[:, 16:24], mod[:, 32:40], 1.0, mtmp[:, 56:64], ALU.add, ALU.mult, ['mod', 'mtmp'], ['vecs'])
                cp(vecs[:, 24:32], mod[:, 24:32], ['mod'], ['vecs'])
                S.barrier()
            dump("mod%d" % l, mod[:], [128, 48], F32, 'mod')

            rmsnorm_mod(l, None, 0, 8)
            dump("xn%d" % l, xn[:], [128, 8, NT], BF16, ('xn', 0, 0))

            def epilogue(k, ysrc, ykeyfn, wb, wg, wo, gp, sg, t):
                sl = slice(t * TT, (t + 1) * TT)
                for oc in range(8):
                    pa, pka = PS()
                    for kc in range(4):
                        mm(pa[:], wb[:, kc, oc * 128:(oc + 1) * 128], ysrc(kc, t), kc == 0, kc == 3,
                           ['wb', ykeyfn(kc, t)], [pka])
                    pb, pkb = PS()
                    for kc in range(8):
                        mm(pb[:], wg[:, kc, oc * 128:(oc + 1) * 128], xn[:, kc, sl], kc == 0, kc == 7,
                           ['wg', ('xn', kc, t)], [pkb])
                    b = oc % 2
                    act(sg[:, b, :], pb[:], AF.Sigmoid, [pkb, 'vecs'], [('sg', b)],
                        bias=vecs[:, 32 + k * 8 + oc:32 + k * 8 + oc + 1])
                    tt(gp[:, oc, :], pa[:], sg[:, b, :], ALU.mult, [pka, ('sg', b)], [('gp', oc)])
                for oc2 in range(8):
                    pc, pkc = PS()
                    for kc in range(8):
                        mm(pc[:], wo[:, kc, oc2 * 128:(oc2 + 1) * 128], gp[:, kc, :], kc == 0, kc == 7,
                           ['wo', ('gp', kc)], [pkc])
                    stt(x[:, oc2, sl], pc[:], mod[:, 16 + oc2:17 + oc2], x[:, oc2, sl], ALU.mult, ALU.add,
                        [pkc, 'mod', ('x', oc2, t)], [('x', oc2, t)])

            S.dma('sp', vecs[:, 32:64], b_gateT[l], (), ['vecs'])

            def run_epi(k, ybuf):
                with T("wo", [128, 8, D], BF16) as wo, \
                        T("wb", [128, 4, D], BF16) as wb, \
                        T("wg", [128, 8, D], BF16) as wg, \
                        T("gp", [128, 8, TT], BF16) as gp, \
                        T("sg", [128, 2, TT], F32) as sg:
                    load_w(wb, w_branch[l, k], 4, 'wb')
                    load_w(wg, w_gate[l][:, k * D:(k + 1) * D], 8, 'wg')
                    load_w(wo, w_out[l], 8, 'wo')
                    for t in range(NTT if not int(os.environ.get('SKE', '0')) else 0):
                        epilogue(k, lambda kc, t_: ybuf[:, kc, t_ * TT:(t_ + 1) * TT],
                                 lambda kc, t_: ('y', kc, t_), wb, wg, wo, gp, sg, t)
                    S.barrier()

            if 'D' in BRANCHES:
              with T("ybuf", [128, 4, NT], BF16) as ybuf:
                with T("wc", [128, 8, 512], BF16) as wc, \
                        T("Tb", [128, 4, 8, 258], BF16) as Tb, \
                        T("ctm", [128, 3, TT], F32) as ctm, \
                        T("cvw", [128, 4, 3], F32) as cvw, \
                        T("sflag", [128, 1], F32) as sflag:
                    S.dma('sp', cvw[:], convwT[l], (), ['cvw'])
                    S.dma('sp', sflag[:], seqflag[:, :], (), ['sflag'])
                    for oc in range(4):
                        memset(Tb[:, oc, :, 0:1], 0.0, [('Tb', oc)])
                        memset(Tb[:, oc, :, 257:258], 0.0, [('Tb', oc)])
                    load_w(wc, w_in[l][:, OFF_CONV:OFF_CONV + 512], 8, 'wc')
                    for t in range(NTT):
                        sl = slice(t * TT, (t + 1) * TT)
                        for oc in range(4):
                            ph, pkh = PS()
                            for kc in range(8):
                                mm(ph[:], wc[:, kc, oc * 128:(oc + 1) * 128], xn[:, kc, sl], kc == 0, kc == 7,
                                   ['wc', ('xn', kc, t)], [pkh])
                            act(Tb[:, oc, 2 * t:2 * t + 2, 1:257], ph[:].rearrange("p (a b) -> p a b", a=2), AF.Copy, [pkh], [('Tb', oc)])
                    load_w(wc, w_in[l][:, OFF_CONV + 1024:OFF_CONV + 1536], 8, 'wc')
                    for t in range(NTT):
                        sl = slice(t * TT, (t + 1) * TT)
                        for oc in range(4):
                            pg, pkg = PS()
                            for kc in range(8):
                                mm(pg[:], wc[:, kc, oc * 128:(oc + 1) * 128], xn[:, kc, sl], kc == 0, kc == 7,
                                   ['wc', ('xn', kc, t)], [pkg])
                            tt(Tb[:, oc, 2 * t:2 * t + 2, 1:257], pg[:].rearrange("p (a b) -> p a b", a=2),
                               Tb[:, oc, 2 * t:2 * t + 2, 1:257], ALU.mult, [pkg, ('Tb', oc)], [('Tb', oc)])
                    for oc in range(4):
                        ts(Tb[:, oc, 1:8, 0:1], Tb[:, oc, 0:7, 256:257], sflag[:, 0:1], None, ALU.mult, None,
                           [('Tb', oc), 'sflag'], [('Tb', oc)])
                        ts(Tb[:, oc, 0:7, 257:258], Tb[:, oc, 1:8, 1:2], sflag[:, 0:1], None, ALU.mult, None,
                           [('Tb', oc), 'sflag'], [('Tb', oc)])
                    load_w(wc, w_in[l][:, OFF_CONV + 512:OFF_CONV + 1024], 8, 'wc')
                    for t in range(NTT):
                        sl = slice(t * TT, (t + 1) * TT)
                        for oc in range(4):
                            pgb, pkgb = PS()
                            for kc in range(8):
                                mm(pgb[:], wc[:, kc, oc * 128:(oc + 1) * 128], xn[:, kc, sl], kc == 0, kc == 7,
                                   ['wc', ('xn', kc, t)], [pkgb])
                            b = oc % 3
                            cv = ctm[:, b, :].rearrange("p (a b) -> p a b", a=2)
                            ts(cv, Tb[:, oc, 2 * t:2 * t + 2, 1:257], cvw[:, oc, 1:2], None, ALU.mult, None,
                               [('Tb', oc), 'cvw'], [('ctm', b)])
                            stt(cv, Tb[:, oc, 2 * t:2 * t + 2, 0:256], cvw[:, oc, 0:1], cv, ALU.mult, ALU.add,
                                [('Tb', oc), ('ctm', b), 'cvw'], [('ctm', b)])
                            stt(cv, Tb[:, oc, 2 * t:2 * t + 2, 2:258], cvw[:, oc, 2:3], cv, ALU.mult, ALU.add,
                                [('Tb', oc), ('ctm', b), 'cvw'], [('ctm', b)])
                            tt(ybuf[:, oc, sl], pgb[:], ctm[:, b, :], ALU.mult, [pkgb, ('ctm', b)], [('y', oc, t)])
                    S.barrier()
                dump("yconv%d" % l, ybuf[:], [128, 4, NT], BF16, ('y', 0, 0))
                run_epi(3, ybuf)

            if 'B' in BRANCHES:
              with T("ybuf", [128, 4, NT], BF16) as ybuf:
                with T("AB", [128, 16, 4, 256], BF16) as AB:
                    with T("wf", [128, 8, 512], BF16) as wf, \
                            T("uf", [128, 4, NT], BF16) as uf, \
                            T("ccsc_sb", [128, 256], BF16) as ccsc_sb:
                        load_w(wf, w_in[l][:, OFF_FFT:OFF_FFT + 512], 8, 'wf')
                        S.dma('sp', ccsc_sb[:], ccsc[:, :], (), ['ccsc'])
                        for t in range(NTT):
                            sl = slice(t * TT, (t + 1) * TT)
                            for grp in range(4):
                                pu, pku = PS()
                                for kc in range(8):
                                    mm(pu[:], wf[:, kc, grp * 128:(grp + 1) * 128], xn[:, kc, sl], kc == 0, kc == 7,
                                       ['wf', ('xn', kc, t)], [pku])
                                if grp % 2 == 0:
                                    act(uf[:, grp, sl], pu[:], AF.Copy, [pku], [('uf', grp, t)])
                                else:
                                    cp(uf[:, grp, sl], pu[:], [pku], [('uf', grp, t)])
                        for grp in range(4):
                            for n2 in range(8):
                                pa, pka = PS()
                                for j in range(2):
                                    n = n2 * 2 + j
                                    mm(pa[:, j * 256:(j + 1) * 256], uf[:, grp, n * 128:(n + 1) * 128], ccsc_sb[:], True, True,
                                       [('uf', grp, n // 4), 'ccsc'], [pka])
                                if n2 % 2 == 0:
                                    act(AB[:, n2 * 2:n2 * 2 + 2, grp, :], pa[:].rearrange("p (a b) -> p a b", a=2), AF.Copy, [pka], [('AB', grp)])
                                else:
                                    cp(AB[:, n2 * 2:n2 * 2 + 2, grp, :], pa[:].rearrange("p (a b) -> p a b", a=2), [pka], [('AB', grp)])
                        S.barrier()
                    with T("tabC", [128, 2, 16, 256], BF16) as tabC, T("tabS", [128, 2, 16, 256], BF16) as tabS:
                        for kt in range(8):
                            b = kt % 2
                            S.dma('sp', tabC[:, b], dftC[:, kt * 256:(kt + 1) * 256].rearrange("(nt p) k -> p nt k", p=128), (), [('tabC', b)])
                            S.dma('sp', tabS[:, b], dftS[:, kt * 256:(kt + 1) * 256].rearrange("(nt p) k -> p nt k", p=128), (), [('tabS', b)])
                            for g2 in range(2):
                                py, pky = PS()
                                for gg in range(2):
                                    grp = g2 * 2 + gg
                                    for n in range(16):
                                        mm(py[:, gg * 256:(gg + 1) * 256], AB[:, n, grp, 0:128], tabC[:, b, n, :], n == 0, False,
                                           [('AB', grp), ('tabC', b)], [pky])
                                        mm(py[:, gg * 256:(gg + 1) * 256], AB[:, n, grp, 128:256], tabS[:, b, n, :], False, n == 15,
                                           [('AB', grp), ('tabS', b)], [pky])
                                cp(ybuf[:, g2 * 2:g2 * 2 + 2, kt * 256:(kt + 1) * 256], py[:].rearrange("p (a b) -> p a b", a=2),
                                   [pky], [('y', g2 * 2, kt // 2), ('y', g2 * 2 + 1, kt // 2)])
                        S.barrier()
                dump("yfft%d" % l, ybuf[:], [128, 4, NT], BF16, ('y', 0, 0))
                run_epi(1, ybuf)
            if 'C' in BRANCHES:
              with T("ybuf", [128, 4, NT], BF16) as ybuf:
                with T("cqn", [128, 3, NT], BF16) as cqn, T("ckvT", [128, 2, NK], BF16) as ckvT, \
                        T("kpeT", [128, NK], BF16) as kpeT:
                    with T("wa", [128, 8, 384], BF16) as wa, T("wkv", [128, 8, 288], BF16) as wkv, T("ctk", [128, 4, 288], F32) as ctk, T("cqf", [128, 3, TT], F32) as cqf, \
                            T("asq", [128, 2, TT], BF16) as asq, T("ars", [128, TT], F32) as ars, \
                            T("atmp", [128, 2, TT], F32) as atmp, T("tkm", [128, 2, 288], F32) as tkm, \
                            T("junk", [128, 256], F32) as junk, T("ss", [128, 2, 8], F32) as ss, \
                            T("kvag_sb", [128, 2], F32) as kvag_sb, T("qag", [128, 3], F32) as qag:
                        load_w(wa, w_in[l][:, OFF_CQ:OFF_CQ + 384], 8, 'wa')
                        load_w(wkv, w_in[l][:, OFF_CKV:OFF_CKV + 288], 8, 'wkv')
                        memset(ss[:, 0, :], 1.0, [('ss', 0)])
                        memset(ss[:, 1, :], 1.0, [('ss', 1)])
                        S.dma('sp', kvag_sb[:], kvag[l], (), ['kvag'])
                        S.dma('sp', qag[:], qagT[l], (), ['qag'])
                        for t in range(NTT if not int(os.environ.get('SKQ','0')) else 0):
                            sl = slice(t * TT, (t + 1) * TT)
                            for c in range(3):
                                pq, pkq = PS()
                                for kc in range(8):
                                    mm(pq[:], wa[:, kc, c * 128:(c + 1) * 128], xn[:, kc, sl], kc == 0, kc == 7, ['wa', ('xn', kc, t)], [pkq])
                                cp(cqf[:, c, :], pq[:], [pkq], [('cqf', c)])
                            pss, pkss = PS()
                            for c in range(3):
                                act(asq[:, c % 2, :], cqf[:, c, :], AF.Square, [('cqf', c)], [('asq', c % 2)])
                                mm(pss[:], ones_bf[:], asq[:, c % 2, :], c == 0, c == 2, ['ones_bf', ('asq', c % 2)], [pkss])
                            cp(ars[:], pss[:], [pkss], ['ars'])
                            rsqrt_inplace(ars[:], 'ars', 1.0 / QR)
                            for c in range(3):
                                tt(atmp[:, c % 2, :], cqf[:, c, :], ars[:], ALU.mult, [('cqf', c), 'ars'], [('atmp', c % 2)])
                                ts(cqn[:, c, sl], atmp[:, c % 2, :], qag[:, c:c + 1], None, ALU.mult, None,
                                   [('atmp', c % 2), 'qag'], [('cqn', c, t)])

                        def tok2feat(b, n):
                            pst, pkst = PS()
                            tr(pst[:, 0:128], tkm[:, b, 0:128], ident[:], [('tkm', b), 'ident'], [pkst])
                            tr(pst[:, 128:256], tkm[:, b, 128:256], ident[:], [('tkm', b), 'ident'], [pkst])
                            tr(pst[0:32, 256:384], tkm[:, b, 256:288], ident[:], [('tkm', b), 'ident'], [pkst])
                            cp(ckvT[:, :, n * 128:(n + 1) * 128], pst[:, 0:256].rearrange("p (a b) -> p a b", a=2), [pkst], [('ckvT', n // 4)])
                            act(kpeT[0:32, n * 128:(n + 1) * 128], pst[0:32, 256:384], AF.Copy, [pkst], [('kpeT', n // 4), pkst])

                        for t in range(NTT):
                            sl = slice(t * TT, (t + 1) * TT)
                            for c in range(2):
                                pq, pkq = PS()
                                for kc in range(8):
                                    mm(pq[:], wkv[:, kc, c * 128:(c + 1) * 128], xn[:, kc, sl], kc == 0, kc == 7, ['wkv', ('xn', kc, t)], [pkq])
                                cp(cqf[:, c, :], pq[:], [pkq], [('cqf', c)])
                            if CUT >= 2:
                                pq, pkq = PS()
                                for kc in range(8):
                                    mm(pq[0:32, :], wkv[:, kc, 256:288], xn[:, kc, sl], kc == 0, kc == 7, ['wkv', ('xn', kc, t)], [pkq])
                                cp(cqf[0:32, 2, :], pq[0:32, :], [pkq], [('cqf', 2)])
                                act(kpeT[0:32, sl], pq[0:32, :], AF.Copy, [pkq], [('kpeT', t), pkq])
                            if CUT >= 3:
                                pss, pkss = PS()
                                for c in range(2):
                                    act(asq[:, c, :], cqf[:, c, :], AF.Square, [('cqf', c)], [('asq', c)])
                                    mm(pss[:], ones_bf[:], asq[:, c, :], c == 0, c == 1, ['ones_bf', ('asq', c)], [pkss])
                                cp(ars[:], pss[:], [pkss], ['ars'])
                                rsqrt_inplace(ars[:], 'ars', 1.0 / KVR)
                                for c in range(2):
                                    tt(atmp[:, c, :], cqf[:, c, :], ars[:], ALU.mult, [('cqf', c), 'ars'], [('atmp', c)])
                                    ts(cqf[:, c, :], atmp[:, c, :], kvag_sb[:, c:c + 1], None, ALU.mult, None,
                                       [('atmp', c), 'kvag'], [('cqf', c)])
                                    act(ckvT[:, c, sl], cqf[:, c, :], AF.Copy, [('cqf', c)], [('ckvT', t)])
                            if CUT >= 4:
                                for blk in range(4):
                                    b = blk % 2
                                    n = t * 4 + blk
                                    bs = slice(blk * 128, (blk + 1) * 128)
                                    pso, pkso = PS()
                                    tr(pso[:, 0:128], cqf[:, 0, bs], ident[:], [('cqf', 0), 'ident'], [pkso])
                                    tr(pso[:, 128:256], cqf[:, 1, bs], ident[:], [('cqf', 1), 'ident'], [pkso])
                                    tr(pso[:, 256:288], cqf[0:32, 2, bs], ident[0:32, 0:32], [('cqf', 2), 'ident'], [pkso])
                                    cp(tkm[:, b, :], pso[:, 0:288], [pkso], [('tkm', b)])
                                    if not int(os.environ.get('SKA', '0')):
                                        S.dma('sp', ckv_out[l, n * 128:(n + 1) * 128, :], tkm[:, b, 0:256], [('tkm', b)], [('ckv_out', n)])
                                        S.dma('sp', kpe_out[l, n * 128:(n + 1) * 128, :], tkm[:, b, 256:288], [('tkm', b)], [('kpe_out', n)])
                        S.barrier()
                        if not int(os.environ.get('SK2', '0')):
                            S.dma('sp', ctk[:, :, 0:256], cache_ckv[l].rearrange("(n p) c -> p n c", p=128), (), ['ctk'])
                            S.dma('sp', ctk[:, :, 256:288], cache_kpe[l].rearrange("(n p) c -> p n c", p=128), (), ['ctk'])
                            for n in range(PAST // 128):
                                pst, pkst = PS()
                                tr(pst[:, 0:128], ctk[:, n, 0:128], ident[:], ['ctk', 'ident'], [pkst])
                                tr(pst[:, 128:256], ctk[:, n, 128:256], ident[:], ['ctk', 'ident'], [pkst])
                                tr(pst[0:32, 256:384], ctk[:, n, 256:288], ident[:], ['ctk', 'ident'], [pkst])
                                nn_ = NT // 128 + n
                                cp(ckvT[:, :, nn_ * 128:(nn_ + 1) * 128], pst[:, 0:256].rearrange("p (a b) -> p a b", a=2), [pkst], [('ckvT', nn_ // 4)])
                                act(kpeT[0:32, nn_ * 128:(nn_ + 1) * 128], pst[0:32, 256:384], AF.Copy, [pkst], [('kpeT', nn_ // 4), pkst])
                        S.barrier()
                    es3 = ExitStack()
                    with es3:
                        def A(name, shape, dt=F32):
                            return es3.enter_context(T(name, shape, dt))
                        wuq_sb = A("wuq_sb", [128, 3, NH * QK], BF16)
                        wuqs_sb = A("wuqs_sb", [128, 3, NH * QK], BF16)
                        wuk_sb = A("wuk_sb", [128, 2, NH * QK], BF16)
                        wuv_sb = A("wuv_sb", [128, 2, NH * 64], BF16)
                        shift_sb = A("shift_sb", [128, 2, 96], BF16)
                        qng_sb = A("qng_sb", [128, 4])
                        rC = A("rC", [128, NT], BF16)
                        rS = A("rS", [128, NT], BF16)
                        Qh = A("Qh", [128, NT], BF16)
                        Kh = A("Kh", [128, NK], BF16)
                        Ve = A("Ve", [128, NK // 128, 128], BF16)
                        Vo = A("Vo", [128, NK // 128, 128], BF16)
                        Pb = A("Pb", [128, 3, TT], BF16)
                        qf = A("qf", [128, 2, TT])
                        hsq = A("hsq", [128, TT], BF16)
                        hrs = A("hrs", [128, TT])
                        ht = A("ht", [128, 2, TT])
                        rec = A("rec", [128, TT])
                        load_w(wuq_sb, w_uq[l], 3, 'wuq')
                        load_w(wuqs_sb, w_uq_sw[l], 3, 'wuqs')
                        load_w(wuk_sb, w_uk[l], 2, 'wuk')
                        load_w(wuv_sb, w_uv[l], 2, 'wuv')
                        S.dma('sp', shift_sb[0:32], shiftm[:, :, :], (), ['shift'])
                        S.dma('sp', qng_sb[:], qng[l], (), ['qng'])
                        S.dma('sp', rC[:], ropeC[:, :], (), ['rC'])
                        S.dma('sp', rS[:], ropeS[:, :], (), ['rS'])
                        memset(Qh[96:128, :], 0.0, ['Qind'])
                        memset(Kh[96:128, :], 0.0, ['Kind'])
                        S.dma('sp', Qh[96:104, :], qind[:, :], (), ['Qind'])
                        S.dma('sp', Kh[96:104, :], kind[:, :], (), ['Kind'])
                        memset(Ve[:, :, 64:128], 1.0, ['Ve1'])
                        memset(Vo[:, :, 0:64], 1.0, ['Vo1'])

                        def normrope(pq, pkq, pqs, pkqs, gcol, dest, dkey, sl):
                            act(qf[0:96, 0, :], pq[0:96, :], AF.Copy, [pkq], [('qf', 0)])
                            act(hsq[0:96, :], qf[0:96, 0, :], AF.Square, [('qf', 0)], ['hsq'])
                            pz, pkz = PS()
                            mm(pz[0:96, :], ones_bf[0:96, 0:96], hsq[0:96, :], True, True, ['ones_bf', 'hsq'], [pkz])
                            cp(hrs[0:96, :], pz[0:96, :], [pkz], ['hrs'])
                            rsqrt_inplace(hrs[0:96, :], 'hrs', 1.0 / QK)
                            if pqs is not None:
                                act(qf[0:96, 1, :], pqs[0:96, :], AF.Copy, [pkqs], [('qf', 1)])
                                stt(ht[0:96, 0, :], qf[0:96, 0, :], qng_sb[0:96, gcol:gcol + 1], rC[0:96, sl], ALU.mult, ALU.mult,
                                    [('qf', 0), 'qng', 'rC'], [('ht', 0)])
                                stt(ht[0:96, 1, :], qf[0:96, 1, :], qng_sb[0:96, gcol + 1:gcol + 2], rS[0:96, sl], ALU.mult, ALU.mult,
                                    [('qf', 1), 'qng', 'rS'], [('ht', 1)])
                                tt(ht[0:96, 0, :], ht[0:96, 0, :], ht[0:96, 1, :], ALU.add, [('ht', 0), ('ht', 1)], [('ht', 0)])
                                tt(dest, ht[0:96, 0, :], hrs[0:96, :], ALU.mult, [('ht', 0), 'hrs'], [dkey])
                            else:
                                stt(dest, qf[0:96, 0, :], qng_sb[0:96, gcol:gcol + 1], hrs[0:96, :], ALU.mult, ALU.mult,
                                    [('qf', 0), 'qng', 'hrs'], [dkey])

                        for h in range(NH if not ATT_SKIP else 0):
                            V = Ve if h % 2 == 0 else Vo
                            vkey = 'Ve' if h % 2 == 0 else 'Vo'
                            voff = 0 if h % 2 == 0 else 64
                            hs = slice(h * QK, (h + 1) * QK)
                            for t in range(NTT):
                                sl = slice(t * TT, (t + 1) * TT)
                                pq, pkq = PS()
                                for c in range(3):
                                    mm(pq[0:96, :], wuq_sb[:, c, hs], cqn[:, c, sl], c == 0, c == 2, ['wuq', ('cqn', c, t)], [pkq])
                                pqs, pkqs = PS()
                                for c in range(3):
                                    mm(pqs[0:96, :], wuqs_sb[:, c, hs], cqn[:, c, sl], c == 0, c == 2, ['wuqs', ('cqn', c, t)], [pkqs])
                                normrope(pq, pkq, pqs, pkqs, 0, Qh[0:96, sl], ('Qh', t), sl)
                            for kt in range(NK // TT if HSTOP >= 2 else 0):
                                sl = slice(kt * TT, (kt + 1) * TT)
                                pk_, pkk = PS()
                                for c in range(2):
                                    mm(pk_[0:96, :], wuk_sb[:, c, hs], ckvT[:, c, sl], c == 0, False, ['wuk', ('ckvT', kt)], [pkk])
                                mm(pk_[0:96, :], shift_sb[0:32, 0, :], kpeT[0:32, sl], False, True, ['shift', ('kpeT', kt)], [pkk])
                                if kt < NTT:
                                    pks, pkks = PS()
                                    mm(pks[0:96, :], shift_sb[0:32, 1, :], kpeT[0:32, sl], True, True, ['shift', ('kpeT', kt)], [pkks])
                                    normrope(pk_, pkk, pks, pkks, 2, Kh[0:96, sl], ('Kh', kt), sl)
                                else:
                                    normrope(pk_, pkk, None, None, 2, Kh[0:96, sl], ('Kh', kt), sl)
                            for n0 in (range(0, NK // 128, 8) if HSTOP >= 3 else []):
                                nn = min(8, NK // 128 - n0)
                                pv, pkv = PS()
                                for j in range(nn):
                                    n = n0 + j
                                    for c in range(2):
                                        mm(pv[:, j * 64:(j + 1) * 64], ckvT[:, c, n * 128:(n + 1) * 128], wuv_sb[:, c, h * 64:(h + 1) * 64],
                                           c == 0, c == 1, [('ckvT', n // 4), 'wuv'], [pkv])
                                cp(V[:, n0:n0 + nn, voff:voff + 64], pv[:, 0:nn * 64].rearrange("p (a b) -> p a b", a=nn), [pkv], [vkey])
                            for qt in range(NTT if HSTOP >= 4 else 0):
                                sl = slice(qt * TT, (qt + 1) * TT)
                                po, pko = PS()
                                NKT = NK // 128
                                LOOK = 3
                                stiles = {}

                                def emitS(n):
                                    ps_, pks_ = PS()
                                    if ps_ is po:
                                        ps_, pks_ = PS()
                                    mm(ps_[:], Kh[:, n * 128:(n + 1) * 128], Qh[:, sl], True, True,
                                       [('Kh', n // 4), 'Kind', ('Qh', qt), 'Qind'], [pks_])
                                    stiles[n] = (ps_, pks_)

                                for n in range(min(LOOK, NKT)):
                                    emitS(n)
                                for n in range(NKT):
                                    if n + LOOK < NKT:
                                        emitS(n + LOOK)
                                    ps_, pks_ = stiles.pop(n)
                                    pb = n % 3
                                    act(Pb[:, pb, :], ps_[:], AF.Exp, [pks_], [('Pb', pb)], scale=float(QK) ** -0.5)
                                    mm(po[:], V[:, n, :], Pb[:, pb, :], n == 0, n == NKT - 1, [vkey, vkey + '1', ('Pb', pb)], [pko])
                                if HSTOP < 5:
                                    continue
                                if h % 2 == 0:
                                    recip(rec[0:64, :], po[64:128, :], [pko], ['rec'])
                                    tt(ybuf[0:64, h // 2, sl], po[0:64, :], rec[0:64, :], ALU.mult, [pko, 'rec'], [('y', h // 2, qt)])
                                else:
                                    recip(rec[64:128, :], po[0:64, :], [pko], ['rec'])
                                    tt(ybuf[64:128, h // 2, sl], po[64:128, :], rec[64:128, :], ALU.mult, [pko, 'rec'], [('y', h // 2, qt)])
                        S.barrier()
                dump("yattn%d" % l, ybuf[:], [128, 4, NT], BF16, ('y', 0, 0))
                run_epi(2, ybuf)
            if 'A' in BRANCHES:
              with T("ybuf", [128, 4, NT], BF16) as ybuf:
                esA = ExitStack()
                with esA:
                    def AA(name, shape, dt=F32):
                        return esA.enter_context(T(name, shape, dt))
                    U = AA("U", [128, 32, 256], BF16)
                    Tz = AA("Tz", [128, 32, 128], BF16)
                    Bb = AA("Bb", [128, 2, 32, 16])
                    Cc = AA("Cc", [128, 2, 32, 16])
                    Qp = AA("Qp", [128, 2, 32, 24])
                    A1x = AA("A1x", [128, 2, 32])
                    A2x = AA("A2x", [128, 2, 32])
                    dsk = AA("dsk", [128, 32])
                    smk = AA("smk", [128, 257])
                    seltp = AA("seltp", [128, 8, 240], BF16)
                    tabA = AA("tabA", [128, 2, 2, 128], BF16)
                    tabB = AA("tabB", [128, 2, 2, 128], BF16)
                    otmp = AA("otmp", [128, 4, 256])
                    tabM = AA("tabM", [128, 2, 2, 2, 128], BF16)
                    hmk = AA("hmk", [128, 2])
                    S.dma('sp', hmk[:], halfmask[:, :], (), ['hmk'])
                    S.dma('sp', Cc[:], c2[l], (), ['Cc'])
                    S.dma('sp', dsk[:], dskT[l], (), ['dsk'])
                    S.dma('sp', smk[:], smask[:, 0, :], (), ['smk'])
                    S.dma('sp', seltp[:], seltpad[:, :, :], (), ['seltp'])
                    with T("wssm", [128, 8, 512], BF16) as wssm, T("uf", [128, 4, NT], BF16) as uf, T("selp", [128, 8, 240], BF16) as selp:
                        S.dma('sp', selp[:], selpad[:, :, :], (), ['selp'])
                        load_w(wssm, w_in[l][:, OFF_SSM:OFF_SSM + 512], 8, 'wssm')
                        for t in range(NTT):
                            sl = slice(t * TT, (t + 1) * TT)
                            for oc in range(4):
                                pu, pku = PS()
                                for kc in range(8):
                                    mm(pu[:], wssm[:, kc, oc * 128:(oc + 1) * 128], xn[:, kc, sl], kc == 0, kc == 7,
                                       ['wssm', ('xn', kc, t)], [pku])
                                if oc % 2 == 0:
                                    act(uf[:, oc, sl], pu[:], AF.Copy, [pku], [('uf', oc)])
                                else:
                                    cp(uf[:, oc, sl], pu[:], [pku], [('uf', oc)])
                        for g2 in range(16):
                            pu, pku = PS()
                            for gi in range(2):
                                g = g2 * 2 + gi
                                for r in range(8):
                                    mm(pu[:, gi * 256:(gi + 1) * 256], selp[:, g % 8, 112 - 16 * r:240 - 16 * r], uf[:, g // 8, r:NT:8],
                                       r == 0, r == 7, ['selp', ('uf', g // 8)], [pku])
                            if g2 % 2 == 0:
                                act(U[:, g2 * 2:g2 * 2 + 2, :], pu[:].rearrange("p (a b) -> p a b", a=2), AF.Copy, [pku], [('U', g2)])
                            else:
                                cp(U[:, g2 * 2:g2 * 2 + 2, :], pu[:].rearrange("p (a b) -> p a b", a=2), [pku], [('U', g2)])
                        S.barrier()
                    S.mute = ASTOP < 2
                    with T("lam", [128, 2, 32]) as lam, T("ldt", [128, 32]) as ldt, T("braw", [128, 2, 32, 16]) as braw, \
                            T("ramp", [128, 32, 24]) as ramp, T("w1", [128, 32, 24]) as w1, T("w2", [128, 32, 24]) as w2, \
                            T("w3", [128, 32, 24]) as w3, T("w4", [128, 32, 24]) as w4, T("wi", [128, 32, 24], I32) as wi, \
                            T("v1", [128, 8, 32]) as v1, T("bt", [128, 2, 32, 16]) as bt:
                        S.dma('sp', lam[:], lam2[l], (), ['lam'])
                        S.dma('sp', ldt[:], logdt2[l], (), ['ldt'])
                        S.dma('sp', braw[:], b2[l], (), ['braw'])
                        S.dma('sp', ramp[:], ramp32[:, :, :], (), ['ramp'])
                        K_ = ['tbl']
                        def bc(ap2):
                            return ap2.unsqueeze(2).broadcast_to([128, 32, 24])
                        act(ldt[:], ldt[:], AF.Exp, ['ldt'], K_)
                        tt(v1[:, 0, :], lam[:, 0, :], ldt[:], ALU.mult, ['lam'] + K_, K_)
                        tt(v1[:, 1, :], lam[:, 1, :], ldt[:], ALU.mult, ['lam'] + K_, K_)
                        tt(w1[:], bc(v1[:, 1, :]), ramp[:], ALU.mult, ['ramp'] + K_, K_)
                        tt(w2[:], bc(v1[:, 0, :]), ramp[:], ALU.mult, ['ramp'] + K_, K_)
                        act(w2[:], w2[:], AF.Exp, K_, K_)
                        ts(w1[:], w1[:], 1.0 / (2.0 * math.pi), None, ALU.mult, None, K_, K_)
                        cp(wi[:], w1[:], K_, K_)
                        cp(w3[:], wi[:], K_, K_)
                        tt(w1[:], w1[:], w3[:], ALU.subtract, K_, K_)
                        act(w3[:], w1[:], AF.Sin, K_, K_, scale=math.pi)
                        act(w4[:], w1[:], AF.Sin, K_, K_, scale=math.pi / 2.0)
                        tt(w4[:], w4[:], w4[:], ALU.mult, K_, K_)
                        ts(w4[:], w4[:], -2.0, 1.0, ALU.mult, ALU.add, K_, K_)
                        tt(w4[:], w4[:], w3[:], ALU.mult, K_, K_)
                        ts(w4[:], w4[:], 2.0, None, ALU.mult, None, K_, K_)
                        tt(w3[:], w3[:], w3[:], ALU.mult, K_, K_)
                        ts(w3[:], w3[:], -2.0, 1.0, ALU.mult, ALU.add, K_, K_)
                        tt(Qp[:, 0], w2[:], w3[:], ALU.mult, K_, K_)
                        tt(Qp[:, 1], w2[:], w4[:], ALU.mult, K_, K_)
                        for ri in range(2):
                            cp(v1[:, 2 + ri, 0:16], Qp[:, ri, 0:16, 8], K_, K_)
                            cp(v1[:, 2 + ri, 16:32], Qp[:, ri, 16:32, 1], K_, K_)
                        cp(A1x[:, 0, 0:16], Qp[:, 0, 0:16, 15], K_, K_)
                        cp(A1x[:, 0, 16:32], Qp[:, 0, 16:32, 8], K_, K_)
                        cp(A1x[:, 1, :], A1x[:, 0, :], K_, K_)
                        cp(A2x[:, 1, 0:16], Qp[:, 1, 0:16, 15], K_, K_)
                        cp(A2x[:, 1, 16:32], Qp[:, 1, 16:32, 8], K_, K_)
                        ts(A2x[:, 0, :], A2x[:, 1, :], -1.0, None, ALU.mult, None, K_, K_)
                        tt(v1[:, 4, :], lam[:, 0, :], lam[:, 0, :], ALU.mult, K_, K_)
                        tt(v1[:, 5, :], lam[:, 1, :], lam[:, 1, :], ALU.mult, K_, K_)
                        tt(v1[:, 4, :], v1[:, 4, :], v1[:, 5, :], ALU.add, K_, K_)
                        recip(v1[:, 4, :], v1[:, 4, :], K_, K_)
                        ts(v1[:, 2, :], v1[:, 2, :], -1.0, None, ALU.add, None, K_, K_)
                        tt(v1[:, 5, :], v1[:, 2, :], lam[:, 0, :], ALU.mult, K_, K_)
                        tt(v1[:, 6, :], v1[:, 3, :], lam[:, 1, :], ALU.mult, K_, K_)
                        tt(v1[:, 5, :], v1[:, 5, :], v1[:, 6, :], ALU.add, K_, K_)
                        tt(v1[:, 5, :], v1[:, 5, :], v1[:, 4, :], ALU.mult, K_, K_)
                        tt(v1[:, 6, :], v1[:, 3, :], lam[:, 0, :], ALU.mult, K_, K_)
                        tt(v1[:, 7, :], v1[:, 2, :], lam[:, 1, :], ALU.mult, K_, K_)
                        tt(v1[:, 6, :], v1[:, 6, :], v1[:, 7, :], ALU.subtract, K_, K_)
                        tt(v1[:, 6, :], v1[:, 6, :], v1[:, 4, :], ALU.mult, K_, K_)
                        def bc16(ap2):
                            return ap2.unsqueeze(2).broadcast_to([128, 32, 16])
                        tt(bt[:, 0], bc16(v1[:, 5, :]), braw[:, 0], ALU.mult, ['braw'] + K_, K_)
                        tt(bt[:, 1], bc16(v1[:, 6, :]), braw[:, 1], ALU.mult, ['braw'] + K_, K_)
                        tt(Bb[:, 0], bt[:, 0], bt[:, 1], ALU.subtract, K_, K_)
                        tt(bt[:, 0], bc16(v1[:, 5, :]), braw[:, 1], ALU.mult, ['braw'] + K_, K_)
                        tt(bt[:, 1], bc16(v1[:, 6, :]), braw[:, 0], ALU.mult, ['braw'] + K_, K_)
                        tt(Bb[:, 1], bt[:, 0], bt[:, 1], ALU.add, K_, K_)
                        S.barrier()

                    def outer2(dst, dname, pair, kind, X, xkey, neg_im):
                        def qb(ri):
                            return Qp[:, ri, pair:32:16, kind * 8:(kind + 1) * 8].unsqueeze(3).broadcast_to([128, 2, 8, 16])
                        def xb(ri):
                            return X[:, ri, pair:32:16, :].unsqueeze(2).broadcast_to([128, 2, 8, 16])
                        o = [otmp[:, i, :].rearrange("p (d a b) -> p d a b", d=2, a=8) for i in range(4)]
                        of = [otmp[:, i, :].rearrange("p (d a) -> p d a", d=2) for i in range(4)]
                        dk = [(dname, 0), (dname, 1)]
                        tt(o[0], qb(0), xb(0), ALU.mult, ['tbl', xkey], [('otmp', 0)])
                        tt(o[1], qb(1), xb(1), ALU.mult, ['tbl', xkey], [('otmp', 1)])
                        tt(dst[:, :, 0, :], of[0], of[1], ALU.subtract, [('otmp', 0), ('otmp', 1)], dk)
                        tt(o[2], qb(0), xb(1), ALU.mult, ['tbl', xkey], [('otmp', 2)])
                        tt(o[3], qb(1), xb(0), ALU.mult, ['tbl', xkey], [('otmp', 3)])
                        if neg_im:
                            stt(dst[:, :, 1, :], of[2], -1.0, of[3], ALU.mult, ALU.subtract, [('otmp', 2), ('otmp', 3)], dk)
                        else:
                            tt(dst[:, :, 1, :], of[2], of[3], ALU.add, [('otmp', 2), ('otmp', 3)], dk)

                    def maskE2():
                        for ge in range(2):
                            ts(tabM[:, ge].rearrange("p d a b -> p (d a b)"), tabB[:].rearrange("p d a b -> p (d a b)"), hmk[:, ge:ge + 1], None,
                               ALU.mult, None, [('tabB', 0), ('tabB', 1), 'hmk'], [('tabM', 0), ('tabM', 1)])

                    S.mute = ASTOP < 3
                    with T("tzt", [128, 2, 256]) as tzt, T("tzm", [128, 2, 256]) as tzm:
                        S.dma('sp', tzm[:], tzmask[:, :, 0:256], (), ['tzm'])
                        for pair in range(16):
                            pT = []
                            outer2(tabA, 'tabA', pair, 2, Bb, 'tbl', False)
                            outer2(tabB, 'tabB', pair, 1, Cc, 'Cc', True)
                            maskE2()
                            for d in range(2):
                                pt_, pkt_ = PS()
                                for ge in range(2 if A3 >= 2 else 0):
                                    mm(pt_[:, ge * 128:(ge + 1) * 128], tabA[:, d, 0, :], tabM[:, ge, d, 0, :], True, False,
                                       [('tabA', d), ('tabM', d)], [pkt_])
                                    mm(pt_[:, ge * 128:(ge + 1) * 128], tabA[:, d, 1, :], tabM[:, ge, d, 1, :], False, True,
                                       [('tabA', d), ('tabM', d)], [pkt_])
                                pT.append((pt_, pkt_))
                            if A3 < 3:
                                continue
                            tt(tzt[:, 0, :], pT[0][0][:, 0:256], tzm[:, 0, :], ALU.mult, [pT[0][1], 'tzm'], [('tzt', 0)])
                            tt(tzt[:, 1, :], pT[1][0][:, 0:256], tzm[:, 1, :], ALU.mult, [pT[1][1], 'tzm'], [('tzt', 1)])
                            tt(tzt[:, 0, :], tzt[:, 0, :], tzt[:, 1, :], ALU.add, [('tzt', 0), ('tzt', 1)], [('tzt', 0)])
                            for ge in range(2):
                                g = 2 * pair + ge
                                stt(Tz[:, g, :], ident[:], dsk[:, g:g + 1], tzt[:, 0, ge * 128:(ge + 1) * 128], ALU.mult, ALU.add,
                                    ['ident', 'dsk', ('tzt', 0)], [('Tz', g)])
                        S.barrier()

                    S.mute = ASTOP < 4
                    with T("SH", [128, 2, 32, 257], BF16) as SH, T("Zs", [128, 2, 32]) as Zs, T("Vs", [128, 2, 32]) as Vs, \
                            T("P1", [128, 2, 32]) as P1, T("P2", [128, 2, 32]) as P2, T("fin", [128, 8, 2, 2, 16]) as fin, \
                            T("Wt", [128, 2, 2, 128], BF16) as Wt, T("Ysb", [128, 8, 256], BF16) as Ysb, \
                            T("gl", [128, 2, 256]) as gl:
                        S.dma('sp', Zs[:], h0[l][:, 0:64], (), ['Zs'])
                        cp(SH[:, :, :, 0], Zs[:], ['Zs'], [('SH', 0)])
                        for pair in range(16):
                            outer2(tabA, 'tabA', pair, 0, Bb, 'tbl', False)
                            for d in range(2):
                                dp = d * 16 + pair
                                ptr_, pktr = PS()
                                ptb = ptr_[:].bitcast(BF16)
                                tr(ptb[:, 0:128], tabA[:, d, 0, :], ident_bf[:], [('tabA', d), 'ident_bf'], [pktr])
                                tr(ptb[:, 128:256], tabA[:, d, 1, :], ident_bf[:], [('tabA', d), 'ident_bf'], [pktr])
                                cp(Wt[:, d, :, :], ptb[:, 0:256].rearrange("p (a b) -> p a b", a=2), [pktr], [('Wt', d)])
                                ps_, pks_ = PS()
                                for ri in range(2):
                                    for ge in range(2):
                                        mm(ps_[ge * 64:(ge + 1) * 64, ri * 256:(ri + 1) * 256], Wt[:, d, ri, ge * 64:(ge + 1) * 64],
                                           U[:, 2 * pair + ge, :], True, True, [('Wt', d), ('U', pair)], [pks_])
                                src_ = ps_[:].rearrange("p (a b) -> p a b", a=2)
                                if d == 0:
                                    cp(SH[:, :, dp, 1:257], src_, [pks_], [('SHs', dp)])
                                else:
                                    cp(SH[:, :, dp, 256:0:-1], src_, [pks_], [('SHs', dp)])
                        S.barrier()
                        S.mute = ASTOP < 5
                        SK = ['scan']
                        for i in range(256):
                            if i == 0:
                                tt(P1[:], Zs[:], A1x[:], ALU.mult, SK, SK)
                                tt(P2[:], Zs[:, ::-1, :], A2x[:], ALU.mult, SK, SK)
                            else:
                                stt(P1[:], Vs[:], smk[:, i:i + 1], A1x[:], ALU.mult, ALU.mult, SK + ['smk'], SK)
                                stt(P2[:], Vs[:, ::-1, :], smk[:, i:i + 1], A2x[:], ALU.mult, ALU.mult, SK + ['smk'], SK)
                            tt(P1[:], P1[:], SH[:, :, :, i + 1], ALU.add, SK, SK)
                            tt(Vs[:], P1[:], P2[:], ALU.add, SK, SK)
                            if (i + 1) % 32 == 0:
                                q = i // 32
                                cp(fin[:, q, :, 0, :], Vs[:, :, 0:16], SK, ['fin'])
                                cp(fin[:, 7 - q, :, 1, :], Vs[:, :, 16:32], SK, ['fin'])
                            act(SH[:, :, :, i + 1], Vs[:], AF.Copy, SK + ['smk'], [('SHo', i)], scale=smk[:, i + 1:i + 2])
                        S.dma('sp', ssm_out[l], fin[:].rearrange("p a b c d -> p (a b c d)"), SK + ['fin'], ['ssm_out'])
                        S.barrier()
                        S.mute = ASTOP < 6
                        for oc in range(4):
                            for pl in range(4):
                                pair = oc * 4 + pl
                                py, pky = PS()
                                outer2(tabB, 'tabB', pair, 1, Cc, 'Cc', True)
                                maskE2()
                                for ge in range(2):
                                    g = 2 * pair + ge
                                    o_ = py[:, ge * 256:(ge + 1) * 256]
                                    mm(o_, Tz[:, g, :], U[:, g, :], True, False, [('Tz', g), ('U', pair)], [pky])
                                    mm(o_, tabM[:, ge, 0, 0, :], SH[:, 0, pair, 0:256], False, False, [('tabM', 0), 'scan'], [pky])
                                    mm(o_, tabM[:, ge, 0, 1, :], SH[:, 1, pair, 0:256], False, False, [('tabM', 0), 'scan'], [pky])
                                    mm(o_, tabM[:, ge, 1, 0, :], SH[:, 0, 16 + pair, 255::-1], False, False, [('tabM', 1), 'scan'], [pky])
                                    mm(o_, tabM[:, ge, 1, 1, :], SH[:, 1, 16 + pair, 255::-1], False, True, [('tabM', 1), 'scan'], [pky])
                                cp(Ysb[:, 2 * pl:2 * pl + 2, :], py[:].rearrange("p (a b) -> p a b", a=2), [pky], [('Ysb', pl)])
                            yv = ybuf[:, oc, :].rearrange("p (j r) -> p r j", r=8)
                            for r2 in range(4):
                                pz, pkz = PS()
                                for ri in range(2):
                                    r = r2 * 2 + ri
                                    for gl_ in range(8):
                                        mm(pz[:, ri * 256:(ri + 1) * 256], seltp[:, r, 112 - 16 * gl_:240 - 16 * gl_], Ysb[:, gl_, :],
                                           gl_ == 0, gl_ == 7, ['seltp', ('Ysb', gl_ // 2)], [pkz])
                                for ri in range(2):
                                    r = r2 * 2 + ri
                                    act(gl[:, 0, :], pz[:, ri * 256:(ri + 1) * 256], AF.Copy, [pkz], [('gl', 0)])
                                    tt(gl[:, 1, :], gl[:, 0, :], gl[:, 0, :], ALU.mult, [('gl', 0)], [('gl', 1)])
                                    ts(gl[:, 1, :], gl[:, 1, :], 0.044715, 1.0, ALU.mult, ALU.add, [('gl', 1)], [('gl', 1)])
                                    tt(gl[:, 1, :], gl[:, 1, :], gl[:, 0, :], ALU.mult, [('gl', 0), ('gl', 1)], [('gl', 1)])
                                    act(gl[:, 1, :], gl[:, 1, :], AF.Sigmoid, [('gl', 1)], [('gl', 1)], scale=1.5957691216057308)
                                    tt(yv[:, r, :], gl[:, 0, :], gl[:, 1, :], ALU.mult,
                                       [('gl', 0), ('gl', 1)], [('y', oc, 0), ('y', oc, 1), ('y', oc, 2), ('y', oc, 3)])
                        S.barrier()
                S.mute = False
                with T("wglu", [128, 4, BW], BF16) as wglu, T("gsg", [128, 4, TT]) as gsg:
                    load_w(wglu, w_glu[l], 4, 'wglu')
                    for t in range(NTT):
                        sl = slice(t * TT, (t + 1) * TT)
                        for oc in range(4):
                            pg, pkg = PS()
                            for kc in range(4):
                                mm(pg[:], wglu[:, kc, oc * 128:(oc + 1) * 128], ybuf[:, kc, sl], kc == 0, kc == 3,
                                   ['wglu', ('y', kc, t)], [pkg])
                            act(gsg[:, oc, :], pg[:], AF.Sigmoid, [pkg], [('gsg', oc)])
                        for oc in range(4):
                            tt(ybuf[:, oc, sl], ybuf[:, oc, sl], gsg[:, oc, :], ALU.mult, [('y', oc, t), ('gsg', oc)], [('y', oc, t)])
                    S.barrier()
                dump("yssm%d" % l, ybuf[:], [128, 4, NT], BF16, ('y', 0, 0))
                run_epi(0, ybuf)
            S.barrier()
            dump("xmid%d" % l, x[:], [128, 8, NT], F32, ('x', 0, 0))

            rmsnorm_mod(l, None, 16, 24)
            for half in range(2 if not int(os.environ.get('SKF', '0')) else 0):
                with T("hT", [128, 22, 1024], BF16) as hT, \
                        T("wfi", [128, 2, 8, 256], BF16) as wfi, \
                        T("wfo", [128, 2, 22, 128], BF16) as wfo, \
                        T("fsg", [128, 2, TT], F32) as fsg:
                    for c in range(22):
                        b = c % 2
                        S.dma('pool', wfi[:, b, :, 0:128], w_ffn_in[l][:, c * 128:(c + 1) * 128].rearrange("(kc p) n -> p kc n", p=128),
                              (), [('wfi', b)])
                        S.dma('pool', wfi[:, b, :, 128:256],
                              w_ffn_in[l][:, DFF + c * 128:DFF + (c + 1) * 128].rearrange("(kc p) n -> p kc n", p=128),
                              (), [('wfi', b)])
                        for tt_ in range(2):
                            t = half * 2 + tt_
                            sl = slice(t * TT, (t + 1) * TT)
                            pg, pkg = PS()
                            for kc in range(8):
                                mm(pg[:], wfi[:, b, kc, 0:128], xn[:, kc, sl], kc == 0, kc == 7, [('wfi', b), ('xn', kc, t)], [pkg])
                            pu, pku = PS()
                            for kc in range(8):
                                mm(pu[:], wfi[:, b, kc, 128:256], xn[:, kc, sl], kc == 0, kc == 7, [('wfi', b), ('xn', kc, t)], [pku])
                            act(fsg[:, tt_, :], pg[:], AF.Silu, [pkg], [('fsg', tt_)])
                            tt(hT[:, c, tt_ * TT:(tt_ + 1) * TT], pu[:], fsg[:, tt_, :], ALU.mult, [pku, ('fsg', tt_)], [('hT', c, tt_)])
                    for tt_ in range(2):
                        t = half * 2 + tt_
                        sl = slice(t * TT, (t + 1) * TT)
                        pcs = [PS() for _ in range(8)] if False else None
                    for oc in range(8):
                        pcs = [PS(), PS()]
                        b = oc % 2
                        S.dma('pool', wfo[:, b, :, :],
                              w_ffn_out[l][:, oc * 128:(oc + 1) * 128].rearrange("(kc p) n -> p kc n", p=128),
                              (), [('wfo', b)])
                        for kc in range(22):
                            for tt_ in range(2):
                                mm(pcs[tt_][0][:], wfo[:, b, kc, :], hT[:, kc, tt_ * TT:(tt_ + 1) * TT],
                                   kc == 0, kc == 21, [('wfo', b), ('hT', kc, tt_)], [pcs[tt_][1]])
                        for tt_ in range(2):
                            t = half * 2 + tt_
                            sl = slice(t * TT, (t + 1) * TT)
                            stt(x[:, oc, sl], pcs[tt_][0][:], mod[:, 40 + oc:41 + oc], x[:, oc, sl], ALU.mult, ALU.add,
                                [pcs[tt_][1], 'mod', ('x', oc, t)], [('x', oc, t)])
                    S.barrier()
            dump("xout%d" % l, x[:], [128, 8, NT], F32, ('x', 0, 0))

        with T("ytm", [128, 2, D], F32) as ytm:
            for n in range(NT // 128):
                b = n % 2
                for hh in range(2):
                    ps, pk = PS()
                    for q in range(4):
                        oc = hh * 4 + q
                        tr(ps[:, q * 128:(q + 1) * 128], x[:, oc, n * 128:(n + 1) * 128], ident[:],
                           [('x', oc, n // 4), 'ident'], [pk])
                    if hh == 0:
                        cp(ytm[:, b, 0:512], ps[:], [pk], [('ytm', b)])
                    else:
                        act(ytm[:, b, 512:1024], ps[:], AF.Copy, [pk], [('ytm', b)])
                S.dma('sp', y_out[n * 128:(n + 1) * 128, :], ytm[:, b, :], [('ytm', b)], ['y_out'])
        S.barrier()
    return nc, dbg_out, S


def _bf(a):
    return np.ascontiguousarray(np.asarray(a, dtype=np.float32)).astype(NPBF)


def _consts(nseq):
    Ls = NT // nseq
    t = np.arange(NT)
    seq = t // Ls
    c = {}
    C = np.ones((128, NT), np.float64)
    Sg = np.zeros((128, NT), np.float64)
    if nseq == 1:
        GRID_W = 64
        row = (t // GRID_W).astype(np.float32)
        col = (t % GRID_W).astype(np.float32)
        inv = (np.float32(10000.0) ** (-np.arange(8, dtype=np.float32) / np.float32(8))).astype(np.float32)
        ang = np.concatenate([row[:, None] * inv, col[:, None] * inv], axis=-1).astype(np.float32)
        cs, sn = np.cos(ang.astype(np.float64)), np.sin(ang.astype(np.float64))
        for i in range(16):
            C[64 + 2 * i] = cs[:, i]
            C[64 + 2 * i + 1] = cs[:, i]
            Sg[64 + 2 * i] = -sn[:, i]
            Sg[64 + 2 * i + 1] = sn[:, i]
    c['ropeC'] = _bf(C)
    c['ropeS'] = _bf(Sg)
    qi = np.zeros((8, NT), np.float32)
    qi[seq, t] = 1.0
    ki = np.zeros((8, NK), np.float32)
    if nseq > 1:
        ki[:, :NT] = -BIG
        ki[seq, t] = 0.0
        ki[:, NT:] = -BIG
    c['qind'] = _bf(qi)
    c['kind'] = _bf(ki)
    nl = (t % Ls).astype(np.float64)
    same = (seq[:, None] == seq[None, :])
    ph = 2.0 * np.pi * ((nl[:, None] * nl[None, :]) % Ls) / Ls
    c['dftC'] = _bf(np.where(same, np.cos(ph) / np.sqrt(Ls), 0.0))
    c['dftS'] = _bf(np.where(same, -np.sin(ph) / np.sqrt(Ls), 0.0))
    cc = np.arange(128, dtype=np.float64)
    ph2 = 2.0 * np.pi * ((cc[:, None] * cc[None, :]) % 128) / 128.0
    c['ccsc'] = _bf(np.concatenate([np.cos(ph2), np.sin(ph2)], axis=1) / np.sqrt(128.0))
    mL = (t % Ls != 0).astype(np.float32)
    mR = (t % Ls != Ls - 1).astype(np.float32)
    c['seqflag'] = np.full((128, 1), 1.0 if nseq == 1 else 0.0, np.float32)
    nchunk = NT // 8
    cps = nchunk // nseq
    mf = np.ones(257, np.float32)
    mb = np.ones(257, np.float32)
    for j in range(nchunk):
        if (j + 1) % cps == 0 and (j + 1) < nchunk:
            mf[j + 1] = 0.0
            mb[j + 1] = 0.0
    c['smask'] = np.ascontiguousarray(np.broadcast_to(np.stack([mf, mb])[None], (128, 2, 257))).astype(np.float32)
    r = np.arange(8, dtype=np.float32)
    kr = np.zeros((2, 3, 8), np.float32)
    kr[0, 0] = 7 - r; kr[0, 1] = r + 1; kr[0, 2] = -(1 + r)
    kr[1, 0] = r;     kr[1, 1] = 8 - r; kr[1, 2] = r - 8
    c['ramp32'] = np.ascontiguousarray(np.broadcast_to(np.repeat(kr.reshape(2, 1, 24), 16, axis=1).reshape(1, 32, 24), (128, 32, 24))).astype(np.float32)
    rr = np.arange(128) // 16
    mF = (rr[None, :] >= rr[:, None]).astype(np.float32)
    mB = (rr[None, :] <= rr[:, None]).astype(np.float32)
    c['tzmask'] = np.ascontiguousarray(np.stack([np.tile(mF, (1, 4)), np.tile(mB, (1, 4))], axis=1)).astype(np.float32)
    sp = np.zeros((128, 8, 240), np.float32)
    stp = np.zeros((128, 8, 240), np.float32)
    for gl in range(8):
        for cc_ in range(16):
            sp[gl * 16 + cc_, gl, 112 + cc_] = 1.0
    for r_ in range(8):
        for cc_ in range(16):
            stp[r_ * 16 + cc_, r_, 112 + cc_] = 1.0
    c['selpad'] = _bf(sp)
    c['seltpad'] = _bf(stp)
    sh = np.zeros((32, 2, 96), np.float32)
    for d in range(32):
        sh[d, 0, 64 + d] = 1.0
        sh[d ^ 1, 1, 64 + d] = 1.0
    c['shiftm'] = _bf(sh)
    c['identf'] = np.eye(128, dtype=np.float32)
    hm = np.zeros((128, 2), np.float32); hm[:64, 0] = 1.0; hm[64:, 1] = 1.0
    c['halfmask'] = hm
    return c


def _colT(v, n):
    return np.ascontiguousarray(np.asarray(v, np.float32).reshape(n, 128).T)


def _shared(inp):
    f = lambda a: np.ascontiguousarray(np.asarray(a, np.float32))
    s = {}
    s['w_ada'] = f(inp['w_ada'])
    s['b_adaT'] = np.stack([_colT(inp['b_ada'][l], 48) for l in range(NL)])
    s['nmgT'] = np.stack([_colT(inp['norm_mix_g'][l], 8) for l in range(NL)])
    s['nfgT'] = np.stack([_colT(inp['norm_ffn_g'][l], 8) for l in range(NL)])
    s['w_in'] = f(inp['w_in'])
    s['qagT'] = np.stack([_colT(inp['q_a_norm_g'][l], 3) for l in range(NL)])
    s['kvag'] = np.stack([_colT(inp['kv_a_norm_g'][l], 2) for l in range(NL)])
    wuq = np.asarray(inp['w_uq'], np.float32)
    s['w_uq'] = f(wuq)
    perm = np.arange(NH * QK)
    dd = perm % QK
    perm = np.where(dd >= 64, (perm // QK) * QK + 64 + ((dd - 64) ^ 1), perm)
    s['w_uq_sw'] = f(wuq[:, :, perm])
    wukv = np.asarray(inp['w_ukv'], np.float32).reshape(NL, KVR, NH, 128)
    wuk = np.zeros((NL, KVR, NH, QK), np.float32)
    wuk[..., :64] = wukv[..., :64]
    s['w_uk'] = f(wuk.reshape(NL, KVR, NH * QK))
    s['w_uv'] = f(wukv[..., 64:].reshape(NL, KVR, NH * 64))
    qg = np.asarray(inp['q_norm_g'], np.float32)
    kg = np.asarray(inp['k_norm_g'], np.float32)
    swp = np.arange(QK)
    swp = np.where(swp >= 64, 64 + ((swp - 64) ^ 1), swp)
    qng = np.zeros((NL, 128, 4), np.float32)
    qng[:, :QK, 0] = qg
    qng[:, :QK, 1] = qg[:, swp]
    qng[:, :QK, 2] = kg
    qng[:, :QK, 3] = kg[:, swp]
    s['qng'] = qng
    def gp(a):
        a = np.asarray(a, np.float32)
        rest = a.shape[4:]
        a = a.reshape((NL, 2, 16, 2, 64) + rest)
        perm = (0, 3, 4, 1, 2) + tuple(range(5, 5 + len(rest)))
        return np.ascontiguousarray(a.transpose(perm).reshape((NL, 128, 32) + rest))
    s['lam2'] = np.ascontiguousarray(np.stack([gp(inp['ssm_lam_re']), gp(inp['ssm_lam_im'])], axis=2))
    ld = np.broadcast_to(np.asarray(inp['ssm_log_dt'], np.float32)[:, :, :, None], (NL, 2, 32, 64))
    s['logdt2'] = gp(ld)
    s['b2'] = np.ascontiguousarray(np.stack([gp(inp['ssm_b_re']), gp(inp['ssm_b_im'])], axis=2))
    cr_ = np.asarray(inp['ssm_c_re'], np.float32).transpose(0, 1, 2, 4, 3)
    ci_ = np.asarray(inp['ssm_c_im'], np.float32).transpose(0, 1, 2, 4, 3)
    s['c2'] = np.ascontiguousarray(np.stack([gp(cr_), gp(ci_)], axis=2))
    dsk = np.asarray(inp['ssm_d'], np.float32).reshape(NL, 32, 16)
    s['dskT'] = f(np.broadcast_to(dsk.transpose(0, 2, 1)[:, None, :, :], (NL, 8, 16, 32)).reshape(NL, 128, 32))
    s['w_glu'] = f(inp['w_glu'])
    cw = np.asarray(inp['conv_w'], np.float32)
    s['convwT'] = f(cw.reshape(NL, 3, 4, 128).transpose(0, 3, 2, 1))
    s['w_branch'] = f(inp['w_branch'])
    s['w_gate'] = f(inp['w_gate'])
    s['b_gateT'] = np.stack([_colT(inp['b_gate'][l], 32) for l in range(NL)])
    s['w_out'] = f(inp['w_out'])
    s['w_ffn_in'] = f(inp['w_ffn_in'])
    s['w_ffn_out'] = f(inp['w_ffn_out'])
    return s


def _core_map(inp, shared, consts, core):
    m = dict(shared)
    if core < 2:
        b = core
        m.update(consts[1])
        m['xin'] = np.ascontiguousarray(np.asarray(inp['x_sample'][b], np.float32))
        m['condT'] = _colT(inp['c'][b], 8)
        m['cache_ckv'] = np.ascontiguousarray(np.asarray(inp['cache_ckv'][b], np.float32))
        m['cache_kpe'] = np.ascontiguousarray(np.asarray(inp['cache_kpe'][b], np.float32))
        st = np.asarray(inp['state_ssm'][b], np.float32)
        st = st.reshape(NL, 2, 16, 2, 64, 2)
        m['h0'] = np.ascontiguousarray(st.transpose(0, 3, 4, 5, 1, 2).reshape(NL, 128, 64))
        h0 = np.zeros((NL, 128, 128), np.float32)
        h0[:, :, :64] = m['h0']
        m['h0'] = h0
    else:
        pc = (core - 2) % 4
        m.update(consts[8])
        m['xin'] = np.ascontiguousarray(np.asarray(inp['x_prompt'][pc * 8:(pc + 1) * 8], np.float32).reshape(NT, D))
        m['condT'] = _colT(inp['c_ctx'], 8)
        m['cache_ckv'] = np.zeros((NL, PAST, KVR), np.float32)
        m['cache_kpe'] = np.zeros((NL, PAST, RD), np.float32)
        m['h0'] = np.zeros((NL, 128, 128), np.float32)
    return m


_CACHE = {}


def kernel(**inputs):
    if 'nc' not in _CACHE:
        _CACHE['nc'] = build()[0]
        _CACHE['consts'] = {1: _consts(1), 8: _consts(8)}
    nc = _CACHE['nc']
    shared = _shared(inputs)
    maps = [_core_map(inputs, shared, _CACHE['consts'], c) for c in range(8)]
    res = run_bass_kernel_spmd(nc, maps, core_ids=list(range(8)))
    R = res.results
    y_s = np.stack([R[b]['y_out'] for b in range(2)]).astype(np.float32)
    y_p = np.concatenate([R[2 + i]['y_out'].reshape(8, 256, D) for i in range(4)]).astype(np.float32)
    ckv = np.concatenate([R[2 + i]['ckv_out'].reshape(NL, 8, 256, KVR).transpose(1, 0, 2, 3) for i in range(4)])
    kpe = np.concatenate([R[2 + i]['kpe_out'].reshape(NL, 8, 256, RD).transpose(1, 0, 2, 3) for i in range(4)])
    ss = []
    for i in range(4):
        a = R[2 + i]['ssm_out'].reshape(NL, 2, 64, 8, 2, 2, 16)
        a = a.transpose(3, 0, 5, 6, 1, 2, 4).reshape(8, NL, 2, 32, 64, 2)
        ss.append(a)
    ssm = np.concatenate(ss)
    return (y_p, y_s, ckv.astype(np.float32), kpe.astype(np.float32), ssm.astype(np.float32))
```

```python
import math
import numpy as np
import ml_dtypes
import concourse.bass as bass
import concourse.mybir as mybir
from concourse.bass_utils import run_bass_kernel_spmd

F32 = mybir.dt.float32
BF16 = mybir.dt.bfloat16
I32 = mybir.dt.int32
ALU = mybir.AluOpType
AF = mybir.ActivationFunctionType
NPBF = ml_dtypes.bfloat16

D = 1024
NT = 2048
DEPTH = 4
NL = DEPTH
PAST = 512
NK = NT + PAST
BW = 512
QR = 384
KVR = 256
RD = 32
QK = 96
NH = 8
DFF = 2816
OFF_SSM, OFF_FFT, OFF_CQ, OFF_CKV, OFF_KPE, OFF_CONV = 0, 512, 1024, 1408, 1664, 1696
IN_COLS = 3232
EPS = 1e-6
BIG = 30000.0
TT = 512
NTT = NT // TT
BRANCHES = 'ABCD'
import os
ATT_SKIP = bool(int(os.environ.get('ATT_SKIP', '0')))
CUT = int(os.environ.get('CUT', '9'))
HSTOP = int(os.environ.get('HSTOP', '9'))
ASTOP = int(os.environ.get('ASTOP', '9'))
A3 = int(os.environ.get('A3', '9'))


class Sched:
    def __init__(self, nc, nds=24):
        self.nc = nc
        self.eng = {'pe': nc.tensor, 'act': nc.scalar, 'dve': nc.vector, 'pool': nc.gpsimd, 'sp': nc.sync}
        self.sem = {e: nc.alloc_semaphore(name='sem_' + e) for e in self.eng}
        self.cnt = {e: 0 for e in self.eng}
        self.dsem = [nc.alloc_semaphore(name='dsem%d' % i) for i in range(nds)]
        self.dcnt = [0] * nds
        self.dpool = {'sp': list(range(0, nds // 2)), 'pool': list(range(nds // 2, nds))}
        self.dnext = {'sp': 0, 'pool': 0}
        self.waited = {e: {} for e in self.eng}
        self.lastw = {}
        self.readers = {}
        self.nwaits = 0

    def _wait(self, e, toks):
        w = self.waited[e]
        best = {}
        for t in toks:
            if t is None:
                continue
            key = (t[0], t[1])
            if t[0] == 'c' and t[1] == e and e == 'pe':
                continue
            if w.get(key, 0) >= t[2]:
                continue
            if best.get(key, 0) < t[2]:
                best[key] = t[2]
        for key, v in best.items():
            s = self.sem[key[1]] if key[0] == 'c' else self.dsem[key[1]]
            self.eng[e].wait_ge(s, v)
            w[key] = v
            self.nwaits += 1

    def _deps(self, reads, writes):
        deps = set()
        for k in reads:
            t = self.lastw.get(k)
            if t is not None:
                deps.add(t)
        for k in writes:
            t = self.lastw.get(k)
            if t is not None:
                deps.add(t)
            for t in self.readers.get(k, {}).values():
                deps.add(t)
        return deps

    def _commit(self, tok, reads, writes):
        for k in writes:
            self.lastw[k] = tok
            self.readers[k] = {}
        for k in reads:
            r = self.readers.setdefault(k, {})
            r[(tok[0], tok[1])] = tok

    mute = False

    def op(self, e, fn, reads=(), writes=()):
        if self.mute:
            return
        self._wait(e, self._deps(reads, writes))
        inst = fn(self.eng[e])
        self.cnt[e] += 1
        inst.then_inc(self.sem[e], 1)
        self._commit(('c', e, self.cnt[e]), reads, writes)

    def dma(self, e, out, in_, reads=(), writes=(), **kw):
        if self.mute:
            return
        deps = self._deps(reads, writes)
        pl = self.dpool[e]
        i = pl[self.dnext[e]]
        self.dnext[e] = (self.dnext[e] + 1) % len(pl)
        if self.dcnt[i] > 0:
            deps.add(('d', i, self.dcnt[i] * 16))
        self._wait(e, deps)
        inst = self.eng[e].dma_start(out=out, in_=in_, **kw)
        self.dcnt[i] += 1
        inst.then_inc(self.dsem[i], 16)
        self._commit(('d', i, self.dcnt[i] * 16), reads, writes)

    def barrier(self):
        toks = [('c', e, self.cnt[e]) for e in self.eng if self.cnt[e] > 0]
        toks += [('d', i, c * 16) for i, c in enumerate(self.dcnt) if c > 0]
        for e in self.eng:
            self._wait(e, toks)
        self.lastw = {}
        self.readers = {}


def build(dbg=None, nlayers=NL):
    dbg = dbg or []
    nc = bass.Bass("TRN2", target_bir_lowering=False)
    S = Sched(nc)

    def din(name, shape, dt=F32):
        return nc.dram_tensor(name, list(shape), dt, kind="ExternalInput").ap()

    def dout(name, shape, dt=F32):
        return nc.dram_tensor(name, list(shape), dt, kind="ExternalOutput").ap()

    xin = din("xin", [NT, D])
    condT = din("condT", [128, 8])
    cache_ckv = din("cache_ckv", [NL, PAST, KVR])
    cache_kpe = din("cache_kpe", [NL, PAST, RD])
    h0 = din("h0", [NL, 128, 128])
    w_ada = din("w_ada", [NL, D, 6 * D])
    b_adaT = din("b_adaT", [NL, 128, 48])
    nmgT = din("nmgT", [NL, 128, 8])
    nfgT = din("nfgT", [NL, 128, 8])
    w_in = din("w_in", [NL, D, IN_COLS])
    qagT = din("qagT", [NL, 128, 3])
    kvag = din("kvag", [NL, 128, 2])
    w_uq = din("w_uq", [NL, QR, NH * QK])
    w_uq_sw = din("w_uq_sw", [NL, QR, NH * QK])
    w_uk = din("w_uk", [NL, KVR, NH * QK])
    w_uv = din("w_uv", [NL, KVR, NH * 64])
    qng = din("qng", [NL, 128, 4])
    dskT = din("dskT", [NL, 128, 32])
    lam2 = din("lam2", [NL, 128, 2, 32])
    logdt2 = din("logdt2", [NL, 128, 32])
    b2 = din("b2", [NL, 128, 2, 32, 16])
    c2 = din("c2", [NL, 128, 2, 32, 16])
    ramp32 = din("ramp32", [128, 32, 24])
    halfmask = din("halfmask", [128, 2])
    w_glu = din("w_glu", [NL, BW, BW])
    convwT = din("convwT", [NL, 128, 4, 3])
    w_branch = din("w_branch", [NL, 4, BW, D])
    w_gate = din("w_gate", [NL, D, 4 * D])
    b_gateT = din("b_gateT", [NL, 128, 32])
    w_out = din("w_out", [NL, D, D])
    w_ffn_in = din("w_ffn_in", [NL, D, 2 * DFF])
    w_ffn_out = din("w_ffn_out", [NL, DFF, D])
    ropeC = din("ropeC", [128, NT], BF16)
    ropeS = din("ropeS", [128, NT], BF16)
    qind = din("qind", [8, NT], BF16)
    kind = din("kind", [8, NK], BF16)
    dftC = din("dftC", [NT, NT], BF16)
    dftS = din("dftS", [NT, NT], BF16)
    ccsc = din("ccsc", [128, 256], BF16)
    seqflag = din("seqflag", [128, 1])
    smask = din("smask", [128, 2, 257])
    tzmask = din("tzmask", [128, 2, 512])
    selpad = din("selpad", [128, 8, 240], BF16)
    seltpad = din("seltpad", [128, 8, 240], BF16)
    shiftm = din("shiftm", [32, 2, 96], BF16)
    identf = din("identf", [128, 128])

    y_out = dout("y_out", [NT, D])
    ckv_out = dout("ckv_out", [NL, NT, KVR])
    kpe_out = dout("kpe_out", [NL, NT, RD])
    ssm_out = dout("ssm_out", [NL, 128, 512])
    dbg_out = {}

    from contextlib import ExitStack
    es = ExitStack()

    _uid = [0]

    def T(name, shape, dt=F32):
        _uid[0] += 1
        return nc.sbuf_tensor("%s_%d" % (name, _uid[0]), list(shape), dt)

    def sb(name, shape, dt=F32):
        return es.enter_context(T(name, list(shape), dt))

    with es:
        x = sb("x", [128, 8, NT])
        xn = sb("xn", [128, 8, NT], BF16)
        ones_bf = sb("ones_bf", [128, 128], BF16)
        ident = sb("ident", [128, 128])
        ident_bf = sb("ident_bf", [128, 128], BF16)
        mod = sb("mod", [128, 48])
        vecs = sb("vecs", [128, 64])
        epsb = sb("epsb", [128, 1])
        psum = [es.enter_context(nc.psum_tensor("ps%d" % i, [128, 512], F32)) for i in range(8)]
        pstate = {'i': 0}

        def PS():
            i = pstate['i']
            pstate['i'] = (i + 1) % 8
            return psum[i], 'ps%d' % i

        def mm(out, lhsT, rhs, start, stop, reads, writes):
            S.op('pe', lambda e: e.matmul(out, lhsT, rhs, start=start, stop=stop), reads, writes)

        def tr(out, in_, idn, reads, writes):
            S.op('pe', lambda e: e.transpose(out, in_, idn), reads, writes)

        def act(out, in_, func, reads, writes, bias=None, scale=None):
            kw = {}
            if bias is not None:
                kw['bias'] = bias
            if scale is not None:
                kw['scale'] = scale
            S.op('act', lambda e: e.activation(out=out, in_=in_, func=func, **kw), reads, writes)

        def tt(out, in0, in1, op, reads, writes, eng='dve'):
            S.op(eng, lambda e: e.tensor_tensor(out=out, in0=in0, in1=in1, op=op), reads, writes)

        def ts(out, in0, s1, s2, op0, op1, reads, writes, eng='dve'):
            if op1 is None:
                S.op(eng, lambda e: e.tensor_scalar(out=out, in0=in0, scalar1=s1, scalar2=None, op0=op0), reads, writes)
            else:
                S.op(eng, lambda e: e.tensor_scalar(out=out, in0=in0, scalar1=s1, scalar2=s2, op0=op0, op1=op1), reads, writes)

        def stt(out, in0, scalar, in1, op0, op1, reads, writes):
            S.op('dve', lambda e: e.scalar_tensor_tensor(out=out, in0=in0, scalar=scalar, in1=in1, op0=op0, op1=op1), reads, writes)

        def cp(out, in_, reads, writes, eng='dve'):
            S.op(eng, lambda e: e.tensor_copy(out=out, in_=in_), reads, writes)

        def recip(out, in_, reads, writes):
            S.op('dve', lambda e: e.reciprocal(out=out, in_=in_), reads, writes)

        def memset(t_ap, val, writes, eng='dve'):
            S.op(eng, lambda e: e.memset(t_ap, val), (), writes)

        def dump(name, ap, shape, dt, key):
            if name in dbg:
                o = dout("dbg_" + name, shape, dt)
                dbg_out[name] = o
                S.dma('sp', o, ap, reads=[key], writes=['dbg_' + name])

        def rsqrt_inplace(ap, key, scale):
            act(ap, ap, AF.Sqrt, [key, 'epsb'], [key], bias=epsb[0:ap.shape[0], 0:1], scale=scale)
            recip(ap, ap, [key], [key])

        memset(ones_bf[:], 1.0, ['ones_bf'])
        memset(epsb[:], EPS, ['epsb'])
        S.dma('sp', ident[:], identf[:, :], (), ['ident'])
        cp(ident_bf[:], ident[:], ['ident'], ['ident_bf'])

        with T("xtm", [128, 4, D], F32) as xtm:
            for t in range(NTT):
                S.dma('sp', xtm[:], xin[t * TT:(t + 1) * TT, :].rearrange("(n p) c -> p n c", p=128), (), ['xtm'])
                for oc in range(8):
                    ps, pk = PS()
                    for n in range(4):
                        tr(ps[:, n * 128:(n + 1) * 128], xtm[:, n, oc * 128:(oc + 1) * 128], ident[:], ['xtm', 'ident'], [pk])
                    if oc % 2 == 0:
                        cp(x[:, oc, t * TT:(t + 1) * TT], ps[:], [pk], [('x', oc, t)])
                    else:
                        act(x[:, oc, t * TT:(t + 1) * TT], ps[:], AF.Copy, [pk], [('x', oc, t)])
            S.barrier()

        def load_w(dst, src, kchunks, keyw, eng='pool'):
            S.dma(eng, dst[:, 0:kchunks, :], src.rearrange("(kc p) n -> p kc n", p=128), (), [keyw])

        def rmsnorm_mod(l, gname, sc_c, sh_c):
            with T("nsq", [128, 2, TT], BF16) as nsq, \
                    T("nrs", [128, TT], F32) as nrs, \
                    T("ntmp", [128, 2, TT], F32) as ntmp:
                for t in range(NTT):
                    sl = slice(t * TT, (t + 1) * TT)
                    ps, pk = PS()
                    for oc in range(8):
                        b = oc % 2
                        act(nsq[:, b, :], x[:, oc, sl], AF.Square, [('x', oc, t)], [('nsq', b)])
                        mm(ps[:], ones_bf[:], nsq[:, b, :], oc == 0, oc == 7, ['ones_bf', ('nsq', b)], [pk])
                    cp(nrs[:], ps[:], [pk], ['nrs'])
                    rsqrt_inplace(nrs[:], 'nrs', 1.0 / D)
                    for oc in range(8):
                        b = oc % 2
                        tt(ntmp[:, b, :], x[:, oc, sl], nrs[:], ALU.mult, [('x', oc, t), 'nrs'], [('ntmp', b)])
                        act(xn[:, oc, sl], ntmp[:, b, :], AF.Identity, [('ntmp', b), 'vecs'], [('xn', oc, t)],
                            bias=vecs[:, sh_c + oc:sh_c + oc + 1], scale=vecs[:, sc_c + oc:sc_c + oc + 1])
            S.barrier()

        for l in range(nlayers):
            with T("wada", [128, 8, 1024], BF16) as wada, \
                    T("scond", [128, 8], BF16) as scond, \
                    T("ctmp", [128, 8], F32) as ctmp, \
                    T("mtmp", [128, 64], F32) as mtmp:
                S.dma('sp', ctmp[:], condT[:, :], (), ['ctmp'])
                act(scond[:], ctmp[:], AF.Silu, ['ctmp'], ['scond'])
                S.dma('sp', mtmp[:, 0:48], b_adaT[l], (), ['mtmp'])
                S.dma('sp', mtmp[:, 48:56], nmgT[l], (), ['mtmp'])
                S.dma('sp', mtmp[:, 56:64], nfgT[l], (), ['mtmp'])
                psm, pkm = PS()
                for piece in range(6):
                    load_w(wada, w_ada[l][:, piece * 1024:(piece + 1) * 1024], 8, 'wada')
                    for c in range(8):
                        col = piece * 8 + c
                        for kc in range(8):
                            mm(psm[:, col:col + 1], wada[:, kc, c * 128:(c + 1) * 128], scond[:, kc:kc + 1],
                               kc == 0, kc == 7, ['wada', 'scond'], [pkm])
                tt(mod[:], psm[:, 0:48], mtmp[:, 0:48], ALU.add, [pkm, 'mtmp'], ['mod'])
                stt(vecs[:, 0:8], mod[:, 8:16], 1.0, mtmp[:, 48:56], ALU.add, ALU.mult, ['mod', 'mtmp'], ['vecs'])
                cp(vecs[:, 8:16], mod[:, 0:8], ['mod'], ['vecs'])
                stt(vecs[:, 16:24], mod[:, 32:40], 1.0, mtmp[:, 56:64], ALU.add, ALU.mult, ['mod', 'mtmp'], ['vecs'])
                cp(vecs[:, 24:32], mod[:, 24:32], ['mod'], ['vecs'])
                S.barrier()
            dump("mod%d" % l, mod[:], [128, 48], F32, 'mod')

            rmsnorm_mod(l, None, 0, 8)
            dump("xn%d" % l, xn[:], [128, 8, NT], BF16, ('xn', 0, 0))

            def epilogue(k, ysrc, ykeyfn, wb, wg, wo, gp, sg, t):
                sl = slice(t * TT, (t + 1) * TT)
                for oc in range(8):
                    pa, pka = PS()
                    for kc in range(4):
                        mm(pa[:], wb[:, kc, oc * 128:(oc + 1) * 128], ysrc(kc, t), kc == 0, kc == 3,
                           ['wb', ykeyfn(kc, t)], [pka])
                    pb, pkb = PS()
                    for kc in range(8):
                        mm(pb[:], wg[:, kc, oc * 128:(oc + 1) * 128], xn[:, kc, sl], kc == 0, kc == 7,
                           ['wg', ('xn', kc, t)], [pkb])
                    b = oc % 2
                    act(sg[:, b, :], pb[:], AF.Sigmoid, [pkb, 'vecs'], [('sg', b)],
                        bias=vecs[:, 32 + k * 8 + oc:32 + k * 8 + oc + 1])
                    tt(gp[:, oc, :], pa[:], sg[:, b, :], ALU.mult, [pka, ('sg', b)], [('gp', oc)])
                for oc2 in range(8):
                    pc, pkc = PS()
                    for kc in range(8):
                        mm(pc[:], wo[:, kc, oc2 * 128:(oc2 + 1) * 128], gp[:, kc, :], kc == 0, kc == 7,
                           ['wo', ('gp', kc)], [pkc])
                    stt(x[:, oc2, sl], pc[:], mod[:, 16 + oc2:17 + oc2], x[:, oc2, sl], ALU.mult, ALU.add,
                        [pkc, 'mod', ('x', oc2, t)], [('x', oc2, t)])

            S.dma('sp', vecs[:, 32:64], b_gateT[l], (), ['vecs'])

            def run_epi(k, ybuf):
                with T("wo", [128, 8, D], BF16) as wo, \
                        T("wb", [128, 4, D], BF16) as wb, \
                        T("wg", [128, 8, D], BF16) as wg, \
                        T("gp", [128, 8, TT], BF16) as gp, \
                        T("sg", [128, 2, TT], F32) as sg:
                    load_w(wb, w_branch[l, k], 4, 'wb')
                    load_w(wg, w_gate[l][:, k * D:(k + 1) * D], 8, 'wg')
                    load_w(wo, w_out[l], 8, 'wo')
                    for t in range(NTT if not int(os.environ.get('SKE', '0')) else 0):
                        epilogue(k, lambda kc, t_: ybuf[:, kc, t_ * TT:(t_ + 1) * TT],
                                 lambda kc, t_: ('y', kc, t_), wb, wg, wo, gp, sg, t)
                    S.barrier()

            if 'D' in BRANCHES:
              with T("ybuf", [128, 4, NT], BF16) as ybuf:
                with T("wc", [128, 8, 512], BF16) as wc, \
                        T("Tb", [128, 4, 8, 258], BF16) as Tb, \
                        T("ctm", [128, 3, TT], F32) as ctm, \
                        T("cvw", [128, 4, 3], F32) as cvw, \
                        T("sflag", [128, 1], F32) as sflag:
                    S.dma('sp', cvw[:], convwT[l], (), ['cvw'])
                    S.dma('sp', sflag[:], seqflag[:, :], (), ['sflag'])
                    for oc in range(4):
                        memset(Tb[:, oc, :, 0:1], 0.0, [('Tb', oc)])
                        memset(Tb[:, oc, :, 257:258], 0.0, [('Tb', oc)])
                    load_w(wc, w_in[l][:, OFF_CONV:OFF_CONV + 512], 8, 'wc')
                    for t in range(NTT):
                        sl = slice(t * TT, (t + 1) * TT)
                        for oc in range(4):
                            ph, pkh = PS()
                            for kc in range(8):
                                mm(ph[:], wc[:, kc, oc * 128:(oc + 1) * 128], xn[:, kc, sl], kc == 0, kc == 7,
                                   ['wc', ('xn', kc, t)], [pkh])
                            act(Tb[:, oc, 2 * t:2 * t + 2, 1:257], ph[:].rearrange("p (a b) -> p a b", a=2), AF.Copy, [pkh], [('Tb', oc)])
                    load_w(wc, w_in[l][:, OFF_CONV + 1024:OFF_CONV + 1536], 8, 'wc')
                    for t in range(NTT):
                        sl = slice(t * TT, (t + 1) * TT)
                        for oc in range(4):
                            pg, pkg = PS()
                            for kc in range(8):
                                mm(pg[:], wc[:, kc, oc * 128:(oc + 1) * 128], xn[:, kc, sl], kc == 0, kc == 7,
                                   ['wc', ('xn', kc, t)], [pkg])
                            tt(Tb[:, oc, 2 * t:2 * t + 2, 1:257], pg[:].rearrange("p (a b) -> p a b", a=2),
                               Tb[:, oc, 2 * t:2 * t + 2, 1:257], ALU.mult, [pkg, ('Tb', oc)], [('Tb', oc)])
                    for oc in range(4):
                        ts(Tb[:, oc, 1:8, 0:1], Tb[:, oc, 0:7, 256:257], sflag[:, 0:1], None, ALU.mult, None,
                           [('Tb', oc), 'sflag'], [('Tb', oc)])
                        ts(Tb[:, oc, 0:7, 257:258], Tb[:, oc, 1:8, 1:2], sflag[:, 0:1], None, ALU.mult, None,
                           [('Tb', oc), 'sflag'], [('Tb', oc)])
                    load_w(wc, w_in[l][:, OFF_CONV + 512:OFF_CONV + 1024], 8, 'wc')
                    for t in range(NTT):
                        sl = slice(t * TT, (t + 1) * TT)
                        for oc in range(4):
                            pgb, pkgb = PS()
                            for kc in range(8):
                                mm(pgb[:], wc[:, kc, oc * 128:(oc + 1) * 128], xn[:, kc, sl], kc == 0, kc == 7,
                                   ['wc', ('xn', kc, t)], [pkgb])
                            b = oc % 3
                            cv = ctm[:, b, :].rearrange("p (a b) -> p a b", a=2)
                            ts(cv, Tb[:, oc, 2 * t:2 * t + 2, 1:257], cvw[:, oc, 1:2], None, ALU.mult, None,
                               [('Tb', oc), 'cvw'], [('ctm', b)])
                            stt(cv, Tb[:, oc, 2 * t:2 * t + 2, 0:256], cvw[:, oc, 0:1], cv, ALU.mult, ALU.add,
                                [('Tb', oc), ('ctm', b), 'cvw'], [('ctm', b)])
                            stt(cv, Tb[:, oc, 2 * t:2 * t + 2, 2:258], cvw[:, oc, 2:3], cv, ALU.mult, ALU.add,
                                [('Tb', oc), ('ctm', b), 'cvw'], [('ctm', b)])
                            tt(ybuf[:, oc, sl], pgb[:], ctm[:, b, :], ALU.mult, [pkgb, ('ctm', b)], [('y', oc, t)])
                    S.barrier()
                dump("yconv%d" % l, ybuf[:], [128, 4, NT], BF16, ('y', 0, 0))
                run_epi(3, ybuf)

            if 'B' in BRANCHES:
              with T("ybuf", [128, 4, NT], BF16) as ybuf:
                with T("AB", [128, 16, 4, 256], BF16) as AB:
                    with T("wf", [128, 8, 512], BF16) as wf, \
                            T("uf", [128, 4, NT], BF16) as uf, \
                            T("ccsc_sb", [128, 256], BF16) as ccsc_sb:
                        load_w(wf, w_in[l][:, OFF_FFT:OFF_FFT + 512], 8, 'wf')
                        S.dma('sp', ccsc_sb[:], ccsc[:, :], (), ['ccsc'])
                        for t in range(NTT):
                            sl = slice(t * TT, (t + 1) * TT)
                            for grp in range(4):
                                pu, pku = PS()
                                for kc in range(8):
                                    mm(pu[:], wf[:, kc, grp * 128:(grp + 1) * 128], xn[:, kc, sl], kc == 0, kc == 7,
                                       ['wf', ('xn', kc, t)], [pku])
                                if grp % 2 == 0:
                                    act(uf[:, grp, sl], pu[:], AF.Copy, [pku], [('uf', grp, t)])
                                else:
                                    cp(uf[:, grp, sl], pu[:], [pku], [('uf', grp, t)])
                        for grp in range(4):
                            for n2 in range(8):
                                pa, pka = PS()
                                for j in range(2):
                                    n = n2 * 2 + j
                                    mm(pa[:, j * 256:(j + 1) * 256], uf[:, grp, n * 128:(n + 1) * 128], ccsc_sb[:], True, True,
                                       [('uf', grp, n // 4), 'ccsc'], [pka])
                                if n2 % 2 == 0:
                                    act(AB[:, n2 * 2:n2 * 2 + 2, grp, :], pa[:].rearrange("p (a b) -> p a b", a=2), AF.Copy, [pka], [('AB', grp)])
                                else:
                                    cp(AB[:, n2 * 2:n2 * 2 + 2, grp, :], pa[:].rearrange("p (a b) -> p a b", a=2), [pka], [('AB', grp)])
                        S.barrier()
                    with T("tabC", [128, 2, 16, 256], BF16) as tabC, T("tabS", [128, 2, 16, 256], BF16) as tabS:
                        for kt in range(8):
                            b = kt % 2
                            S.dma('sp', tabC[:, b], dftC[:, kt * 256:(kt + 1) * 256].rearrange("(nt p) k -> p nt k", p=128), (), [('tabC', b)])
                            S.dma('sp', tabS[:, b], dftS[:, kt * 256:(kt + 1) * 256].rearrange("(nt p) k -> p nt k", p=128), (), [('tabS', b)])
                            for g2 in range(2):
                                py, pky = PS()
                                for gg in range(2):
                                    grp = g2 * 2 + gg
                                    for n in range(16):
                                        mm(py[:, gg * 256:(gg + 1) * 256], AB[:, n, grp, 0:128], tabC[:, b, n, :], n == 0, False,
                                           [('AB', grp), ('tabC', b)], [pky])
                                        mm(py[:, gg * 256:(gg + 1) * 256], AB[:, n, grp, 128:256], tabS[:, b, n, :], False, n == 15,
                                           [('AB', grp), ('tabS', b)], [pky])
                                cp(ybuf[:, g2 * 2:g2 * 2 + 2, kt * 256:(kt + 1) * 256], py[:].rearrange("p (a b) -> p a b", a=2),
                                   [pky], [('y', g2 * 2, kt // 2), ('y', g2 * 2 + 1, kt // 2)])
                        S.barrier()
                dump("yfft%d" % l, ybuf[:], [128, 4, NT], BF16, ('y', 0, 0))
                run_epi(1, ybuf)
            if 'C' in BRANCHES:
              with T("ybuf", [128, 4, NT], BF16) as ybuf:
                with T("cqn", [128, 3, NT], BF16) as cqn, T("ckvT", [128, 2, NK], BF16) as ckvT, \
                        T("kpeT", [128, NK], BF16) as kpeT:
                    with T("wa", [128, 8, 384], BF16) as wa, T("wkv", [128, 8, 288], BF16) as wkv, T("ctk", [128, 4, 288], F32) as ctk, T("cqf", [128, 3, TT], F32) as cqf, \
                            T("asq", [128, 2, TT], BF16) as asq, T("ars", [128, TT], F32) as ars, \
                            T("atmp", [128, 2, TT], F32) as atmp, T("tkm", [128, 2, 288], F32) as tkm, \
                            T("junk", [128, 256], F32) as junk, T("ss", [128, 2, 8], F32) as ss, \
                            T("kvag_sb", [128, 2], F32) as kvag_sb, T("qag", [128, 3], F32) as qag:
                        load_w(wa, w_in[l][:, OFF_CQ:OFF_CQ + 384], 8, 'wa')
                        load_w(wkv, w_in[l][:, OFF_CKV:OFF_CKV + 288], 8, 'wkv')
                        memset(ss[:, 0, :], 1.0, [('ss', 0)])
                        memset(ss[:, 1, :], 1.0, [('ss', 1)])
                        S.dma('sp', kvag_sb[:], kvag[l], (), ['kvag'])
                        S.dma('sp', qag[:], qagT[l], (), ['qag'])
                        for t in range(NTT if not int(os.environ.get('SKQ','0')) else 0):
                            sl = slice(t * TT, (t + 1) * TT)
                            for c in range(3):
                                pq, pkq = PS()
                                for kc in range(8):
                                    mm(pq[:], wa[:, kc, c * 128:(c + 1) * 128], xn[:, kc, sl], kc == 0, kc == 7, ['wa', ('xn', kc, t)], [pkq])
                                cp(cqf[:, c, :], pq[:], [pkq], [('cqf', c)])
                            pss, pkss = PS()
                            for c in range(3):
                                act(asq[:, c % 2, :], cqf[:, c, :], AF.Square, [('cqf', c)], [('asq', c % 2)])
                                mm(pss[:], ones_bf[:], asq[:, c % 2, :], c == 0, c == 2, ['ones_bf', ('asq', c % 2)], [pkss])
                            cp(ars[:], pss[:], [pkss], ['ars'])
                            rsqrt_inplace(ars[:], 'ars', 1.0 / QR)
                            for c in range(3):
                                tt(atmp[:, c % 2, :], cqf[:, c, :], ars[:], ALU.mult, [('cqf', c), 'ars'], [('atmp', c % 2)])
                                ts(cqn[:, c, sl], atmp[:, c % 2, :], qag[:, c:c + 1], None, ALU.mult, None,
                                   [('atmp', c % 2), 'qag'], [('cqn', c, t)])

                        def tok2feat(b, n):
                            pst, pkst = PS()
                            tr(pst[:, 0:128], tkm[:, b, 0:128], ident[:], [('tkm', b), 'ident'], [pkst])
                            tr(pst[:, 128:256], tkm[:, b, 128:256], ident[:], [('tkm', b), 'ident'], [pkst])
                            tr(pst[0:32, 256:384], tkm[:, b, 256:288], ident[:], [('tkm', b), 'ident'], [pkst])
                            cp(ckvT[:, :, n * 128:(n + 1) * 128], pst[:, 0:256].rearrange("p (a b) -> p a b", a=2), [pkst], [('ckvT', n // 4)])
                            act(kpeT[0:32, n * 128:(n + 1) * 128], pst[0:32, 256:384], AF.Copy, [pkst], [('kpeT', n // 4), pkst])

                        for t in range(NTT):
                            sl = slice(t * TT, (t + 1) * TT)
                            for c in range(2):
                                pq, pkq = PS()
                                for kc in range(8):
                                    mm(pq[:], wkv[:, kc, c * 128:(c + 1) * 128], xn[:, kc, sl], kc == 0, kc == 7, ['wkv', ('xn', kc, t)], [pkq])
                                cp(cqf[:, c, :], pq[:], [pkq], [('cqf', c)])
                            if CUT >= 2:
                                pq, pkq = PS()
                                for kc in range(8):
                                    mm(pq[0:32, :], wkv[:, kc, 256:288], xn[:, kc, sl], kc == 0, kc == 7, ['wkv', ('xn', kc, t)], [pkq])
                                cp(cqf[0:32, 2, :], pq[0:32, :], [pkq], [('cqf', 2)])
                                act(kpeT[0:32, sl], pq[0:32, :], AF.Copy, [pkq], [('kpeT', t), pkq])
                            if CUT >= 3:
                                pss, pkss = PS()
                                for c in range(2):
                                    act(asq[:, c, :], cqf[:, c, :], AF.Square, [('cqf', c)], [('asq', c)])
                                    mm(pss[:], ones_bf[:], asq[:, c, :], c == 0, c == 1, ['ones_bf', ('asq', c)], [pkss])
                                cp(ars[:], pss[:], [pkss], ['ars'])
                                rsqrt_inplace(ars[:], 'ars', 1.0 / KVR)
                                for c in range(2):
                                    tt(atmp[:, c, :], cqf[:, c, :], ars[:], ALU.mult, [('cqf', c), 'ars'], [('atmp', c)])
                                    ts(cqf[:, c, :], atmp[:, c, :], kvag_sb[:, c:c + 1], None, ALU.mult, None,
                                       [('atmp', c), 'kvag'], [('cqf', c)])
                                    act(ckvT[:, c, sl], cqf[:, c, :], AF.Copy, [('cqf', c)], [('ckvT', t)])
                            if CUT >= 4:
                                for blk in range(4):
                                    b = blk % 2
                                    n = t * 4 + blk
                                    bs = slice(blk * 128, (blk + 1) * 128)
                                    pso, pkso = PS()
                                    tr(pso[:, 0:128], cqf[:, 0, bs], ident[:], [('cqf', 0), 'ident'], [pkso])
                                    tr(pso[:, 128:256], cqf[:, 1, bs], ident[:], [('cqf', 1), 'ident'], [pkso])
                                    tr(pso[:, 256:288], cqf[0:32, 2, bs], ident[0:32, 0:32], [('cqf', 2), 'ident'], [pkso])
                                    cp(tkm[:, b, :], pso[:, 0:288], [pkso], [('tkm', b)])
                                    if not int(os.environ.get('SKA', '0')):
                                        S.dma('sp', ckv_out[l, n * 128:(n + 1) * 128, :], tkm[:, b, 0:256], [('tkm', b)], [('ckv_out', n)])
                                        S.dma('sp', kpe_out[l, n * 128:(n + 1) * 128, :], tkm[:, b, 256:288], [('tkm', b)], [('kpe_out', n)])
                        S.barrier()
                        if not int(os.environ.get('SK2', '0')):
                            S.dma('sp', ctk[:, :, 0:256], cache_ckv[l].rearrange("(n p) c -> p n c", p=128), (), ['ctk'])
                            S.dma('sp', ctk[:, :, 256:288], cache_kpe[l].rearrange("(n p) c -> p n c", p=128), (), ['ctk'])
                            for n in range(PAST // 128):
                                pst, pkst = PS()
                                tr(pst[:, 0:128], ctk[:, n, 0:128], ident[:], ['ctk', 'ident'], [pkst])
                                tr(pst[:, 128:256], ctk[:, n, 128:256], ident[:], ['ctk', 'ident'], [pkst])
                                tr(pst[0:32, 256:384], ctk[:, n, 256:288], ident[:], ['ctk', 'ident'], [pkst])
                                nn_ = NT // 128 + n
                                cp(ckvT[:, :, nn_ * 128:(nn_ + 1) * 128], pst[:, 0:256].rearrange("p (a b) -> p a b", a=2), [pkst], [('ckvT', nn_ // 4)])
                                act(kpeT[0:32, nn_ * 128:(nn_ + 1) * 128], pst[0:32, 256:384], AF.Copy, [pkst], [('kpeT', nn_ // 4), pkst])
                        S.barrier()
                    es3 = ExitStack()
                    with es3:
                        def A(name, shape, dt=F32):
                            return es3.enter_context(T(name, shape, dt))
                        wuq_sb = A("wuq_sb", [128, 3, NH * QK], BF16)
                        wuqs_sb = A("wuqs_sb", [128, 3, NH * QK], BF16)
                        wuk_sb = A("wuk_sb", [128, 2, NH * QK], BF16)
                        wuv_sb = A("wuv_sb", [128, 2, NH * 64], BF16)
                        shift_sb = A("shift_sb", [128, 2, 96], BF16)
                        qng_sb = A("qng_sb", [128, 4])
                        rC = A("rC", [128, NT], BF16)
                        rS = A("rS", [128, NT], BF16)
                        Qh = A("Qh", [128, NT], BF16)
                        Kh = A("Kh", [128, NK], BF16)
                        Ve = A("Ve", [128, NK // 128, 128], BF16)
                        Vo = A("Vo", [128, NK // 128, 128], BF16)
                        Pb = A("Pb", [128, 3, TT], BF16)
                        qf = A("qf", [128, 2, TT])
                        hsq = A("hsq", [128, TT], BF16)
                        hrs = A("hrs", [128, TT])
                        ht = A("ht", [128, 2, TT])
                        rec = A("rec", [128, TT])
                        load_w(wuq_sb, w_uq[l], 3, 'wuq')
                        load_w(wuqs_sb, w_uq_sw[l], 3, 'wuqs')
                        load_w(wuk_sb, w_uk[l], 2, 'wuk')
                        load_w(wuv_sb, w_uv[l], 2, 'wuv')
                        S.dma('sp', shift_sb[0:32], shiftm[:, :, :], (), ['shift'])
                        S.dma('sp', qng_sb[:], qng[l], (), ['qng'])
                        S.dma('sp', rC[:], ropeC[:, :], (), ['rC'])
                        S.dma('sp', rS[:], ropeS[:, :], (), ['rS'])
                        memset(Qh[96:128, :], 0.0, ['Qind'])
                        memset(Kh[96:128, :], 0.0, ['Kind'])
                        S.dma('sp', Qh[96:104, :], qind[:, :], (), ['Qind'])
                        S.dma('sp', Kh[96:104, :], kind[:, :], (), ['Kind'])
                        memset(Ve[:, :, 64:128], 1.0, ['Ve1'])
                        memset(Vo[:, :, 0:64], 1.0, ['Vo1'])

                        def normrope(pq, pkq, pqs, pkqs, gcol, dest, dkey, sl):
                            act(qf[0:96, 0, :], pq[0:96, :], AF.Copy, [pkq], [('qf', 0)])
                            act(hsq[0:96, :], qf[0:96, 0, :], AF.Square, [('qf', 0)], ['hsq'])
                            pz, pkz = PS()
                            mm(pz[0:96, :], ones_bf[0:96, 0:96], hsq[0:96, :], True, True, ['ones_bf', 'hsq'], [pkz])
                            cp(hrs[0:96, :], pz[0:96, :], [pkz], ['hrs'])
                            rsqrt_inplace(hrs[0:96, :], 'hrs', 1.0 / QK)
                            if pqs is not None:
                                act(qf[0:96, 1, :], pqs[0:96, :], AF.Copy, [pkqs], [('qf', 1)])
                                stt(ht[0:96, 0, :], qf[0:96, 0, :], qng_sb[0:96, gcol:gcol + 1], rC[0:96, sl], ALU.mult, ALU.mult,
                                    [('qf', 0), 'qng', 'rC'], [('ht', 0)])
                                stt(ht[0:96, 1, :], qf[0:96, 1, :], qng_sb[0:96, gcol + 1:gcol + 2], rS[0:96, sl], ALU.mult, ALU.mult,
                                    [('qf', 1), 'qng', 'rS'], [('ht', 1)])
                                tt(ht[0:96, 0, :], ht[0:96, 0, :], ht[0:96, 1, :], ALU.add, [('ht', 0), ('ht', 1)], [('ht', 0)])
                                tt(dest, ht[0:96, 0, :], hrs[0:96, :], ALU.mult, [('ht', 0), 'hrs'], [dkey])
                            else:
                                stt(dest, qf[0:96, 0, :], qng_sb[0:96, gcol:gcol + 1], hrs[0:96, :], ALU.mult, ALU.mult,
                                    [('qf', 0), 'qng', 'hrs'], [dkey])

                        for h in range(NH if not ATT_SKIP else 0):
                            V = Ve if h % 2 == 0 else Vo
                            vkey = 'Ve' if h % 2 == 0 else 'Vo'
                            voff = 0 if h % 2 == 0 else 64
                            hs = slice(h * QK, (h + 1) * QK)
                            for t in range(NTT):
                                sl = slice(t * TT, (t + 1) * TT)
                                pq, pkq = PS()
                                for c in range(3):
                                    mm(pq[0:96, :], wuq_sb[:, c, hs], cqn[:, c, sl], c == 0, c == 2, ['wuq', ('cqn', c, t)], [pkq])
                                pqs, pkqs = PS()
                                for c in range(3):
                                    mm(pqs[0:96, :], wuqs_sb[:, c, hs], cqn[:, c, sl], c == 0, c == 2, ['wuqs', ('cqn', c, t)], [pkqs])
                                normrope(pq, pkq, pqs, pkqs, 0, Qh[0:96, sl], ('Qh', t), sl)
                            for kt in range(NK // TT if HSTOP >= 2 else 0):
                                sl = slice(kt * TT, (kt + 1) * TT)
                                pk_, pkk = PS()
                                for c in range(2):
                                    mm(pk_[0:96, :], wuk_sb[:, c, hs], ckvT[:, c, sl], c == 0, False, ['wuk', ('ckvT', kt)], [pkk])
                                mm(pk_[0:96, :], shift_sb[0:32, 0, :], kpeT[0:32, sl], False, True, ['shift', ('kpeT', kt)], [pkk])
                                if kt < NTT:
                                    pks, pkks = PS()
                                    mm(pks[0:96, :], shift_sb[0:32, 1, :], kpeT[0:32, sl], True, True, ['shift', ('kpeT', kt)], [pkks])
                                    normrope(pk_, pkk, pks, pkks, 2, Kh[0:96, sl], ('Kh', kt), sl)
                                else:
                                    normrope(pk_, pkk, None, None, 2, Kh[0:96, sl], ('Kh', kt), sl)
                            for n0 in (range(0, NK // 128, 8) if HSTOP >= 3 else []):
                                nn = min(8, NK // 128 - n0)
                                pv, pkv = PS()
                                for j in range(nn):
                                    n = n0 + j
                                    for c in range(2):
                                        mm(pv[:, j * 64:(j + 1) * 64], ckvT[:, c, n * 128:(n + 1) * 128], wuv_sb[:, c, h * 64:(h + 1) * 64],
                                           c == 0, c == 1, [('ckvT', n // 4), 'wuv'], [pkv])
                                cp(V[:, n0:n0 + nn, voff:voff + 64], pv[:, 0:nn * 64].rearrange("p (a b) -> p a b", a=nn), [pkv], [vkey])
                            for qt in range(NTT if HSTOP >= 4 else 0):
                                sl = slice(qt * TT, (qt + 1) * TT)
                                po, pko = PS()
                                NKT = NK // 128
                                LOOK = 3
                                stiles = {}

                                def emitS(n):
                                    ps_, pks_ = PS()
                                    if ps_ is po:
                                        ps_, pks_ = PS()
                                    mm(ps_[:], Kh[:, n * 128:(n + 1) * 128], Qh[:, sl], True, True,
                                       [('Kh', n // 4), 'Kind', ('Qh', qt), 'Qind'], [pks_])
                                    stiles[n] = (ps_, pks_)

                                for n in range(min(LOOK, NKT)):
                                    emitS(n)
                                for n in range(NKT):
                                    if n + LOOK < NKT:
                                        emitS(n + LOOK)
                                    ps_, pks_ = stiles.pop(n)
                                    pb = n % 3
                                    act(Pb[:, pb, :], ps_[:], AF.Exp, [pks_], [('Pb', pb)], scale=float(QK) ** -0.5)
                                    mm(po[:], V[:, n, :], Pb[:, pb, :], n == 0, n == NKT - 1, [vkey, vkey + '1', ('Pb', pb)], [pko])
                                if HSTOP < 5:
                                    continue
                                if h % 2 == 0:
                                    recip(rec[0:64, :], po[64:128, :], [pko], ['rec'])
                                    tt(ybuf[0:64, h // 2, sl], po[0:64, :], rec[0:64, :], ALU.mult, [pko, 'rec'], [('y', h // 2, qt)])
                                else:
                                    recip(rec[64:128, :], po[0:64, :], [pko], ['rec'])
                                    tt(ybuf[64:128, h // 2, sl], po[64:128, :], rec[64:128, :], ALU.mult, [pko, 'rec'], [('y', h // 2, qt)])
                        S.barrier()
                dump("yattn%d" % l, ybuf[:], [128, 4, NT], BF16, ('y', 0, 0))
                run_epi(2, ybuf)
            if 'A' in BRANCHES:
              with T("ybuf", [128, 4, NT], BF16) as ybuf:
                esA = ExitStack()
                with esA:
                    def AA(name, shape, dt=F32):
                        return esA.enter_context(T(name, shape, dt))
                    U = AA("U", [128, 32, 256], BF16)
                    Tz = AA("Tz", [128, 32, 128], BF16)
                    Bb = AA("Bb", [128, 2, 32, 16])
                    Cc = AA("Cc", [128, 2, 32, 16])
                    Qp = AA("Qp", [128, 2, 32, 24])
                    A1x = AA("A1x", [128, 2, 32])
                    A2x = AA("A2x", [128, 2, 32])
                    dsk = AA("dsk", [128, 32])
                    smk = AA("smk", [128, 257])
                    seltp = AA("seltp", [128, 8, 240], BF16)
                    tabA = AA("tabA", [128, 2, 2, 128], BF16)
                    tabB = AA("tabB", [128, 2, 2, 128], BF16)
                    otmp = AA("otmp", [128, 4, 128])
                    tabM = AA("tabM", [128, 2, 2, 2, 128], BF16)
                    hmk = AA("hmk", [128, 2])
                    S.dma('sp', hmk[:], halfmask[:, :], (), ['hmk'])
                    S.dma('sp', Cc[:], c2[l], (), ['Cc'])
                    S.dma('sp', dsk[:], dskT[l], (), ['dsk'])
                    S.dma('sp', smk[:], smask[:, 0, :], (), ['smk'])
                    S.dma('sp', seltp[:], seltpad[:, :, :], (), ['seltp'])
                    with T("wssm", [128, 8, 512], BF16) as wssm, T("uf", [128, 4, NT], BF16) as uf, T("selp", [128, 8, 240], BF16) as selp:
                        S.dma('sp', selp[:], selpad[:, :, :], (), ['selp'])
                        load_w(wssm, w_in[l][:, OFF_SSM:OFF_SSM + 512], 8, 'wssm')
                        for t in range(NTT):
                            sl = slice(t * TT, (t + 1) * TT)
                            for oc in range(4):
                                pu, pku = PS()
                                for kc in range(8):
                                    mm(pu[:], wssm[:, kc, oc * 128:(oc + 1) * 128], xn[:, kc, sl], kc == 0, kc == 7,
                                       ['wssm', ('xn', kc, t)], [pku])
                                if oc % 2 == 0:
                                    act(uf[:, oc, sl], pu[:], AF.Copy, [pku], [('uf', oc)])
                                else:
                                    cp(uf[:, oc, sl], pu[:], [pku], [('uf', oc)])
                        for g2 in range(16):
                            pu, pku = PS()
                            for gi in range(2):
                                g = g2 * 2 + gi
                                for r in range(8):
                                    mm(pu[:, gi * 256:(gi + 1) * 256], selp[:, g % 8, 112 - 16 * r:240 - 16 * r], uf[:, g // 8, r:NT:8],
                                       r == 0, r == 7, ['selp', ('uf', g // 8)], [pku])
                            if g2 % 2 == 0:
                                act(U[:, g2 * 2:g2 * 2 + 2, :], pu[:].rearrange("p (a b) -> p a b", a=2), AF.Copy, [pku], [('U', g2)])
                            else:
                                cp(U[:, g2 * 2:g2 * 2 + 2, :], pu[:].rearrange("p (a b) -> p a b", a=2), [pku], [('U', g2)])
                        S.barrier()
                    S.mute = ASTOP < 2
                    with T("lam", [128, 2, 32]) as lam, T("ldt", [128, 32]) as ldt, T("braw", [128, 2, 32, 16]) as braw, \
                            T("ramp", [128, 32, 24]) as ramp, T("w1", [128, 32, 24]) as w1, T("w2", [128, 32, 24]) as w2, \
                            T("w3", [128, 32, 24]) as w3, T("w4", [128, 32, 24]) as w4, T("wi", [128, 32, 24], I32) as wi, \
                            T("v1", [128, 8, 32]) as v1, T("bt", [128, 2, 32, 16]) as bt:
                        S.dma('sp', lam[:], lam2[l], (), ['lam'])
                        S.dma('sp', ldt[:], logdt2[l], (), ['ldt'])
                        S.dma('sp', braw[:], b2[l], (), ['braw'])
                        S.dma('sp', ramp[:], ramp32[:, :, :], (), ['ramp'])
                        K_ = ['tbl']
                        def bc(ap2):
                            return ap2.unsqueeze(2).broadcast_to([128, 32, 24])
                        act(ldt[:], ldt[:], AF.Exp, ['ldt'], K_)
                        tt(v1[:, 0, :], lam[:, 0, :], ldt[:], ALU.mult, ['lam'] + K_, K_)
                        tt(v1[:, 1, :], lam[:, 1, :], ldt[:], ALU.mult, ['lam'] + K_, K_)
                        tt(w1[:], bc(v1[:, 1, :]), ramp[:], ALU.mult, ['ramp'] + K_, K_)
                        tt(w2[:], bc(v1[:, 0, :]), ramp[:], ALU.mult, ['ramp'] + K_, K_)
                        act(w2[:], w2[:], AF.Exp, K_, K_)
                        ts(w1[:], w1[:], 1.0 / (2.0 * math.pi), None, ALU.mult, None, K_, K_)
                        cp(wi[:], w1[:], K_, K_)
                        cp(w3[:], wi[:], K_, K_)
                        tt(w1[:], w1[:], w3[:], ALU.subtract, K_, K_)
                        act(w3[:], w1[:], AF.Sin, K_, K_, scale=math.pi)
                        act(w4[:], w1[:], AF.Sin, K_, K_, scale=math.pi / 2.0)
                        tt(w4[:], w4[:], w4[:], ALU.mult, K_, K_)
                        ts(w4[:], w4[:], -2.0, 1.0, ALU.mult, ALU.add, K_, K_)
                        tt(w4[:], w4[:], w3[:], ALU.mult, K_, K_)
                        ts(w4[:], w4[:], 2.0, None, ALU.mult, None, K_, K_)
                        tt(w3[:], w3[:], w3[:], ALU.mult, K_, K_)
                        ts(w3[:], w3[:], -2.0, 1.0, ALU.mult, ALU.add, K_, K_)
                        tt(Qp[:, 0], w2[:], w3[:], ALU.mult, K_, K_)
                        tt(Qp[:, 1], w2[:], w4[:], ALU.mult, K_, K_)
                        for ri in range(2):
                            cp(v1[:, 2 + ri, 0:16], Qp[:, ri, 0:16, 8], K_, K_)
                            cp(v1[:, 2 + ri, 16:32], Qp[:, ri, 16:32, 1], K_, K_)
                        cp(A1x[:, 0, 0:16], Qp[:, 0, 0:16, 15], K_, K_)
                        cp(A1x[:, 0, 16:32], Qp[:, 0, 16:32, 8], K_, K_)
                        cp(A1x[:, 1, :], A1x[:, 0, :], K_, K_)
                        cp(A2x[:, 1, 0:16], Qp[:, 1, 0:16, 15], K_, K_)
                        cp(A2x[:, 1, 16:32], Qp[:, 1, 16:32, 8], K_, K_)
                        ts(A2x[:, 0, :], A2x[:, 1, :], -1.0, None, ALU.mult, None, K_, K_)
                        tt(v1[:, 4, :], lam[:, 0, :], lam[:, 0, :], ALU.mult, K_, K_)
                        tt(v1[:, 5, :], lam[:, 1, :], lam[:, 1, :], ALU.mult, K_, K_)
                        tt(v1[:, 4, :], v1[:, 4, :], v1[:, 5, :], ALU.add, K_, K_)
                        recip(v1[:, 4, :], v1[:, 4, :], K_, K_)
                        ts(v1[:, 2, :], v1[:, 2, :], -1.0, None, ALU.add, None, K_, K_)
                        tt(v1[:, 5, :], v1[:, 2, :], lam[:, 0, :], ALU.mult, K_, K_)
                        tt(v1[:, 6, :], v1[:, 3, :], lam[:, 1, :], ALU.mult, K_, K_)
                        tt(v1[:, 5, :], v1[:, 5, :], v1[:, 6, :], ALU.add, K_, K_)
                        tt(v1[:, 5, :], v1[:, 5, :], v1[:, 4, :], ALU.mult, K_, K_)
                        tt(v1[:, 6, :], v1[:, 3, :], lam[:, 0, :], ALU.mult, K_, K_)
                        tt(v1[:, 7, :], v1[:, 2, :], lam[:, 1, :], ALU.mult, K_, K_)
                        tt(v1[:, 6, :], v1[:, 6, :], v1[:, 7, :], ALU.subtract, K_, K_)
                        tt(v1[:, 6, :], v1[:, 6, :], v1[:, 4, :], ALU.mult, K_, K_)
                        def bc16(ap2):
                            return ap2.unsqueeze(2).broadcast_to([128, 32, 16])
                        tt(bt[:, 0], bc16(v1[:, 5, :]), braw[:, 0], ALU.mult, ['braw'] + K_, K_)
                        tt(bt[:, 1], bc16(v1[:, 6, :]), braw[:, 1], ALU.mult, ['braw'] + K_, K_)
                        tt(Bb[:, 0], bt[:, 0], bt[:, 1], ALU.subtract, K_, K_)
                        tt(bt[:, 0], bc16(v1[:, 5, :]), braw[:, 1], ALU.mult, ['braw'] + K_, K_)
                        tt(bt[:, 1], bc16(v1[:, 6, :]), braw[:, 0], ALU.mult, ['braw'] + K_, K_)
                        tt(Bb[:, 1], bt[:, 0], bt[:, 1], ALU.add, K_, K_)
                        S.barrier()

                    def outer(dst, dkey, dp, kind, X, xkey, neg_im):
                        q_re = Qp[:, 0, dp, kind * 8:(kind + 1) * 8].unsqueeze(2).broadcast_to([128, 8, 16])
                        q_im = Qp[:, 1, dp, kind * 8:(kind + 1) * 8].unsqueeze(2).broadcast_to([128, 8, 16])
                        x_re = X[:, 0, dp, :].unsqueeze(1).broadcast_to([128, 8, 16])
                        x_im = X[:, 1, dp, :].unsqueeze(1).broadcast_to([128, 8, 16])
                        o = [otmp[:, i, :].rearrange("p (a b) -> p a b", a=8) for i in range(4)]
                        d_re = dst[:, 0, :].rearrange("p (a b) -> p a b", a=8)
                        d_im = dst[:, 1, :].rearrange("p (a b) -> p a b", a=8)
                        tt(o[0], q_re, x_re, ALU.mult, ['tbl', xkey], [('otmp', 0)])
                        tt(o[1], q_im, x_im, ALU.mult, ['tbl', xkey], [('otmp', 1)])
                        tt(d_re, o[0], o[1], ALU.subtract, [('otmp', 0), ('otmp', 1)], [dkey])
                        tt(o[2], q_re, x_im, ALU.mult, ['tbl', xkey], [('otmp', 2)])
                        tt(o[3], q_im, x_re, ALU.mult, ['tbl', xkey], [('otmp', 3)])
                        if neg_im:
                            stt(d_im, o[2], -1.0, o[3], ALU.mult, ALU.subtract, [('otmp', 2), ('otmp', 3)], [dkey])
                        else:
                            tt(d_im, o[2], o[3], ALU.add, [('otmp', 2), ('otmp', 3)], [dkey])

                    def maskE(d):
                        for ge in range(2):
                            ts(tabM[:, ge, d].rearrange("p a b -> p (a b)"), tabB[:, d].rearrange("p a b -> p (a b)"), hmk[:, ge:ge + 1], None,
                               ALU.mult, None, [('tabB', d), 'hmk'], [('tabM', d)])

                    S.mute = ASTOP < 3
                    with T("tzt", [128, 2, 256]) as tzt, T("tzm", [128, 2, 256]) as tzm:
                        S.dma('sp', tzm[:], tzmask[:, :, 0:256], (), ['tzm'])
                        for pair in range(16):
                            pT = []
                            for d in range(2):
                                dp = d * 16 + pair
                                outer(tabA[:, d], ('tabA', d), dp, 2, Bb, 'tbl', False)
                                outer(tabB[:, d], ('tabB', d), dp, 1, Cc, 'Cc', True)
                                pt_, pkt_ = PS()
                                maskE(d)
                                for ge in range(2 if A3 >= 2 else 0):
                                    mm(pt_[:, ge * 128:(ge + 1) * 128], tabA[:, d, 0, :], tabM[:, ge, d, 0, :], True, False,
                                       [('tabA', d), ('tabM', d)], [pkt_])
                                    mm(pt_[:, ge * 128:(ge + 1) * 128], tabA[:, d, 1, :], tabM[:, ge, d, 1, :], False, True,
                                       [('tabA', d), ('tabM', d)], [pkt_])
                                pT.append((pt_, pkt_))
                            if A3 < 3:
                                continue
                            tt(tzt[:, 0, :], pT[0][0][:, 0:256], tzm[:, 0, :], ALU.mult, [pT[0][1], 'tzm'], [('tzt', 0)])
                            tt(tzt[:, 1, :], pT[1][0][:, 0:256], tzm[:, 1, :], ALU.mult, [pT[1][1], 'tzm'], [('tzt', 1)])
                            tt(tzt[:, 0, :], tzt[:, 0, :], tzt[:, 1, :], ALU.add, [('tzt', 0), ('tzt', 1)], [('tzt', 0)])
                            for ge in range(2):
                                g = 2 * pair + ge
                                stt(Tz[:, g, :], ident[:], dsk[:, g:g + 1], tzt[:, 0, ge * 128:(ge + 1) * 128], ALU.mult, ALU.add,
                                    ['ident', 'dsk', ('tzt', 0)], [('Tz', g)])
                        S.barrier()

                    S.mute = ASTOP < 4
                    with T("SH", [128, 2, 32, 257], BF16) as SH, T("Zs", [128, 2, 32]) as Zs, T("Vs", [128, 2, 32]) as Vs, \
                            T("P1", [128, 2, 32]) as P1, T("P2", [128, 2, 32]) as P2, T("fin", [128, 8, 2, 2, 16]) as fin, \
                            T("Wt", [128, 2, 2, 128], BF16) as Wt, T("Ysb", [128, 8, 256], BF16) as Ysb, \
                            T("gl", [128, 2, TT]) as gl:
                        S.dma('sp', Zs[:], h0[l][:, 0:64], (), ['Zs'])
                        cp(SH[:, :, :, 0], Zs[:], ['Zs'], [('SH', 0)])
                        for pair in range(16):
                            for d in range(2):
                                dp = d * 16 + pair
                                outer(tabA[:, d], ('tabA', d), dp, 0, Bb, 'tbl', False)
                                ptr_, pktr = PS()
                                ptb = ptr_[:].bitcast(BF16)
                                tr(ptb[:, 0:128], tabA[:, d, 0, :], ident_bf[:], [('tabA', d), 'ident_bf'], [pktr])
                                tr(ptb[:, 128:256], tabA[:, d, 1, :], ident_bf[:], [('tabA', d), 'ident_bf'], [pktr])
                                cp(Wt[:, d, :, :], ptb[:, 0:256].rearrange("p (a b) -> p a b", a=2), [pktr], [('Wt', d)])
                                ps_, pks_ = PS()
                                for ri in range(2):
                                    for ge in range(2):
                                        mm(ps_[ge * 64:(ge + 1) * 64, ri * 256:(ri + 1) * 256], Wt[:, d, ri, ge * 64:(ge + 1) * 64],
                                           U[:, 2 * pair + ge, :], True, True, [('Wt', d), ('U', pair)], [pks_])
                                src_ = ps_[:].rearrange("p (a b) -> p a b", a=2)
                                if d == 0:
                                    cp(SH[:, :, dp, 1:257], src_, [pks_], [('SHs', dp)])
                                else:
                                    cp(SH[:, :, dp, 256:0:-1], src_, [pks_], [('SHs', dp)])
                        S.barrier()
                        S.mute = ASTOP < 5
                        SK = ['scan']
                        for i in range(256):
                            if i == 0:
                                tt(P1[:], Zs[:], A1x[:], ALU.mult, ['Zs'], ['sP1'])
                                tt(P2[:], Zs[:, ::-1, :], A2x[:], ALU.mult, ['Zs'], ['sP2'])
                            else:
                                stt(P1[:], Vs[:], smk[:, i:i + 1], A1x[:], ALU.mult, ALU.mult, ['sV', 'smk'], ['sP1'])
                                stt(P2[:], Vs[:, ::-1, :], smk[:, i:i + 1], A2x[:], ALU.mult, ALU.mult, ['sV', 'smk'], ['sP2'])
                            tt(P1[:], P1[:], SH[:, :, :, i + 1], ALU.add, ['sP1'], ['sP1'])
                            tt(Vs[:], P1[:], P2[:], ALU.add, ['sP1', 'sP2'], ['sV'])
                            if (i + 1) % 32 == 0:
                                q = i // 32
                                cp(fin[:, q, :, 0, :], Vs[:, :, 0:16], ['sV'], ['fin'])
                                cp(fin[:, 7 - q, :, 1, :], Vs[:, :, 16:32], ['sV'], ['fin'])
                            act(SH[:, :, :, i + 1], Vs[:], AF.Copy, ['sV', 'smk'], [('SHo', i)], scale=smk[:, i + 1:i + 2])
                        S.dma('sp', ssm_out[l], fin[:].rearrange("p a b c d -> p (a b c d)"), ['fin'], ['ssm_out'])
                        S.barrier()
                        S.mute = ASTOP < 6
                        for oc in range(4):
                            for pl in range(4):
                                pair = oc * 4 + pl
                                py, pky = PS()
                                for d in range(2):
                                    outer(tabB[:, d], ('tabB', d), d * 16 + pair, 1, Cc, 'Cc', True)
                                    maskE(d)
                                for ge in range(2):
                                    g = 2 * pair + ge
                                    o_ = py[:, ge * 256:(ge + 1) * 256]
                                    mm(o_, Tz[:, g, :], U[:, g, :], True, False, [('Tz', g), ('U', pair)], [pky])
                                    mm(o_, tabM[:, ge, 0, 0, :], SH[:, 0, pair, 0:256], False, False, [('tabM', 0), 'scan'], [pky])
                                    mm(o_, tabM[:, ge, 0, 1, :], SH[:, 1, pair, 0:256], False, False, [('tabM', 0), 'scan'], [pky])
                                    mm(o_, tabM[:, ge, 1, 0, :], SH[:, 0, 16 + pair, 255::-1], False, False, [('tabM', 1), 'scan'], [pky])
                                    mm(o_, tabM[:, ge, 1, 1, :], SH[:, 1, 16 + pair, 255::-1], False, True, [('tabM', 1), 'scan'], [pky])
                                cp(Ysb[:, 2 * pl:2 * pl + 2, :], py[:].rearrange("p (a b) -> p a b", a=2), [pky], [('Ysb', pl)])
                            yv = ybuf[:, oc, :].rearrange("p (j r) -> p r j", r=8)
                            for r2 in range(4):
                                pz, pkz = PS()
                                for ri in range(2):
                                    r = r2 * 2 + ri
                                    for gl_ in range(8):
                                        mm(pz[:, ri * 256:(ri + 1) * 256], seltp[:, r, 112 - 16 * gl_:240 - 16 * gl_], Ysb[:, gl_, :],
                                           gl_ == 0, gl_ == 7, ['seltp', ('Ysb', gl_ // 2)], [pkz])
                                b = 0
                                act(gl[:, 2 * b, :], pz[:], AF.Copy, [pkz], [('gl', 2 * b)])
                                tt(gl[:, 2 * b + 1, :], gl[:, 2 * b, :], gl[:, 2 * b, :], ALU.mult, [('gl', 2 * b)], [('gl', 2 * b + 1)])
                                ts(gl[:, 2 * b + 1, :], gl[:, 2 * b + 1, :], 0.044715, 1.0, ALU.mult, ALU.add, [('gl', 2 * b + 1)], [('gl', 2 * b + 1)])
                                tt(gl[:, 2 * b + 1, :], gl[:, 2 * b + 1, :], gl[:, 2 * b, :], ALU.mult, [('gl', 2 * b), ('gl', 2 * b + 1)], [('gl', 2 * b + 1)])
                                act(gl[:, 2 * b + 1, :], gl[:, 2 * b + 1, :], AF.Sigmoid, [('gl', 2 * b + 1)], [('gl', 2 * b + 1)], scale=1.5957691216057308)
                                tt(yv[:, r2 * 2:r2 * 2 + 2, :], gl[:, 2 * b, :].rearrange("p (a b) -> p a b", a=2),
                                   gl[:, 2 * b + 1, :].rearrange("p (a b) -> p a b", a=2), ALU.mult,
                                   [('gl', 2 * b), ('gl', 2 * b + 1)], [('y', oc, 0), ('y', oc, 1), ('y', oc, 2), ('y', oc, 3)])
                        S.barrier()
                S.mute = False
                with T("wglu", [128, 4, BW], BF16) as wglu, T("gsg", [128, 4, TT]) as gsg:
                    load_w(wglu, w_glu[l], 4, 'wglu')
                    for t in range(NTT):
                        sl = slice(t * TT, (t + 1) * TT)
                        for oc in range(4):
                            pg, pkg = PS()
                            for kc in range(4):
                                mm(pg[:], wglu[:, kc, oc * 128:(oc + 1) * 128], ybuf[:, kc, sl], kc == 0, kc == 3,
                                   ['wglu', ('y', kc, t)], [pkg])
                            act(gsg[:, oc, :], pg[:], AF.Sigmoid, [pkg], [('gsg', oc)])
                        for oc in range(4):
                            tt(ybuf[:, oc, sl], ybuf[:, oc, sl], gsg[:, oc, :], ALU.mult, [('y', oc, t), ('gsg', oc)], [('y', oc, t)])
                    S.barrier()
                dump("yssm%d" % l, ybuf[:], [128, 4, NT], BF16, ('y', 0, 0))
                run_epi(0, ybuf)
            S.barrier()
            dump("xmid%d" % l, x[:], [128, 8, NT], F32, ('x', 0, 0))

            rmsnorm_mod(l, None, 16, 24)
            for half in range(2 if not int(os.environ.get('SKF', '0')) else 0):
                with T("hT", [128, 22, 1024], BF16) as hT, \
                        T("wfi", [128, 2, 8, 256], BF16) as wfi, \
                        T("wfo", [128, 2, 22, 128], BF16) as wfo, \
                        T("fsg", [128, 2, TT], F32) as fsg:
                    for c in range(22):
                        b = c % 2
                        S.dma('pool', wfi[:, b, :, 0:128], w_ffn_in[l][:, c * 128:(c + 1) * 128].rearrange("(kc p) n -> p kc n", p=128),
                              (), [('wfi', b)])
                        S.dma('pool', wfi[:, b, :, 128:256],
                              w_ffn_in[l][:, DFF + c * 128:DFF + (c + 1) * 128].rearrange("(kc p) n -> p kc n", p=128),
                              (), [('wfi', b)])
                        for tt_ in range(2):
                            t = half * 2 + tt_
                            sl = slice(t * TT, (t + 1) * TT)
                            pg, pkg = PS()
                            for kc in range(8):
                                mm(pg[:], wfi[:, b, kc, 0:128], xn[:, kc, sl], kc == 0, kc == 7, [('wfi', b), ('xn', kc, t)], [pkg])
                            pu, pku = PS()
                            for kc in range(8):
                                mm(pu[:], wfi[:, b, kc, 128:256], xn[:, kc, sl], kc == 0, kc == 7, [('wfi', b), ('xn', kc, t)], [pku])
                            act(fsg[:, tt_, :], pg[:], AF.Silu, [pkg], [('fsg', tt_)])
                            tt(hT[:, c, tt_ * TT:(tt_ + 1) * TT], pu[:], fsg[:, tt_, :], ALU.mult, [pku, ('fsg', tt_)], [('hT', c, tt_)])
                    for tt_ in range(2):
                        t = half * 2 + tt_
                        sl = slice(t * TT, (t + 1) * TT)
                        pcs = [PS() for _ in range(8)] if False else None
                    for oc in range(8):
                        pcs = [PS(), PS()]
                        b = oc % 2
                        S.dma('pool', wfo[:, b, :, :],
                              w_ffn_out[l][:, oc * 128:(oc + 1) * 128].rearrange("(kc p) n -> p kc n", p=128),
                              (), [('wfo', b)])
                        for kc in range(22):
                            for tt_ in range(2):
                                mm(pcs[tt_][0][:], wfo[:, b, kc, :], hT[:, kc, tt_ * TT:(tt_ + 1) * TT],
                                   kc == 0, kc == 21, [('wfo', b), ('hT', kc, tt_)], [pcs[tt_][1]])
                        for tt_ in range(2):
                            t = half * 2 + tt_
                            sl = slice(t * TT, (t + 1) * TT)
                            stt(x[:, oc, sl], pcs[tt_][0][:], mod[:, 40 + oc:41 + oc], x[:, oc, sl], ALU.mult, ALU.add,
                                [pcs[tt_][1], 'mod', ('x', oc, t)], [('x', oc, t)])
                    S.barrier()
            dump("xout%d" % l, x[:], [128, 8, NT], F32, ('x', 0, 0))

        with T("ytm", [128, 2, D], F32) as ytm:
            for n in range(NT // 128):
                b = n % 2
                for hh in range(2):
                    ps, pk = PS()
                    for q in range(4):
                        oc = hh * 4 + q
                        tr(ps[:, q * 128:(q + 1) * 128], x[:, oc, n * 128:(n + 1) * 128], ident[:],
                           [('x', oc, n // 4), 'ident'], [pk])
                    if hh == 0:
                        cp(ytm[:, b, 0:512], ps[:], [pk], [('ytm', b)])
                    else:
                        act(ytm[:, b, 512:1024], ps[:], AF.Copy, [pk], [('ytm', b)])
                S.dma('sp', y_out[n * 128:(n + 1) * 128, :], ytm[:, b, :], [('ytm', b)], ['y_out'])
        S.barrier()
    return nc, dbg_out, S


def _bf(a):
    return np.ascontiguousarray(np.asarray(a, dtype=np.float32)).astype(NPBF)


def _consts(nseq):
    Ls = NT // nseq
    t = np.arange(NT)
    seq = t // Ls
    c = {}
    C = np.ones((128, NT), np.float64)
    Sg = np.zeros((128, NT), np.float64)
    if nseq == 1:
        GRID_W = 64
        row = (t // GRID_W).astype(np.float32)
        col = (t % GRID_W).astype(np.float32)
        inv = (np.float32(10000.0) ** (-np.arange(8, dtype=np.float32) / np.float32(8))).astype(np.float32)
        ang = np.concatenate([row[:, None] * inv, col[:, None] * inv], axis=-1).astype(np.float32)
        cs, sn = np.cos(ang.astype(np.float64)), np.sin(ang.astype(np.float64))
        for i in range(16):
            C[64 + 2 * i] = cs[:, i]
            C[64 + 2 * i + 1] = cs[:, i]
            Sg[64 + 2 * i] = -sn[:, i]
            Sg[64 + 2 * i + 1] = sn[:, i]
    c['ropeC'] = _bf(C)
    c['ropeS'] = _bf(Sg)
    qi = np.zeros((8, NT), np.float32)
    qi[seq, t] = 1.0
    ki = np.zeros((8, NK), np.float32)
    if nseq > 1:
        ki[:, :NT] = -BIG
        ki[seq, t] = 0.0
        ki[:, NT:] = -BIG
    c['qind'] = _bf(qi)
    c['kind'] = _bf(ki)
    nl = (t % Ls).astype(np.float64)
    same = (seq[:, None] == seq[None, :])
    ph = 2.0 * np.pi * ((nl[:, None] * nl[None, :]) % Ls) / Ls
    c['dftC'] = _bf(np.where(same, np.cos(ph) / np.sqrt(Ls), 0.0))
    c['dftS'] = _bf(np.where(same, -np.sin(ph) / np.sqrt(Ls), 0.0))
    cc = np.arange(128, dtype=np.float64)
    ph2 = 2.0 * np.pi * ((cc[:, None] * cc[None, :]) % 128) / 128.0
    c['ccsc'] = _bf(np.concatenate([np.cos(ph2), np.sin(ph2)], axis=1) / np.sqrt(128.0))
    mL = (t % Ls != 0).astype(np.float32)
    mR = (t % Ls != Ls - 1).astype(np.float32)
    c['seqflag'] = np.full((128, 1), 1.0 if nseq == 1 else 0.0, np.float32)
    nchunk = NT // 8
    cps = nchunk // nseq
    mf = np.ones(257, np.float32)
    mb = np.ones(257, np.float32)
    for j in range(nchunk):
        if (j + 1) % cps == 0 and (j + 1) < nchunk:
            mf[j + 1] = 0.0
            mb[j + 1] = 0.0
    c['smask'] = np.ascontiguousarray(np.broadcast_to(np.stack([mf, mb])[None], (128, 2, 257))).astype(np.float32)
    r = np.arange(8, dtype=np.float32)
    kr = np.zeros((2, 3, 8), np.float32)
    kr[0, 0] = 7 - r; kr[0, 1] = r + 1; kr[0, 2] = -(1 + r)
    kr[1, 0] = r;     kr[1, 1] = 8 - r; kr[1, 2] = r - 8
    c['ramp32'] = np.ascontiguousarray(np.broadcast_to(np.repeat(kr.reshape(2, 1, 24), 16, axis=1).reshape(1, 32, 24), (128, 32, 24))).astype(np.float32)
    rr = np.arange(128) // 16
    mF = (rr[None, :] >= rr[:, None]).astype(np.float32)
    mB = (rr[None, :] <= rr[:, None]).astype(np.float32)
    c['tzmask'] = np.ascontiguousarray(np.stack([np.tile(mF, (1, 4)), np.tile(mB, (1, 4))], axis=1)).astype(np.float32)
    sp = np.zeros((128, 8, 240), np.float32)
    stp = np.zeros((128, 8, 240), np.float32)
    for gl in range(8):
        for cc_ in range(16):
            sp[gl * 16 + cc_, gl, 112 + cc_] = 1.0
    for r_ in range(8):
        for cc_ in range(16):
            stp[r_ * 16 + cc_, r_, 112 + cc_] = 1.0
    c['selpad'] = _bf(sp)
    c['seltpad'] = _bf(stp)
    sh = np.zeros((32, 2, 96), np.float32)
    for d in range(32):
        sh[d, 0, 64 + d] = 1.0
        sh[d ^ 1, 1, 64 + d] = 1.0
    c['shiftm'] = _bf(sh)
    c['identf'] = np.eye(128, dtype=np.float32)
    hm = np.zeros((128, 2), np.float32); hm[:64, 0] = 1.0; hm[64:, 1] = 1.0
    c['halfmask'] = hm
    return c


def _colT(v, n):
    return np.ascontiguousarray(np.asarray(v, np.float32).reshape(n, 128).T)


def _shared(inp):
    f = lambda a: np.ascontiguousarray(np.asarray(a, np.float32))
    s = {}
    s['w_ada'] = f(inp['w_ada'])
    s['b_adaT'] = np.stack([_colT(inp['b_ada'][l], 48) for l in range(NL)])
    s['nmgT'] = np.stack([_colT(inp['norm_mix_g'][l], 8) for l in range(NL)])
    s['nfgT'] = np.stack([_colT(inp['norm_ffn_g'][l], 8) for l in range(NL)])
    s['w_in'] = f(inp['w_in'])
    s['qagT'] = np.stack([_colT(inp['q_a_norm_g'][l], 3) for l in range(NL)])
    s['kvag'] = np.stack([_colT(inp['kv_a_norm_g'][l], 2) for l in range(NL)])
    wuq = np.asarray(inp['w_uq'], np.float32)
    s['w_uq'] = f(wuq)
    perm = np.arange(NH * QK)
    dd = perm % QK
    perm = np.where(dd >= 64, (perm // QK) * QK + 64 + ((dd - 64) ^ 1), perm)
    s['w_uq_sw'] = f(wuq[:, :, perm])
    wukv = np.asarray(inp['w_ukv'], np.float32).reshape(NL, KVR, NH, 128)
    wuk = np.zeros((NL, KVR, NH, QK), np.float32)
    wuk[..., :64] = wukv[..., :64]
    s['w_uk'] = f(wuk.reshape(NL, KVR, NH * QK))
    s['w_uv'] = f(wukv[..., 64:].reshape(NL, KVR, NH * 64))
    qg = np.asarray(inp['q_norm_g'], np.float32)
    kg = np.asarray(inp['k_norm_g'], np.float32)
    swp = np.arange(QK)
    swp = np.where(swp >= 64, 64 + ((swp - 64) ^ 1), swp)
    qng = np.zeros((NL, 128, 4), np.float32)
    qng[:, :QK, 0] = qg
    qng[:, :QK, 1] = qg[:, swp]
    qng[:, :QK, 2] = kg
    qng[:, :QK, 3] = kg[:, swp]
    s['qng'] = qng
    def gp(a):
        a = np.asarray(a, np.float32)
        rest = a.shape[4:]
        a = a.reshape((NL, 2, 16, 2, 64) + rest)
        perm = (0, 3, 4, 1, 2) + tuple(range(5, 5 + len(rest)))
        return np.ascontiguousarray(a.transpose(perm).reshape((NL, 128, 32) + rest))
    s['lam2'] = np.ascontiguousarray(np.stack([gp(inp['ssm_lam_re']), gp(inp['ssm_lam_im'])], axis=2))
    ld = np.broadcast_to(np.asarray(inp['ssm_log_dt'], np.float32)[:, :, :, None], (NL, 2, 32, 64))
    s['logdt2'] = gp(ld)
    s['b2'] = np.ascontiguousarray(np.stack([gp(inp['ssm_b_re']), gp(inp['ssm_b_im'])], axis=2))
    cr_ = np.asarray(inp['ssm_c_re'], np.float32).transpose(0, 1, 2, 4, 3)
    ci_ = np.asarray(inp['ssm_c_im'], np.float32).transpose(0, 1, 2, 4, 3)
    s['c2'] = np.ascontiguousarray(np.stack([gp(cr_), gp(ci_)], axis=2))
    dsk = np.asarray(inp['ssm_d'], np.float32).reshape(NL, 32, 16)
    s['dskT'] = f(np.broadcast_to(dsk.transpose(0, 2, 1)[:, None, :, :], (NL, 8, 16, 32)).reshape(NL, 128, 32))
    s['w_glu'] = f(inp['w_glu'])
    cw = np.asarray(inp['conv_w'], np.float32)
    s['convwT'] = f(cw.reshape(NL, 3, 4, 128).transpose(0, 3, 2, 1))
    s['w_branch'] = f(inp['w_branch'])
    s['w_gate'] = f(inp['w_gate'])
    s['b_gateT'] = np.stack([_colT(inp['b_gate'][l], 32) for l in range(NL)])
    s['w_out'] = f(inp['w_out'])
    s['w_ffn_in'] = f(inp['w_ffn_in'])
    s['w_ffn_out'] = f(inp['w_ffn_out'])
    return s


def _core_map(inp, shared, consts, core):
    m = dict(shared)
    if core < 2:
        b = core
        m.update(consts[1])
        m['xin'] = np.ascontiguousarray(np.asarray(inp['x_sample'][b], np.float32))
        m['condT'] = _colT(inp['c'][b], 8)
        m['cache_ckv'] = np.ascontiguousarray(np.asarray(inp['cache_ckv'][b], np.float32))
        m['cache_kpe'] = np.ascontiguousarray(np.asarray(inp['cache_kpe'][b], np.float32))
        st = np.asarray(inp['state_ssm'][b], np.float32)
        st = st.reshape(NL, 2, 16, 2, 64, 2)
        m['h0'] = np.ascontiguousarray(st.transpose(0, 3, 4, 5, 1, 2).reshape(NL, 128, 64))
        h0 = np.zeros((NL, 128, 128), np.float32)
        h0[:, :, :64] = m['h0']
        m['h0'] = h0
    else:
        pc = (core - 2) % 4
        m.update(consts[8])
        m['xin'] = np.ascontiguousarray(np.asarray(inp['x_prompt'][pc * 8:(pc + 1) * 8], np.float32).reshape(NT, D))
        m['condT'] = _colT(inp['c_ctx'], 8)
        m['cache_ckv'] = np.zeros((NL, PAST, KVR), np.float32)
        m['cache_kpe'] = np.zeros((NL, PAST, RD), np.float32)
        m['h0'] = np.zeros((NL, 128, 128), np.float32)
    return m


_CACHE = {}


def kernel(**inputs):
    if 'nc' not in _CACHE:
        _CACHE['nc'] = build()[0]
        _CACHE['consts'] = {1: _consts(1), 8: _consts(8)}
    nc = _CACHE['nc']
    shared = _shared(inputs)
    maps = [_core_map(inputs, shared, _CACHE['consts'], c) for c in range(8)]
    res = run_bass_kernel_spmd(nc, maps, core_ids=list(range(8)))
    R = res.results
    y_s = np.stack([R[b]['y_out'] for b in range(2)]).astype(np.float32)
    y_p = np.concatenate([R[2 + i]['y_out'].reshape(8, 256, D) for i in range(4)]).astype(np.float32)
    ckv = np.concatenate([R[2 + i]['ckv_out'].reshape(NL, 8, 256, KVR).transpose(1, 0, 2, 3) for i in range(4)])
    kpe = np.concatenate([R[2 + i]['kpe_out'].reshape(NL, 8, 256, RD).transpose(1, 0, 2, 3) for i in range(4)])
    ss = []
    for i in range(4):
        a = R[2 + i]['ssm_out'].reshape(NL, 2, 64, 8, 2, 2, 16)
        a = a.transpose(3, 0, 5, 6, 1, 2, 4).reshape(8, NL, 2, 32, 64, 2)
        ss.append(a)
    ssm = np.concatenate(ss)
    return (y_p, y_s, ckv.astype(np.float32), kpe.astype(np.float32), ssm.astype(np.float32))
```

```python
import math
import numpy as np
import ml_dtypes
import concourse.bass as bass
import concourse.mybir as mybir
from concourse.bass_utils import run_bass_kernel_spmd

F32 = mybir.dt.float32
BF16 = mybir.dt.bfloat16
I32 = mybir.dt.int32
ALU = mybir.AluOpType
AF = mybir.ActivationFunctionType
NPBF = ml_dtypes.bfloat16

D = 1024
NT = 2048
DEPTH = 4
NL = DEPTH
PAST = 512
NK = NT + PAST
BW = 512
QR = 384
KVR = 256
RD = 32
QK = 96
NH = 8
DFF = 2816
OFF_SSM, OFF_FFT, OFF_CQ, OFF_CKV, OFF_KPE, OFF_CONV = 0, 512, 1024, 1408, 1664, 1696
IN_COLS = 3232
EPS = 1e-6
BIG = 30000.0
TT = 512
NTT = NT // TT
BRANCHES = 'ABCD'
import os
ATT_SKIP = bool(int(os.environ.get('ATT_SKIP', '0')))
CUT = int(os.environ.get('CUT', '9'))
HSTOP = int(os.environ.get('HSTOP', '9'))
ASTOP = int(os.environ.get('ASTOP', '9'))
A3 = int(os.environ.get('A3', '9'))


class Sched:
    def __init__(self, nc, nds=24):
        self.nc = nc
        self.eng = {'pe': nc.tensor, 'act': nc.scalar, 'dve': nc.vector, 'pool': nc.gpsimd, 'sp': nc.sync}
        self.sem = {e: nc.alloc_semaphore(name='sem_' + e) for e in self.eng}
        self.cnt = {e: 0 for e in self.eng}
        self.dsem = [nc.alloc_semaphore(name='dsem%d' % i) for i in range(nds)]
        self.dcnt = [0] * nds
        self.dpool = {'sp': list(range(0, nds // 2)), 'pool': list(range(nds // 2, nds))}
        self.dnext = {'sp': 0, 'pool': 0}
        self.waited = {e: {} for e in self.eng}
        self.lastw = {}
        self.readers = {}
        self.nwaits = 0

    def _wait(self, e, toks):
        w = self.waited[e]
        best = {}
        for t in toks:
            if t is None:
                continue
            key = (t[0], t[1])
            if t[0] == 'c' and t[1] == e and e == 'pe':
                continue
            if w.get(key, 0) >= t[2]:
                continue
            if best.get(key, 0) < t[2]:
                best[key] = t[2]
        for key, v in best.items():
            s = self.sem[key[1]] if key[0] == 'c' else self.dsem[key[1]]
            self.eng[e].wait_ge(s, v)
            w[key] = v
            self.nwaits += 1

    def _deps(self, reads, writes):
        deps = set()
        for k in reads:
            t = self.lastw.get(k)
            if t is not None:
                deps.add(t)
        for k in writes:
            t = self.lastw.get(k)
            if t is not None:
                deps.add(t)
            for t in self.readers.get(k, {}).values():
                deps.add(t)
        return deps

    def _commit(self, tok, reads, writes):
        for k in writes:
            self.lastw[k] = tok
            self.readers[k] = {}
        for k in reads:
            r = self.readers.setdefault(k, {})
            r[(tok[0], tok[1])] = tok

    mute = False

    def op(self, e, fn, reads=(), writes=()):
        if self.mute:
            return
        self._wait(e, self._deps(reads, writes))
        inst = fn(self.eng[e])
        self.cnt[e] += 1
        inst.then_inc(self.sem[e], 1)
        self._commit(('c', e, self.cnt[e]), reads, writes)

    def dma(self, e, out, in_, reads=(), writes=(), **kw):
        if self.mute:
            return
        deps = self._deps(reads, writes)
        pl = self.dpool[e]
        i = pl[self.dnext[e]]
        self.dnext[e] = (self.dnext[e] + 1) % len(pl)
        if self.dcnt[i] > 0:
            deps.add(('d', i, self.dcnt[i] * 16))
        self._wait(e, deps)
        inst = self.eng[e].dma_start(out=out, in_=in_, **kw)
        self.dcnt[i] += 1
        inst.then_inc(self.dsem[i], 16)
        self._commit(('d', i, self.dcnt[i] * 16), reads, writes)

    def barrier(self):
        toks = [('c', e, self.cnt[e]) for e in self.eng if self.cnt[e] > 0]
        toks += [('d', i, c * 16) for i, c in enumerate(self.dcnt) if c > 0]
        for e in self.eng:
            self._wait(e, toks)
        self.lastw = {}
        self.readers = {}


def build(dbg=None, nlayers=NL):
    dbg = dbg or []
    nc = bass.Bass("TRN2", target_bir_lowering=False)
    S = Sched(nc)

    def din(name, shape, dt=F32):
        return nc.dram_tensor(name, list(shape), dt, kind="ExternalInput").ap()

    def dout(name, shape, dt=F32):
        return nc.dram_tensor(name, list(shape), dt, kind="ExternalOutput").ap()

    xin = din("xin", [NT, D])
    condT = din("condT", [128, 8])
    cache_ckv = din("cache_ckv", [NL, PAST, KVR])
    cache_kpe = din("cache_kpe", [NL, PAST, RD])
    h0 = din("h0", [NL, 128, 128])
    w_ada = din("w_ada", [NL, D, 6 * D])
    b_adaT = din("b_adaT", [NL, 128, 48])
    nmgT = din("nmgT", [NL, 128, 8])
    nfgT = din("nfgT", [NL, 128, 8])
    w_in = din("w_in", [NL, D, IN_COLS])
    qagT = din("qagT", [NL, 128, 3])
    kvag = din("kvag", [NL, 128, 2])
    w_uq = din("w_uq", [NL, QR, NH * QK])
    w_uq_sw = din("w_uq_sw", [NL, QR, NH * QK])
    w_uk = din("w_uk", [NL, KVR, NH * QK])
    w_uv = din("w_uv", [NL, KVR, NH * 64])
    qng = din("qng", [NL, 128, 4])
    dskT = din("dskT", [NL, 128, 32])
    lam2 = din("lam2", [NL, 128, 2, 32])
    logdt2 = din("logdt2", [NL, 128, 32])
    b2 = din("b2", [NL, 128, 2, 32, 16])
    c2 = din("c2", [NL, 128, 2, 32, 16])
    ramp32 = din("ramp32", [128, 32, 24])
    halfmask = din("halfmask", [128, 2])
    w_glu = din("w_glu", [NL, BW, BW])
    convwT = din("convwT", [NL, 128, 4, 3])
    w_branch = din("w_branch", [NL, 4, BW, D])
    w_gate = din("w_gate", [NL, D, 4 * D])
    b_gateT = din("b_gateT", [NL, 128, 32])
    w_out = din("w_out", [NL, D, D])
    w_ffn_in = din("w_ffn_in", [NL, D, 2 * DFF])
    w_ffn_out = din("w_ffn_out", [NL, DFF, D])
    ropeC = din("ropeC", [128, NT], BF16)
    ropeS = din("ropeS", [128, NT], BF16)
    qind = din("qind", [8, NT], BF16)
    kind = din("kind", [8, NK], BF16)
    dftC = din("dftC", [NT, NT], BF16)
    dftS = din("dftS", [NT, NT], BF16)
    ccsc = din("ccsc", [128, 256], BF16)
    seqflag = din("seqflag", [128, 1])
    smask = din("smask", [128, 2, 257])
    tzmask = din("tzmask", [128, 2, 512])
    selpad = din("selpad", [128, 8, 240], BF16)
    seltpad = din("seltpad", [128, 8, 240], BF16)
    shiftm = din("shiftm", [32, 2, 96], BF16)
    identf = din("identf", [128, 128])

    y_out = dout("y_out", [NT, D])
    ckv_out = dout("ckv_out", [NL, NT, KVR])
    kpe_out = dout("kpe_out", [NL, NT, RD])
    ssm_out = dout("ssm_out", [NL, 128, 512])
    dbg_out = {}

    from contextlib import ExitStack
    es = ExitStack()

    _uid = [0]

    def T(name, shape, dt=F32):
        _uid[0] += 1
        return nc.sbuf_tensor("%s_%d" % (name, _uid[0]), list(shape), dt)

    def sb(name, shape, dt=F32):
        return es.enter_context(T(name, list(shape), dt))

    with es:
        x = sb("x", [128, 8, NT])
        xn = sb("xn", [128, 8, NT], BF16)
        ones_bf = sb("ones_bf", [128, 128], BF16)
        ident = sb("ident", [128, 128])
        ident_bf = sb("ident_bf", [128, 128], BF16)
        mod = sb("mod", [128, 48])
        vecs = sb("vecs", [128, 64])
        epsb = sb("epsb", [128, 1])
        psum = [es.enter_context(nc.psum_tensor("ps%d" % i, [128, 512], F32)) for i in range(8)]
        pstate = {'i': 0}

        def PS():
            i = pstate['i']
            pstate['i'] = (i + 1) % 8
            return psum[i], 'ps%d' % i

        def mm(out, lhsT, rhs, start, stop, reads, writes):
            S.op('pe', lambda e: e.matmul(out, lhsT, rhs, start=start, stop=stop), reads, writes)

        def tr(out, in_, idn, reads, writes):
            S.op('pe', lambda e: e.transpose(out, in_, idn), reads, writes)

        def act(out, in_, func, reads, writes, bias=None, scale=None):
            kw = {}
            if bias is not None:
                kw['bias'] = bias
            if scale is not None:
                kw['scale'] = scale
            S.op('act', lambda e: e.activation(out=out, in_=in_, func=func, **kw), reads, writes)

        def tt(out, in0, in1, op, reads, writes, eng='dve'):
            S.op(eng, lambda e: e.tensor_tensor(out=out, in0=in0, in1=in1, op=op), reads, writes)

        def ts(out, in0, s1, s2, op0, op1, reads, writes, eng='dve'):
            if op1 is None:
                S.op(eng, lambda e: e.tensor_scalar(out=out, in0=in0, scalar1=s1, scalar2=None, op0=op0), reads, writes)
            else:
                S.op(eng, lambda e: e.tensor_scalar(out=out, in0=in0, scalar1=s1, scalar2=s2, op0=op0, op1=op1), reads, writes)

        def stt(out, in0, scalar, in1, op0, op1, reads, writes):
            S.op('dve', lambda e: e.scalar_tensor_tensor(out=out, in0=in0, scalar=scalar, in1=in1, op0=op0, op1=op1), reads, writes)

        def cp(out, in_, reads, writes, eng='dve'):
            S.op(eng, lambda e: e.tensor_copy(out=out, in_=in_), reads, writes)

        def recip(out, in_, reads, writes):
            S.op('dve', lambda e: e.reciprocal(out=out, in_=in_), reads, writes)

        def memset(t_ap, val, writes, eng='dve'):
            S.op(eng, lambda e: e.memset(t_ap, val), (), writes)

        def dump(name, ap, shape, dt, key):
            if name in dbg:
                o = dout("dbg_" + name, shape, dt)
                dbg_out[name] = o
                S.dma('sp', o, ap, reads=[key], writes=['dbg_' + name])

        def rsqrt_inplace(ap, key, scale):
            act(ap, ap, AF.Sqrt, [key, 'epsb'], [key], bias=epsb[0:ap.shape[0], 0:1], scale=scale)
            recip(ap, ap, [key], [key])

        memset(ones_bf[:], 1.0, ['ones_bf'])
        memset(epsb[:], EPS, ['epsb'])
        S.dma('sp', ident[:], identf[:, :], (), ['ident'])
        cp(ident_bf[:], ident[:], ['ident'], ['ident_bf'])

        with T("xtm", [128, 4, D], F32) as xtm:
            for t in range(NTT):
                S.dma('sp', xtm[:], xin[t * TT:(t + 1) * TT, :].rearrange("(n p) c -> p n c", p=128), (), ['xtm'])
                for oc in range(8):
                    ps, pk = PS()
                    for n in range(4):
                        tr(ps[:, n * 128:(n + 1) * 128], xtm[:, n, oc * 128:(oc + 1) * 128], ident[:], ['xtm', 'ident'], [pk])
                    if oc % 2 == 0:
                        cp(x[:, oc, t * TT:(t + 1) * TT], ps[:], [pk], [('x', oc, t)])
                    else:
                        act(x[:, oc, t * TT:(t + 1) * TT], ps[:], AF.Copy, [pk], [('x', oc, t)])
            S.barrier()

        def load_w(dst, src, kchunks, keyw, eng='pool'):
            S.dma(eng, dst[:, 0:kchunks, :], src.rearrange("(kc p) n -> p kc n", p=128), (), [keyw])

        def rmsnorm_mod(l, gname, sc_c, sh_c):
            with T("nsq", [128, 2, TT], BF16) as nsq, \
                    T("nrs", [128, TT], F32) as nrs, \
                    T("ntmp", [128, 2, TT], F32) as ntmp:
                for t in range(NTT):
                    sl = slice(t * TT, (t + 1) * TT)
                    ps, pk = PS()
                    for oc in range(8):
                        b = oc % 2
                        act(nsq[:, b, :], x[:, oc, sl], AF.Square, [('x', oc, t)], [('nsq', b)])
                        mm(ps[:], ones_bf[:], nsq[:, b, :], oc == 0, oc == 7, ['ones_bf', ('nsq', b)], [pk])
                    cp(nrs[:], ps[:], [pk], ['nrs'])
                    rsqrt_inplace(nrs[:], 'nrs', 1.0 / D)
                    for oc in range(8):
                        b = oc % 2
                        tt(ntmp[:, b, :], x[:, oc, sl], nrs[:], ALU.mult, [('x', oc, t), 'nrs'], [('ntmp', b)])
                        act(xn[:, oc, sl], ntmp[:, b, :], AF.Identity, [('ntmp', b), 'vecs'], [('xn', oc, t)],
                            bias=vecs[:, sh_c + oc:sh_c + oc + 1], scale=vecs[:, sc_c + oc:sc_c + oc + 1])
            S.barrier()

        for l in range(nlayers):
            with T("wada", [128, 2, 8, 1024], BF16) as wada, \
                    T("scond", [128, 8], BF16) as scond, \
                    T("ctmp", [128, 8], F32) as ctmp, \
                    T("mtmp", [128, 64], F32) as mtmp:
                S.dma('sp', ctmp[:], condT[:, :], (), ['ctmp'])
                act(scond[:], ctmp[:], AF.Silu, ['ctmp'], ['scond'])
                S.dma('sp', mtmp[:, 0:48], b_adaT[l], (), ['mtmp'])
                S.dma('sp', mtmp[:, 48:56], nmgT[l], (), ['mtmp'])
                S.dma('sp', mtmp[:, 56:64], nfgT[l], (), ['mtmp'])
                psm, pkm = PS()
                for piece in range(6):
                    wb_ = piece % 2
                    load_w(wada[:, wb_], w_ada[l][:, piece * 1024:(piece + 1) * 1024], 8, ('wada', wb_))
                    for c in range(8):
                        col = piece * 8 + c
                        for kc in range(8):
                            mm(psm[:, col:col + 1], wada[:, wb_, kc, c * 128:(c + 1) * 128], scond[:, kc:kc + 1],
                               kc == 0, kc == 7, [('wada', wb_), 'scond'], [pkm])
                tt(mod[:], psm[:, 0:48], mtmp[:, 0:48], ALU.add, [pkm, 'mtmp'], ['mod'])
                stt(vecs[:, 0:8], mod[:, 8:16], 1.0, mtmp[:, 48:56], ALU.add, ALU.mult, ['mod', 'mtmp'], ['vecs'])
                cp(vecs[:, 8:16], mod[:, 0:8], ['mod'], ['vecs'])
                stt(vecs[:, 16:24], mod[:, 32:40], 1.0, mtmp[:, 56:64], ALU.add, ALU.mult, ['mod', 'mtmp'], ['vecs'])
                cp(vecs[:, 24:32], mod[:, 24:32], ['mod'], ['vecs'])
                S.barrier()
            dump("mod%d" % l, mod[:], [128, 48], F32, 'mod')

            rmsnorm_mod(l, None, 0, 8)
            dump("xn%d" % l, xn[:], [128, 8, NT], BF16, ('xn', 0, 0))

            def epilogue(k, ysrc, ykeyfn, wb, wg, wo, gp, sg, t):
                sl = slice(t * TT, (t + 1) * TT)
                for oc in range(8):
                    pa, pka = PS()
                    for kc in range(4):
                        mm(pa[:], wb[:, kc, oc * 128:(oc + 1) * 128], ysrc(kc, t), kc == 0, kc == 3,
                           ['wb', ykeyfn(kc, t)], [pka])
                    pb, pkb = PS()
                    for kc in range(8):
                        mm(pb[:], wg[:, kc, oc * 128:(oc + 1) * 128], xn[:, kc, sl], kc == 0, kc == 7,
                           ['wg', ('xn', kc, t)], [pkb])
                    b = oc % 2
                    act(sg[:, b, :], pb[:], AF.Sigmoid, [pkb, 'vecs'], [('sg', b)],
                        bias=vecs[:, 32 + k * 8 + oc:32 + k * 8 + oc + 1])
                    tt(gp[:, oc, :], pa[:], sg[:, b, :], ALU.mult, [pka, ('sg', b)], [('gp', oc)])
                for oc2 in range(8):
                    pc, pkc = PS()
                    for kc in range(8):
                        mm(pc[:], wo[:, kc, oc2 * 128:(oc2 + 1) * 128], gp[:, kc, :], kc == 0, kc == 7,
                           ['wo', ('gp', kc)], [pkc])
                    stt(x[:, oc2, sl], pc[:], mod[:, 16 + oc2:17 + oc2], x[:, oc2, sl], ALU.mult, ALU.add,
                        [pkc, 'mod', ('x', oc2, t)], [('x', oc2, t)])

            S.dma('sp', vecs[:, 32:64], b_gateT[l], (), ['vecs'])

            def run_epi(k, ybuf):
                with T("wo", [128, 8, D], BF16) as wo, \
                        T("wb", [128, 4, D], BF16) as wb, \
                        T("wg", [128, 8, D], BF16) as wg, \
                        T("gp", [128, 8, TT], BF16) as gp, \
                        T("sg", [128, 2, TT], F32) as sg:
                    load_w(wb, w_branch[l, k], 4, 'wb')
                    load_w(wg, w_gate[l][:, k * D:(k + 1) * D], 8, 'wg')
                    load_w(wo, w_out[l], 8, 'wo')
                    for t in range(NTT if not int(os.environ.get('SKE', '0')) else 0):
                        epilogue(k, lambda kc, t_: ybuf[:, kc, t_ * TT:(t_ + 1) * TT],
                                 lambda kc, t_: ('y', kc, t_), wb, wg, wo, gp, sg, t)
                    S.barrier()

            if 'D' in BRANCHES:
              with T("ybuf", [128, 4, NT], BF16) as ybuf:
                with T("wc", [128, 8, 512], BF16) as wc, \
                        T("Tb", [128, 4, 8, 258], BF16) as Tb, \
                        T("ctm", [128, 3, TT], F32) as ctm, \
                        T("cvw", [128, 4, 3], F32) as cvw, \
                        T("sflag", [128, 1], F32) as sflag:
                    S.dma('sp', cvw[:], convwT[l], (), ['cvw'])
                    S.dma('sp', sflag[:], seqflag[:, :], (), ['sflag'])
                    for oc in range(4):
                        memset(Tb[:, oc, :, 0:1], 0.0, [('Tb', oc)])
                        memset(Tb[:, oc, :, 257:258], 0.0, [('Tb', oc)])
                    load_w(wc, w_in[l][:, OFF_CONV:OFF_CONV + 512], 8, 'wc')
                    for t in range(NTT):
                        sl = slice(t * TT, (t + 1) * TT)
                        for oc in range(4):
                            ph, pkh = PS()
                            for kc in range(8):
                                mm(ph[:], wc[:, kc, oc * 128:(oc + 1) * 128], xn[:, kc, sl], kc == 0, kc == 7,
                                   ['wc', ('xn', kc, t)], [pkh])
                            act(Tb[:, oc, 2 * t:2 * t + 2, 1:257], ph[:].rearrange("p (a b) -> p a b", a=2), AF.Copy, [pkh], [('Tb', oc)])
                    load_w(wc, w_in[l][:, OFF_CONV + 1024:OFF_CONV + 1536], 8, 'wc')
                    for t in range(NTT):
                        sl = slice(t * TT, (t + 1) * TT)
                        for oc in range(4):
                            pg, pkg = PS()
                            for kc in range(8):
                                mm(pg[:], wc[:, kc, oc * 128:(oc + 1) * 128], xn[:, kc, sl], kc == 0, kc == 7,
                                   ['wc', ('xn', kc, t)], [pkg])
                            tt(Tb[:, oc, 2 * t:2 * t + 2, 1:257], pg[:].rearrange("p (a b) -> p a b", a=2),
                               Tb[:, oc, 2 * t:2 * t + 2, 1:257], ALU.mult, [pkg, ('Tb', oc)], [('Tb', oc)])
                    for oc in range(4):
                        ts(Tb[:, oc, 1:8, 0:1], Tb[:, oc, 0:7, 256:257], sflag[:, 0:1], None, ALU.mult, None,
                           [('Tb', oc), 'sflag'], [('Tb', oc)])
                        ts(Tb[:, oc, 0:7, 257:258], Tb[:, oc, 1:8, 1:2], sflag[:, 0:1], None, ALU.mult, None,
                           [('Tb', oc), 'sflag'], [('Tb', oc)])
                    load_w(wc, w_in[l][:, OFF_CONV + 512:OFF_CONV + 1024], 8, 'wc')
                    for t in range(NTT):
                        sl = slice(t * TT, (t + 1) * TT)
                        for oc in range(4):
                            pgb, pkgb = PS()
                            for kc in range(8):
                                mm(pgb[:], wc[:, kc, oc * 128:(oc + 1) * 128], xn[:, kc, sl], kc == 0, kc == 7,
                                   ['wc', ('xn', kc, t)], [pkgb])
                            b = oc % 3
                            cv = ctm[:, b, :].rearrange("p (a b) -> p a b", a=2)
                            ts(cv, Tb[:, oc, 2 * t:2 * t + 2, 1:257], cvw[:, oc, 1:2], None, ALU.mult, None,
                               [('Tb', oc), 'cvw'], [('ctm', b)])
                            stt(cv, Tb[:, oc, 2 * t:2 * t + 2, 0:256], cvw[:, oc, 0:1], cv, ALU.mult, ALU.add,
                                [('Tb', oc), ('ctm', b), 'cvw'], [('ctm', b)])
                            stt(cv, Tb[:, oc, 2 * t:2 * t + 2, 2:258], cvw[:, oc, 2:3], cv, ALU.mult, ALU.add,
                                [('Tb', oc), ('ctm', b), 'cvw'], [('ctm', b)])
                            tt(ybuf[:, oc, sl], pgb[:], ctm[:, b, :], ALU.mult, [pkgb, ('ctm', b)], [('y', oc, t)])
                    S.barrier()
                dump("yconv%d" % l, ybuf[:], [128, 4, NT], BF16, ('y', 0, 0))
                run_epi(3, ybuf)

            if 'B' in BRANCHES:
              with T("ybuf", [128, 4, NT], BF16) as ybuf:
                with T("AB", [128, 16, 4, 256], BF16) as AB:
                    with T("wf", [128, 8, 512], BF16) as wf, \
                            T("uf", [128, 4, NT], BF16) as uf, \
                            T("ccsc_sb", [128, 256], BF16) as ccsc_sb:
                        load_w(wf, w_in[l][:, OFF_FFT:OFF_FFT + 512], 8, 'wf')
                        S.dma('sp', ccsc_sb[:], ccsc[:, :], (), ['ccsc'])
                        for t in range(NTT):
                            sl = slice(t * TT, (t + 1) * TT)
                            for grp in range(4):
                                pu, pku = PS()
                                for kc in range(8):
                                    mm(pu[:], wf[:, kc, grp * 128:(grp + 1) * 128], xn[:, kc, sl], kc == 0, kc == 7,
                                       ['wf', ('xn', kc, t)], [pku])
                                if grp % 2 == 0:
                                    act(uf[:, grp, sl], pu[:], AF.Copy, [pku], [('uf', grp, t)])
                                else:
                                    cp(uf[:, grp, sl], pu[:], [pku], [('uf', grp, t)])
                        for grp in range(4):
                            for n2 in range(8):
                                pa, pka = PS()
                                for j in range(2):
                                    n = n2 * 2 + j
                                    mm(pa[:, j * 256:(j + 1) * 256], uf[:, grp, n * 128:(n + 1) * 128], ccsc_sb[:], True, True,
                                       [('uf', grp, n // 4), 'ccsc'], [pka])
                                if n2 % 2 == 0:
                                    act(AB[:, n2 * 2:n2 * 2 + 2, grp, :], pa[:].rearrange("p (a b) -> p a b", a=2), AF.Copy, [pka], [('AB', grp)])
                                else:
                                    cp(AB[:, n2 * 2:n2 * 2 + 2, grp, :], pa[:].rearrange("p (a b) -> p a b", a=2), [pka], [('AB', grp)])
                        S.barrier()
                    with T("tabC", [128, 2, 16, 256], BF16) as tabC, T("tabS", [128, 2, 16, 256], BF16) as tabS:
                        for kt in range(8):
                            b = kt % 2
                            S.dma('sp', tabC[:, b], dftC[:, kt * 256:(kt + 1) * 256].rearrange("(nt p) k -> p nt k", p=128), (), [('tabC', b)])
                            S.dma('sp', tabS[:, b], dftS[:, kt * 256:(kt + 1) * 256].rearrange("(nt p) k -> p nt k", p=128), (), [('tabS', b)])
                            for g2 in range(2):
                                py, pky = PS()
                                for gg in range(2):
                                    grp = g2 * 2 + gg
                                    for n in range(16):
                                        mm(py[:, gg * 256:(gg + 1) * 256], AB[:, n, grp, 0:128], tabC[:, b, n, :], n == 0, False,
                                           [('AB', grp), ('tabC', b)], [pky])
                                        mm(py[:, gg * 256:(gg + 1) * 256], AB[:, n, grp, 128:256], tabS[:, b, n, :], False, n == 15,
                                           [('AB', grp), ('tabS', b)], [pky])
                                cp(ybuf[:, g2 * 2:g2 * 2 + 2, kt * 256:(kt + 1) * 256], py[:].rearrange("p (a b) -> p a b", a=2),
                                   [pky], [('y', g2 * 2, kt // 2), ('y', g2 * 2 + 1, kt // 2)])
                        S.barrier()
                dump("yfft%d" % l, ybuf[:], [128, 4, NT], BF16, ('y', 0, 0))
                run_epi(1, ybuf)
            if 'C' in BRANCHES:
              with T("ybuf", [128, 4, NT], BF16) as ybuf:
                with T("cqn", [128, 3, NT], BF16) as cqn, T("ckvT", [128, 2, NK], BF16) as ckvT, \
                        T("kpeT", [128, NK], BF16) as kpeT:
                    with T("wa", [128, 8, 384], BF16) as wa, T("wkv", [128, 8, 288], BF16) as wkv, T("ctk", [128, 4, 288], F32) as ctk, T("cqf", [128, 3, TT], F32) as cqf, \
                            T("asq", [128, 2, TT], BF16) as asq, T("ars", [128, TT], F32) as ars, \
                            T("atmp", [128, 2, TT], F32) as atmp, T("tkm", [128, 2, 288], F32) as tkm, \
                            T("junk", [128, 256], F32) as junk, T("ss", [128, 2, 8], F32) as ss, \
                            T("kvag_sb", [128, 2], F32) as kvag_sb, T("qag", [128, 3], F32) as qag:
                        load_w(wa, w_in[l][:, OFF_CQ:OFF_CQ + 384], 8, 'wa')
                        load_w(wkv, w_in[l][:, OFF_CKV:OFF_CKV + 288], 8, 'wkv')
                        memset(ss[:, 0, :], 1.0, [('ss', 0)])
                        memset(ss[:, 1, :], 1.0, [('ss', 1)])
                        S.dma('sp', kvag_sb[:], kvag[l], (), ['kvag'])
                        S.dma('sp', qag[:], qagT[l], (), ['qag'])
                        for t in range(NTT if not int(os.environ.get('SKQ','0')) else 0):
                            sl = slice(t * TT, (t + 1) * TT)
                            for c in range(3):
                                pq, pkq = PS()
                                for kc in range(8):
                                    mm(pq[:], wa[:, kc, c * 128:(c + 1) * 128], xn[:, kc, sl], kc == 0, kc == 7, ['wa', ('xn', kc, t)], [pkq])
                                cp(cqf[:, c, :], pq[:], [pkq], [('cqf', c)])
                            pss, pkss = PS()
                            for c in range(3):
                                act(asq[:, c % 2, :], cqf[:, c, :], AF.Square, [('cqf', c)], [('asq', c % 2)])
                                mm(pss[:], ones_bf[:], asq[:, c % 2, :], c == 0, c == 2, ['ones_bf', ('asq', c % 2)], [pkss])
                            cp(ars[:], pss[:], [pkss], ['ars'])
                            rsqrt_inplace(ars[:], 'ars', 1.0 / QR)
                            for c in range(3):
                                tt(atmp[:, c % 2, :], cqf[:, c, :], ars[:], ALU.mult, [('cqf', c), 'ars'], [('atmp', c % 2)])
                                ts(cqn[:, c, sl], atmp[:, c % 2, :], qag[:, c:c + 1], None, ALU.mult, None,
                                   [('atmp', c % 2), 'qag'], [('cqn', c, t)])

                        def tok2feat(b, n):
                            pst, pkst = PS()
                            tr(pst[:, 0:128], tkm[:, b, 0:128], ident[:], [('tkm', b), 'ident'], [pkst])
                            tr(pst[:, 128:256], tkm[:, b, 128:256], ident[:], [('tkm', b), 'ident'], [pkst])
                            tr(pst[0:32, 256:384], tkm[:, b, 256:288], ident[:], [('tkm', b), 'ident'], [pkst])
                            cp(ckvT[:, :, n * 128:(n + 1) * 128], pst[:, 0:256].rearrange("p (a b) -> p a b", a=2), [pkst], [('ckvT', n // 4)])
                            act(kpeT[0:32, n * 128:(n + 1) * 128], pst[0:32, 256:384], AF.Copy, [pkst], [('kpeT', n // 4), pkst])

                        for t in range(NTT):
                            sl = slice(t * TT, (t + 1) * TT)
                            for c in range(2):
                                pq, pkq = PS()
                                for kc in range(8):
                                    mm(pq[:], wkv[:, kc, c * 128:(c + 1) * 128], xn[:, kc, sl], kc == 0, kc == 7, ['wkv', ('xn', kc, t)], [pkq])
                                cp(cqf[:, c, :], pq[:], [pkq], [('cqf', c)])
                            if CUT >= 2:
                                pq, pkq = PS()
                                for kc in range(8):
                                    mm(pq[0:32, :], wkv[:, kc, 256:288], xn[:, kc, sl], kc == 0, kc == 7, ['wkv', ('xn', kc, t)], [pkq])
                                cp(cqf[0:32, 2, :], pq[0:32, :], [pkq], [('cqf', 2)])
                                act(kpeT[0:32, sl], pq[0:32, :], AF.Copy, [pkq], [('kpeT', t), pkq])
                            if CUT >= 3:
                                pss, pkss = PS()
                                for c in range(2):
                                    act(asq[:, c, :], cqf[:, c, :], AF.Square, [('cqf', c)], [('asq', c)])
                                    mm(pss[:], ones_bf[:], asq[:, c, :], c == 0, c == 1, ['ones_bf', ('asq', c)], [pkss])
                                cp(ars[:], pss[:], [pkss], ['ars'])
                                rsqrt_inplace(ars[:], 'ars', 1.0 / KVR)
                                for c in range(2):
                                    tt(atmp[:, c, :], cqf[:, c, :], ars[:], ALU.mult, [('cqf', c), 'ars'], [('atmp', c)])
                                    ts(cqf[:, c, :], atmp[:, c, :], kvag_sb[:, c:c + 1], None, ALU.mult, None,
                                       [('atmp', c), 'kvag'], [('cqf', c)])
                                    act(ckvT[:, c, sl], cqf[:, c, :], AF.Copy, [('cqf', c)], [('ckvT', t)])
                            if CUT >= 4:
                                for blk in range(4):
                                    b = blk % 2
                                    n = t * 4 + blk
                                    bs = slice(blk * 128, (blk + 1) * 128)
                                    pso, pkso = PS()
                                    tr(pso[:, 0:128], cqf[:, 0, bs], ident[:], [('cqf', 0), 'ident'], [pkso])
                                    tr(pso[:, 128:256], cqf[:, 1, bs], ident[:], [('cqf', 1), 'ident'], [pkso])
                                    tr(pso[:, 256:288], cqf[0:32, 2, bs], ident[0:32, 0:32], [('cqf', 2), 'ident'], [pkso])
                                    cp(tkm[:, b, :], pso[:, 0:288], [pkso], [('tkm', b)])
                                    if not int(os.environ.get('SKA', '0')):
                                        S.dma('sp', ckv_out[l, n * 128:(n + 1) * 128, :], tkm[:, b, 0:256], [('tkm', b)], [('ckv_out', n)])
                                        S.dma('sp', kpe_out[l, n * 128:(n + 1) * 128, :], tkm[:, b, 256:288], [('tkm', b)], [('kpe_out', n)])
                        S.barrier()
                        if not int(os.environ.get('SK2', '0')):
                            S.dma('sp', ctk[:, :, 0:256], cache_ckv[l].rearrange("(n p) c -> p n c", p=128), (), ['ctk'])
                            S.dma('sp', ctk[:, :, 256:288], cache_kpe[l].rearrange("(n p) c -> p n c", p=128), (), ['ctk'])
                            for n in range(PAST // 128):
                                pst, pkst = PS()
                                tr(pst[:, 0:128], ctk[:, n, 0:128], ident[:], ['ctk', 'ident'], [pkst])
                                tr(pst[:, 128:256], ctk[:, n, 128:256], ident[:], ['ctk', 'ident'], [pkst])
                                tr(pst[0:32, 256:384], ctk[:, n, 256:288], ident[:], ['ctk', 'ident'], [pkst])
                                nn_ = NT // 128 + n
                                cp(ckvT[:, :, nn_ * 128:(nn_ + 1) * 128], pst[:, 0:256].rearrange("p (a b) -> p a b", a=2), [pkst], [('ckvT', nn_ // 4)])
                                act(kpeT[0:32, nn_ * 128:(nn_ + 1) * 128], pst[0:32, 256:384], AF.Copy, [pkst], [('kpeT', nn_ // 4), pkst])
                        S.barrier()
                    es3 = ExitStack()
                    with es3:
                        def A(name, shape, dt=F32):
                            return es3.enter_context(T(name, shape, dt))
                        wuq_sb = A("wuq_sb", [128, 3, NH * QK], BF16)
                        wuqs_sb = A("wuqs_sb", [128, 3, NH * QK], BF16)
                        wuk_sb = A("wuk_sb", [128, 2, NH * QK], BF16)
                        wuv_sb = A("wuv_sb", [128, 2, NH * 64], BF16)
                        shift_sb = A("shift_sb", [128, 2, 96], BF16)
                        qng_sb = A("qng_sb", [128, 4])
                        rC = A("rC", [128, NT], BF16)
                        rS = A("rS", [128, NT], BF16)
                        Qh = A("Qh", [128, NT], BF16)
                        Kh = A("Kh", [128, NK], BF16)
                        Ve = A("Ve", [128, NK // 128, 128], BF16)
                        Vo = A("Vo", [128, NK // 128, 128], BF16)
                        Pb = A("Pb", [128, 3, TT], BF16)
                        qf = A("qf", [128, 2, TT])
                        hsq = A("hsq", [128, TT], BF16)
                        hrs = A("hrs", [128, TT])
                        ht = A("ht", [128, 2, TT])
                        rec = A("rec", [128, TT])
                        load_w(wuq_sb, w_uq[l], 3, 'wuq')
                        load_w(wuqs_sb, w_uq_sw[l], 3, 'wuqs')
                        load_w(wuk_sb, w_uk[l], 2, 'wuk')
                        load_w(wuv_sb, w_uv[l], 2, 'wuv')
                        S.dma('sp', shift_sb[0:32], shiftm[:, :, :], (), ['shift'])
                        S.dma('sp', qng_sb[:], qng[l], (), ['qng'])
                        S.dma('sp', rC[:], ropeC[:, :], (), ['rC'])
                        S.dma('sp', rS[:], ropeS[:, :], (), ['rS'])
                        memset(Qh[96:128, :], 0.0, ['Qind'])
                        memset(Kh[96:128, :], 0.0, ['Kind'])
                        S.dma('sp', Qh[96:104, :], qind[:, :], (), ['Qind'])
                        S.dma('sp', Kh[96:104, :], kind[:, :], (), ['Kind'])
                        memset(Ve[:, :, 64:128], 1.0, ['Ve1'])
                        memset(Vo[:, :, 0:64], 1.0, ['Vo1'])

                        def normrope(pq, pkq, pqs, pkqs, gcol, dest, dkey, sl):
                            act(qf[0:96, 0, :], pq[0:96, :], AF.Copy, [pkq], [('qf', 0)])
                            act(hsq[0:96, :], qf[0:96, 0, :], AF.Square, [('qf', 0)], ['hsq'])
                            pz, pkz = PS()
                            mm(pz[0:96, :], ones_bf[0:96, 0:96], hsq[0:96, :], True, True, ['ones_bf', 'hsq'], [pkz])
                            cp(hrs[0:96, :], pz[0:96, :], [pkz], ['hrs'])
                            rsqrt_inplace(hrs[0:96, :], 'hrs', 1.0 / QK)
                            if pqs is not None:
                                act(qf[0:96, 1, :], pqs[0:96, :], AF.Copy, [pkqs], [('qf', 1)])
                                stt(ht[0:96, 0, :], qf[0:96, 0, :], qng_sb[0:96, gcol:gcol + 1], rC[0:96, sl], ALU.mult, ALU.mult,
                                    [('qf', 0), 'qng', 'rC'], [('ht', 0)])
                                stt(ht[0:96, 1, :], qf[0:96, 1, :], qng_sb[0:96, gcol + 1:gcol + 2], rS[0:96, sl], ALU.mult, ALU.mult,
                                    [('qf', 1), 'qng', 'rS'], [('ht', 1)])
                                tt(ht[0:96, 0, :], ht[0:96, 0, :], ht[0:96, 1, :], ALU.add, [('ht', 0), ('ht', 1)], [('ht', 0)])
                                tt(dest, ht[0:96, 0, :], hrs[0:96, :], ALU.mult, [('ht', 0), 'hrs'], [dkey])
                            else:
                                stt(dest, qf[0:96, 0, :], qng_sb[0:96, gcol:gcol + 1], hrs[0:96, :], ALU.mult, ALU.mult,
                                    [('qf', 0), 'qng', 'hrs'], [dkey])

                        for h in range(NH if not ATT_SKIP else 0):
                            V = Ve if h % 2 == 0 else Vo
                            vkey = 'Ve' if h % 2 == 0 else 'Vo'
                            voff = 0 if h % 2 == 0 else 64
                            hs = slice(h * QK, (h + 1) * QK)
                            for t in range(NTT):
                                sl = slice(t * TT, (t + 1) * TT)
                                pq, pkq = PS()
                                for c in range(3):
                                    mm(pq[0:96, :], wuq_sb[:, c, hs], cqn[:, c, sl], c == 0, c == 2, ['wuq', ('cqn', c, t)], [pkq])
                                pqs, pkqs = PS()
                                for c in range(3):
                                    mm(pqs[0:96, :], wuqs_sb[:, c, hs], cqn[:, c, sl], c == 0, c == 2, ['wuqs', ('cqn', c, t)], [pkqs])
                                normrope(pq, pkq, pqs, pkqs, 0, Qh[0:96, sl], ('Qh', t), sl)
                            for kt in range(NK // TT if HSTOP >= 2 else 0):
                                sl = slice(kt * TT, (kt + 1) * TT)
                                pk_, pkk = PS()
                                for c in range(2):
                                    mm(pk_[0:96, :], wuk_sb[:, c, hs], ckvT[:, c, sl], c == 0, False, ['wuk', ('ckvT', kt)], [pkk])
                                mm(pk_[0:96, :], shift_sb[0:32, 0, :], kpeT[0:32, sl], False, True, ['shift', ('kpeT', kt)], [pkk])
                                if kt < NTT:
                                    pks, pkks = PS()
                                    mm(pks[0:96, :], shift_sb[0:32, 1, :], kpeT[0:32, sl], True, True, ['shift', ('kpeT', kt)], [pkks])
                                    normrope(pk_, pkk, pks, pkks, 2, Kh[0:96, sl], ('Kh', kt), sl)
                                else:
                                    normrope(pk_, pkk, None, None, 2, Kh[0:96, sl], ('Kh', kt), sl)
                            for n0 in (range(0, NK // 128, 8) if HSTOP >= 3 else []):
                                nn = min(8, NK // 128 - n0)
                                pv, pkv = PS()
                                for j in range(nn):
                                    n = n0 + j
                                    for c in range(2):
                                        mm(pv[:, j * 64:(j + 1) * 64], ckvT[:, c, n * 128:(n + 1) * 128], wuv_sb[:, c, h * 64:(h + 1) * 64],
                                           c == 0, c == 1, [('ckvT', n // 4), 'wuv'], [pkv])
                                cp(V[:, n0:n0 + nn, voff:voff + 64], pv[:, 0:nn * 64].rearrange("p (a b) -> p a b", a=nn), [pkv], [vkey])
                            for qt in range(NTT if HSTOP >= 4 else 0):
                                sl = slice(qt * TT, (qt + 1) * TT)
                                po, pko = PS()
                                NKT = NK // 128
                                LOOK = 3
                                stiles = {}

                                def emitS(n):
                                    ps_, pks_ = PS()
                                    if ps_ is po:
                                        ps_, pks_ = PS()
                                    mm(ps_[:], Kh[:, n * 128:(n + 1) * 128], Qh[:, sl], True, True,
                                       [('Kh', n // 4), 'Kind', ('Qh', qt), 'Qind'], [pks_])
                                    stiles[n] = (ps_, pks_)

                                for n in range(min(LOOK, NKT)):
                                    emitS(n)
                                for n in range(NKT):
                                    if n + LOOK < NKT:
                                        emitS(n + LOOK)
                                    ps_, pks_ = stiles.pop(n)
                                    pb = n % 3
                                    act(Pb[:, pb, :], ps_[:], AF.Exp, [pks_], [('Pb', pb)], scale=float(QK) ** -0.5)
                                    mm(po[:], V[:, n, :], Pb[:, pb, :], n == 0, n == NKT - 1, [vkey, vkey + '1', ('Pb', pb)], [pko])
                                if HSTOP < 5:
                                    continue
                                if h % 2 == 0:
                                    recip(rec[0:64, :], po[64:128, :], [pko], ['rec'])
                                    tt(ybuf[0:64, h // 2, sl], po[0:64, :], rec[0:64, :], ALU.mult, [pko, 'rec'], [('y', h // 2, qt)])
                                else:
                                    recip(rec[64:128, :], po[0:64, :], [pko], ['rec'])
                                    tt(ybuf[64:128, h // 2, sl], po[64:128, :], rec[64:128, :], ALU.mult, [pko, 'rec'], [('y', h // 2, qt)])
                        S.barrier()
                dump("yattn%d" % l, ybuf[:], [128, 4, NT], BF16, ('y', 0, 0))
                run_epi(2, ybuf)
            if 'A' in BRANCHES:
              with T("ybuf", [128, 4, NT], BF16) as ybuf:
                esA = ExitStack()
                with esA:
                    def AA(name, shape, dt=F32):
                        return esA.enter_context(T(name, shape, dt))
                    U = AA("U", [128, 32, 256], BF16)
                    Tz = AA("Tz", [128, 32, 128], BF16)
                    Bb = AA("Bb", [128, 2, 32, 16])
                    Cc = AA("Cc", [128, 2, 32, 16])
                    Qp = AA("Qp", [128, 2, 32, 24])
                    A1x = AA("A1x", [128, 2, 32])
                    A2x = AA("A2x", [128, 2, 32])
                    dsk = AA("dsk", [128, 32])
                    smk = AA("smk", [128, 257])
                    seltp = AA("seltp", [128, 8, 240], BF16)
                    tabA = AA("tabA", [128, 2, 2, 128], BF16)
                    tabB = AA("tabB", [128, 2, 2, 128], BF16)
                    otmp = AA("otmp", [128, 4, 128])
                    tabM = AA("tabM", [128, 2, 2, 2, 128], BF16)
                    hmk = AA("hmk", [128, 2])
                    S.dma('sp', hmk[:], halfmask[:, :], (), ['hmk'])
                    S.dma('sp', Cc[:], c2[l], (), ['Cc'])
                    S.dma('sp', dsk[:], dskT[l], (), ['dsk'])
                    S.dma('sp', smk[:], smask[:, 0, :], (), ['smk'])
                    S.dma('sp', seltp[:], seltpad[:, :, :], (), ['seltp'])
                    with T("wssm", [128, 8, 512], BF16) as wssm, T("uf", [128, 4, NT], BF16) as uf, T("selp", [128, 8, 240], BF16) as selp:
                        S.dma('sp', selp[:], selpad[:, :, :], (), ['selp'])
                        load_w(wssm, w_in[l][:, OFF_SSM:OFF_SSM + 512], 8, 'wssm')
                        for t in range(NTT):
                            sl = slice(t * TT, (t + 1) * TT)
                            for oc in range(4):
                                pu, pku = PS()
                                for kc in range(8):
                                    mm(pu[:], wssm[:, kc, oc * 128:(oc + 1) * 128], xn[:, kc, sl], kc == 0, kc == 7,
                                       ['wssm', ('xn', kc, t)], [pku])
                                if oc % 2 == 0:
                                    act(uf[:, oc, sl], pu[:], AF.Copy, [pku], [('uf', oc)])
                                else:
                                    cp(uf[:, oc, sl], pu[:], [pku], [('uf', oc)])
                        for g2 in range(16):
                            pu, pku = PS()
                            for gi in range(2):
                                g = g2 * 2 + gi
                                for r in range(8):
                                    mm(pu[:, gi * 256:(gi + 1) * 256], selp[:, g % 8, 112 - 16 * r:240 - 16 * r], uf[:, g // 8, r:NT:8],
                                       r == 0, r == 7, ['selp', ('uf', g // 8)], [pku])
                            if g2 % 2 == 0:
                                act(U[:, g2 * 2:g2 * 2 + 2, :], pu[:].rearrange("p (a b) -> p a b", a=2), AF.Copy, [pku], [('U', g2)])
                            else:
                                cp(U[:, g2 * 2:g2 * 2 + 2, :], pu[:].rearrange("p (a b) -> p a b", a=2), [pku], [('U', g2)])
                        S.barrier()
                    S.mute = ASTOP < 2
                    with T("lam", [128, 2, 32]) as lam, T("ldt", [128, 32]) as ldt, T("braw", [128, 2, 32, 16]) as braw, \
                            T("ramp", [128, 32, 24]) as ramp, T("w1", [128, 32, 24]) as w1, T("w2", [128, 32, 24]) as w2, \
                            T("w3", [128, 32, 24]) as w3, T("w4", [128, 32, 24]) as w4, T("wi", [128, 32, 24], I32) as wi, \
                            T("v1", [128, 8, 32]) as v1, T("bt", [128, 2, 32, 16]) as bt:
                        S.dma('sp', lam[:], lam2[l], (), ['lam'])
                        S.dma('sp', ldt[:], logdt2[l], (), ['ldt'])
                        S.dma('sp', braw[:], b2[l], (), ['braw'])
                        S.dma('sp', ramp[:], ramp32[:, :, :], (), ['ramp'])
                        K_ = ['tbl']
                        def bc(ap2):
                            return ap2.unsqueeze(2).broadcast_to([128, 32, 24])
                        act(ldt[:], ldt[:], AF.Exp, ['ldt'], K_)
                        tt(v1[:, 0, :], lam[:, 0, :], ldt[:], ALU.mult, ['lam'] + K_, K_)
                        tt(v1[:, 1, :], lam[:, 1, :], ldt[:], ALU.mult, ['lam'] + K_, K_)
                        tt(w1[:], bc(v1[:, 1, :]), ramp[:], ALU.mult, ['ramp'] + K_, K_)
                        tt(w2[:], bc(v1[:, 0, :]), ramp[:], ALU.mult, ['ramp'] + K_, K_)
                        act(w2[:], w2[:], AF.Exp, K_, K_)
                        ts(w1[:], w1[:], 1.0 / (2.0 * math.pi), None, ALU.mult, None, K_, K_)
                        cp(wi[:], w1[:], K_, K_)
                        cp(w3[:], wi[:], K_, K_)
                        tt(w1[:], w1[:], w3[:], ALU.subtract, K_, K_)
                        act(w3[:], w1[:], AF.Sin, K_, K_, scale=math.pi)
                        act(w4[:], w1[:], AF.Sin, K_, K_, scale=math.pi / 2.0)
                        tt(w4[:], w4[:], w4[:], ALU.mult, K_, K_)
                        ts(w4[:], w4[:], -2.0, 1.0, ALU.mult, ALU.add, K_, K_)
                        tt(w4[:], w4[:], w3[:], ALU.mult, K_, K_)
                        ts(w4[:], w4[:], 2.0, None, ALU.mult, None, K_, K_)
                        tt(w3[:], w3[:], w3[:], ALU.mult, K_, K_)
                        ts(w3[:], w3[:], -2.0, 1.0, ALU.mult, ALU.add, K_, K_)
                        tt(Qp[:, 0], w2[:], w3[:], ALU.mult, K_, K_)
                        tt(Qp[:, 1], w2[:], w4[:], ALU.mult, K_, K_)
                        for ri in range(2):
                            cp(v1[:, 2 + ri, 0:16], Qp[:, ri, 0:16, 8], K_, K_)
                            cp(v1[:, 2 + ri, 16:32], Qp[:, ri, 16:32, 1], K_, K_)
                        cp(A1x[:, 0, 0:16], Qp[:, 0, 0:16, 15], K_, K_)
                        cp(A1x[:, 0, 16:32], Qp[:, 0, 16:32, 8], K_, K_)
                        cp(A1x[:, 1, :], A1x[:, 0, :], K_, K_)
                        cp(A2x[:, 1, 0:16], Qp[:, 1, 0:16, 15], K_, K_)
                        cp(A2x[:, 1, 16:32], Qp[:, 1, 16:32, 8], K_, K_)
                        ts(A2x[:, 0, :], A2x[:, 1, :], -1.0, None, ALU.mult, None, K_, K_)
                        tt(v1[:, 4, :], lam[:, 0, :], lam[:, 0, :], ALU.mult, K_, K_)
                        tt(v1[:, 5, :], lam[:, 1, :], lam[:, 1, :], ALU.mult, K_, K_)
                        tt(v1[:, 4, :], v1[:, 4, :], v1[:, 5, :], ALU.add, K_, K_)
                        recip(v1[:, 4, :], v1[:, 4, :], K_, K_)
                        ts(v1[:, 2, :], v1[:, 2, :], -1.0, None, ALU.add, None, K_, K_)
                        tt(v1[:, 5, :], v1[:, 2, :], lam[:, 0, :], ALU.mult, K_, K_)
                        tt(v1[:, 6, :], v1[:, 3, :], lam[:, 1, :], ALU.mult, K_, K_)
                        tt(v1[:, 5, :], v1[:, 5, :], v1[:, 6, :], ALU.add, K_, K_)
                        tt(v1[:, 5, :], v1[:, 5, :], v1[:, 4, :], ALU.mult, K_, K_)
                        tt(v1[:, 6, :], v1[:, 3, :], lam[:, 0, :], ALU.mult, K_, K_)
                        tt(v1[:, 7, :], v1[:, 2, :], lam[:, 1, :], ALU.mult, K_, K_)
                        tt(v1[:, 6, :], v1[:, 6, :], v1[:, 7, :], ALU.subtract, K_, K_)
                        tt(v1[:, 6, :], v1[:, 6, :], v1[:, 4, :], ALU.mult, K_, K_)
                        def bc16(ap2):
                            return ap2.unsqueeze(2).broadcast_to([128, 32, 16])
                        tt(bt[:, 0], bc16(v1[:, 5, :]), braw[:, 0], ALU.mult, ['braw'] + K_, K_)
                        tt(bt[:, 1], bc16(v1[:, 6, :]), braw[:, 1], ALU.mult, ['braw'] + K_, K_)
                        tt(Bb[:, 0], bt[:, 0], bt[:, 1], ALU.subtract, K_, K_)
                        tt(bt[:, 0], bc16(v1[:, 5, :]), braw[:, 1], ALU.mult, ['braw'] + K_, K_)
                        tt(bt[:, 1], bc16(v1[:, 6, :]), braw[:, 0], ALU.mult, ['braw'] + K_, K_)
                        tt(Bb[:, 1], bt[:, 0], bt[:, 1], ALU.add, K_, K_)
                        S.barrier()

                    def outer(dst, dkey, dp, kind, X, xkey, neg_im):
                        q_re = Qp[:, 0, dp, kind * 8:(kind + 1) * 8].unsqueeze(2).broadcast_to([128, 8, 16])
                        q_im = Qp[:, 1, dp, kind * 8:(kind + 1) * 8].unsqueeze(2).broadcast_to([128, 8, 16])
                        x_re = X[:, 0, dp, :].unsqueeze(1).broadcast_to([128, 8, 16])
                        x_im = X[:, 1, dp, :].unsqueeze(1).broadcast_to([128, 8, 16])
                        o = [otmp[:, i, :].rearrange("p (a b) -> p a b", a=8) for i in range(4)]
                        d_re = dst[:, 0, :].rearrange("p (a b) -> p a b", a=8)
                        d_im = dst[:, 1, :].rearrange("p (a b) -> p a b", a=8)
                        tt(o[0], q_re, x_re, ALU.mult, ['tbl', xkey], [('otmp', 0)])
                        tt(o[1], q_im, x_im, ALU.mult, ['tbl', xkey], [('otmp', 1)])
                        tt(d_re, o[0], o[1], ALU.subtract, [('otmp', 0), ('otmp', 1)], [dkey])
                        tt(o[2], q_re, x_im, ALU.mult, ['tbl', xkey], [('otmp', 2)])
                        tt(o[3], q_im, x_re, ALU.mult, ['tbl', xkey], [('otmp', 3)])
                        if neg_im:
                            stt(d_im, o[2], -1.0, o[3], ALU.mult, ALU.subtract, [('otmp', 2), ('otmp', 3)], [dkey])
                        else:
                            tt(d_im, o[2], o[3], ALU.add, [('otmp', 2), ('otmp', 3)], [dkey])

                    def maskE(d):
                        for ge in range(2):
                            ts(tabM[:, ge, d].rearrange("p a b -> p (a b)"), tabB[:, d].rearrange("p a b -> p (a b)"), hmk[:, ge:ge + 1], None,
                               ALU.mult, None, [('tabB', d), 'hmk'], [('tabM', d)])

                    S.mute = ASTOP < 3
                    with T("tzt", [128, 2, 256]) as tzt, T("tzm", [128, 2, 256]) as tzm:
                        S.dma('sp', tzm[:], tzmask[:, :, 0:256], (), ['tzm'])
                        for pair in range(16):
                            pT = []
                            for d in range(2):
                                dp = d * 16 + pair
                                outer(tabA[:, d], ('tabA', d), dp, 2, Bb, 'tbl', False)
                                outer(tabB[:, d], ('tabB', d), dp, 1, Cc, 'Cc', True)
                                pt_, pkt_ = PS()
                                maskE(d)
                                for ge in range(2 if A3 >= 2 else 0):
                                    mm(pt_[:, ge * 128:(ge + 1) * 128], tabA[:, d, 0, :], tabM[:, ge, d, 0, :], True, False,
                                       [('tabA', d), ('tabM', d)], [pkt_])
                                    mm(pt_[:, ge * 128:(ge + 1) * 128], tabA[:, d, 1, :], tabM[:, ge, d, 1, :], False, True,
                                       [('tabA', d), ('tabM', d)], [pkt_])
                                pT.append((pt_, pkt_))
                            if A3 < 3:
                                continue
                            tt(tzt[:, 0, :], pT[0][0][:, 0:256], tzm[:, 0, :], ALU.mult, [pT[0][1], 'tzm'], [('tzt', 0)])
                            tt(tzt[:, 1, :], pT[1][0][:, 0:256], tzm[:, 1, :], ALU.mult, [pT[1][1], 'tzm'], [('tzt', 1)])
                            tt(tzt[:, 0, :], tzt[:, 0, :], tzt[:, 1, :], ALU.add, [('tzt', 0), ('tzt', 1)], [('tzt', 0)])
                            for ge in range(2):
                                g = 2 * pair + ge
                                stt(Tz[:, g, :], ident[:], dsk[:, g:g + 1], tzt[:, 0, ge * 128:(ge + 1) * 128], ALU.mult, ALU.add,
                                    ['ident', 'dsk', ('tzt', 0)], [('Tz', g)])
                        S.barrier()

                    S.mute = ASTOP < 4
                    with T("SH", [128, 2, 32, 257], BF16) as SH, T("Zs", [128, 2, 32]) as Zs, T("Vs", [128, 2, 32]) as Vs, \
                            T("P1", [128, 2, 32]) as P1, T("P2", [128, 2, 32]) as P2, T("fin", [128, 8, 2, 2, 16]) as fin, \
                            T("Wt", [128, 2, 2, 128], BF16) as Wt, T("Ysb", [128, 8, 256], BF16) as Ysb, \
                            T("gl", [128, 2, TT]) as gl:
                        S.dma('sp', Zs[:], h0[l][:, 0:64], (), ['Zs'])
                        cp(SH[:, :, :, 0], Zs[:], ['Zs'], [('SH', 0)])
                        for pair in range(16):
                            for d in range(2):
                                dp = d * 16 + pair
                                outer(tabA[:, d], ('tabA', d), dp, 0, Bb, 'tbl', False)
                                ptr_, pktr = PS()
                                ptb = ptr_[:].bitcast(BF16)
                                tr(ptb[:, 0:128], tabA[:, d, 0, :], ident_bf[:], [('tabA', d), 'ident_bf'], [pktr])
                                tr(ptb[:, 128:256], tabA[:, d, 1, :], ident_bf[:], [('tabA', d), 'ident_bf'], [pktr])
                                cp(Wt[:, d, :, :], ptb[:, 0:256].rearrange("p (a b) -> p a b", a=2), [pktr], [('Wt', d)])
                                ps_, pks_ = PS()
                                for ri in range(2):
                                    for ge in range(2):
                                        mm(ps_[ge * 64:(ge + 1) * 64, ri * 256:(ri + 1) * 256], Wt[:, d, ri, ge * 64:(ge + 1) * 64],
                                           U[:, 2 * pair + ge, :], True, True, [('Wt', d), ('U', pair)], [pks_])
                                src_ = ps_[:].rearrange("p (a b) -> p a b", a=2)
                                if d == 0:
                                    cp(SH[:, :, dp, 1:257], src_, [pks_], [('SHs', dp)])
                                else:
                                    cp(SH[:, :, dp, 256:0:-1], src_, [pks_], [('SHs', dp)])
                        S.barrier()
                        S.mute = ASTOP < 5
                        SK = ['scan']
                        for i in range(256):
                            if i == 0:
                                tt(P1[:], Zs[:], A1x[:], ALU.mult, ['Zs'], ['sP1'])
                                tt(P2[:], Zs[:, ::-1, :], A2x[:], ALU.mult, ['Zs'], ['sP2'])
                            else:
                                stt(P1[:], Vs[:], smk[:, i:i + 1], A1x[:], ALU.mult, ALU.mult, ['sV', 'smk'], ['sP1'])
                                stt(P2[:], Vs[:, ::-1, :], smk[:, i:i + 1], A2x[:], ALU.mult, ALU.mult, ['sV', 'smk'], ['sP2'])
                            tt(P1[:], P1[:], SH[:, :, :, i + 1], ALU.add, ['sP1'], ['sP1'])
                            tt(Vs[:], P1[:], P2[:], ALU.add, ['sP1', 'sP2'], ['sV'])
                            if (i + 1) % 32 == 0:
                                q = i // 32
                                cp(fin[:, q, :, 0, :], Vs[:, :, 0:16], ['sV'], ['fin'])
                                cp(fin[:, 7 - q, :, 1, :], Vs[:, :, 16:32], ['sV'], ['fin'])
                            act(SH[:, :, :, i + 1], Vs[:], AF.Copy, ['sV', 'smk'], [('SHo', i)], scale=smk[:, i + 1:i + 2])
                        S.dma('sp', ssm_out[l], fin[:].rearrange("p a b c d -> p (a b c d)"), ['fin'], ['ssm_out'])
                        S.barrier()
                        S.mute = ASTOP < 6
                        for oc in range(4):
                            for pl in range(4):
                                pair = oc * 4 + pl
                                py, pky = PS()
                                for d in range(2):
                                    outer(tabB[:, d], ('tabB', d), d * 16 + pair, 1, Cc, 'Cc', True)
                                    maskE(d)
                                for ge in range(2):
                                    g = 2 * pair + ge
                                    o_ = py[:, ge * 256:(ge + 1) * 256]
                                    mm(o_, Tz[:, g, :], U[:, g, :], True, False, [('Tz', g), ('U', pair)], [pky])
                                    mm(o_, tabM[:, ge, 0, 0, :], SH[:, 0, pair, 0:256], False, False, [('tabM', 0), 'scan'], [pky])
                                    mm(o_, tabM[:, ge, 0, 1, :], SH[:, 1, pair, 0:256], False, False, [('tabM', 0), 'scan'], [pky])
                                    mm(o_, tabM[:, ge, 1, 0, :], SH[:, 0, 16 + pair, 255::-1], False, False, [('tabM', 1), 'scan'], [pky])
                                    mm(o_, tabM[:, ge, 1, 1, :], SH[:, 1, 16 + pair, 255::-1], False, True, [('tabM', 1), 'scan'], [pky])
                                cp(Ysb[:, 2 * pl:2 * pl + 2, :], py[:].rearrange("p (a b) -> p a b", a=2), [pky], [('Ysb', pl)])
                            yv = ybuf[:, oc, :].rearrange("p (j r) -> p r j", r=8)
                            for r2 in range(4):
                                pz, pkz = PS()
                                for ri in range(2):
                                    r = r2 * 2 + ri
                                    for gl_ in range(8):
                                        mm(pz[:, ri * 256:(ri + 1) * 256], seltp[:, r, 112 - 16 * gl_:240 - 16 * gl_], Ysb[:, gl_, :],
                                           gl_ == 0, gl_ == 7, ['seltp', ('Ysb', gl_ // 2)], [pkz])
                                b = 0
                                act(gl[:, 2 * b, :], pz[:], AF.Copy, [pkz], [('gl', 2 * b)])
                                tt(gl[:, 2 * b + 1, :], gl[:, 2 * b, :], gl[:, 2 * b, :], ALU.mult, [('gl', 2 * b)], [('gl', 2 * b + 1)])
                                ts(gl[:, 2 * b + 1, :], gl[:, 2 * b + 1, :], 0.044715, 1.0, ALU.mult, ALU.add, [('gl', 2 * b + 1)], [('gl', 2 * b + 1)])
                                tt(gl[:, 2 * b + 1, :], gl[:, 2 * b + 1, :], gl[:, 2 * b, :], ALU.mult, [('gl', 2 * b), ('gl', 2 * b + 1)], [('gl', 2 * b + 1)])
                                act(gl[:, 2 * b + 1, :], gl[:, 2 * b + 1, :], AF.Sigmoid, [('gl', 2 * b + 1)], [('gl', 2 * b + 1)], scale=1.5957691216057308)
                                tt(yv[:, r2 * 2:r2 * 2 + 2, :], gl[:, 2 * b, :].rearrange("p (a b) -> p a b", a=2),
                                   gl[:, 2 * b + 1, :].rearrange("p (a b) -> p a b", a=2), ALU.mult,
                                   [('gl', 2 * b), ('gl', 2 * b + 1)], [('y', oc, 0), ('y', oc, 1), ('y', oc, 2), ('y', oc, 3)])
                        S.barrier()
                S.mute = False
                with T("wglu", [128, 4, BW], BF16) as wglu, T("gsg", [128, 4, TT]) as gsg:
                    load_w(wglu, w_glu[l], 4, 'wglu')
                    for t in range(NTT):
                        sl = slice(t * TT, (t + 1) * TT)
                        for oc in range(4):
                            pg, pkg = PS()
                            for kc in range(4):
                                mm(pg[:], wglu[:, kc, oc * 128:(oc + 1) * 128], ybuf[:, kc, sl], kc == 0, kc == 3,
                                   ['wglu', ('y', kc, t)], [pkg])
                            act(gsg[:, oc, :], pg[:], AF.Sigmoid, [pkg], [('gsg', oc)])
                        for oc in range(4):
                            tt(ybuf[:, oc, sl], ybuf[:, oc, sl], gsg[:, oc, :], ALU.mult, [('y', oc, t), ('gsg', oc)], [('y', oc, t)])
                    S.barrier()
                dump("yssm%d" % l, ybuf[:], [128, 4, NT], BF16, ('y', 0, 0))
                run_epi(0, ybuf)
            S.barrier()
            dump("xmid%d" % l, x[:], [128, 8, NT], F32, ('x', 0, 0))

            rmsnorm_mod(l, None, 16, 24)
            for half in range(2 if not int(os.environ.get('SKF', '0')) else 0):
                with T("hT", [128, 22, 1024], BF16) as hT, \
                        T("wfi", [128, 2, 8, 256], BF16) as wfi, \
                        T("wfo", [128, 2, 22, 128], BF16) as wfo, \
                        T("fsg", [128, 2, TT], F32) as fsg:
                    for c in range(22):
                        b = c % 2
                        S.dma('pool', wfi[:, b, :, 0:128], w_ffn_in[l][:, c * 128:(c + 1) * 128].rearrange("(kc p) n -> p kc n", p=128),
                              (), [('wfi', b)])
                        S.dma('pool', wfi[:, b, :, 128:256],
                              w_ffn_in[l][:, DFF + c * 128:DFF + (c + 1) * 128].rearrange("(kc p) n -> p kc n", p=128),
                              (), [('wfi', b)])
                        for tt_ in range(2):
                            t = half * 2 + tt_
                            sl = slice(t * TT, (t + 1) * TT)
                            pg, pkg = PS()
                            for kc in range(8):
                                mm(pg[:], wfi[:, b, kc, 0:128], xn[:, kc, sl], kc == 0, kc == 7, [('wfi', b), ('xn', kc, t)], [pkg])
                            pu, pku = PS()
                            for kc in range(8):
                                mm(pu[:], wfi[:, b, kc, 128:256], xn[:, kc, sl], kc == 0, kc == 7, [('wfi', b), ('xn', kc, t)], [pku])
                            act(fsg[:, tt_, :], pg[:], AF.Silu, [pkg], [('fsg', tt_)])
                            tt(hT[:, c, tt_ * TT:(tt_ + 1) * TT], pu[:], fsg[:, tt_, :], ALU.mult, [pku, ('fsg', tt_)], [('hT', c, tt_)])
                    for tt_ in range(2):
                        t = half * 2 + tt_
                        sl = slice(t * TT, (t + 1) * TT)
                        pcs = [PS() for _ in range(8)] if False else None
                    for oc in range(8):
                        pcs = [PS(), PS()]
                        b = oc % 2
                        S.dma('pool', wfo[:, b, :, :],
                              w_ffn_out[l][:, oc * 128:(oc + 1) * 128].rearrange("(kc p) n -> p kc n", p=128),
                              (), [('wfo', b)])
                        for kc in range(22):
                            for tt_ in range(2):
                                mm(pcs[tt_][0][:], wfo[:, b, kc, :], hT[:, kc, tt_ * TT:(tt_ + 1) * TT],
                                   kc == 0, kc == 21, [('wfo', b), ('hT', kc, tt_)], [pcs[tt_][1]])
                        for tt_ in range(2):
                            t = half * 2 + tt_
                            sl = slice(t * TT, (t + 1) * TT)
                            stt(x[:, oc, sl], pcs[tt_][0][:], mod[:, 40 + oc:41 + oc], x[:, oc, sl], ALU.mult, ALU.add,
                                [pcs[tt_][1], 'mod', ('x', oc, t)], [('x', oc, t)])
                    S.barrier()
            dump("xout%d" % l, x[:], [128, 8, NT], F32, ('x', 0, 0))

        with T("ytm", [128, 2, D], F32) as ytm:
            for n in range(NT // 128):
                b = n % 2
                for hh in range(2):
                    ps, pk = PS()
                    for q in range(4):
                        oc = hh * 4 + q
                        tr(ps[:, q * 128:(q + 1) * 128], x[:, oc, n * 128:(n + 1) * 128], ident[:],
                           [('x', oc, n // 4), 'ident'], [pk])
                    if hh == 0:
                        cp(ytm[:, b, 0:512], ps[:], [pk], [('ytm', b)])
                    else:
                        act(ytm[:, b, 512:1024], ps[:], AF.Copy, [pk], [('ytm', b)])
                S.dma('sp', y_out[n * 128:(n + 1) * 128, :], ytm[:, b, :], [('ytm', b)], ['y_out'])
        S.barrier()
    return nc, dbg_out, S


def _bf(a):
    return np.ascontiguousarray(np.asarray(a, dtype=np.float32)).astype(NPBF)


def _consts(nseq):
    Ls = NT // nseq
    t = np.arange(NT)
    seq = t // Ls
    c = {}
    C = np.ones((128, NT), np.float64)
    Sg = np.zeros((128, NT), np.float64)
    if nseq == 1:
        GRID_W = 64
        row = (t // GRID_W).astype(np.float32)
        col = (t % GRID_W).astype(np.float32)
        inv = (np.float32(10000.0) ** (-np.arange(8, dtype=np.float32) / np.float32(8))).astype(np.float32)
        ang = np.concatenate([row[:, None] * inv, col[:, None] * inv], axis=-1).astype(np.float32)
        cs, sn = np.cos(ang.astype(np.float64)), np.sin(ang.astype(np.float64))
        for i in range(16):
            C[64 + 2 * i] = cs[:, i]
            C[64 + 2 * i + 1] = cs[:, i]
            Sg[64 + 2 * i] = -sn[:, i]
            Sg[64 + 2 * i + 1] = sn[:, i]
    c['ropeC'] = _bf(C)
    c['ropeS'] = _bf(Sg)
    qi = np.zeros((8, NT), np.float32)
    qi[seq, t] = 1.0
    ki = np.zeros((8, NK), np.float32)
    if nseq > 1:
        ki[:, :NT] = -BIG
        ki[seq, t] = 0.0
        ki[:, NT:] = -BIG
    c['qind'] = _bf(qi)
    c['kind'] = _bf(ki)
    nl = (t % Ls).astype(np.float64)
    same = (seq[:, None] == seq[None, :])
    ph = 2.0 * np.pi * ((nl[:, None] * nl[None, :]) % Ls) / Ls
    c['dftC'] = _bf(np.where(same, np.cos(ph) / np.sqrt(Ls), 0.0))
    c['dftS'] = _bf(np.where(same, -np.sin(ph) / np.sqrt(Ls), 0.0))
    cc = np.arange(128, dtype=np.float64)
    ph2 = 2.0 * np.pi * ((cc[:, None] * cc[None, :]) % 128) / 128.0
    c['ccsc'] = _bf(np.concatenate([np.cos(ph2), np.sin(ph2)], axis=1) / np.sqrt(128.0))
    mL = (t % Ls != 0).astype(np.float32)
    mR = (t % Ls != Ls - 1).astype(np.float32)
    c['seqflag'] = np.full((128, 1), 1.0 if nseq == 1 else 0.0, np.float32)
    nchunk = NT // 8
    cps = nchunk // nseq
    mf = np.ones(257, np.float32)
    mb = np.ones(257, np.float32)
    for j in range(nchunk):
        if (j + 1) % cps == 0 and (j + 1) < nchunk:
            mf[j + 1] = 0.0
            mb[j + 1] = 0.0
    c['smask'] = np.ascontiguousarray(np.broadcast_to(np.stack([mf, mb])[None], (128, 2, 257))).astype(np.float32)
    r = np.arange(8, dtype=np.float32)
    kr = np.zeros((2, 3, 8), np.float32)
    kr[0, 0] = 7 - r; kr[0, 1] = r + 1; kr[0, 2] = -(1 + r)
    kr[1, 0] = r;     kr[1, 1] = 8 - r; kr[1, 2] = r - 8
    c['ramp32'] = np.ascontiguousarray(np.broadcast_to(np.repeat(kr.reshape(2, 1, 24), 16, axis=1).reshape(1, 32, 24), (128, 32, 24))).astype(np.float32)
    rr = np.arange(128) // 16
    mF = (rr[None, :] >= rr[:, None]).astype(np.float32)
    mB = (rr[None, :] <= rr[:, None]).astype(np.float32)
    c['tzmask'] = np.ascontiguousarray(np.stack([np.tile(mF, (1, 4)), np.tile(mB, (1, 4))], axis=1)).astype(np.float32)
    sp = np.zeros((128, 8, 240), np.float32)
    stp = np.zeros((128, 8, 240), np.float32)
    for gl in range(8):
        for cc_ in range(16):
            sp[gl * 16 + cc_, gl, 112 + cc_] = 1.0
    for r_ in range(8):
        for cc_ in range(16):
            stp[r_ * 16 + cc_, r_, 112 + cc_] = 1.0
    c['selpad'] = _bf(sp)
    c['seltpad'] = _bf(stp)
    sh = np.zeros((32, 2, 96), np.float32)
    for d in range(32):
        sh[d, 0, 64 + d] = 1.0
        sh[d ^ 1, 1, 64 + d] = 1.0
    c['shiftm'] = _bf(sh)
    c['identf'] = np.eye(128, dtype=np.float32)
    hm = np.zeros((128, 2), np.float32); hm[:64, 0] = 1.0; hm[64:, 1] = 1.0
    c['halfmask'] = hm
    return c


def _colT(v, n):
    return np.ascontiguousarray(np.asarray(v, np.float32).reshape(n, 128).T)


def _shared(inp):
    f = lambda a: np.ascontiguousarray(np.asarray(a, np.float32))
    s = {}
    s['w_ada'] = f(inp['w_ada'])
    s['b_adaT'] = np.stack([_colT(inp['b_ada'][l], 48) for l in range(NL)])
    s['nmgT'] = np.stack([_colT(inp['norm_mix_g'][l], 8) for l in range(NL)])
    s['nfgT'] = np.stack([_colT(inp['norm_ffn_g'][l], 8) for l in range(NL)])
    s['w_in'] = f(inp['w_in'])
    s['qagT'] = np.stack([_colT(inp['q_a_norm_g'][l], 3) for l in range(NL)])
    s['kvag'] = np.stack([_colT(inp['kv_a_norm_g'][l], 2) for l in range(NL)])
    wuq = np.asarray(inp['w_uq'], np.float32)
    s['w_uq'] = f(wuq)
    perm = np.arange(NH * QK)
    dd = perm % QK
    perm = np.where(dd >= 64, (perm // QK) * QK + 64 + ((dd - 64) ^ 1), perm)
    s['w_uq_sw'] = f(wuq[:, :, perm])
    wukv = np.asarray(inp['w_ukv'], np.float32).reshape(NL, KVR, NH, 128)
    wuk = np.zeros((NL, KVR, NH, QK), np.float32)
    wuk[..., :64] = wukv[..., :64]
    s['w_uk'] = f(wuk.reshape(NL, KVR, NH * QK))
    s['w_uv'] = f(wukv[..., 64:].reshape(NL, KVR, NH * 64))
    qg = np.asarray(inp['q_norm_g'], np.float32)
    kg = np.asarray(inp['k_norm_g'], np.float32)
    swp = np.arange(QK)
    swp = np.where(swp >= 64, 64 + ((swp - 64) ^ 1), swp)
    qng = np.zeros((NL, 128, 4), np.float32)
    qng[:, :QK, 0] = qg
    qng[:, :QK, 1] = qg[:, swp]
    qng[:, :QK, 2] = kg
    qng[:, :QK, 3] = kg[:, swp]
    s['qng'] = qng
    def gp(a):
        a = np.asarray(a, np.float32)
        rest = a.shape[4:]
        a = a.reshape((NL, 2, 16, 2, 64) + rest)
        perm = (0, 3, 4, 1, 2) + tuple(range(5, 5 + len(rest)))
        return np.ascontiguousarray(a.transpose(perm).reshape((NL, 128, 32) + rest))
    s['lam2'] = np.ascontiguousarray(np.stack([gp(inp['ssm_lam_re']), gp(inp['ssm_lam_im'])], axis=2))
    ld = np.broadcast_to(np.asarray(inp['ssm_log_dt'], np.float32)[:, :, :, None], (NL, 2, 32, 64))
    s['logdt2'] = gp(ld)
    s['b2'] = np.ascontiguousarray(np.stack([gp(inp['ssm_b_re']), gp(inp['ssm_b_im'])], axis=2))
    cr_ = np.asarray(inp['ssm_c_re'], np.float32).transpose(0, 1, 2, 4, 3)
    ci_ = np.asarray(inp['ssm_c_im'], np.float32).transpose(0, 1, 2, 4, 3)
    s['c2'] = np.ascontiguousarray(np.stack([gp(cr_), gp(ci_)], axis=2))
    dsk = np.asarray(inp['ssm_d'], np.float32).reshape(NL, 32, 16)
    s['dskT'] = f(np.broadcast_to(dsk.transpose(0, 2, 1)[:, None, :, :], (NL, 8, 16, 32)).reshape(NL, 128, 32))
    s['w_glu'] = f(inp['w_glu'])
    cw = np.asarray(inp['conv_w'], np.float32)
    s['convwT'] = f(cw.reshape(NL, 3, 4, 128).transpose(0, 3, 2, 1))
    s['w_branch'] = f(inp['w_branch'])
    s['w_gate'] = f(inp['w_gate'])
    s['b_gateT'] = np.stack([_colT(inp['b_gate'][l], 32) for l in range(NL)])
    s['w_out'] = f(inp['w_out'])
    s['w_ffn_in'] = f(inp['w_ffn_in'])
    s['w_ffn_out'] = f(inp['w_ffn_out'])
    return s


def _core_map(inp, shared, consts, core):
    m = dict(shared)
    if core < 2:
        b = core
        m.update(consts[1])
        m['xin'] = np.ascontiguousarray(np.asarray(inp['x_sample'][b], np.float32))
        m['condT'] = _colT(inp['c'][b], 8)
        m['cache_ckv'] = np.ascontiguousarray(np.asarray(inp['cache_ckv'][b], np.float32))
        m['cache_kpe'] = np.ascontiguousarray(np.asarray(inp['cache_kpe'][b], np.float32))
        st = np.asarray(inp['state_ssm'][b], np.float32)
        st = st.reshape(NL, 2, 16, 2, 64, 2)
        m['h0'] = np.ascontiguousarray(st.transpose(0, 3, 4, 5, 1, 2).reshape(NL, 128, 64))
        h0 = np.zeros((NL, 128, 128), np.float32)
        h0[:, :, :64] = m['h0']
        m['h0'] = h0
    else:
        pc = (core - 2) % 4
        m.update(consts[8])
        m['xin'] = np.ascontiguousarray(np.asarray(inp['x_prompt'][pc * 8:(pc + 1) * 8], np.float32).reshape(NT, D))
        m['condT'] = _colT(inp['c_ctx'], 8)
        m['cache_ckv'] = np.zeros((NL, PAST, KVR), np.float32)
        m['cache_kpe'] = np.zeros((NL, PAST, RD), np.float32)
        m['h0'] = np.zeros((NL, 128, 128), np.float32)
    return m


_CACHE = {}


def kernel(**inputs):
    if 'nc' not in _CACHE:
        _CACHE['nc'] = build()[0]
        _CACHE['consts'] = {1: _consts(1), 8: _consts(8)}
    nc = _CACHE['nc']
    shared = _shared(inputs)
    maps = [_core_map(inputs, shared, _CACHE['consts'], c) for c in range(8)]
    res = run_bass_kernel_spmd(nc, maps, core_ids=list(range(8)))
    R = res.results
    y_s = np.stack([R[b]['y_out'] for b in range(2)]).astype(np.float32)
    y_p = np.concatenate([R[2 + i]['y_out'].reshape(8, 256, D) for i in range(4)]).astype(np.float32)
    ckv = np.concatenate([R[2 + i]['ckv_out'].reshape(NL, 8, 256, KVR).transpose(1, 0, 2, 3) for i in range(4)])
    kpe = np.concatenate([R[2 + i]['kpe_out'].reshape(NL, 8, 256, RD).transpose(1, 0, 2, 3) for i in range(4)])
    ss = []
    for i in range(4):
        a = R[2 + i]['ssm_out'].reshape(NL, 2, 64, 8, 2, 2, 16)
        a = a.transpose(3, 0, 5, 6, 1, 2, 4).reshape(8, NL, 2, 32, 64, 2)
        ss.append(a)
    ssm = np.concatenate(ss)
    return (y_p, y_s, ckv.astype(np.float32), kpe.astype(np.float32), ssm.astype(np.float32))
```
